# Optimizing a Trainium2 kernel written in Bass

```python
import math
import jax, jax.numpy as jnp
from jax import lax
import numpy as np

D_MODEL = 1024
BATCH = 8
SEQ = 2048
DEPTH = 4
DEC_BATCH = 128
DEC_SEQ = 4
PAST_LEN = 16384
PAGE_SIZE = 128

N_MIXERS = 2
N_RET = (DEPTH + 1) // 2
N_HGRN = DEPTH // 2
RET_HEADS = 4
RET_DK = D_MODEL // RET_HEADS
RET_DV = 2 * RET_DK
RET_QK = RET_HEADS * RET_DK
RET_V = RET_HEADS * RET_DV
RET_CHUNK = 128
ROPE_BASE = 10000.0
HG_EXPAND = 128
HG_HEADS = D_MODEL // HG_EXPAND
HG_DK = HG_EXPAND
HG_DV = D_MODEL // HG_HEADS
HG_CHUNK = 16
D_FF = 2816
EPS = 1e-6

kernel_name = "retnet_hgrn2_macaron_hybrid_step"


def rmsnorm(x, gain):
    xf = x.astype(jnp.float32)
    y = xf * lax.rsqrt(jnp.mean(xf * xf, axis=-1, keepdims=True) + EPS)
    return (y * gain.astype(jnp.float32)).astype(x.dtype)


def head_rmsnorm(o, gain):
    y = o * lax.rsqrt(jnp.mean(o * o, axis=-1, keepdims=True) + EPS)
    return y * gain.astype(jnp.float32)[None, :, None, :]


def swiglu_ffn(x, w_up, w_down):
    a, b = jnp.split(x @ w_up, 2, axis=-1)
    return (jax.nn.silu(a) * b) @ w_down


def split_heads(a, n_heads):
    B, T, _ = a.shape
    return a.reshape(B, T, n_heads, -1).transpose(0, 2, 1, 3).astype(jnp.float32)


def merge_heads(o):
    B, H, T, d = o.shape
    return o.transpose(0, 2, 1, 3).reshape(B, T, H * d)


def rotary(x, pos):
    half = x.shape[-1] // 2
    inv_freq = ROPE_BASE ** (-jnp.arange(half, dtype=jnp.float32) / half)
    ang = pos[:, None] * inv_freq[None, :]
    cos, sin = jnp.cos(ang), jnp.sin(ang)
    x1, x2 = x[..., :half], x[..., half:]
    return jnp.concatenate([x1 * cos - x2 * sin, x1 * sin + x2 * cos], axis=-1)


def to_chunks(a, C):
    B, H, T, d = a.shape
    return jnp.moveaxis(a.reshape(B, H, T // C, C, d), 2, 0)


def from_chunks(a):
    n, B, H, C, d = a.shape
    return jnp.moveaxis(a, 0, 2).reshape(B, H, n * C, d)


def retention_chunkwise(q, k, v, S0, log_gamma):
    T = q.shape[2]
    C = math.gcd(T, RET_CHUNK)
    idx = jnp.arange(C, dtype=jnp.float32)
    diff = idx[:, None] - idx[None, :]
    lg = log_gamma[:, None, None]
    decay_mat = jnp.where(diff[None] >= 0, jnp.exp(diff[None] * lg), 0.0)
    q_decay = jnp.exp((idx[None, :] + 1.0) * log_gamma[:, None])
    k_decay = jnp.exp((C - 1.0 - idx[None, :]) * log_gamma[:, None])
    chunk_decay = jnp.exp(C * log_gamma)

    def step(S, inp):
        qc, kc, vc = inp
        scores = jnp.einsum('bhtd,bhsd->bhts', qc, kc) * decay_mat[None]
        o = (jnp.einsum('bhts,bhse->bhte', scores, vc)
             + jnp.einsum('bhtd,bhde->bhte', qc, S) * q_decay[None, :, :, None])
        S = (S * chunk_decay[None, :, None, None]
             + jnp.einsum('bhsd,bhse->bhde', kc * k_decay[None, :, :, None], vc))
        return S, o

    S, o = lax.scan(step, S0, (to_chunks(q, C), to_chunks(k, C), to_chunks(v, C)))
    return from_chunks(o), S


def hgrn2_chunkwise(q, k, v, g, S0):
    T = q.shape[2]
    C = math.gcd(T, HG_CHUNK)
    causal = jnp.tril(jnp.ones((C, C), dtype=bool))

    def step(S, inp):
        qc, kc, vc, gc = inp
        b = jnp.cumsum(gc, axis=2)
        rel = jnp.where(causal[None, None, :, :, None],
                        b[:, :, :, None, :] - b[:, :, None, :, :], -jnp.inf)
        scores = jnp.einsum('bhtd,bhsd,bhtsd->bhts', qc, kc, jnp.exp(rel))
        o = (jnp.einsum('bhts,bhse->bhte', scores, vc)
             + jnp.einsum('bhtd,bhde->bhte', qc * jnp.exp(b), S))
        b_last = b[:, :, -1:, :]
        S = (jnp.exp(b_last)[:, :, 0, :, None] * S
             + jnp.einsum('bhsd,bhse->bhde', kc * jnp.exp(b_last - b), vc))
        return S, o

    S, o = lax.scan(step, S0, (to_chunks(q, C), to_chunks(k, C), to_chunks(v, C), to_chunks(g, C)))
    return from_chunks(o), S


def retention_mixer(x, pos, S0, w_in, gn, w_out):
    h = x @ w_in
    q, k, v, gate = jnp.split(h, [RET_QK, 2 * RET_QK, 2 * RET_QK + RET_V], axis=-1)
    q = rotary(split_heads(q, RET_HEADS), pos)
    k = rotary(split_heads(k, RET_HEADS), pos) * (RET_DK ** -0.5)
    v = split_heads(v, RET_HEADS)
    log_gamma = jnp.log(1.0 - jnp.power(2.0, -5.0 - jnp.arange(RET_HEADS, dtype=jnp.float32)))
    o, S = retention_chunkwise(q, k, v, S0.astype(jnp.float32), log_gamma)
    o = merge_heads(head_rmsnorm(o, gn)).astype(x.dtype)
    return (jax.nn.silu(gate) * o) @ w_out, S


def hgrn2_mixer(x, S0, lb, w_in, gn, w_out):
    h = x @ w_in
    q, z, i, gate = jnp.split(h, 4, axis=-1)
    q = jax.nn.silu(split_heads(q, HG_HEADS))
    z = split_heads(z, HG_HEADS)
    v = split_heads(i, HG_HEADS)
    lbh = lb.astype(jnp.float32).reshape(HG_HEADS, HG_DK)[None, :, None, :]
    g = jnp.logaddexp(jnp.log(lbh), jnp.log1p(-lbh) + jax.nn.log_sigmoid(z))
    k = (1.0 - lbh) * jax.nn.sigmoid(-z)
    o, S = hgrn2_chunkwise(q, k, v, g, S0.astype(jnp.float32))
    o = merge_heads(head_rmsnorm(o, gn)).astype(x.dtype)
    return (jax.nn.silu(gate) * o) @ w_out, S


def trunk(x, pos, ret_state, hg_state, norm_gain, ffn_w_up, ffn_w_down,
          ret_w_in, ret_norm, ret_w_out, hg_w_in, hg_lb_logits, hg_norm, hg_w_out, final_norm):
    lb_all = jnp.cumsum(jax.nn.softmax(hg_lb_logits.astype(jnp.float32), axis=0), axis=0)
    lb_all = lb_all - lb_all[:1]
    new_ret, new_hg = [], []
    for layer in range(DEPTH):
        x = x + 0.5 * swiglu_ffn(rmsnorm(x, norm_gain[layer, 0]), ffn_w_up[layer, 0], ffn_w_down[layer, 0])
        h = rmsnorm(x, norm_gain[layer, 1])
        j = layer // N_MIXERS
        if layer % N_MIXERS == 0:
            y, S = retention_mixer(h, pos, ret_state[j], ret_w_in[j], ret_norm[j], ret_w_out[j])
            new_ret.append(S.astype(ret_state.dtype))
        else:
            y, S = hgrn2_mixer(h, hg_state[j], lb_all[j], hg_w_in[j], hg_norm[j], hg_w_out[j])
            new_hg.append(S.astype(hg_state.dtype))
        x = x + y
        x = x + 0.5 * swiglu_ffn(rmsnorm(x, norm_gain[layer, 2]), ffn_w_up[layer, 1], ffn_w_down[layer, 1])
    return rmsnorm(x, final_norm), jnp.stack(new_ret), jnp.stack(new_hg)


def setup_inputs(seed: int = 0) -> dict:
    key = jax.random.key(seed)
    ks = jax.random.split(key, 16)
    f32 = jnp.float32
    nrm = lambda k, shape, s: jax.random.normal(k, shape, f32) * s
    return {
        "x_prompt": nrm(ks[0], (BATCH, SEQ, D_MODEL), 1.0),
        "x_sample": nrm(ks[1], (DEC_BATCH, DEC_SEQ, D_MODEL), 1.0),
        "state_ret": nrm(ks[2], (N_RET, DEC_BATCH, RET_HEADS, RET_DK, RET_DV), 0.05),
        "state_hgrn": nrm(ks[3], (N_HGRN, DEC_BATCH, HG_HEADS, HG_DK, HG_DV), 0.5),
        "norm_gain": 1.0 + nrm(ks[4], (DEPTH, 3, D_MODEL), 0.02),
        "ffn_w_up": nrm(ks[5], (DEPTH, 2, D_MODEL, 2 * D_FF), D_MODEL ** -0.5),
        "ffn_w_down": nrm(ks[6], (DEPTH, 2, D_FF, D_MODEL), D_FF ** -0.5),
        "ret_w_in": nrm(ks[7], (N_RET, D_MODEL, 2 * RET_QK + 2 * RET_V), D_MODEL ** -0.5),
        "ret_norm": 1.0 + nrm(ks[8], (N_RET, RET_HEADS, RET_DV), 0.02),
        "ret_w_out": nrm(ks[9], (N_RET, RET_V, D_MODEL), RET_V ** -0.5),
        "hg_w_in": nrm(ks[10], (N_HGRN, D_MODEL, 4 * D_MODEL), D_MODEL ** -0.5),
        "hg_lb_logits": nrm(ks[11], (N_HGRN, HG_HEADS * HG_DK), 0.1),
        "hg_norm": 1.0 + nrm(ks[12], (N_HGRN, HG_HEADS, HG_DV), 0.02),
        "hg_w_out": nrm(ks[13], (N_HGRN, D_MODEL, D_MODEL), D_MODEL ** -0.5),
        "final_norm": 1.0 + nrm(ks[14], (D_MODEL,), 0.02),
    }


def reference(x_prompt, x_sample, state_ret, state_hgrn, norm_gain, ffn_w_up, ffn_w_down,
              ret_w_in, ret_norm, ret_w_out, hg_w_in, hg_lb_logits, hg_norm, hg_w_out, final_norm):
    pos_prompt = jnp.arange(SEQ, dtype=jnp.float32)
    pos_sample = PAST_LEN + jnp.arange(DEC_SEQ, dtype=jnp.float32)
    ret0 = jnp.zeros((N_RET, BATCH, RET_HEADS, RET_DK, RET_DV), state_ret.dtype)
    hg0 = jnp.zeros((N_HGRN, BATCH, HG_HEADS, HG_DK, HG_DV), state_hgrn.dtype)
    y_prompt, ret_prompt, hg_prompt = trunk(
        x_prompt, pos_prompt, ret0, hg0, norm_gain, ffn_w_up, ffn_w_down,
        ret_w_in, ret_norm, ret_w_out, hg_w_in, hg_lb_logits, hg_norm, hg_w_out, final_norm)
    y_sample, ret_sample, hg_sample = trunk(
        x_sample, pos_sample, state_ret, state_hgrn, norm_gain, ffn_w_up, ffn_w_down,
        ret_w_in, ret_norm, ret_w_out, hg_w_in, hg_lb_logits, hg_norm, hg_w_out, final_norm)
    return (y_prompt, y_sample, ret_prompt, ret_sample, hg_prompt, hg_sample)
```

```python
import numpy as np
from contextlib import ExitStack
import concourse.bass as bass
import concourse.mybir as mybir
from concourse.bass_utils import run_bass_kernel_spmd

F32 = mybir.dt.float32
BF16 = mybir.dt.bfloat16
AF = mybir.ActivationFunctionType
ALU = mybir.AluOpType

D = 1024
SEQ = 2048
NSS = 16
DEC = 4
NTOK = SEQ + NSS * DEC
DFF = 2816
NF = DFF // 128
EPS = 1e-6
TT = [(0, 512), (512, 512), (1024, 512), (1536, 512), (2048, 64)]


class Op:
    __slots__ = ("eng", "fn", "deps", "pos", "sem", "val", "need_sig", "is_dma", "done")


class Slot:
    def __init__(self, sem):
        self.sem = sem
        self.count = 0


class Prog:
    ENGS = ["pe", "act", "dve", "pool", "sp"]
    ENGOBJ = {"pe": "tensor", "act": "scalar", "dve": "vector", "pool": "gpsimd", "sp": "sync"}

    def __init__(self, nc, stack):
        self.nc = nc
        self.stack = stack
        self.q = {e: [] for e in self.ENGS}
        self.esem = {e: stack.enter_context(nc.semaphore("s_" + e)) for e in ["pe", "act", "dve", "pool"]}
        self.ecount = {e: 0 for e in self.ENGS}
        self.slots = []
        self.nphase = 0

    def slot(self):
        s = Slot(self.stack.enter_context(self.nc.semaphore("d%d" % len(self.slots))))
        self.slots.append(s)
        return s

    def op(self, eng, fn, deps=()):
        o = Op()
        o.eng = eng
        o.fn = fn
        o.deps = [d for d in deps if d is not None]
        o.pos = len(self.q[eng])
        o.is_dma = False
        o.need_sig = False
        o.sem = None
        o.val = None
        o.done = False
        self.q[eng].append(o)
        return o

    def dma(self, eng, fn, slot, deps=()):
        o = self.op(eng, fn, deps)
        o.is_dma = True
        slot.count += 16
        o.sem = slot.sem
        o.val = slot.count
        return o

    def _needs_wait(self, o, d):
        if d.done:
            return False
        if d.is_dma:
            return True
        if d.eng == o.eng:
            if o.eng == "pe":
                return False
            return (o.pos - d.pos) <= 2
        return True

    def flush(self):
        nc = self.nc
        drain_deps = []
        for e in self.ENGS:
            last = {}
            for o in self.q[e]:
                if o.is_dma:
                    last[id(o.sem)] = o
            drain_deps += list(last.values())
        self.op("sp", lambda e: e.nop(), drain_deps)
        for e in self.ENGS:
            for o in self.q[e]:
                for d in o.deps:
                    if not d.is_dma and self._needs_wait(o, d):
                        d.need_sig = True
        for e in self.ENGS:
            c = self.ecount[e]
            for o in self.q[e]:
                if o.is_dma:
                    continue
                if o.need_sig:
                    assert e != "sp"
                    c += 1
                    o.sem = self.esem[e]
                    o.val = c
            self.ecount[e] = c
        self.nphase += 1
        with nc.Block() as block:
            for e in self.ENGS:
                ops = self.q[e]
                if not ops:
                    continue

                def body(eng, ops=ops):
                    waited = {}
                    for o in ops:
                        need = {}
                        for d in o.deps:
                            if not self._needs_wait(o, d):
                                continue
                            key = id(d.sem)
                            if key not in need or need[key][1] < d.val:
                                need[key] = (d.sem, d.val)
                        for key, (sem, val) in need.items():
                            if waited.get(key, 0) >= val:
                                continue
                            eng.wait_ge(sem, val)
                            waited[key] = val
                        ins = o.fn(eng)
                        if o.is_dma:
                            ins.then_inc(o.sem, 16)
                        elif o.need_sig:
                            ins.then_inc(o.sem, 1)

                getattr(block, self.ENGOBJ[e])(body)
        for e in self.ENGS:
            for o in self.q[e]:
                o.done = True
                o.fn = None
                o.deps = None
            self.q[e] = []


class Ctx:
    pass


def dbg_dump(C, name, ap, bufs, ncols, bf=False):
    if not getattr(C, "dbg", None) or name in C.dbg["seen"] or name not in C.dbg["want"]:
        return
    key = "b" if bf else "f"
    off = C.dbg["off"][key]
    C.dbg["off"][key] = off + ncols
    C.dbg["seen"][name] = (key, off, ncols)
    dst = C.dbg[key][:, off:off + ncols]
    bop(C.P, "sp", lambda e: e.dma_start(out=dst, in_=ap), r=bufs, slot=C.dbg["slot"])


class Buf:
    def __init__(self):
        self.w = None
        self.r = []


class PBuf(Buf):
    excl = True


def bop(P, eng, fn, r=(), w=(), slot=None, deps=()):
    xr = [b for b in r if getattr(b, "excl", False)]
    if xr:
        r = [b for b in r if not getattr(b, "excl", False)]
        w = list(w) + [b for b in xr if b not in w]
    d = list(deps)
    for b in r:
        d.append(b.w)
    for b in w:
        d.append(b.w)
        d.extend(b.r)
    o = P.dma(eng, fn, slot, d) if slot is not None else P.op(eng, fn, d)
    for b in r:
        if not o.is_dma:
            b.r = [x for x in b.r if x.is_dma or x.eng != eng]
        b.r.append(o)
    for b in w:
        b.w = o
        b.r = []
    return o


_UID = [0]


def mk_alloc(C, st):
    _UID[0] += 1
    u = _UID[0]
    nc = C.nc
    sb = lambda name, shape, dt: st.enter_context(nc.sbuf_tensor("%s_%d" % (name, u), shape, dt))
    pt = lambda name, shape, dt: st.enter_context(nc.psum_tensor("%s_%d" % (name, u), shape, dt))
    return sb, pt


def emit_norm(C, ph, gcol, out_fn=None):
    P = C.P
    sq, rs, psn = ph.sq, ph.rs, ph.psn
    last = []
    sq_rd = [None, None]
    rs_rd = [None, None]
    ps_rd = None
    for ti, (t0, n) in enumerate(TT):
        b = ti % 2
        a = P.op("act", lambda e, b=b, t0=t0, n=n: e.activation(out=sq[b][:, :, :n], in_=C.xres[:, :, t0:t0 + n], func=AF.Square),
                 [sq_rd[b]])
        mm = None
        for k in range(8):
            mm = P.op("pe", lambda e, b=b, k=k, n=n: e.matmul(psn[:, :n], lhsT=C.ones[:], rhs=sq[b][:, k, :n], start=(k == 0), stop=(k == 7)),
                      [a, ps_rd])
        sq_rd[b] = mm
        v = P.op("act", lambda e, b=b, n=n: e.activation(out=rs[b][:, :n], in_=psn[:, :n], func=AF.Ln, scale=1.0 / D, bias=C.epsc[:, 0:1]),
                 [mm, rs_rd[b]])
        ps_rd = v
        r = P.op("act", lambda e, b=b, n=n: e.activation(out=rs[b][:, :n], in_=rs[b][:, :n], func=AF.Exp, scale=-0.5), [v])
        o = None
        for k in range(8):
            if out_fn is None:
                o = P.op("dve", lambda e, b=b, k=k, t0=t0, n=n: e.scalar_tensor_tensor(
                    out=C.xn[:, k, t0:t0 + n], in0=C.xres[:, k, t0:t0 + n], scalar=C.gains[:, gcol * 8 + k:gcol * 8 + k + 1],
                    in1=rs[b][:, :n], op0=ALU.mult, op1=ALU.mult), [r])
            else:
                o = out_fn(ti, k, t0, n, rs[b], r)
        rs_rd[b] = o
        last.append(o)
    return last


def emit_normphase(C, gcol):
    with ExitStack() as st:
        sb, pt = mk_alloc(C, st)
        ph = Ctx()
        ph.sq = [sb("sq%d" % i, [128, 8, 512], BF16) for i in range(2)]
        ph.rs = [sb("rs%d" % i, [128, 512], F32) for i in range(2)]
        ph.psn = pt("psn", [128, 512], F32)
        emit_norm(C, ph, gcol)
        C.P.flush()


def emit_ffn(C, gcol, wup, wdn):
    P, nc = C.P, C.nc
    emit_normphase(C, gcol)
    with ExitStack() as st:
        sb, pt = mk_alloc(C, st)
        hid = sb("hid", [128, 11, NTOK], BF16)
        wu = [sb("wu%d" % i, [128, 8, 2, 128], BF16) for i in range(3)]
        wd = sb("wd", [128, 11, 1024], BF16)
        sa = [sb("sa%d" % i, [128, 512], F32) for i in range(2)]
        psA = [pt("psA%d" % i, [128, 512], F32) for i in range(2)]
        psB = [pt("psB%d" % i, [128, 512], F32) for i in range(2)]
        psD = [pt("psD%d" % i, [128, 512], F32) for i in range(2)]
        s_wu = [P.slot() for _ in range(3)]
        s_wd = P.slot()

        nlast = [None] * 5

        wu_rd = [None, None, None]
        sa_rd = [None, None]
        psAB_rd = [None, None]
        psD_rd = [None, None]
        wd_rd = None
        cnt = 0
        dcnt = 0
        for half in range(2):
            wdl = P.dma("pool", lambda e, half=half: e.dma_start(out=wd[:].rearrange("p f n -> p (f n)"), in_=wdn[half], max_dma_last_dim=8192),
                        s_wd, [wd_rd])
            mlast = None
            for f in range(11):
                fi = half * 11 + f
                s = fi % 3
                wl = P.dma("pool", lambda e, s=s, fi=fi: e.dma_start(out=wu[s][:].rearrange("p k a c -> p (k a c)"), in_=wup[fi], max_dma_last_dim=8192),
                           s_wu[s], [wu_rd[s]])
                for ti, (t0, n) in enumerate(TT):
                    b = cnt % 2
                    cnt += 1
                    mmA = None
                    for k in range(8):
                        mmA = P.op("pe", lambda e, b=b, s=s, k=k, t0=t0, n=n: e.matmul(psA[b][:, :n], lhsT=wu[s][:, k, 0, :], rhs=C.xn[:, k, t0:t0 + n], start=(k == 0), stop=(k == 7)),
                                   [wl, nlast[ti], psAB_rd[b]])
                    mmB = None
                    for k in range(8):
                        mmB = P.op("pe", lambda e, b=b, s=s, k=k, t0=t0, n=n: e.matmul(psB[b][:, :n], lhsT=wu[s][:, k, 1, :], rhs=C.xn[:, k, t0:t0 + n], start=(k == 0), stop=(k == 7)),
                                   [wl, nlast[ti], psAB_rd[b]])
                    a = P.op("act", lambda e, b=b, n=n: e.activation(out=sa[b][:, :n], in_=psA[b][:, :n], func=AF.Silu), [mmA, sa_rd[b]])
                    m = P.op("dve", lambda e, b=b, f=f, t0=t0, n=n: e.tensor_tensor(out=hid[:, f, t0:t0 + n], in0=sa[b][:, :n], in1=psB[b][:, :n], op=ALU.mult), [a, mmB])
                    sa_rd[b] = m
                    psAB_rd[b] = m
                    mlast = m
                wu_rd[s] = mmB
            for ti, (t0, n) in enumerate(TT):
                for mo in range(8):
                    b = dcnt % 2
                    dcnt += 1
                    mm = None
                    for f in range(11):
                        mm = P.op("pe", lambda e, b=b, f=f, mo=mo, t0=t0, n=n: e.matmul(psD[b][:, :n], lhsT=wd[:, f, mo * 128:(mo + 1) * 128], rhs=hid[:, f, t0:t0 + n], start=(f == 0), stop=(f == 10)),
                                  [wdl, mlast, psD_rd[b]])
                    r = P.op("dve", lambda e, b=b, mo=mo, t0=t0, n=n: e.scalar_tensor_tensor(
                        out=C.xres[:, mo, t0:t0 + n], in0=psD[b][:, :n], scalar=0.5, in1=C.xres[:, mo, t0:t0 + n], op0=ALU.mult, op1=ALU.add), [mm])
                    psD_rd[b] = r
                    wd_rd = mm
        P.flush()


RET_H = 4
CR_MP = 0
CR_MS = 512
CR_KDP = 768
CR_EPP = 772
CR_KDS = 776
CR_EPS = 780
CR_RM = 784
CRW = 800


def ret_gammas():
    return [1.0 - 2.0 ** (-5.0 - h) for h in range(RET_H)]


def emit_ret(C, j, rwin, rwout, rnorm, cs, st_in, st_out_p, st_out_s):
    P, nc = C.P, C.nc
    gam = ret_gammas()
    with ExitStack() as st:
        sb, pt = mk_alloc(C, st)
        wh = [sb("wh%d" % i, [128, 8, 1536], BF16) for i in range(2)]
        wo = sb("wo", [128, 4, 1024], BF16)
        cst = sb("cs", [128, 2, 512], F32)
        qT = sb("qT", [128, 2, 512], BF16)
        kT = sb("kT", [128, 2, 512], BF16)
        vtok = [sb("vtok%d" % i, [128, 512], BF16) for i in range(2)]
        ktl = [sb("ktl%d" % i, [128, 256], BF16) for i in range(2)]
        scm = [sb("scm%d" % i, [128, 128], BF16) for i in range(2)]
        gs = sb("gs", [128, 512], F32)
        on = sb("on", [128, 512], F32)
        go = [sb("go%d" % i, [128, 512], BF16) for i in range(2)]
        goT = sb("goT", [128, 4, 512], BF16)
        S = sb("S", [128, 2, 512], F32)
        Sb = sb("Sb", [128, 2, 512], BF16)
        gn = sb("gn", [128, 512], F32)
        qz = sb("qz", [128, 2, 16, 64], BF16)
        kz = [sb("kz%d" % i, [64, 256], BF16) for i in range(2)]
        S0f = [sb("S0f%d" % i, [128, 512], F32) for i in range(2)]
        S0b = [sb("S0b%d" % i, [128, 512], BF16) for i in range(2)]
        st4 = sb("st4", [128, 4], F32)
        B = [pt("b%d" % i, [128, 512], F32) for i in range(8)]
        PT = [B[6][:, 0:128].bitcast(BF16), B[3][:, 0:128].bitcast(BF16)]
        SC = [B[6][:, 128:256], B[3][:, 128:256]]
        TR = B[3][:, 256:512].bitcast(BF16)
        cret = C.cret

        bwh = [Buf(), Buf()]; bwo = Buf(); bcs = Buf(); bt12 = Buf(); bqT = Buf(); bkT = Buf()
        bvtok = [Buf(), Buf()]; bktl = [Buf(), Buf()]; bscm = [Buf(), Buf()]; bgs = Buf(); bon = Buf(); bgo = [Buf(), Buf()]; bgoT = Buf()
        bS = Buf(); bSb = Buf(); bgn = Buf(); bqz = Buf(); bkz = [Buf(), Buf()]
        bS0f = [Buf(), Buf()]; bS0b = [Buf(), Buf()]; bst4 = Buf()
        bB = [PBuf() for _ in range(8)]
        bB7t = bB[3]
        bPT = [bB[6], bB[3]]; bSC = [bB[6], bB[3]]
        bx = [[Buf() for _ in TT] for _ in range(8)]
        s_wh = [P.slot(), P.slot()]; s_wo = P.slot(); s_cs = P.slot(); s_gn = P.slot()
        s_S0f = [P.slot(), P.slot()]; s_S0b = [P.slot(), P.slot()]; s_so = [P.slot(), P.slot()]; s_sp = P.slot()

        def load_wh(h):
            sl = h % 2
            bop(P, "pool", lambda e, sl=sl: e.dma_start(out=wh[sl][:].rearrange("p k n -> p (k n)"), in_=rwin[h], max_dma_last_dim=8192),
                w=[bwh[sl]], slot=s_wh[sl])

        load_wh(0)
        dbg_dump(C, "xn0", C.xn[:, 0, 0:512], [], 512, bf=True)
        dbg_dump(C, "wh0", wh[0][:, 0, 0:512], [bwh[0]], 512, bf=True)
        ucnt = 0
        for h in range(RET_H):
            sl = h % 2
            g = gam[h]
            bop(P, "pool", lambda e, h=h: e.dma_start(out=wo[:].rearrange("p a n -> p (a n)"), in_=rwout[h], max_dma_last_dim=8192),
                w=[bwo], slot=s_wo)
            if h + 1 < RET_H:
                load_wh(h + 1)
            bop(P, "sp", lambda e, h=h: e.dma_start(out=gn[:], in_=rnorm[h].partition_broadcast(128)), w=[bgn], slot=s_gn)
            bop(P, "pool", lambda e: e.memset(S[:], 0.0), w=[bS])
            bop(P, "pool", lambda e: e.memset(Sb[:], 0.0), w=[bSb])
            for ti, (t0, n) in enumerate(TT):
                sample = (ti == 4)
                bop(P, "sp", lambda e, t0=t0, n=n: e.dma_start(out=cst[:, :, :n], in_=cs[:, :, t0:t0 + n].rearrange("a p t -> p a t")),
                    w=[bcs], slot=s_cs)
                for qi in range(4):
                    for k in range(8):
                        bop(P, "pe", lambda e, sl=sl, qi=qi, k=k, t0=t0, n=n: e.matmul(B[qi][:, :n], lhsT=wh[sl][:, k, qi * 128:(qi + 1) * 128], rhs=C.xn[:, k, t0:t0 + n], start=(k == 0), stop=(k == 7)),
                            r=[bwh[sl]], w=[bB[qi]])
                for (dst, bd, b0, b1, sc) in ((qT, bqT, 0, 1, 1.0), (kT, bkT, 2, 3, 0.0625)):
                    for half in range(2):
                        ca, cb = (0, 1) if half == 0 else (1, 0)
                        bop(P, "dve", lambda e, b0=b0, ca=ca, sc=sc, n=n: e.scalar_tensor_tensor(out=gs[:, :n], in0=B[b0][:, :n], scalar=sc, in1=cst[:, ca, :n], op0=ALU.mult, op1=ALU.mult),
                            r=[bB[b0], bcs], w=[bgs])
                        bop(P, "dve", lambda e, b1=b1, cb=cb, sc=sc, n=n: e.scalar_tensor_tensor(out=on[:, :n], in0=B[b1][:, :n], scalar=sc, in1=cst[:, cb, :n], op0=ALU.mult, op1=ALU.mult),
                            r=[bB[b1], bcs], w=[bon])
                        bop(P, "dve", lambda e, dst=dst, half=half, n=n: e.tensor_tensor(out=dst[:, half, :n], in0=gs[:, :n], in1=on[:, :n], op=(ALU.subtract if half == 0 else ALU.add)),
                            r=[bgs, bon], w=[bd])
                blocks = [(0, 64)] if sample else [(c * 128, 128) for c in range(n // 128)]
                MC = (CR_MS + h * 64) if sample else (CR_MP + h * 128)
                KD = (CR_KDS if sample else CR_KDP) + h
                EP = (CR_EPS if sample else CR_EPP) + h

                def stage_A(ci, c0, nb, sl=sl, t0=t0, MC=MC, KD=KD):
                    pb = ci % 2
                    a0 = t0 + c0
                    vb = 4 + pb
                    for k in range(8):
                        bop(P, "pe", lambda e, k=k: e.matmul(B[vb][:nb, :], lhsT=C.xn[:, k, a0:a0 + nb], rhs=wh[sl][:, k, 512:1024], start=(k == 0), stop=(k == 7)),
                            r=[bwh[sl]], w=[bB[vb]])
                    bop(P, "act", lambda e: e.activation(out=vtok[pb][:nb, :], in_=B[vb][:nb, :], func=AF.Copy), r=[bB[vb]], w=[bvtok[pb]])
                    for jj in range(2):
                        bop(P, "pe", lambda e, jj=jj: e.transpose(out=PT[pb][:nb, jj * 128:(jj + 1) * 128], in_=kT[:, jj, c0:c0 + nb], identity=C.ident),
                            r=[bkT], w=[bPT[pb]])
                    for jj in range(2):
                        bop(P, "pe", lambda e, jj=jj: e.matmul(SC[pb][:nb, :nb], lhsT=kT[:, jj, c0:c0 + nb], rhs=qT[:, jj, c0:c0 + nb], start=(jj == 0), stop=(jj == 1)),
                            r=[bkT, bqT], w=[bSC[pb]])
                    bop(P, "dve", lambda e: e.tensor_scalar(out=ktl[pb][:nb, :], in0=PT[pb][:nb, :], scalar1=cret[:nb, KD:KD + 1], scalar2=None, op0=ALU.mult),
                        r=[bPT[pb]], w=[bktl[pb]])
                    bop(P, "dve", lambda e: e.tensor_tensor(out=scm[pb][:nb, :nb], in0=SC[pb][:nb, :nb], in1=cret[:nb, MC:MC + nb], op=ALU.mult),
                        r=[bSC[pb]], w=[bscm[pb]])

                def stage_B(ci, c0, nb, sl=sl, t0=t0, EP=EP, sample=sample, g=g, h=h):
                    nonlocal ucnt
                    pb = ci % 2
                    a0 = t0 + c0
                    bop(P, "pe", lambda e: e.matmul(B[7][:nb, :], lhsT=scm[pb][:nb, :nb], rhs=vtok[pb][:nb, :], start=True, stop=False),
                        r=[bscm[pb], bvtok[pb]], w=[bB[7]])
                    if not sample:
                        for jj in range(2):
                            bop(P, "pe", lambda e, jj=jj: e.matmul(B[7][:nb, :], lhsT=qT[:, jj, c0:c0 + nb], rhs=Sb[:, jj, :], start=False, stop=(jj == 1)),
                                r=[bqT, bSb], w=[bB[7]])
                        for jj in range(2):
                            bop(P, "pe", lambda e, jj=jj: e.matmul(B[jj][:, :], lhsT=ktl[pb][:nb, jj * 128:(jj + 1) * 128], rhs=vtok[pb][:nb, :], start=True, stop=True),
                                r=[bktl[pb], bvtok[pb]], w=[bB[jj]])
                        cd = g ** 128
                        for jj in range(2):
                            bop(P, "dve", lambda e, jj=jj: e.scalar_tensor_tensor(out=S[:, jj, :], in0=S[:, jj, :], scalar=cd, in1=B[jj][:, :], op0=ALU.mult, op1=ALU.add),
                                r=[bB[jj]], w=[bS])
                        for jj in range(2):
                            bop(P, "act", lambda e, jj=jj: e.activation(out=Sb[:, jj, :], in_=S[:, jj, :], func=AF.Copy), r=[bS], w=[bSb])
                    else:
                        for jj in range(2):
                            bop(P, "dve", lambda e, jj=jj: e.tensor_tensor(out=qz[:, jj, :, :], in0=qT[:, jj, 0:64].unsqueeze(1).broadcast_to([128, 16, 64]), in1=C.bm[:, :, :], op=ALU.mult),
                                r=[bqT], w=[bqz])
                        cd = g ** 4
                        for i in range(NSS):
                            kb = i % 2
                            bop(P, "dve", lambda e, i=i, kb=kb: e.tensor_scalar(out=kz[kb][:, :], in0=ktl[pb][:64, :], scalar1=cret[:64, CR_RM + i:CR_RM + i + 1], scalar2=None, op0=ALU.mult),
                                r=[bktl[pb]], w=[bkz[kb]])
                            for jj in range(2):
                                u = ucnt % 2
                                ucnt += 1
                                bop(P, "pool", lambda e, u=u, i=i, jj=jj: e.dma_start(out=S0b[u][:, :], in_=st_in[i, h, jj * 128:(jj + 1) * 128, :]), w=[bS0b[u]], slot=s_S0b[u])
                                bop(P, "sp", lambda e, u=u, i=i, jj=jj: e.dma_start(out=S0f[u][:, :], in_=st_in[i, h, jj * 128:(jj + 1) * 128, :]), w=[bS0f[u]], slot=s_S0f[u])
                                last = (i == NSS - 1 and jj == 1)
                                bop(P, "pe", lambda e, u=u, i=i, jj=jj, last=last: e.matmul(B[7][:64, :], lhsT=qz[:, jj, i, :], rhs=S0b[u][:, :], start=False, stop=last),
                                    r=[bqz, bS0b[u]], w=[bB[7]])
                                bop(P, "pe", lambda e, kb=kb, jj=jj: e.matmul(B[jj][:, :], lhsT=kz[kb][:, jj * 128:(jj + 1) * 128], rhs=vtok[pb][:64, :], start=True, stop=True),
                                    r=[bkz[kb], bvtok[pb]], w=[bB[jj]])
                                bop(P, "dve", lambda e, u=u, jj=jj: e.scalar_tensor_tensor(out=S0f[u][:, :], in0=S0f[u][:, :], scalar=cd, in1=B[jj][:, :], op0=ALU.mult, op1=ALU.add),
                                    r=[bB[jj]], w=[bS0f[u]])
                                bop(P, "sp", lambda e, u=u, i=i, jj=jj: e.dma_start(out=st_out_s[i, h, jj * 128:(jj + 1) * 128, :], in_=S0f[u][:, :]), r=[bS0f[u]], slot=s_so[u])
                    for k in range(8):
                        bop(P, "pe", lambda e, k=k: e.matmul(B[2][:nb, :], lhsT=C.xn[:, k, a0:a0 + nb], rhs=wh[sl][:, k, 1024:1536], start=(k == 0), stop=(k == 7)),
                            r=[bwh[sl]], w=[bB[2]])
                    bop(P, "act", lambda e: e.activation(out=gs[:nb, :], in_=B[2][:nb, :], func=AF.Exp, scale=-1.0), r=[bB[2]], w=[bgs])
                    bop(P, "act", lambda e: e.activation(out=gs[:nb, :], in_=gs[:nb, :], func=AF.Ln, bias=C.one[:nb, 0:1]), r=[bgs], w=[bgs])
                    bop(P, "act", lambda e: e.activation(out=gs[:nb, :], in_=gs[:nb, :], func=AF.Exp, scale=-1.0), r=[bgs], w=[bgs])
                    bop(P, "dve", lambda e: e.tensor_tensor(out=gs[:nb, :], in0=B[2][:nb, :], in1=gs[:nb, :], op=ALU.mult), r=[bgs, bB[2]], w=[bgs])
                    bop(P, "act", lambda e: e.activation(out=on[:nb, :], in_=B[7][:nb, :], func=AF.Square, accum_out=st4[:nb, 0:1]),
                        r=[bB[7]], w=[bon, bst4])
                    bop(P, "act", lambda e: e.activation(out=st4[:nb, 2:3], in_=st4[:nb, 0:1], func=AF.Ln, scale=1.0 / 512, bias=cret[:nb, EP:EP + 1]), r=[bst4], w=[bst4])
                    bop(P, "act", lambda e: e.activation(out=st4[:nb, 3:4], in_=st4[:nb, 2:3], func=AF.Exp, scale=-0.5), r=[bst4], w=[bst4])
                    bop(P, "dve", lambda e: e.scalar_tensor_tensor(out=on[:nb, :], in0=B[7][:nb, :], scalar=st4[:nb, 3:4], in1=gn[:nb, :], op0=ALU.mult, op1=ALU.mult),
                        r=[bB[7], bst4, bgn], w=[bon])
                    bop(P, "dve", lambda e: e.tensor_tensor(out=go[pb][:nb, :], in0=on[:nb, :], in1=gs[:nb, :], op=ALU.mult), r=[bon, bgs], w=[bgo[pb]])

                def stage_T(ci, c0, nb):
                    pb = ci % 2
                    for e4 in range(4):
                        bop(P, "pe", lambda e, e4=e4: e.transpose(out=TR[:, e4 * 128:e4 * 128 + nb], in_=go[pb][:nb, e4 * 128:(e4 + 1) * 128], identity=C.ident[:nb, :nb]),
                            r=[bgo[pb]], w=[bB7t])
                    bop(P, "act", lambda e: e.activation(out=goT[:, :, c0:c0 + nb], in_=TR.rearrange("p (a t) -> p a t", a=4)[:, :, :nb], func=AF.Copy),
                        r=[bB7t], w=[bgoT])

                nblk = len(blocks)
                sched = []
                if nblk == 1:
                    sched = [("A", 0), ("B", 0), ("T", 0)]
                else:
                    sched = [("A", 0), ("A", 1), ("B", 0), ("A", 2), ("B", 1), ("T", 0), ("A", 3), ("B", 2), ("T", 1), ("B", 3), ("T", 2), ("T", 3)]
                for (kind, ci) in sched:
                    c0, nb = blocks[ci]
                    if kind == "A":
                        stage_A(ci, c0, nb)
                    elif kind == "B":
                        stage_B(ci, c0, nb)
                    else:
                        stage_T(ci, c0, nb)
                for m in range(8):
                    wb = 4 + (m % 2)
                    for e4 in range(4):
                        bop(P, "pe", lambda e, m=m, e4=e4, n=n, wb=wb: e.matmul(B[wb][:, :n], lhsT=wo[:, e4, m * 128:(m + 1) * 128], rhs=goT[:, e4, :n], start=(e4 == 0), stop=(e4 == 3)),
                            r=[bwo, bgoT], w=[bB[wb]])
                    bop(P, "dve", lambda e, m=m, t0=t0, n=n, wb=wb: e.tensor_tensor(out=C.xres[:, m, t0:t0 + n], in0=C.xres[:, m, t0:t0 + n], in1=B[wb][:, :n], op=ALU.add),
                        r=[bB[wb]], w=[bx[m][ti]])
                if ti == 3:
                    bop(P, "sp", lambda e, h=h: e.dma_start(out=st_out_p[h].rearrange("(a p) n -> p a n", p=128), in_=S[:, :, :]), r=[bS], slot=s_sp)
        P.flush()


HG_H = 8
CH_MP = 800
CH_MS = 928
CH_RMP = 992
CH_CMP = 1000
CH_CMS = 1512
CRW2 = 1576


def emit_hg(C, j, hwin, hwout, hnorm, lbl, st_in, st_out_p, st_out_s):
    P, nc = C.P, C.nc
    with ExitStack() as st:
        sb, pt = mk_alloc(C, st)
        wh = [sb("hwh%d" % i, [128, 8, 512], BF16) for i in range(2)]
        wo = [sb("hwo%d" % i, [128, 1024], BF16) for i in range(2)]
        F = {nm: sb("h" + nm, [128, 512], F32) for nm in ("qs", "ez", "r", "f", "kk", "b", "d1", "X", "Y")}
        qc = [sb("hqc%d" % i, [128, 512], BF16) for i in range(2)]
        kc = [sb("hkc%d" % i, [128, 512], BF16) for i in range(2)]
        eB = [sb("heB%d" % i, [128, 16], F32) for i in range(2)]
        vtok = [sb("hvtok%d" % i, [128, 128], BF16) for i in range(2)]
        gsil = [sb("hgsil%d" % i, [128, 128], F32) for i in range(2)]
        ktl = [sb("hktl%d" % i, [128, 128], BF16) for i in range(2)]
        scm = [sb("hscm%d" % i, [128, 128], BF16) for i in range(2)]
        qz = [sb("hqz%d" % i, [128, 1024], BF16) for i in range(2)]
        kz = [sb("hkz%d" % i, [128, 2048], BF16) for i in range(2)]
        Sdb = [sb("hSdb%d" % i, [128, 16, 128], BF16) for i in range(2)]
        S = sb("hS", [128, 128], F32)
        S0 = sb("hS0", [128, 16, 128], F32)
        on = sb("hon", [128, 128], F32)
        go = [sb("hgo%d" % i, [128, 128], BF16) for i in range(2)]
        goT = sb("hgoT", [128, 512], BF16)
        gn = sb("hgn", [128, 128], F32)
        lb = sb("hlb", [128, 2, 8], F32)
        lbv = sb("hlbv", [128, 8], F32)
        oml = sb("homl", [128, 8], F32)
        st4 = sb("hst4", [128, 4], F32)
        B = [pt("hb%d" % i, [128, 512], F32) for i in range(8)]
        VG = [B[4][:, 0:256], B[6][:, 0:256]]
        PT = [B[4][:, 256:320].bitcast(BF16), B[6][:, 256:320].bitcast(BF16)]
        SC = [B[4][:, 320:448], B[6][:, 320:448]]
        TR = B[3][:, 256:320].bitcast(BF16)
        UR = [B[5][:, i * 128:(i + 1) * 128] for i in range(4)] + [B[7][:, i * 128:(i + 1) * 128] for i in range(4)]
        cret = C.cret
        bF = {nm: Buf() for nm in F}
        bwh = [Buf(), Buf()]; bwo = [Buf(), Buf()]; bqc = [Buf(), Buf()]; bkc = [Buf(), Buf()]; beB = [Buf(), Buf()]
        bvtok = [Buf(), Buf()]; bgsil = [Buf(), Buf()]; bktl = [Buf(), Buf()]; bscm = [Buf(), Buf()]
        bqz = [Buf(), Buf()]; bkz = [Buf(), Buf()]; bSdb = [Buf(), Buf()]; bgo = [Buf(), Buf()]
        bS = Buf(); bS0 = Buf(); bon = Buf(); bgoT = Buf(); bgn = Buf(); blb = Buf(); bst4 = Buf()
        bB = [PBuf() for _ in range(8)]
        bVG = [bB[4], bB[6]]; bPT = [bB[4], bB[6]]; bSC = [bB[4], bB[6]]; bTR = bB[3]; bUR = [bB[5]] * 4 + [bB[7]] * 4
        bx = [[Buf() for _ in TT] for _ in range(8)]
        s_wh = [P.slot(), P.slot()]; s_wo = [P.slot(), P.slot()]; s_gn = P.slot(); s_lb = P.slot()
        s_S0 = P.slot(); s_so = P.slot(); s_sp = P.slot()

        def sigmoid_act(dst, src, bdst, bsrc):
            bop(P, "act", lambda e: e.activation(out=dst, in_=src, func=AF.Exp, scale=-1.0), r=[bsrc], w=[bdst])
            bop(P, "act", lambda e: e.activation(out=dst, in_=dst, func=AF.Ln, bias=C.one[:dst.shape[0], 0:1]), r=[bdst], w=[bdst])
            bop(P, "act", lambda e: e.activation(out=dst, in_=dst, func=AF.Exp, scale=-1.0), r=[bdst], w=[bdst])

        bop(P, "sp", lambda e: e.dma_start(out=lb[:], in_=lbl), w=[blb], slot=s_lb)
        if j == 0:
            bop(P, "dve", lambda e: e.memset(lbv[:], 0.0), w=[blb])
            bop(P, "dve", lambda e: e.memset(oml[:], 1.0), w=[blb])
        else:
            bop(P, "dve", lambda e: e.tensor_tensor(out=lbv[:], in0=lb[:, 0, :], in1=lb[:, 1, :], op=ALU.subtract), r=[blb], w=[blb])
            bop(P, "act", lambda e: e.activation(out=oml[:], in_=lbv[:], func=AF.Exp), r=[blb], w=[blb])
            bop(P, "dve", lambda e: e.tensor_scalar(out=lbv[:], in0=oml[:], scalar1=1.0, scalar2=None, op0=ALU.add), r=[blb], w=[blb])
            bop(P, "dve", lambda e: e.reciprocal(out=lbv[:], in_=lbv[:]), r=[blb], w=[blb])
            bop(P, "dve", lambda e: e.tensor_tensor(out=oml[:], in0=oml[:], in1=lbv[:], op=ALU.mult), r=[blb], w=[blb])

        def load_w(h):
            sl = h % 2
            bop(P, "pool", lambda e, sl=sl, h=h: e.dma_start(out=wh[sl][:].rearrange("p k n -> p (k n)"), in_=hwin[h], max_dma_last_dim=8192),
                w=[bwh[sl]], slot=s_wh[sl])
            bop(P, "pool", lambda e, sl=sl, h=h: e.dma_start(out=wo[sl][:], in_=hwout[h], max_dma_last_dim=8192), w=[bwo[sl]], slot=s_wo[sl])

        load_w(0)
        ugl = 0
        jobs = [(h, ti) for h in range(HG_H) for ti in range(len(TT))]

        def head_setup(h):
            if h + 1 < HG_H:
                load_w(h + 1)
            bop(P, "sp", lambda e: e.dma_start(out=gn[:], in_=hnorm[h].partition_broadcast(128)), w=[bgn], slot=s_gn)
            bop(P, "pool", lambda e: e.memset(S[:], 0.0), w=[bS])
            bop(P, "sp", lambda e: e.dma_start(out=S0[:], in_=st_in[:, h, :, :].rearrange("i d e -> d i e")), w=[bS0], slot=s_S0)

        def make_s1(h, ti, kp):
            sl = h % 2
            t0, n = TT[ti]
            sample = (ti == 4)
            CL = 4 if sample else 32
            nch = n // CL
            CM = CH_CMS if sample else CH_CMP
            qcj, kcj, eBj = qc[kp], kc[kp], eB[kp]

            def p0():
                for qi in range(2):
                    for k in range(8):
                        bop(P, "pe", lambda e, qi=qi, k=k: e.matmul(B[qi][:, :n], lhsT=wh[sl][:, k, qi * 128:(qi + 1) * 128], rhs=C.xn[:, k, t0:t0 + n], start=(k == 0), stop=(k == 7)),
                            r=[bwh[sl]], w=[bB[qi]])

            def p1():
                sigmoid_act(F["qs"][:, :n], B[0][:, :n], bF["qs"], bB[0])
                bop(P, "act", lambda e: e.activation(out=F["ez"][:, :n], in_=B[1][:, :n], func=AF.Exp, scale=-1.0), r=[bB[1]], w=[bF["ez"]])
                bop(P, "act", lambda e: e.activation(out=F["r"][:, :n], in_=F["ez"][:, :n], func=AF.Ln, bias=C.one[:, 0:1]), r=[bF["ez"]], w=[bF["r"]])
                bop(P, "act", lambda e: e.activation(out=F["r"][:, :n], in_=F["r"][:, :n], func=AF.Exp, scale=-1.0), r=[bF["r"]], w=[bF["r"]])

            def p2():
                bop(P, "dve", lambda e: e.tensor_tensor(out=F["qs"][:, :n], in0=B[0][:, :n], in1=F["qs"][:, :n], op=ALU.mult), r=[bF["qs"], bB[0]], w=[bF["qs"]])
                bop(P, "dve", lambda e: e.tensor_scalar(out=F["f"][:, :n], in0=F["r"][:, :n], scalar1=oml[:, h:h + 1], scalar2=lbv[:, h:h + 1], op0=ALU.mult, op1=ALU.add),
                    r=[bF["r"], blb], w=[bF["f"]])
                bop(P, "dve", lambda e: e.scalar_tensor_tensor(out=F["kk"][:, :n], in0=F["ez"][:, :n], scalar=oml[:, h:h + 1], in1=F["r"][:, :n], op0=ALU.mult, op1=ALU.mult),
                    r=[bF["ez"], bF["r"], blb], w=[bF["kk"]])
                bop(P, "act", lambda e: e.activation(out=F["f"][:, :n], in_=F["f"][:, :n], func=AF.Ln), r=[bF["f"]], w=[bF["f"]])

            def p3():
                bop(P, "dve", lambda e: e.tensor_tensor_scan(out=F["b"][:, :n], data0=cret[:, CM:CM + n], data1=F["f"][:, :n], initial=0.0, op0=ALU.mult, op1=ALU.add),
                    r=[bF["f"]], w=[bF["b"]])
                bop(P, "pool", lambda e: e.tensor_tensor(
                    out=F["d1"][:, :n].rearrange("p (c s) -> p c s", s=CL), in0=F["b"][:, :n].rearrange("p (c s) -> p c s", s=CL),
                    in1=F["b"][:, :n].rearrange("p (c s) -> p c s", s=CL)[:, :, CL - 1:CL].broadcast_to([128, nch, CL]), op=ALU.subtract),
                    r=[bF["b"]], w=[bF["d1"]])

            def p4():
                bop(P, "act", lambda e: e.activation(out=F["X"][:, :n], in_=F["d1"][:, :n], func=AF.Exp), r=[bF["d1"]], w=[bF["X"]])
                bop(P, "act", lambda e: e.activation(out=F["Y"][:, :n], in_=F["d1"][:, :n], func=AF.Exp, scale=-1.0), r=[bF["d1"]], w=[bF["Y"]])
                bop(P, "act", lambda e: e.activation(out=eBj[:, :nch], in_=F["b"][:, :n].rearrange("p (c s) -> p c s", s=CL)[:, :, CL - 1], func=AF.Exp),
                    r=[bF["b"]], w=[beB[kp]])

            def p5():
                bop(P, "pool", lambda e: e.tensor_tensor(out=qcj[:, :n], in0=F["qs"][:, :n], in1=F["X"][:, :n], op=ALU.mult), r=[bF["qs"], bF["X"]], w=[bqc[kp]])
                bop(P, "pool", lambda e: e.tensor_tensor(out=kcj[:, :n], in0=F["kk"][:, :n], in1=F["Y"][:, :n], op=ALU.mult), r=[bF["kk"], bF["Y"]], w=[bkc[kp]])

            return [p0, p1, p2, p3, p4, p5]

        def make_blocks(h, ti, kp):
            sl = h % 2
            t0, n = TT[ti]
            sample = (ti == 4)
            CL = 4 if sample else 32
            qcj, kcj, eBj = qc[kp], kc[kp], eB[kp]
            blocks = [(0, 64)] if sample else [(c * 128, 128) for c in range(4)]
            MC = CH_MS if sample else CH_MP
            nbc = 16 if sample else 4
            if sample:
                bmask = C.bm
                rmv = cret[:64, CR_RM:CR_RM + 16]
            else:
                bmask = C.bmp
                rmv = cret[:, CH_RMP:CH_RMP + 4]
            ubase = {}

            def A_pe(ci):
                c0, nb = blocks[ci]
                pb = ci % 2
                a0 = t0 + c0
                for k in range(8):
                    bop(P, "pe", lambda e, k=k: e.matmul(VG[pb][:nb, :], lhsT=C.xn[:, k, a0:a0 + nb], rhs=wh[sl][:, k, 256:512], start=(k == 0), stop=(k == 7)),
                        r=[bwh[sl]], w=[bVG[pb]])
                bop(P, "pe", lambda e: e.transpose(out=PT[pb][:nb, :], in_=kcj[:, c0:c0 + nb], identity=C.ident), r=[bkc[kp]], w=[bPT[pb]])
                bop(P, "pe", lambda e: e.matmul(SC[pb][:nb, :nb], lhsT=kcj[:, c0:c0 + nb], rhs=qcj[:, c0:c0 + nb], start=True, stop=True), r=[bkc[kp], bqc[kp]], w=[bSC[pb]])

            def A_ev(ci):
                nonlocal ugl
                c0, nb = blocks[ci]
                pb = ci % 2
                bop(P, "dve", lambda e: e.tensor_copy(out=ktl[pb][:nb, :], in_=PT[pb][:nb, :]), r=[bPT[pb]], w=[bktl[pb]])
                bop(P, "act", lambda e: e.activation(out=vtok[pb][:nb, :], in_=VG[pb][:nb, 0:128], func=AF.Copy), r=[bVG[pb]], w=[bvtok[pb]])
                qzv = qz[pb][:, 0:nbc * nb].rearrange("p (c t) -> p c t", c=nbc)
                kzv = kz[pb][:nb, 0:nbc * 128].rearrange("p (c d) -> p c d", c=nbc)
                bop(P, "pool", lambda e: e.tensor_tensor(out=kzv, in0=ktl[pb][:nb, :].unsqueeze(1).broadcast_to([nb, nbc, 128]), in1=rmv.unsqueeze(2).broadcast_to([nb, nbc, 128]), op=ALU.mult),
                    r=[bktl[pb]], w=[bkz[pb]])
                bop(P, "pool", lambda e: e.tensor_tensor(out=qzv, in0=qcj[:, c0:c0 + nb].unsqueeze(1).broadcast_to([128, nbc, nb]), in1=bmask, op=ALU.mult),
                    r=[bqc[kp]], w=[bqz[pb]])
                bop(P, "dve", lambda e: e.tensor_tensor(out=scm[pb][:nb, :nb], in0=SC[pb][:nb, :nb], in1=cret[:nb, MC:MC + nb], op=ALU.mult), r=[bSC[pb]], w=[bscm[pb]])
                sigmoid_act(gsil[pb][:nb, :], VG[pb][:nb, 128:256], bgsil[pb], bVG[pb])
                bop(P, "dve", lambda e: e.tensor_tensor(out=gsil[pb][:nb, :], in0=VG[pb][:nb, 128:256], in1=gsil[pb][:nb, :], op=ALU.mult), r=[bgsil[pb], bVG[pb]], w=[bgsil[pb]])
                ubase[ci] = ugl
                if nbc <= 4:
                    for c in range(nbc):
                        u = 4 * pb + c
                        bop(P, "pe", lambda e, c=c, u=u: e.matmul(UR[u], lhsT=kzv[:, c, :], rhs=vtok[pb][:nb, :], start=True, stop=True),
                            r=[bkz[pb], bvtok[pb]], w=[bUR[u]])
                ugl += nbc

            def B_chain(ci):
                c0, nb = blocks[ci]
                pb = ci % 2
                kzv = kz[pb][:nb, 0:nbc * 128].rearrange("p (c d) -> p c d", c=nbc)
                for c in range(nbc):
                    ec = (c0 // CL + c)
                    u = (4 * pb + c) if nbc <= 4 else (4 * (c % 2) + (c // 2) % 4)
                    Sin = S0[:, c, :] if sample else S[:, :]
                    bSin = bS0 if sample else bS
                    if nbc > 4:
                        bop(P, "pe", lambda e, c=c, u=u: e.matmul(UR[u], lhsT=kzv[:, c, :], rhs=vtok[pb][:nb, :], start=True, stop=True),
                            r=[bkz[pb], bvtok[pb]], w=[bUR[u]])
                    bop(P, "dve", lambda e, c=c, ec=ec, Sin=Sin: e.tensor_scalar(out=Sdb[pb][:, c, :], in0=Sin, scalar1=eBj[:, ec:ec + 1], scalar2=None, op0=ALU.mult),
                        r=[bSin, beB[kp]], w=[bSdb[pb]])
                    bop(P, "dve", lambda e, ec=ec, u=u, Sin=Sin: e.scalar_tensor_tensor(out=Sin, in0=Sin, scalar=eBj[:, ec:ec + 1], in1=UR[u], op0=ALU.mult, op1=ALU.add),
                        r=[bUR[u], beB[kp]], w=[bSin])

            def B_pe(ci):
                c0, nb = blocks[ci]
                pb = ci % 2
                qzv = qz[pb][:, 0:nbc * nb].rearrange("p (c t) -> p c t", c=nbc)
                bop(P, "pe", lambda e: e.matmul(B[2][:nb, 0:128], lhsT=scm[pb][:nb, :nb], rhs=vtok[pb][:nb, :], start=True, stop=False), r=[bscm[pb], bvtok[pb]], w=[bB[2]])
                for c in range(nbc):
                    bop(P, "pe", lambda e, c=c: e.matmul(B[2][:nb, 0:128], lhsT=qzv[:, c, :], rhs=Sdb[pb][:, c, :], start=False, stop=(c == nbc - 1)),
                        r=[bqz[pb], bSdb[pb]], w=[bB[2]])

            def B_norm(ci):
                c0, nb = blocks[ci]
                pb = ci % 2
                bop(P, "act", lambda e: e.activation(out=on[:nb, :], in_=B[2][:nb, 0:128], func=AF.Square, accum_out=st4[:nb, 0:1]), r=[bB[2]], w=[bon, bst4])
                bop(P, "act", lambda e: e.activation(out=st4[:nb, 2:3], in_=st4[:nb, 0:1], func=AF.Ln, scale=1.0 / 128, bias=C.epsc[:nb, 0:1]), r=[bst4], w=[bst4])
                bop(P, "act", lambda e: e.activation(out=st4[:nb, 3:4], in_=st4[:nb, 2:3], func=AF.Exp, scale=-0.5), r=[bst4], w=[bst4])
                bop(P, "dve", lambda e: e.scalar_tensor_tensor(out=on[:nb, :], in0=B[2][:nb, 0:128], scalar=st4[:nb, 3:4], in1=gn[:nb, :], op0=ALU.mult, op1=ALU.mult),
                    r=[bB[2], bst4, bgn], w=[bon])
                bop(P, "pool", lambda e: e.tensor_tensor(out=go[pb][:nb, :], in0=on[:nb, :], in1=gsil[pb][:nb, :], op=ALU.mult), r=[bon, bgsil[pb]], w=[bgo[pb]])

            def stage_T(ci):
                c0, nb = blocks[ci]
                pb = ci % 2
                bop(P, "pe", lambda e: e.transpose(out=TR[:, :nb], in_=go[pb][:nb, :], identity=C.ident[:nb, :nb]), r=[bgo[pb]], w=[bTR])
                bop(P, "act", lambda e: e.activation(out=goT[:, c0:c0 + nb], in_=TR[:, :nb], func=AF.Copy), r=[bTR], w=[bgoT])

            def wout():
                for m in range(8):
                    bop(P, "pe", lambda e, m=m: e.matmul(B[3][:, :n], lhsT=wo[sl][:, m * 128:(m + 1) * 128], rhs=goT[:, :n], start=True, stop=True),
                        r=[bwo[sl], bgoT], w=[bB[3]])
                    bop(P, "dve", lambda e, m=m: e.tensor_tensor(out=C.xres[:, m, t0:t0 + n], in0=C.xres[:, m, t0:t0 + n], in1=B[3][:, :n], op=ALU.add),
                        r=[bB[3]], w=[bx[m][ti]])
                if ti == 3:
                    bop(P, "sp", lambda e: e.dma_start(out=st_out_p[h], in_=S[:, :]), r=[bS], slot=s_sp)
                if sample:
                    bop(P, "sp", lambda e: e.dma_start(out=st_out_s[:, h, :, :].rearrange("i d e -> d i e"), in_=S0[:, :, :]), r=[bS0], slot=s_so)

            mk = lambda fn, ci: (lambda: fn(ci))
            if len(blocks) == 1:
                sched = [mk(A_pe, 0), mk(A_ev, 0), mk(B_chain, 0), mk(B_pe, 0), mk(B_norm, 0), mk(stage_T, 0)]
                slots_after = {1: [0, 1], 3: [2, 3], 5: [4, 5]}
            else:
                sched = [mk(A_pe, 0), mk(A_ev, 0), mk(A_pe, 1),
                         mk(B_chain, 0), mk(B_pe, 0), mk(A_ev, 1), mk(B_norm, 0), mk(A_pe, 2),
                         mk(B_chain, 1), mk(B_pe, 1), mk(A_ev, 2), mk(B_norm, 1), mk(stage_T, 0), mk(A_pe, 3),
                         mk(B_chain, 2), mk(B_pe, 2), mk(A_ev, 3), mk(B_norm, 2), mk(stage_T, 1),
                         mk(B_chain, 3), mk(B_pe, 3), mk(B_norm, 3), mk(stage_T, 2), mk(stage_T, 3)]
                slots_after = {7: [0, 1], 13: [2], 18: [3, 4], 22: [5]}
            return sched, slots_after, wout

        for p in make_s1(jobs[0][0], jobs[0][1], 0):
            p()
        for kj, (h, ti) in enumerate(jobs):
            kp = kj % 2
            if ti == 0:
                head_setup(h)
            sched, slots_after, wout = make_blocks(h, ti, kp)
            nxt = make_s1(jobs[kj + 1][0], jobs[kj + 1][1], 1 - kp) if kj + 1 < len(jobs) else None
            for si, stage in enumerate(sched):
                stage()
                if nxt is not None:
                    for pi in slots_after.get(si, []):
                        nxt[pi]()
            wout()
        P.flush()


def build_program(cfg):
    nc = bass.Bass("TRN2", target_bir_lowering=False)
    dr = lambda name, shape, kind="ExternalInput", dt=F32: nc.dram_tensor(name, shape, dt, kind=kind).ap()
    xT = dr("xT", [128, 8, NTOK])
    gains_d = dr("gains", [128, 13 * 8])
    cbf_d = dr("cbf", [128, 256 + 1024 + 512], dt=BF16)
    cret_d = dr("cret", [128, CRW2])
    cs_d = dr("cs", [2, 128, NTOK])
    wup_d = dr("wup", [8, NF, 128, 2048])
    wdn_d = dr("wdn", [8, 2, 128, 11 * 1024])
    rwin_d = dr("rwin", [2, 4, 128, 12288])
    rwout_d = dr("rwout", [2, 4, 128, 4096])
    rnorm_d = dr("rnorm", [2, 4, 512])
    sret_d = dr("sret", [2, NSS, 4, 256, 512])
    hwin_d = dr("hwin", [2, 8, 128, 4096])
    hwout_d = dr("hwout", [2, 8, 128, 1024])
    hnorm_d = dr("hnorm", [2, 8, 128])
    lbl_d = dr("lbl", [128, 2, 8])
    shg_d = dr("shg", [2, NSS, 8, 128, 128])
    nhp_d = dr("nhp", [2, 8, 128, 128], kind="ExternalOutput")
    nhs_d = dr("nhs", [2, NSS, 8, 128, 128], kind="ExternalOutput")
    yT = dr("yT", [128, 8, NTOK], kind="ExternalOutput")
    nrp_d = dr("nrp", [2, 4, 256, 512], kind="ExternalOutput")
    nrs_d = dr("nrs", [2, NSS, 4, 256, 512], kind="ExternalOutput")

    with ExitStack() as st:
        P = Prog(nc, st)
        C = Ctx()
        C.P, C.nc = P, nc
        C.dbg = None
        if cfg.get("debug"):
            C.dbg = {"want": set(cfg["debug"]), "seen": {}, "off": {"f": 0, "b": 0}, "slot": P.slot(),
                     "f": dr("dbgf", [128, 8192], kind="ExternalOutput"), "b": dr("dbgb", [128, 8192], kind="ExternalOutput", dt=BF16)}
        cfg["_dbg"] = C.dbg
        sb = lambda name, shape, dt: st.enter_context(nc.sbuf_tensor(name, shape, dt))
        C.xres = sb("xres", [128, 8, NTOK], F32)
        C.xn = sb("xn", [128, 8, NTOK], BF16)
        C.gains = sb("gains_sb", [128, 13 * 8], F32)
        cbf = sb("cbf_sb", [128, 256 + 1024 + 512], BF16)
        C.ident = cbf[:, 0:128]
        C.ones = cbf[:, 128:256]
        C.bm = cbf[:, 256:1280].rearrange("p (a t) -> p a t", a=16)
        C.bmp = cbf[:, 1280:1792].rearrange("p (a t) -> p a t", a=4)
        C.epsc = sb("epsc", [128, 2], F32)
        C.one = sb("onec", [128, 2], F32)
        C.cret = sb("cret_sb", [128, CRW2], F32)

        s_in = P.slot()
        for k in range(8):
            P.dma("sp", lambda e, k=k: e.dma_start(out=C.xres[:, k, :], in_=xT[:, k, :]), s_in)
        P.dma("sp", lambda e: e.dma_start(out=C.gains[:], in_=gains_d), s_in)
        P.dma("sp", lambda e: e.dma_start(out=cbf[:], in_=cbf_d), s_in)
        P.dma("sp", lambda e: e.dma_start(out=C.cret[:], in_=cret_d), s_in)
        P.op("pool", lambda e: e.memset(C.epsc[:], EPS))
        P.op("pool", lambda e: e.memset(C.one[:], 1.0))
        P.flush()

        for blk in cfg["blocks"]:
            if blk[0] == "ffn":
                _, l, i = blk
                emit_ffn(C, l * 3 + (0 if i == 0 else 2), wup_d[l * 2 + i], wdn_d[l * 2 + i])
            elif blk[0] == "ret":
                _, l = blk
                j = l // 2
                emit_normphase(C, l * 3 + 1)
                emit_ret(C, j, rwin_d[j], rwout_d[j], rnorm_d[j], cs_d, sret_d[j], nrp_d[j], nrs_d[j])
            elif blk[0] == "hg":
                _, l = blk
                j = l // 2
                emit_normphase(C, l * 3 + 1)
                emit_hg(C, j, hwin_d[j], hwout_d[j], hnorm_d[j], lbl_d, shg_d[j], nhp_d[j], nhs_d[j])

        with ExitStack() as st2:
            sb2, pt2 = mk_alloc(C, st2)
            ph = Ctx()
            ph.sq = [sb2("sq%d" % i, [128, 8, 512], BF16) for i in range(2)]
            ph.rs = [sb2("rs%d" % i, [128, 512], F32) for i in range(2)]
            ph.psn = pt2("psn", [128, 512], F32)
            yo = [sb2("yo%d" % i, [128, 8, 512], F32) for i in range(2)]
            s_out = [P.slot(), P.slot()]
            yo_rd = [None, None]
            if cfg.get("final_norm", True):
                def out_fn2(ti, k, t0, n, rsb, r):
                    b = ti % 2
                    o = P.op("dve", lambda e: e.scalar_tensor_tensor(
                        out=yo[b][:, k, :n], in0=C.xres[:, k, t0:t0 + n], scalar=C.gains[:, 96 + k:96 + k + 1],
                        in1=rsb[:, :n], op0=ALU.mult, op1=ALU.mult), [r, yo_rd[b]])
                    if k == 7:
                        yo_rd[b] = P.dma("sp", lambda e: e.dma_start(out=yT[:, :, t0:t0 + n], in_=yo[b][:, :, :n]), s_out[b], [o])
                    return o
                emit_norm(C, ph, 12, out_fn2)
            else:
                for k in range(8):
                    P.dma("sp", lambda e, k=k: e.dma_start(out=yT[:, k, :], in_=C.xres[:, k, :]), s_out[0])
            P.flush()
    return nc


def host_consts():
    import ml_dtypes
    c = np.zeros((128, 256 + 1024 + 512), np.float32)
    c[:, 0:128] = np.eye(128, dtype=np.float32)
    c[:, 128:256] = 1.0
    bm = np.zeros((16, 64), np.float32)
    for i in range(16):
        bm[i, 4 * i:4 * i + 4] = 1.0
    c[:, 256:1280] = bm.reshape(1, 1024)
    bmp = np.zeros((4, 128), np.float32)
    for i in range(4):
        bmp[i, 32 * i:32 * i + 32] = 1.0
    c[:, 1280:1792] = bmp.reshape(1, 512)
    out = {"cbf": c.astype(ml_dtypes.bfloat16)}
    cr = np.zeros((128, CRW2), np.float64)
    t = np.arange(128)
    ts = np.arange(64)
    for h, g in enumerate(ret_gammas()):
        lg = np.log(np.float64(g))
        mp = np.where(t[:, None] <= t[None, :], np.exp(-(t[:, None] + 1.0) * lg), 0.0)
        cr[:, CR_MP + h * 128:CR_MP + (h + 1) * 128] = mp
        same = (ts[:, None] // 4) == (ts[None, :] // 4)
        ms = np.where(same & ((ts[:, None] % 4) <= (ts[None, :] % 4)), np.exp(-((ts[:, None] % 4) + 1.0) * lg), 0.0)
        cr[:64, CR_MS + h * 64:CR_MS + (h + 1) * 64] = ms
        cr[:, CR_KDP + h] = np.exp((127.0 - t) * lg)
        cr[:, CR_EPP + h] = EPS * np.exp(-2.0 * (t + 1.0) * lg)
        cr[:64, CR_KDS + h] = np.exp((3.0 - (ts % 4)) * lg)
        cr[:64, CR_EPS + h] = EPS * np.exp(-2.0 * ((ts % 4) + 1.0) * lg)
    for i in range(16):
        cr[4 * i:4 * i + 4, CR_RM + i] = 1.0
    cr[:, CH_MP:CH_MP + 128] = ((t[:, None] // 32) == (t[None, :] // 32)) & (t[:, None] <= t[None, :])
    cr[:64, CH_MS:CH_MS + 64] = ((ts[:, None] // 4) == (ts[None, :] // 4)) & (ts[:, None] <= ts[None, :])
    for i in range(4):
        cr[32 * i:32 * i + 32, CH_RMP + i] = 1.0
    cr[:, CH_CMP:CH_CMP + 512] = (np.arange(512) % 32 != 0)[None, :]
    cr[:, CH_CMS:CH_CMS + 64] = (np.arange(64) % 4 != 0)[None, :]
    out["cret"] = cr.astype(np.float32)
    half = 128
    inv_freq = (np.float32(10000.0) ** (-np.arange(half, dtype=np.float32) / np.float32(half))).astype(np.float32)
    pos = np.concatenate([np.arange(SEQ, dtype=np.float32), np.tile(np.float32(16384.0) + np.arange(DEC, dtype=np.float32), NSS)])
    ang = (pos[None, :] * inv_freq[:, None]).astype(np.float32)
    out["cs"] = np.stack([np.cos(ang), np.sin(ang)]).astype(np.float32)
    return out


def host_weights(inp):
    w = {}
    f32 = lambda a: np.asarray(a, np.float32)
    up = f32(inp["ffn_w_up"]).reshape(8, 8, 128, 2, NF, 128)
    w["wup"] = np.ascontiguousarray(up.transpose(0, 4, 2, 1, 3, 5)).reshape(8, NF, 128, 2048)
    dn = f32(inp["ffn_w_down"]).reshape(8, 2, 11, 128, 1024)
    w["wdn"] = np.ascontiguousarray(dn.transpose(0, 1, 3, 2, 4)).reshape(8, 2, 128, 11 * 1024)
    g = np.concatenate([f32(inp["norm_gain"]).reshape(12, 1024), f32(inp["final_norm"]).reshape(1, 1024)], 0)
    w["gains"] = np.ascontiguousarray(g.reshape(13, 8, 128).transpose(2, 0, 1)).reshape(128, 104)
    wi = f32(inp["ret_w_in"]).reshape(2, 8, 128, 6144)
    parts = []
    for h in range(4):
        parts.append(np.concatenate([wi[..., h * 256:(h + 1) * 256], wi[..., 1024 + h * 256:1024 + (h + 1) * 256],
                                     wi[..., 2048 + h * 512:2048 + (h + 1) * 512], wi[..., 4096 + h * 512:4096 + (h + 1) * 512]], -1))
    wih = np.stack(parts, 1)
    w["rwin"] = np.ascontiguousarray(wih.transpose(0, 1, 3, 2, 4)).reshape(2, 4, 128, 12288)
    wo = f32(inp["ret_w_out"]).reshape(2, 4, 4, 128, 1024)
    w["rwout"] = np.ascontiguousarray(wo.transpose(0, 1, 3, 2, 4)).reshape(2, 4, 128, 4096)
    w["rnorm"] = f32(inp["ret_norm"])
    hi = f32(inp["hg_w_in"]).reshape(2, 8, 128, 4, 8, 128)
    w["hwin"] = np.ascontiguousarray(hi.transpose(0, 4, 2, 1, 3, 5)).reshape(2, 8, 128, 4096)
    w["hwout"] = np.ascontiguousarray(f32(inp["hg_w_out"]).reshape(2, 8, 128, 1024))
    w["hnorm"] = f32(inp["hg_norm"])
    w["lbl"] = np.ascontiguousarray(f32(inp["hg_lb_logits"]).reshape(2, 8, 128).transpose(2, 0, 1))
    return w


def host_core_inputs(inp, c):
    xp = np.asarray(inp["x_prompt"], np.float32)[c]
    xs = np.asarray(inp["x_sample"], np.float32)[c * NSS:(c + 1) * NSS].reshape(NSS * DEC, D)
    x = np.concatenate([xp, xs], 0)
    xT = np.ascontiguousarray(x.T.reshape(8, 128, NTOK).transpose(1, 0, 2))
    m = {"xT": xT}
    m["sret"] = np.ascontiguousarray(np.asarray(inp["state_ret"], np.float32)[:, c * NSS:(c + 1) * NSS])
    m["shg"] = np.ascontiguousarray(np.asarray(inp["state_hgrn"], np.float32)[:, c * NSS:(c + 1) * NSS])
    return m


def full_cfg():
    blocks = []
    for l in range(4):
        blocks.append(("ffn", l, 0))
        blocks.append(("ret", l) if l % 2 == 0 else ("hg", l))
        blocks.append(("ffn", l, 1))
    return {"blocks": blocks, "final_norm": True}


def kernel(**inputs):
    nc = build_program(full_cfg())
    shared = {}
    shared.update(host_consts())
    shared.update(host_weights(inputs))
    in_maps = []
    for c in range(8):
        m = dict(shared)
        m.update(host_core_inputs(inputs, c))
        in_maps.append(m)
    res = run_bass_kernel_spmd(nc, in_maps, core_ids=list(range(8)))
    rs = res.results
    y_prompt = np.empty((8, SEQ, D), np.float32)
    y_sample = np.empty((8 * NSS, DEC, D), np.float32)
    nrp = np.empty((2, 8, 4, 256, 512), np.float32)
    nrs = np.empty((2, 8 * NSS, 4, 256, 512), np.float32)
    nhp = np.empty((2, 8, 8, 128, 128), np.float32)
    nhs = np.empty((2, 8 * NSS, 8, 128, 128), np.float32)
    for c in range(8):
        r = rs[c]
        y = np.asarray(r["yT"]).transpose(1, 0, 2).reshape(D, NTOK).T
        y_prompt[c] = y[:SEQ]
        y_sample[c * NSS:(c + 1) * NSS] = y[SEQ:].reshape(NSS, DEC, D)
        nrp[:, c] = r["nrp"]
        nrs[:, c * NSS:(c + 1) * NSS] = r["nrs"]
        nhp[:, c] = r["nhp"]
        nhs[:, c * NSS:(c + 1) * NSS] = r["nhs"]
    return (y_prompt, y_sample, nrp, nrs, nhp, nhs)
```

```python
import numpy as np
from contextlib import ExitStack
import concourse.bass as bass
import concourse.mybir as mybir
from concourse.bass_utils import run_bass_kernel_spmd

F32 = mybir.dt.float32
BF16 = mybir.dt.bfloat16
AF = mybir.ActivationFunctionType
ALU = mybir.AluOpType

D = 1024
SEQ = 2048
NSS = 16
DEC = 4
NTOK = SEQ + NSS * DEC
DFF = 2816
NF = DFF // 128
EPS = 1e-6
TT = [(0, 512), (512, 512), (1024, 512), (1536, 512), (2048, 64)]


class Op:
    __slots__ = ("eng", "fn", "deps", "pos", "sem", "val", "need_sig", "is_dma", "done")


class Slot:
    def __init__(self, sem):
        self.sem = sem
        self.count = 0


class Prog:
    ENGS = ["pe", "act", "dve", "pool", "sp"]
    ENGOBJ = {"pe": "tensor", "act": "scalar", "dve": "vector", "pool": "gpsimd", "sp": "sync"}

    def __init__(self, nc, stack):
        self.nc = nc
        self.stack = stack
        self.q = {e: [] for e in self.ENGS}
        self.esem = {e: stack.enter_context(nc.semaphore("s_" + e)) for e in ["pe", "act", "dve", "pool"]}
        self.ecount = {e: 0 for e in self.ENGS}
        self.slots = []
        self.nphase = 0

    def slot(self):
        s = Slot(self.stack.enter_context(self.nc.semaphore("d%d" % len(self.slots))))
        self.slots.append(s)
        return s

    def op(self, eng, fn, deps=()):
        o = Op()
        o.eng = eng
        o.fn = fn
        o.deps = [d for d in deps if d is not None]
        o.pos = len(self.q[eng])
        o.is_dma = False
        o.need_sig = False
        o.sem = None
        o.val = None
        o.done = False
        self.q[eng].append(o)
        return o

    def dma(self, eng, fn, slot, deps=()):
        o = self.op(eng, fn, deps)
        o.is_dma = True
        slot.count += 16
        o.sem = slot.sem
        o.val = slot.count
        return o

    def _needs_wait(self, o, d):
        if d.done:
            return False
        if d.is_dma:
            return True
        if d.eng == o.eng:
            if o.eng == "pe":
                return False
            return (o.pos - d.pos) <= 2
        return True

    def flush(self):
        nc = self.nc
        drain_deps = []
        for e in self.ENGS:
            last = {}
            for o in self.q[e]:
                if o.is_dma:
                    last[id(o.sem)] = o
            drain_deps += list(last.values())
        self.op("sp", lambda e: e.nop(), drain_deps)
        for e in self.ENGS:
            for o in self.q[e]:
                for d in o.deps:
                    if not d.is_dma and self._needs_wait(o, d):
                        d.need_sig = True
        for e in self.ENGS:
            c = self.ecount[e]
            for o in self.q[e]:
                if o.is_dma:
                    continue
                if o.need_sig:
                    assert e != "sp"
                    c += 1
                    o.sem = self.esem[e]
                    o.val = c
            self.ecount[e] = c
        self.nphase += 1
        with nc.Block() as block:
            for e in self.ENGS:
                ops = self.q[e]
                if not ops:
                    continue

                def body(eng, ops=ops):
                    waited = {}
                    for o in ops:
                        need = {}
                        for d in o.deps:
                            if not self._needs_wait(o, d):
                                continue
                            key = id(d.sem)
                            if key not in need or need[key][1] < d.val:
                                need[key] = (d.sem, d.val)
                        for key, (sem, val) in need.items():
                            if waited.get(key, 0) >= val:
                                continue
                            eng.wait_ge(sem, val)
                            waited[key] = val
                        ins = o.fn(eng)
                        if o.is_dma:
                            ins.then_inc(o.sem, 16)
                        elif o.need_sig:
                            ins.then_inc(o.sem, 1)

                getattr(block, self.ENGOBJ[e])(body)
        for e in self.ENGS:
            for o in self.q[e]:
                o.done = True
                o.fn = None
                o.deps = None
            self.q[e] = []


class Ctx:
    pass


def dbg_dump(C, name, ap, bufs, ncols, bf=False):
    if not getattr(C, "dbg", None) or name in C.dbg["seen"] or name not in C.dbg["want"]:
        return
    key = "b" if bf else "f"
    off = C.dbg["off"][key]
    C.dbg["off"][key] = off + ncols
    C.dbg["seen"][name] = (key, off, ncols)
    dst = C.dbg[key][:, off:off + ncols]
    bop(C.P, "sp", lambda e: e.dma_start(out=dst, in_=ap), r=bufs, slot=C.dbg["slot"])


class Buf:
    def __init__(self):
        self.w = None
        self.r = []


class PBuf(Buf):
    excl = True


def bop(P, eng, fn, r=(), w=(), slot=None, deps=()):
    xr = [b for b in r if getattr(b, "excl", False)]
    if xr:
        r = [b for b in r if not getattr(b, "excl", False)]
        w = list(w) + [b for b in xr if b not in w]
    d = list(deps)
    for b in r:
        d.append(b.w)
    for b in w:
        d.append(b.w)
        d.extend(b.r)
    o = P.dma(eng, fn, slot, d) if slot is not None else P.op(eng, fn, d)
    for b in r:
        if not o.is_dma:
            b.r = [x for x in b.r if x.is_dma or x.eng != eng]
        b.r.append(o)
    for b in w:
        b.w = o
        b.r = []
    return o


_UID = [0]


def mk_alloc(C, st):
    _UID[0] += 1
    u = _UID[0]
    nc = C.nc
    sb = lambda name, shape, dt: st.enter_context(nc.sbuf_tensor("%s_%d" % (name, u), shape, dt))
    pt = lambda name, shape, dt: st.enter_context(nc.psum_tensor("%s_%d" % (name, u), shape, dt))
    return sb, pt


def emit_norm(C, ph, gcol, out_fn=None):
    P = C.P
    sq, rs, psn = ph.sq, ph.rs, ph.psn
    last = []
    sq_rd = [None, None]
    rs_rd = [None, None]
    ps_rd = None
    for ti, (t0, n) in enumerate(TT):
        b = ti % 2
        a = P.op("act", lambda e, b=b, t0=t0, n=n: e.activation(out=sq[b][:, :, :n], in_=C.xres[:, :, t0:t0 + n], func=AF.Square),
                 [sq_rd[b]])
        mm = None
        for k in range(8):
            mm = P.op("pe", lambda e, b=b, k=k, n=n: e.matmul(psn[:, :n], lhsT=C.ones[:], rhs=sq[b][:, k, :n], start=(k == 0), stop=(k == 7)),
                      [a, ps_rd])
        sq_rd[b] = mm
        v = P.op("act", lambda e, b=b, n=n: e.activation(out=rs[b][:, :n], in_=psn[:, :n], func=AF.Ln, scale=1.0 / D, bias=C.epsc[:, 0:1]),
                 [mm, rs_rd[b]])
        ps_rd = v
        r = P.op("act", lambda e, b=b, n=n: e.activation(out=rs[b][:, :n], in_=rs[b][:, :n], func=AF.Exp, scale=-0.5), [v])
        o = None
        for k in range(8):
            if out_fn is None:
                o = P.op("dve", lambda e, b=b, k=k, t0=t0, n=n: e.scalar_tensor_tensor(
                    out=C.xn[:, k, t0:t0 + n], in0=C.xres[:, k, t0:t0 + n], scalar=C.gains[:, gcol * 8 + k:gcol * 8 + k + 1],
                    in1=rs[b][:, :n], op0=ALU.mult, op1=ALU.mult), [r])
            else:
                o = out_fn(ti, k, t0, n, rs[b], r)
        rs_rd[b] = o
        last.append(o)
    return last


def emit_normphase(C, gcol):
    with ExitStack() as st:
        sb, pt = mk_alloc(C, st)
        ph = Ctx()
        ph.sq = [sb("sq%d" % i, [128, 8, 512], BF16) for i in range(2)]
        ph.rs = [sb("rs%d" % i, [128, 512], F32) for i in range(2)]
        ph.psn = pt("psn", [128, 512], F32)
        emit_norm(C, ph, gcol)
        C.P.flush()


def emit_ffn(C, gcol, wup, wdn):
    P, nc = C.P, C.nc
    with ExitStack() as st:
        sb, pt = mk_alloc(C, st)
        hid = sb("hid", [128, 11, NTOK], BF16)
        wu = [sb("wu%d" % i, [128, 8, 2, 128], BF16) for i in range(3)]
        wd = sb("wd", [128, 11, 1024], BF16)
        sa = [sb("sa%d" % i, [128, 512], F32) for i in range(2)]
        sqs = [sb("sqs%d" % i, [128, 512], BF16) for i in range(4)]
        rs = [sb("rs%d" % i, [128, 512], F32) for i in range(2)]
        psn = pt("psn", [128, 512], F32)
        psA = [pt("psA%d" % i, [128, 512], F32) for i in range(2)]
        psB = [pt("psB%d" % i, [128, 512], F32) for i in range(2)]
        psD = [pt("psD%d" % i, [128, 512], F32) for i in range(2)]
        bwu = [Buf() for _ in range(3)]; bwd = Buf(); bsa = [Buf(), Buf()]; bsq = [Buf() for _ in range(4)]; brs = [Buf(), Buf()]
        bpsn = PBuf(); bpsA = [PBuf(), PBuf()]; bpsB = [PBuf(), PBuf()]; bpsD = [PBuf(), PBuf()]
        bxn = [Buf() for _ in TT]; bxr = [Buf() for _ in TT]; bhid = [Buf() for _ in TT]
        s_wu = [P.slot() for _ in range(3)]
        s_wd = P.slot()

        def load_wu(fi):
            s = fi % 3
            bop(P, "pool", lambda e: e.dma_start(out=wu[s][:].rearrange("p k a c -> p (k a c)"), in_=wup[fi], max_dma_last_dim=8192), w=[bwu[s]], slot=s_wu[s])

        def load_wd(half):
            bop(P, "pool", lambda e: e.dma_start(out=wd[:].rearrange("p f n -> p (f n)"), in_=wdn[half], max_dma_last_dim=8192), w=[bwd], slot=s_wd)

        sqc = [0]

        def norm(ti):
            t0, n = TT[ti]
            b = ti % 2
            for k in range(8):
                q = sqc[0] % 4
                sqc[0] += 1
                bop(P, "act", lambda e, k=k, q=q: e.activation(out=sqs[q][:, :n], in_=C.xres[:, k, t0:t0 + n], func=AF.Square), r=[bxr[ti]], w=[bsq[q]])
                bop(P, "pe", lambda e, k=k, q=q: e.matmul(psn[:, :n], lhsT=C.ones[:], rhs=sqs[q][:, :n], start=(k == 0), stop=(k == 7)), r=[bsq[q]], w=[bpsn])
            bop(P, "act", lambda e: e.activation(out=rs[b][:, :n], in_=psn[:, :n], func=AF.Ln, scale=1.0 / D, bias=C.epsc[:, 0:1]), r=[bpsn], w=[brs[b]])
            bop(P, "act", lambda e: e.activation(out=rs[b][:, :n], in_=rs[b][:, :n], func=AF.Exp, scale=-0.5), r=[brs[b]], w=[brs[b]])
            for k in range(8):
                bop(P, "dve", lambda e, k=k: e.scalar_tensor_tensor(
                    out=C.xn[:, k, t0:t0 + n], in0=C.xres[:, k, t0:t0 + n], scalar=C.gains[:, gcol * 8 + k:gcol * 8 + k + 1],
                    in1=rs[b][:, :n], op0=ALU.mult, op1=ALU.mult), r=[brs[b], bxr[ti]], w=[bxn[ti]])

        load_wd(0)
        for fi in range(3):
            load_wu(fi)
        norm(0)
        norm(1)
        cnt = 0
        dcnt = 0
        for half in range(2):
            if half == 1:
                load_wd(1)
            for f in range(11):
                fi = half * 11 + f
                s = fi % 3
                for ti, (t0, n) in enumerate(TT):
                    b = cnt % 2
                    cnt += 1
                    for k in range(8):
                        bop(P, "pe", lambda e, b=b, s=s, k=k, t0=t0, n=n: e.matmul(psA[b][:, :n], lhsT=wu[s][:, k, 0, :], rhs=C.xn[:, k, t0:t0 + n], start=(k == 0), stop=(k == 7)),
                            r=[bwu[s], bxn[ti]], w=[bpsA[b]])
                    for k in range(8):
                        bop(P, "pe", lambda e, b=b, s=s, k=k, t0=t0, n=n: e.matmul(psB[b][:, :n], lhsT=wu[s][:, k, 1, :], rhs=C.xn[:, k, t0:t0 + n], start=(k == 0), stop=(k == 7)),
                            r=[bwu[s], bxn[ti]], w=[bpsB[b]])
                    bop(P, "act", lambda e, b=b, n=n: e.activation(out=sa[b][:, :n], in_=psA[b][:, :n], func=AF.Silu), r=[bpsA[b]], w=[bsa[b]])
                    bop(P, "dve", lambda e, b=b, f=f, t0=t0, n=n: e.tensor_tensor(out=hid[:, f, t0:t0 + n], in0=sa[b][:, :n], in1=psB[b][:, :n], op=ALU.mult),
                        r=[bsa[b], bpsB[b]], w=[bhid[ti]])
                    if fi == 0 and ti + 2 < len(TT):
                        norm(ti + 2)
                if fi + 3 < NF:
                    load_wu(fi + 3)
            for ti, (t0, n) in enumerate(TT):
                for mo in range(8):
                    b = dcnt % 2
                    dcnt += 1
                    for f in range(11):
                        bop(P, "pe", lambda e, b=b, f=f, mo=mo, t0=t0, n=n: e.matmul(psD[b][:, :n], lhsT=wd[:, f, mo * 128:(mo + 1) * 128], rhs=hid[:, f, t0:t0 + n], start=(f == 0), stop=(f == 10)),
                            r=[bwd, bhid[ti]], w=[bpsD[b]])
                    bop(P, "dve", lambda e, b=b, mo=mo, t0=t0, n=n: e.scalar_tensor_tensor(
                        out=C.xres[:, mo, t0:t0 + n], in0=psD[b][:, :n], scalar=0.5, in1=C.xres[:, mo, t0:t0 + n], op0=ALU.mult, op1=ALU.add),
                        r=[bpsD[b]], w=[bxr[ti]])
        P.flush()


RET_H = 4
CR_MP = 0
CR_MS = 512
CR_KDP = 768
CR_EPP = 772
CR_KDS = 776
CR_EPS = 780
CR_RM = 784
CRW = 800


def ret_gammas():
    return [1.0 - 2.0 ** (-5.0 - h) for h in range(RET_H)]


def emit_ret(C, j, rwin, rwout, rnorm, cs, st_in, st_out_p, st_out_s):
    P, nc = C.P, C.nc
    gam = ret_gammas()
    with ExitStack() as st:
        sb, pt = mk_alloc(C, st)
        wh = [sb("wh%d" % i, [128, 8, 1536], BF16) for i in range(2)]
        wo = sb("wo", [128, 4, 1024], BF16)
        cst = sb("cs", [128, 2, 512], F32)
        qT = sb("qT", [128, 2, 512], BF16)
        kT = sb("kT", [128, 2, 512], BF16)
        vtok = [sb("vtok%d" % i, [128, 512], BF16) for i in range(2)]
        ktl = [sb("ktl%d" % i, [128, 256], BF16) for i in range(2)]
        scm = [sb("scm%d" % i, [128, 128], BF16) for i in range(2)]
        gs = sb("gs", [128, 512], F32)
        on = sb("on", [128, 512], F32)
        go = [sb("go%d" % i, [128, 512], BF16) for i in range(2)]
        goT = sb("goT", [128, 4, 512], BF16)
        S = sb("S", [128, 2, 512], F32)
        Sb = sb("Sb", [128, 2, 512], BF16)
        gn = sb("gn", [128, 512], F32)
        qz = sb("qz", [128, 2, 16, 64], BF16)
        kz = [sb("kz%d" % i, [64, 256], BF16) for i in range(2)]
        S0f = [sb("S0f%d" % i, [128, 512], F32) for i in range(2)]
        S0b = [sb("S0b%d" % i, [128, 512], BF16) for i in range(2)]
        st4 = sb("st4", [128, 4], F32)
        B = [pt("b%d" % i, [128, 512], F32) for i in range(8)]
        PT = [B[6][:, 0:128].bitcast(BF16), B[3][:, 0:128].bitcast(BF16)]
        SC = [B[6][:, 128:256], B[3][:, 128:256]]
        TR = B[3][:, 256:512].bitcast(BF16)
        cret = C.cret

        bwh = [Buf(), Buf()]; bwo = Buf(); bcs = Buf(); bt12 = Buf(); bqT = Buf(); bkT = Buf()
        bvtok = [Buf(), Buf()]; bktl = [Buf(), Buf()]; bscm = [Buf(), Buf()]; bgs = Buf(); bon = Buf(); bgo = [Buf(), Buf()]; bgoT = Buf()
        bS = Buf(); bSb = Buf(); bgn = Buf(); bqz = Buf(); bkz = [Buf(), Buf()]
        bS0f = [Buf(), Buf()]; bS0b = [Buf(), Buf()]; bst4 = Buf()
        bB = [PBuf() for _ in range(8)]
        bB7t = bB[3]
        bPT = [bB[6], bB[3]]; bSC = [bB[6], bB[3]]
        bx = [[Buf() for _ in TT] for _ in range(8)]
        s_wh = [P.slot(), P.slot()]; s_wo = P.slot(); s_cs = P.slot(); s_gn = P.slot()
        s_S0f = [P.slot(), P.slot()]; s_S0b = [P.slot(), P.slot()]; s_so = [P.slot(), P.slot()]; s_sp = P.slot()

        def load_wh(h):
            sl = h % 2
            bop(P, "pool", lambda e, sl=sl: e.dma_start(out=wh[sl][:].rearrange("p k n -> p (k n)"), in_=rwin[h], max_dma_last_dim=8192),
                w=[bwh[sl]], slot=s_wh[sl])

        load_wh(0)
        dbg_dump(C, "xn0", C.xn[:, 0, 0:512], [], 512, bf=True)
        dbg_dump(C, "wh0", wh[0][:, 0, 0:512], [bwh[0]], 512, bf=True)
        ucnt = 0
        for h in range(RET_H):
            sl = h % 2
            g = gam[h]
            bop(P, "pool", lambda e, h=h: e.dma_start(out=wo[:].rearrange("p a n -> p (a n)"), in_=rwout[h], max_dma_last_dim=8192),
                w=[bwo], slot=s_wo)
            if h + 1 < RET_H:
                load_wh(h + 1)
            bop(P, "sp", lambda e, h=h: e.dma_start(out=gn[:], in_=rnorm[h].partition_broadcast(128)), w=[bgn], slot=s_gn)
            bop(P, "pool", lambda e: e.memset(S[:], 0.0), w=[bS])
            bop(P, "pool", lambda e: e.memset(Sb[:], 0.0), w=[bSb])
            for ti, (t0, n) in enumerate(TT):
                sample = (ti == 4)
                bop(P, "sp", lambda e, t0=t0, n=n: e.dma_start(out=cst[:, :, :n], in_=cs[:, :, t0:t0 + n].rearrange("a p t -> p a t")),
                    w=[bcs], slot=s_cs)
                for qi in range(4):
                    for k in range(8):
                        bop(P, "pe", lambda e, sl=sl, qi=qi, k=k, t0=t0, n=n: e.matmul(B[qi][:, :n], lhsT=wh[sl][:, k, qi * 128:(qi + 1) * 128], rhs=C.xn[:, k, t0:t0 + n], start=(k == 0), stop=(k == 7)),
                            r=[bwh[sl]], w=[bB[qi]])
                for (dst, bd, b0, b1, sc) in ((qT, bqT, 0, 1, 1.0), (kT, bkT, 2, 3, 0.0625)):
                    for half in range(2):
                        ca, cb = (0, 1) if half == 0 else (1, 0)
                        bop(P, "dve", lambda e, b0=b0, ca=ca, sc=sc, n=n: e.scalar_tensor_tensor(out=gs[:, :n], in0=B[b0][:, :n], scalar=sc, in1=cst[:, ca, :n], op0=ALU.mult, op1=ALU.mult),
                            r=[bB[b0], bcs], w=[bgs])
                        bop(P, "dve", lambda e, b1=b1, cb=cb, sc=sc, n=n: e.scalar_tensor_tensor(out=on[:, :n], in0=B[b1][:, :n], scalar=sc, in1=cst[:, cb, :n], op0=ALU.mult, op1=ALU.mult),
                            r=[bB[b1], bcs], w=[bon])
                        bop(P, "dve", lambda e, dst=dst, half=half, n=n: e.tensor_tensor(out=dst[:, half, :n], in0=gs[:, :n], in1=on[:, :n], op=(ALU.subtract if half == 0 else ALU.add)),
                            r=[bgs, bon], w=[bd])
                blocks = [(0, 64)] if sample else [(c * 128, 128) for c in range(n // 128)]
                MC = (CR_MS + h * 64) if sample else (CR_MP + h * 128)
                KD = (CR_KDS if sample else CR_KDP) + h
                EP = (CR_EPS if sample else CR_EPP) + h

                def stage_A(ci, c0, nb, sl=sl, t0=t0, MC=MC, KD=KD):
                    pb = ci % 2
                    a0 = t0 + c0
                    vb = 4 + pb
                    for k in range(8):
                        bop(P, "pe", lambda e, k=k: e.matmul(B[vb][:nb, :], lhsT=C.xn[:, k, a0:a0 + nb], rhs=wh[sl][:, k, 512:1024], start=(k == 0), stop=(k == 7)),
                            r=[bwh[sl]], w=[bB[vb]])
                    bop(P, "act", lambda e: e.activation(out=vtok[pb][:nb, :], in_=B[vb][:nb, :], func=AF.Copy), r=[bB[vb]], w=[bvtok[pb]])
                    for jj in range(2):
                        bop(P, "pe", lambda e, jj=jj: e.transpose(out=PT[pb][:nb, jj * 128:(jj + 1) * 128], in_=kT[:, jj, c0:c0 + nb], identity=C.ident),
                            r=[bkT], w=[bPT[pb]])
                    for jj in range(2):
                        bop(P, "pe", lambda e, jj=jj: e.matmul(SC[pb][:nb, :nb], lhsT=kT[:, jj, c0:c0 + nb], rhs=qT[:, jj, c0:c0 + nb], start=(jj == 0), stop=(jj == 1)),
                            r=[bkT, bqT], w=[bSC[pb]])
                    bop(P, "dve", lambda e: e.tensor_scalar(out=ktl[pb][:nb, :], in0=PT[pb][:nb, :], scalar1=cret[:nb, KD:KD + 1], scalar2=None, op0=ALU.mult),
                        r=[bPT[pb]], w=[bktl[pb]])
                    bop(P, "dve", lambda e: e.tensor_tensor(out=scm[pb][:nb, :nb], in0=SC[pb][:nb, :nb], in1=cret[:nb, MC:MC + nb], op=ALU.mult),
                        r=[bSC[pb]], w=[bscm[pb]])

                def stage_B(ci, c0, nb, sl=sl, t0=t0, EP=EP, sample=sample, g=g, h=h):
                    nonlocal ucnt
                    pb = ci % 2
                    a0 = t0 + c0
                    bop(P, "pe", lambda e: e.matmul(B[7][:nb, :], lhsT=scm[pb][:nb, :nb], rhs=vtok[pb][:nb, :], start=True, stop=False),
                        r=[bscm[pb], bvtok[pb]], w=[bB[7]])
                    if not sample:
                        for jj in range(2):
                            bop(P, "pe", lambda e, jj=jj: e.matmul(B[7][:nb, :], lhsT=qT[:, jj, c0:c0 + nb], rhs=Sb[:, jj, :], start=False, stop=(jj == 1)),
                                r=[bqT, bSb], w=[bB[7]])
                        for jj in range(2):
                            bop(P, "pe", lambda e, jj=jj: e.matmul(B[jj][:, :], lhsT=ktl[pb][:nb, jj * 128:(jj + 1) * 128], rhs=vtok[pb][:nb, :], start=True, stop=True),
                                r=[bktl[pb], bvtok[pb]], w=[bB[jj]])
                        cd = g ** 128
                        for jj in range(2):
                            bop(P, "dve", lambda e, jj=jj: e.scalar_tensor_tensor(out=S[:, jj, :], in0=S[:, jj, :], scalar=cd, in1=B[jj][:, :], op0=ALU.mult, op1=ALU.add),
                                r=[bB[jj]], w=[bS])
                        for jj in range(2):
                            bop(P, "act", lambda e, jj=jj: e.activation(out=Sb[:, jj, :], in_=S[:, jj, :], func=AF.Copy), r=[bS], w=[bSb])
                    else:
                        for jj in range(2):
                            bop(P, "dve", lambda e, jj=jj: e.tensor_tensor(out=qz[:, jj, :, :], in0=qT[:, jj, 0:64].unsqueeze(1).broadcast_to([128, 16, 64]), in1=C.bm[:, :, :], op=ALU.mult),
                                r=[bqT], w=[bqz])
                        cd = g ** 4
                        for i in range(NSS):
                            kb = i % 2
                            bop(P, "dve", lambda e, i=i, kb=kb: e.tensor_scalar(out=kz[kb][:, :], in0=ktl[pb][:64, :], scalar1=cret[:64, CR_RM + i:CR_RM + i + 1], scalar2=None, op0=ALU.mult),
                                r=[bktl[pb]], w=[bkz[kb]])
                            for jj in range(2):
                                u = ucnt % 2
                                ucnt += 1
                                bop(P, "pool", lambda e, u=u, i=i, jj=jj: e.dma_start(out=S0b[u][:, :], in_=st_in[i, h, jj * 128:(jj + 1) * 128, :]), w=[bS0b[u]], slot=s_S0b[u])
                                bop(P, "sp", lambda e, u=u, i=i, jj=jj: e.dma_start(out=S0f[u][:, :], in_=st_in[i, h, jj * 128:(jj + 1) * 128, :]), w=[bS0f[u]], slot=s_S0f[u])
                                last = (i == NSS - 1 and jj == 1)
                                bop(P, "pe", lambda e, u=u, i=i, jj=jj, last=last: e.matmul(B[7][:64, :], lhsT=qz[:, jj, i, :], rhs=S0b[u][:, :], start=False, stop=last),
                                    r=[bqz, bS0b[u]], w=[bB[7]])
                                bop(P, "pe", lambda e, kb=kb, jj=jj: e.matmul(B[jj][:, :], lhsT=kz[kb][:, jj * 128:(jj + 1) * 128], rhs=vtok[pb][:64, :], start=True, stop=True),
                                    r=[bkz[kb], bvtok[pb]], w=[bB[jj]])
                                bop(P, "dve", lambda e, u=u, jj=jj: e.scalar_tensor_tensor(out=S0f[u][:, :], in0=S0f[u][:, :], scalar=cd, in1=B[jj][:, :], op0=ALU.mult, op1=ALU.add),
                                    r=[bB[jj]], w=[bS0f[u]])
                                bop(P, "sp", lambda e, u=u, i=i, jj=jj: e.dma_start(out=st_out_s[i, h, jj * 128:(jj + 1) * 128, :], in_=S0f[u][:, :]), r=[bS0f[u]], slot=s_so[u])
                    for k in range(8):
                        bop(P, "pe", lambda e, k=k: e.matmul(B[2][:nb, :], lhsT=C.xn[:, k, a0:a0 + nb], rhs=wh[sl][:, k, 1024:1536], start=(k == 0), stop=(k == 7)),
                            r=[bwh[sl]], w=[bB[2]])
                    bop(P, "act", lambda e: e.activation(out=gs[:nb, :], in_=B[2][:nb, :], func=AF.Exp, scale=-1.0), r=[bB[2]], w=[bgs])
                    bop(P, "act", lambda e: e.activation(out=gs[:nb, :], in_=gs[:nb, :], func=AF.Ln, bias=C.one[:nb, 0:1]), r=[bgs], w=[bgs])
                    bop(P, "act", lambda e: e.activation(out=gs[:nb, :], in_=gs[:nb, :], func=AF.Exp, scale=-1.0), r=[bgs], w=[bgs])
                    bop(P, "dve", lambda e: e.tensor_tensor(out=gs[:nb, :], in0=B[2][:nb, :], in1=gs[:nb, :], op=ALU.mult), r=[bgs, bB[2]], w=[bgs])
                    bop(P, "act", lambda e: e.activation(out=on[:nb, :], in_=B[7][:nb, :], func=AF.Square, accum_out=st4[:nb, 0:1]),
                        r=[bB[7]], w=[bon, bst4])
                    bop(P, "act", lambda e: e.activation(out=st4[:nb, 2:3], in_=st4[:nb, 0:1], func=AF.Ln, scale=1.0 / 512, bias=cret[:nb, EP:EP + 1]), r=[bst4], w=[bst4])
                    bop(P, "act", lambda e: e.activation(out=st4[:nb, 3:4], in_=st4[:nb, 2:3], func=AF.Exp, scale=-0.5), r=[bst4], w=[bst4])
                    bop(P, "dve", lambda e: e.scalar_tensor_tensor(out=on[:nb, :], in0=B[7][:nb, :], scalar=st4[:nb, 3:4], in1=gn[:nb, :], op0=ALU.mult, op1=ALU.mult),
                        r=[bB[7], bst4, bgn], w=[bon])
                    bop(P, "dve", lambda e: e.tensor_tensor(out=go[pb][:nb, :], in0=on[:nb, :], in1=gs[:nb, :], op=ALU.mult), r=[bon, bgs], w=[bgo[pb]])

                def stage_T(ci, c0, nb):
                    pb = ci % 2
                    for e4 in range(4):
                        bop(P, "pe", lambda e, e4=e4: e.transpose(out=TR[:, e4 * 128:e4 * 128 + nb], in_=go[pb][:nb, e4 * 128:(e4 + 1) * 128], identity=C.ident[:nb, :nb]),
                            r=[bgo[pb]], w=[bB7t])
                    bop(P, "act", lambda e: e.activation(out=goT[:, :, c0:c0 + nb], in_=TR.rearrange("p (a t) -> p a t", a=4)[:, :, :nb], func=AF.Copy),
                        r=[bB7t], w=[bgoT])

                nblk = len(blocks)
                sched = []
                if nblk == 1:
                    sched = [("A", 0), ("B", 0), ("T", 0)]
                else:
                    sched = [("A", 0), ("A", 1), ("B", 0), ("A", 2), ("B", 1), ("T", 0), ("A", 3), ("B", 2), ("T", 1), ("B", 3), ("T", 2), ("T", 3)]
                for (kind, ci) in sched:
                    c0, nb = blocks[ci]
                    if kind == "A":
                        stage_A(ci, c0, nb)
                    elif kind == "B":
                        stage_B(ci, c0, nb)
                    else:
                        stage_T(ci, c0, nb)
                for m in range(8):
                    wb = 4 + (m % 2)
                    for e4 in range(4):
                        bop(P, "pe", lambda e, m=m, e4=e4, n=n, wb=wb: e.matmul(B[wb][:, :n], lhsT=wo[:, e4, m * 128:(m + 1) * 128], rhs=goT[:, e4, :n], start=(e4 == 0), stop=(e4 == 3)),
                            r=[bwo, bgoT], w=[bB[wb]])
                    bop(P, "dve", lambda e, m=m, t0=t0, n=n, wb=wb: e.tensor_tensor(out=C.xres[:, m, t0:t0 + n], in0=C.xres[:, m, t0:t0 + n], in1=B[wb][:, :n], op=ALU.add),
                        r=[bB[wb]], w=[bx[m][ti]])
                if ti == 3:
                    bop(P, "sp", lambda e, h=h: e.dma_start(out=st_out_p[h].rearrange("(a p) n -> p a n", p=128), in_=S[:, :, :]), r=[bS], slot=s_sp)
        P.flush()


HG_H = 8
CH_MP = 800
CH_MS = 928
CH_RMP = 992
CH_CMP = 1000
CH_CMS = 1512
CRW2 = 1576


def emit_hg(C, j, hwin, hwout, hnorm, lbl, st_in, st_out_p, st_out_s):
    P, nc = C.P, C.nc
    with ExitStack() as st:
        sb, pt = mk_alloc(C, st)
        wh = [sb("hwh%d" % i, [128, 8, 512], BF16) for i in range(2)]
        wo = [sb("hwo%d" % i, [128, 1024], BF16) for i in range(2)]
        F = {nm: sb("h" + nm, [128, 512], F32) for nm in ("qs", "ez", "r", "f", "kk", "b", "d1", "X", "Y")}
        qc = [sb("hqc%d" % i, [128, 512], BF16) for i in range(2)]
        kc = [sb("hkc%d" % i, [128, 512], BF16) for i in range(2)]
        eB = [sb("heB%d" % i, [128, 16], F32) for i in range(2)]
        vtok = [sb("hvtok%d" % i, [128, 128], BF16) for i in range(2)]
        gsil = [sb("hgsil%d" % i, [128, 128], F32) for i in range(2)]
        ktl = [sb("hktl%d" % i, [128, 128], BF16) for i in range(2)]
        scm = [sb("hscm%d" % i, [128, 128], BF16) for i in range(2)]
        qz = [sb("hqz%d" % i, [128, 1024], BF16) for i in range(2)]
        kz = [sb("hkz%d" % i, [128, 2048], BF16) for i in range(2)]
        Sdb = [sb("hSdb%d" % i, [128, 16, 128], BF16) for i in range(2)]
        S = sb("hS", [128, 128], F32)
        S0 = sb("hS0", [128, 16, 128], F32)
        on = sb("hon", [128, 128], F32)
        go = [sb("hgo%d" % i, [128, 128], BF16) for i in range(2)]
        goT = sb("hgoT", [128, 512], BF16)
        gn = sb("hgn", [128, 128], F32)
        lb = sb("hlb", [128, 2, 8], F32)
        lbv = sb("hlbv", [128, 8], F32)
        oml = sb("homl", [128, 8], F32)
        st4 = sb("hst4", [128, 4], F32)
        B = [pt("hb%d" % i, [128, 512], F32) for i in range(8)]
        VG = [B[4][:, 0:256], B[6][:, 0:256]]
        PT = [B[4][:, 256:320].bitcast(BF16), B[6][:, 256:320].bitcast(BF16)]
        SC = [B[4][:, 320:448], B[6][:, 320:448]]
        TR = B[3][:, 256:320].bitcast(BF16)
        UR = [B[5][:, i * 128:(i + 1) * 128] for i in range(4)] + [B[7][:, i * 128:(i + 1) * 128] for i in range(4)]
        cret = C.cret
        bF = {nm: Buf() for nm in F}
        bwh = [Buf(), Buf()]; bwo = [Buf(), Buf()]; bqc = [Buf(), Buf()]; bkc = [Buf(), Buf()]; beB = [Buf(), Buf()]
        bvtok = [Buf(), Buf()]; bgsil = [Buf(), Buf()]; bktl = [Buf(), Buf()]; bscm = [Buf(), Buf()]
        bqz = [Buf(), Buf()]; bkz = [Buf(), Buf()]; bSdb = [Buf(), Buf()]; bgo = [Buf(), Buf()]
        bS = Buf(); bS0 = Buf(); bon = Buf(); bgoT = Buf(); bgn = Buf(); blb = Buf(); bst4 = Buf()
        bB = [PBuf() for _ in range(8)]
        bVG = [bB[4], bB[6]]; bPT = [bB[4], bB[6]]; bSC = [bB[4], bB[6]]; bTR = bB[3]; bUR = [bB[5]] * 4 + [bB[7]] * 4
        bx = [[Buf() for _ in TT] for _ in range(8)]
        s_wh = [P.slot(), P.slot()]; s_wo = [P.slot(), P.slot()]; s_gn = P.slot(); s_lb = P.slot()
        s_S0 = P.slot(); s_so = P.slot(); s_sp = P.slot()

        def sigmoid_act(dst, src, bdst, bsrc):
            bop(P, "act", lambda e: e.activation(out=dst, in_=src, func=AF.Exp, scale=-1.0), r=[bsrc], w=[bdst])
            bop(P, "act", lambda e: e.activation(out=dst, in_=dst, func=AF.Ln, bias=C.one[:dst.shape[0], 0:1]), r=[bdst], w=[bdst])
            bop(P, "act", lambda e: e.activation(out=dst, in_=dst, func=AF.Exp, scale=-1.0), r=[bdst], w=[bdst])

        bop(P, "sp", lambda e: e.dma_start(out=lb[:], in_=lbl), w=[blb], slot=s_lb)
        if j == 0:
            bop(P, "dve", lambda e: e.memset(lbv[:], 0.0), w=[blb])
            bop(P, "dve", lambda e: e.memset(oml[:], 1.0), w=[blb])
        else:
            bop(P, "dve", lambda e: e.tensor_tensor(out=lbv[:], in0=lb[:, 0, :], in1=lb[:, 1, :], op=ALU.subtract), r=[blb], w=[blb])
            bop(P, "act", lambda e: e.activation(out=oml[:], in_=lbv[:], func=AF.Exp), r=[blb], w=[blb])
            bop(P, "dve", lambda e: e.tensor_scalar(out=lbv[:], in0=oml[:], scalar1=1.0, scalar2=None, op0=ALU.add), r=[blb], w=[blb])
            bop(P, "dve", lambda e: e.reciprocal(out=lbv[:], in_=lbv[:]), r=[blb], w=[blb])
            bop(P, "dve", lambda e: e.tensor_tensor(out=oml[:], in0=oml[:], in1=lbv[:], op=ALU.mult), r=[blb], w=[blb])

        def load_w(h):
            sl = h % 2
            bop(P, "pool", lambda e, sl=sl, h=h: e.dma_start(out=wh[sl][:].rearrange("p k n -> p (k n)"), in_=hwin[h], max_dma_last_dim=8192),
                w=[bwh[sl]], slot=s_wh[sl])
            bop(P, "pool", lambda e, sl=sl, h=h: e.dma_start(out=wo[sl][:], in_=hwout[h], max_dma_last_dim=8192), w=[bwo[sl]], slot=s_wo[sl])

        load_w(0)
        ugl = 0
        jobs = [(h, ti) for h in range(HG_H) for ti in range(len(TT))]

        def head_setup(h):
            if h + 1 < HG_H:
                load_w(h + 1)
            bop(P, "sp", lambda e: e.dma_start(out=gn[:], in_=hnorm[h].partition_broadcast(128)), w=[bgn], slot=s_gn)
            bop(P, "pool", lambda e: e.memset(S[:], 0.0), w=[bS])
            bop(P, "sp", lambda e: e.dma_start(out=S0[:], in_=st_in[:, h, :, :].rearrange("i d e -> d i e")), w=[bS0], slot=s_S0)

        def make_s1(h, ti, kp):
            sl = h % 2
            t0, n = TT[ti]
            sample = (ti == 4)
            CL = 4 if sample else 32
            nch = n // CL
            CM = CH_CMS if sample else CH_CMP
            qcj, kcj, eBj = qc[kp], kc[kp], eB[kp]

            def p0():
                for qi in range(2):
                    for k in range(8):
                        bop(P, "pe", lambda e, qi=qi, k=k: e.matmul(B[qi][:, :n], lhsT=wh[sl][:, k, qi * 128:(qi + 1) * 128], rhs=C.xn[:, k, t0:t0 + n], start=(k == 0), stop=(k == 7)),
                            r=[bwh[sl]], w=[bB[qi]])

            def p1():
                sigmoid_act(F["qs"][:, :n], B[0][:, :n], bF["qs"], bB[0])
                bop(P, "act", lambda e: e.activation(out=F["ez"][:, :n], in_=B[1][:, :n], func=AF.Exp, scale=-1.0), r=[bB[1]], w=[bF["ez"]])
                bop(P, "act", lambda e: e.activation(out=F["r"][:, :n], in_=F["ez"][:, :n], func=AF.Ln, bias=C.one[:, 0:1]), r=[bF["ez"]], w=[bF["r"]])
                bop(P, "act", lambda e: e.activation(out=F["r"][:, :n], in_=F["r"][:, :n], func=AF.Exp, scale=-1.0), r=[bF["r"]], w=[bF["r"]])

            def p2():
                bop(P, "dve", lambda e: e.tensor_tensor(out=F["qs"][:, :n], in0=B[0][:, :n], in1=F["qs"][:, :n], op=ALU.mult), r=[bF["qs"], bB[0]], w=[bF["qs"]])
                bop(P, "dve", lambda e: e.tensor_scalar(out=F["f"][:, :n], in0=F["r"][:, :n], scalar1=oml[:, h:h + 1], scalar2=lbv[:, h:h + 1], op0=ALU.mult, op1=ALU.add),
                    r=[bF["r"], blb], w=[bF["f"]])
                bop(P, "dve", lambda e: e.scalar_tensor_tensor(out=F["kk"][:, :n], in0=F["ez"][:, :n], scalar=oml[:, h:h + 1], in1=F["r"][:, :n], op0=ALU.mult, op1=ALU.mult),
                    r=[bF["ez"], bF["r"], blb], w=[bF["kk"]])
                bop(P, "act", lambda e: e.activation(out=F["f"][:, :n], in_=F["f"][:, :n], func=AF.Ln), r=[bF["f"]], w=[bF["f"]])

            def p3():
                bop(P, "dve", lambda e: e.tensor_tensor_scan(out=F["b"][:, :n], data0=cret[:, CM:CM + n], data1=F["f"][:, :n], initial=0.0, op0=ALU.mult, op1=ALU.add),
                    r=[bF["f"]], w=[bF["b"]])
                bop(P, "pool", lambda e: e.tensor_tensor(
                    out=F["d1"][:, :n].rearrange("p (c s) -> p c s", s=CL), in0=F["b"][:, :n].rearrange("p (c s) -> p c s", s=CL),
                    in1=F["b"][:, :n].rearrange("p (c s) -> p c s", s=CL)[:, :, CL - 1:CL].broadcast_to([128, nch, CL]), op=ALU.subtract),
                    r=[bF["b"]], w=[bF["d1"]])

            def p4():
                bop(P, "act", lambda e: e.activation(out=F["X"][:, :n], in_=F["d1"][:, :n], func=AF.Exp), r=[bF["d1"]], w=[bF["X"]])
                bop(P, "act", lambda e: e.activation(out=F["Y"][:, :n], in_=F["d1"][:, :n], func=AF.Exp, scale=-1.0), r=[bF["d1"]], w=[bF["Y"]])
                bop(P, "act", lambda e: e.activation(out=eBj[:, :nch], in_=F["b"][:, :n].rearrange("p (c s) -> p c s", s=CL)[:, :, CL - 1], func=AF.Exp),
                    r=[bF["b"]], w=[beB[kp]])

            def p5():
                bop(P, "pool", lambda e: e.tensor_tensor(out=qcj[:, :n], in0=F["qs"][:, :n], in1=F["X"][:, :n], op=ALU.mult), r=[bF["qs"], bF["X"]], w=[bqc[kp]])
                bop(P, "pool", lambda e: e.tensor_tensor(out=kcj[:, :n], in0=F["kk"][:, :n], in1=F["Y"][:, :n], op=ALU.mult), r=[bF["kk"], bF["Y"]], w=[bkc[kp]])

            return [p0, p1, p2, p3, p4, p5]

        def make_blocks(h, ti, kp):
            sl = h % 2
            t0, n = TT[ti]
            sample = (ti == 4)
            CL = 4 if sample else 32
            qcj, kcj, eBj = qc[kp], kc[kp], eB[kp]
            blocks = [(0, 64)] if sample else [(c * 128, 128) for c in range(4)]
            MC = CH_MS if sample else CH_MP
            nbc = 16 if sample else 4
            if sample:
                bmask = C.bm
                rmv = cret[:64, CR_RM:CR_RM + 16]
            else:
                bmask = C.bmp
                rmv = cret[:, CH_RMP:CH_RMP + 4]
            ubase = {}

            def A_pe(ci):
                c0, nb = blocks[ci]
                pb = ci % 2
                a0 = t0 + c0
                for k in range(8):
                    bop(P, "pe", lambda e, k=k: e.matmul(VG[pb][:nb, :], lhsT=C.xn[:, k, a0:a0 + nb], rhs=wh[sl][:, k, 256:512], start=(k == 0), stop=(k == 7)),
                        r=[bwh[sl]], w=[bVG[pb]])
                bop(P, "pe", lambda e: e.transpose(out=PT[pb][:nb, :], in_=kcj[:, c0:c0 + nb], identity=C.ident), r=[bkc[kp]], w=[bPT[pb]])
                bop(P, "pe", lambda e: e.matmul(SC[pb][:nb, :nb], lhsT=kcj[:, c0:c0 + nb], rhs=qcj[:, c0:c0 + nb], start=True, stop=True), r=[bkc[kp], bqc[kp]], w=[bSC[pb]])

            def A_ev(ci):
                nonlocal ugl
                c0, nb = blocks[ci]
                pb = ci % 2
                bop(P, "dve", lambda e: e.tensor_copy(out=ktl[pb][:nb, :], in_=PT[pb][:nb, :]), r=[bPT[pb]], w=[bktl[pb]])
                bop(P, "act", lambda e: e.activation(out=vtok[pb][:nb, :], in_=VG[pb][:nb, 0:128], func=AF.Copy), r=[bVG[pb]], w=[bvtok[pb]])
                qzv = qz[pb][:, 0:nbc * nb].rearrange("p (c t) -> p c t", c=nbc)
                kzv = kz[pb][:nb, 0:nbc * 128].rearrange("p (c d) -> p c d", c=nbc)
                bop(P, "pool", lambda e: e.tensor_tensor(out=kzv, in0=ktl[pb][:nb, :].unsqueeze(1).broadcast_to([nb, nbc, 128]), in1=rmv.unsqueeze(2).broadcast_to([nb, nbc, 128]), op=ALU.mult),
                    r=[bktl[pb]], w=[bkz[pb]])
                bop(P, "pool", lambda e: e.tensor_tensor(out=qzv, in0=qcj[:, c0:c0 + nb].unsqueeze(1).broadcast_to([128, nbc, nb]), in1=bmask, op=ALU.mult),
                    r=[bqc[kp]], w=[bqz[pb]])
                bop(P, "dve", lambda e: e.tensor_tensor(out=scm[pb][:nb, :nb], in0=SC[pb][:nb, :nb], in1=cret[:nb, MC:MC + nb], op=ALU.mult), r=[bSC[pb]], w=[bscm[pb]])
                sigmoid_act(gsil[pb][:nb, :], VG[pb][:nb, 128:256], bgsil[pb], bVG[pb])
                bop(P, "dve", lambda e: e.tensor_tensor(out=gsil[pb][:nb, :], in0=VG[pb][:nb, 128:256], in1=gsil[pb][:nb, :], op=ALU.mult), r=[bgsil[pb], bVG[pb]], w=[bgsil[pb]])
                ubase[ci] = ugl
                if nbc <= 4:
                    for c in range(nbc):
                        u = 4 * pb + c
                        bop(P, "pe", lambda e, c=c, u=u: e.matmul(UR[u], lhsT=kzv[:, c, :], rhs=vtok[pb][:nb, :], start=True, stop=True),
                            r=[bkz[pb], bvtok[pb]], w=[bUR[u]])
                ugl += nbc

            def B_chain(ci):
                c0, nb = blocks[ci]
                pb = ci % 2
                kzv = kz[pb][:nb, 0:nbc * 128].rearrange("p (c d) -> p c d", c=nbc)
                for c in range(nbc):
                    ec = (c0 // CL + c)
                    u = (4 * pb + c) if nbc <= 4 else (4 * (c % 2) + (c // 2) % 4)
                    Sin = S0[:, c, :] if sample else S[:, :]
                    bSin = bS0 if sample else bS
                    if nbc > 4:
                        bop(P, "pe", lambda e, c=c, u=u: e.matmul(UR[u], lhsT=kzv[:, c, :], rhs=vtok[pb][:nb, :], start=True, stop=True),
                            r=[bkz[pb], bvtok[pb]], w=[bUR[u]])
                    bop(P, "dve", lambda e, c=c, ec=ec, Sin=Sin: e.tensor_scalar(out=Sdb[pb][:, c, :], in0=Sin, scalar1=eBj[:, ec:ec + 1], scalar2=None, op0=ALU.mult),
                        r=[bSin, beB[kp]], w=[bSdb[pb]])
                    bop(P, "dve", lambda e, ec=ec, u=u, Sin=Sin: e.scalar_tensor_tensor(out=Sin, in0=Sin, scalar=eBj[:, ec:ec + 1], in1=UR[u], op0=ALU.mult, op1=ALU.add),
                        r=[bUR[u], beB[kp]], w=[bSin])

            def B_pe(ci):
                c0, nb = blocks[ci]
                pb = ci % 2
                qzv = qz[pb][:, 0:nbc * nb].rearrange("p (c t) -> p c t", c=nbc)
                bop(P, "pe", lambda e: e.matmul(B[2][:nb, 0:128], lhsT=scm[pb][:nb, :nb], rhs=vtok[pb][:nb, :], start=True, stop=False), r=[bscm[pb], bvtok[pb]], w=[bB[2]])
                for c in range(nbc):
                    bop(P, "pe", lambda e, c=c: e.matmul(B[2][:nb, 0:128], lhsT=qzv[:, c, :], rhs=Sdb[pb][:, c, :], start=False, stop=(c == nbc - 1)),
                        r=[bqz[pb], bSdb[pb]], w=[bB[2]])

            def B_norm(ci):
                c0, nb = blocks[ci]
                pb = ci % 2
                bop(P, "act", lambda e: e.activation(out=on[:nb, :], in_=B[2][:nb, 0:128], func=AF.Square, accum_out=st4[:nb, 0:1]), r=[bB[2]], w=[bon, bst4])
                bop(P, "act", lambda e: e.activation(out=st4[:nb, 2:3], in_=st4[:nb, 0:1], func=AF.Ln, scale=1.0 / 128, bias=C.epsc[:nb, 0:1]), r=[bst4], w=[bst4])
                bop(P, "act", lambda e: e.activation(out=st4[:nb, 3:4], in_=st4[:nb, 2:3], func=AF.Exp, scale=-0.5), r=[bst4], w=[bst4])
                bop(P, "dve", lambda e: e.scalar_tensor_tensor(out=on[:nb, :], in0=B[2][:nb, 0:128], scalar=st4[:nb, 3:4], in1=gn[:nb, :], op0=ALU.mult, op1=ALU.mult),
                    r=[bB[2], bst4, bgn], w=[bon])
                bop(P, "pool", lambda e: e.tensor_tensor(out=go[pb][:nb, :], in0=on[:nb, :], in1=gsil[pb][:nb, :], op=ALU.mult), r=[bon, bgsil[pb]], w=[bgo[pb]])

            def stage_T(ci):
                c0, nb = blocks[ci]
                pb = ci % 2
                bop(P, "pe", lambda e: e.transpose(out=TR[:, :nb], in_=go[pb][:nb, :], identity=C.ident[:nb, :nb]), r=[bgo[pb]], w=[bTR])
                bop(P, "act", lambda e: e.activation(out=goT[:, c0:c0 + nb], in_=TR[:, :nb], func=AF.Copy), r=[bTR], w=[bgoT])

            def wout():
                for m in range(8):
                    bop(P, "pe", lambda e, m=m: e.matmul(B[3][:, :n], lhsT=wo[sl][:, m * 128:(m + 1) * 128], rhs=goT[:, :n], start=True, stop=True),
                        r=[bwo[sl], bgoT], w=[bB[3]])
                    bop(P, "dve", lambda e, m=m: e.tensor_tensor(out=C.xres[:, m, t0:t0 + n], in0=C.xres[:, m, t0:t0 + n], in1=B[3][:, :n], op=ALU.add),
                        r=[bB[3]], w=[bx[m][ti]])
                if ti == 3:
                    bop(P, "sp", lambda e: e.dma_start(out=st_out_p[h], in_=S[:, :]), r=[bS], slot=s_sp)
                if sample:
                    bop(P, "sp", lambda e: e.dma_start(out=st_out_s[:, h, :, :].rearrange("i d e -> d i e"), in_=S0[:, :, :]), r=[bS0], slot=s_so)

            mk = lambda fn, ci: (lambda: fn(ci))
            if len(blocks) == 1:
                sched = [mk(A_pe, 0), mk(A_ev, 0), mk(B_chain, 0), mk(B_pe, 0), mk(B_norm, 0), mk(stage_T, 0)]
                slots_after = {1: [0, 1], 3: [2, 3], 5: [4, 5]}
            else:
                sched = [mk(A_pe, 0), mk(A_ev, 0), mk(A_pe, 1),
                         mk(B_chain, 0), mk(B_pe, 0), mk(A_ev, 1), mk(B_norm, 0), mk(A_pe, 2),
                         mk(B_chain, 1), mk(B_pe, 1), mk(A_ev, 2), mk(B_norm, 1), mk(stage_T, 0), mk(A_pe, 3),
                         mk(B_chain, 2), mk(B_pe, 2), mk(A_ev, 3), mk(B_norm, 2), mk(stage_T, 1),
                         mk(B_chain, 3), mk(B_pe, 3), mk(B_norm, 3), mk(stage_T, 2), mk(stage_T, 3)]
                slots_after = {7: [0, 1], 13: [2], 18: [3, 4], 22: [5]}
            return sched, slots_after, wout

        for p in make_s1(jobs[0][0], jobs[0][1], 0):
            p()
        for kj, (h, ti) in enumerate(jobs):
            kp = kj % 2
            if ti == 0:
                head_setup(h)
            sched, slots_after, wout = make_blocks(h, ti, kp)
            nxt = make_s1(jobs[kj + 1][0], jobs[kj + 1][1], 1 - kp) if kj + 1 < len(jobs) else None
            for si, stage in enumerate(sched):
                stage()
                if nxt is not None:
                    for pi in slots_after.get(si, []):
                        nxt[pi]()
            wout()
        P.flush()


def build_program(cfg):
    nc = bass.Bass("TRN2", target_bir_lowering=False)
    dr = lambda name, shape, kind="ExternalInput", dt=F32: nc.dram_tensor(name, shape, dt, kind=kind).ap()
    xT = dr("xT", [128, 8, NTOK])
    gains_d = dr("gains", [128, 13 * 8])
    cbf_d = dr("cbf", [128, 256 + 1024 + 512], dt=BF16)
    cret_d = dr("cret", [128, CRW2])
    cs_d = dr("cs", [2, 128, NTOK])
    wup_d = dr("wup", [8, NF, 128, 2048])
    wdn_d = dr("wdn", [8, 2, 128, 11 * 1024])
    rwin_d = dr("rwin", [2, 4, 128, 12288])
    rwout_d = dr("rwout", [2, 4, 128, 4096])
    rnorm_d = dr("rnorm", [2, 4, 512])
    sret_d = dr("sret", [2, NSS, 4, 256, 512])
    hwin_d = dr("hwin", [2, 8, 128, 4096])
    hwout_d = dr("hwout", [2, 8, 128, 1024])
    hnorm_d = dr("hnorm", [2, 8, 128])
    lbl_d = dr("lbl", [128, 2, 8])
    shg_d = dr("shg", [2, NSS, 8, 128, 128])
    nhp_d = dr("nhp", [2, 8, 128, 128], kind="ExternalOutput")
    nhs_d = dr("nhs", [2, NSS, 8, 128, 128], kind="ExternalOutput")
    yT = dr("yT", [128, 8, NTOK], kind="ExternalOutput")
    nrp_d = dr("nrp", [2, 4, 256, 512], kind="ExternalOutput")
    nrs_d = dr("nrs", [2, NSS, 4, 256, 512], kind="ExternalOutput")

    with ExitStack() as st:
        P = Prog(nc, st)
        C = Ctx()
        C.P, C.nc = P, nc
        C.dbg = None
        if cfg.get("debug"):
            C.dbg = {"want": set(cfg["debug"]), "seen": {}, "off": {"f": 0, "b": 0}, "slot": P.slot(),
                     "f": dr("dbgf", [128, 8192], kind="ExternalOutput"), "b": dr("dbgb", [128, 8192], kind="ExternalOutput", dt=BF16)}
        cfg["_dbg"] = C.dbg
        sb = lambda name, shape, dt: st.enter_context(nc.sbuf_tensor(name, shape, dt))
        C.xres = sb("xres", [128, 8, NTOK], F32)
        C.xn = sb("xn", [128, 8, NTOK], BF16)
        C.gains = sb("gains_sb", [128, 13 * 8], F32)
        cbf = sb("cbf_sb", [128, 256 + 1024 + 512], BF16)
        C.ident = cbf[:, 0:128]
        C.ones = cbf[:, 128:256]
        C.bm = cbf[:, 256:1280].rearrange("p (a t) -> p a t", a=16)
        C.bmp = cbf[:, 1280:1792].rearrange("p (a t) -> p a t", a=4)
        C.epsc = sb("epsc", [128, 2], F32)
        C.one = sb("onec", [128, 2], F32)
        C.cret = sb("cret_sb", [128, CRW2], F32)

        s_in = P.slot()
        for k in range(8):
            P.dma("sp", lambda e, k=k: e.dma_start(out=C.xres[:, k, :], in_=xT[:, k, :]), s_in)
        P.dma("sp", lambda e: e.dma_start(out=C.gains[:], in_=gains_d), s_in)
        P.dma("sp", lambda e: e.dma_start(out=cbf[:], in_=cbf_d), s_in)
        P.dma("sp", lambda e: e.dma_start(out=C.cret[:], in_=cret_d), s_in)
        P.op("pool", lambda e: e.memset(C.epsc[:], EPS))
        P.op("pool", lambda e: e.memset(C.one[:], 1.0))
        P.flush()

        for blk in cfg["blocks"]:
            if blk[0] == "ffn":
                _, l, i = blk
                emit_ffn(C, l * 3 + (0 if i == 0 else 2), wup_d[l * 2 + i], wdn_d[l * 2 + i])
            elif blk[0] == "ret":
                _, l = blk
                j = l // 2
                emit_normphase(C, l * 3 + 1)
                emit_ret(C, j, rwin_d[j], rwout_d[j], rnorm_d[j], cs_d, sret_d[j], nrp_d[j], nrs_d[j])
            elif blk[0] == "hg":
                _, l = blk
                j = l // 2
                emit_normphase(C, l * 3 + 1)
                emit_hg(C, j, hwin_d[j], hwout_d[j], hnorm_d[j], lbl_d, shg_d[j], nhp_d[j], nhs_d[j])

        with ExitStack() as st2:
            sb2, pt2 = mk_alloc(C, st2)
            ph = Ctx()
            ph.sq = [sb2("sq%d" % i, [128, 8, 512], BF16) for i in range(2)]
            ph.rs = [sb2("rs%d" % i, [128, 512], F32) for i in range(2)]
            ph.psn = pt2("psn", [128, 512], F32)
            yo = [sb2("yo%d" % i, [128, 8, 512], F32) for i in range(2)]
            s_out = [P.slot(), P.slot()]
            yo_rd = [None, None]
            if cfg.get("final_norm", True):
                def out_fn2(ti, k, t0, n, rsb, r):
                    b = ti % 2
                    o = P.op("dve", lambda e: e.scalar_tensor_tensor(
                        out=yo[b][:, k, :n], in0=C.xres[:, k, t0:t0 + n], scalar=C.gains[:, 96 + k:96 + k + 1],
                        in1=rsb[:, :n], op0=ALU.mult, op1=ALU.mult), [r, yo_rd[b]])
                    if k == 7:
                        yo_rd[b] = P.dma("sp", lambda e: e.dma_start(out=yT[:, :, t0:t0 + n], in_=yo[b][:, :, :n]), s_out[b], [o])
                    return o
                emit_norm(C, ph, 12, out_fn2)
            else:
                for k in range(8):
                    P.dma("sp", lambda e, k=k: e.dma_start(out=yT[:, k, :], in_=C.xres[:, k, :]), s_out[0])
            P.flush()
    return nc


def host_consts():
    import ml_dtypes
    c = np.zeros((128, 256 + 1024 + 512), np.float32)
    c[:, 0:128] = np.eye(128, dtype=np.float32)
    c[:, 128:256] = 1.0
    bm = np.zeros((16, 64), np.float32)
    for i in range(16):
        bm[i, 4 * i:4 * i + 4] = 1.0
    c[:, 256:1280] = bm.reshape(1, 1024)
    bmp = np.zeros((4, 128), np.float32)
    for i in range(4):
        bmp[i, 32 * i:32 * i + 32] = 1.0
    c[:, 1280:1792] = bmp.reshape(1, 512)
    out = {"cbf": c.astype(ml_dtypes.bfloat16)}
    cr = np.zeros((128, CRW2), np.float64)
    t = np.arange(128)
    ts = np.arange(64)
    for h, g in enumerate(ret_gammas()):
        lg = np.log(np.float64(g))
        mp = np.where(t[:, None] <= t[None, :], np.exp(-(t[:, None] + 1.0) * lg), 0.0)
        cr[:, CR_MP + h * 128:CR_MP + (h + 1) * 128] = mp
        same = (ts[:, None] // 4) == (ts[None, :] // 4)
        ms = np.where(same & ((ts[:, None] % 4) <= (ts[None, :] % 4)), np.exp(-((ts[:, None] % 4) + 1.0) * lg), 0.0)
        cr[:64, CR_MS + h * 64:CR_MS + (h + 1) * 64] = ms
        cr[:, CR_KDP + h] = np.exp((127.0 - t) * lg)
        cr[:, CR_EPP + h] = EPS * np.exp(-2.0 * (t + 1.0) * lg)
        cr[:64, CR_KDS + h] = np.exp((3.0 - (ts % 4)) * lg)
        cr[:64, CR_EPS + h] = EPS * np.exp(-2.0 * ((ts % 4) + 1.0) * lg)
    for i in range(16):
        cr[4 * i:4 * i + 4, CR_RM + i] = 1.0
    cr[:, CH_MP:CH_MP + 128] = ((t[:, None] // 32) == (t[None, :] // 32)) & (t[:, None] <= t[None, :])
    cr[:64, CH_MS:CH_MS + 64] = ((ts[:, None] // 4) == (ts[None, :] // 4)) & (ts[:, None] <= ts[None, :])
    for i in range(4):
        cr[32 * i:32 * i + 32, CH_RMP + i] = 1.0
    cr[:, CH_CMP:CH_CMP + 512] = (np.arange(512) % 32 != 0)[None, :]
    cr[:, CH_CMS:CH_CMS + 64] = (np.arange(64) % 4 != 0)[None, :]
    out["cret"] = cr.astype(np.float32)
    half = 128
    inv_freq = (np.float32(10000.0) ** (-np.arange(half, dtype=np.float32) / np.float32(half))).astype(np.float32)
    pos = np.concatenate([np.arange(SEQ, dtype=np.float32), np.tile(np.float32(16384.0) + np.arange(DEC, dtype=np.float32), NSS)])
    ang = (pos[None, :] * inv_freq[:, None]).astype(np.float32)
    out["cs"] = np.stack([np.cos(ang), np.sin(ang)]).astype(np.float32)
    return out


def host_weights(inp):
    w = {}
    f32 = lambda a: np.asarray(a, np.float32)
    up = f32(inp["ffn_w_up"]).reshape(8, 8, 128, 2, NF, 128)
    w["wup"] = np.ascontiguousarray(up.transpose(0, 4, 2, 1, 3, 5)).reshape(8, NF, 128, 2048)
    dn = f32(inp["ffn_w_down"]).reshape(8, 2, 11, 128, 1024)
    w["wdn"] = np.ascontiguousarray(dn.transpose(0, 1, 3, 2, 4)).reshape(8, 2, 128, 11 * 1024)
    g = np.concatenate([f32(inp["norm_gain"]).reshape(12, 1024), f32(inp["final_norm"]).reshape(1, 1024)], 0)
    w["gains"] = np.ascontiguousarray(g.reshape(13, 8, 128).transpose(2, 0, 1)).reshape(128, 104)
    wi = f32(inp["ret_w_in"]).reshape(2, 8, 128, 6144)
    parts = []
    for h in range(4):
        parts.append(np.concatenate([wi[..., h * 256:(h + 1) * 256], wi[..., 1024 + h * 256:1024 + (h + 1) * 256],
                                     wi[..., 2048 + h * 512:2048 + (h + 1) * 512], wi[..., 4096 + h * 512:4096 + (h + 1) * 512]], -1))
    wih = np.stack(parts, 1)
    w["rwin"] = np.ascontiguousarray(wih.transpose(0, 1, 3, 2, 4)).reshape(2, 4, 128, 12288)
    wo = f32(inp["ret_w_out"]).reshape(2, 4, 4, 128, 1024)
    w["rwout"] = np.ascontiguousarray(wo.transpose(0, 1, 3, 2, 4)).reshape(2, 4, 128, 4096)
    w["rnorm"] = f32(inp["ret_norm"])
    hi = f32(inp["hg_w_in"]).reshape(2, 8, 128, 4, 8, 128)
    w["hwin"] = np.ascontiguousarray(hi.transpose(0, 4, 2, 1, 3, 5)).reshape(2, 8, 128, 4096)
    w["hwout"] = np.ascontiguousarray(f32(inp["hg_w_out"]).reshape(2, 8, 128, 1024))
    w["hnorm"] = f32(inp["hg_norm"])
    w["lbl"] = np.ascontiguousarray(f32(inp["hg_lb_logits"]).reshape(2, 8, 128).transpose(2, 0, 1))
    return w


def host_core_inputs(inp, c):
    xp = np.asarray(inp["x_prompt"], np.float32)[c]
    xs = np.asarray(inp["x_sample"], np.float32)[c * NSS:(c + 1) * NSS].reshape(NSS * DEC, D)
    x = np.concatenate([xp, xs], 0)
    xT = np.ascontiguousarray(x.T.reshape(8, 128, NTOK).transpose(1, 0, 2))
    m = {"xT": xT}
    m["sret"] = np.ascontiguousarray(np.asarray(inp["state_ret"], np.float32)[:, c * NSS:(c + 1) * NSS])
    m["shg"] = np.ascontiguousarray(np.asarray(inp["state_hgrn"], np.float32)[:, c * NSS:(c + 1) * NSS])
    return m


def full_cfg():
    blocks = []
    for l in range(4):
        blocks.append(("ffn", l, 0))
        blocks.append(("ret", l) if l % 2 == 0 else ("hg", l))
        blocks.append(("ffn", l, 1))
    return {"blocks": blocks, "final_norm": True}


def kernel(**inputs):
    nc = build_program(full_cfg())
    shared = {}
    shared.update(host_consts())
    shared.update(host_weights(inputs))
    in_maps = []
    for c in range(8):
        m = dict(shared)
        m.update(host_core_inputs(inputs, c))
        in_maps.append(m)
    res = run_bass_kernel_spmd(nc, in_maps, core_ids=list(range(8)))
    rs = res.results
    y_prompt = np.empty((8, SEQ, D), np.float32)
    y_sample = np.empty((8 * NSS, DEC, D), np.float32)
    nrp = np.empty((2, 8, 4, 256, 512), np.float32)
    nrs = np.empty((2, 8 * NSS, 4, 256, 512), np.float32)
    nhp = np.empty((2, 8, 8, 128, 128), np.float32)
    nhs = np.empty((2, 8 * NSS, 8, 128, 128), np.float32)
    for c in range(8):
        r = rs[c]
        y = np.asarray(r["yT"]).transpose(1, 0, 2).reshape(D, NTOK).T
        y_prompt[c] = y[:SEQ]
        y_sample[c * NSS:(c + 1) * NSS] = y[SEQ:].reshape(NSS, DEC, D)
        nrp[:, c] = r["nrp"]
        nrs[:, c * NSS:(c + 1) * NSS] = r["nrs"]
        nhp[:, c] = r["nhp"]
        nhs[:, c * NSS:(c + 1) * NSS] = r["nhs"]
    return (y_prompt, y_sample, nrp, nrs, nhp, nhs)
```

```python
import numpy as np
from contextlib import ExitStack
import concourse.bass as bass
import concourse.mybir as mybir
from concourse.bass_utils import run_bass_kernel_spmd

F32 = mybir.dt.float32
BF16 = mybir.dt.bfloat16
AF = mybir.ActivationFunctionType
ALU = mybir.AluOpType

D = 1024
SEQ = 2048
NSS = 16
DEC = 4
NTOK = SEQ + NSS * DEC
DFF = 2816
NF = DFF // 128
EPS = 1e-6
TT = [(0, 512), (512, 512), (1024, 512), (1536, 512), (2048, 64)]


class Op:
    __slots__ = ("eng", "fn", "deps", "pos", "sem", "val", "need_sig", "is_dma", "done")


class Slot:
    def __init__(self, sem):
        self.sem = sem
        self.count = 0


class Prog:
    ENGS = ["pe", "act", "dve", "pool", "sp"]
    ENGOBJ = {"pe": "tensor", "act": "scalar", "dve": "vector", "pool": "gpsimd", "sp": "sync"}

    def __init__(self, nc, stack):
        self.nc = nc
        self.stack = stack
        self.q = {e: [] for e in self.ENGS}
        self.esem = {e: stack.enter_context(nc.semaphore("s_" + e)) for e in ["pe", "act", "dve", "pool"]}
        self.ecount = {e: 0 for e in self.ENGS}
        self.slots = []
        self.nphase = 0

    def slot(self):
        s = Slot(self.stack.enter_context(self.nc.semaphore("d%d" % len(self.slots))))
        self.slots.append(s)
        return s

    def op(self, eng, fn, deps=()):
        o = Op()
        o.eng = eng
        o.fn = fn
        o.deps = [d for d in deps if d is not None]
        o.pos = len(self.q[eng])
        o.is_dma = False
        o.need_sig = False
        o.sem = None
        o.val = None
        o.done = False
        self.q[eng].append(o)
        return o

    def dma(self, eng, fn, slot, deps=()):
        o = self.op(eng, fn, deps)
        o.is_dma = True
        slot.count += 16
        o.sem = slot.sem
        o.val = slot.count
        return o

    def _needs_wait(self, o, d):
        if d.done:
            return False
        if d.is_dma:
            return True
        if d.eng == o.eng:
            if o.eng == "pe":
                return False
            return (o.pos - d.pos) <= 2
        return True

    def flush(self):
        nc = self.nc
        drain_deps = []
        for e in self.ENGS:
            last = {}
            for o in self.q[e]:
                if o.is_dma:
                    last[id(o.sem)] = o
            drain_deps += list(last.values())
        self.op("sp", lambda e: e.nop(), drain_deps)
        for e in self.ENGS:
            for o in self.q[e]:
                for d in o.deps:
                    if not d.is_dma and self._needs_wait(o, d):
                        d.need_sig = True
        for e in self.ENGS:
            c = self.ecount[e]
            for o in self.q[e]:
                if o.is_dma:
                    continue
                if o.need_sig:
                    assert e != "sp"
                    c += 1
                    o.sem = self.esem[e]
                    o.val = c
            self.ecount[e] = c
        self.nphase += 1
        with nc.Block() as block:
            for e in self.ENGS:
                ops = self.q[e]
                if not ops:
                    continue

                def body(eng, ops=ops):
                    waited = {}
                    for o in ops:
                        need = {}
                        for d in o.deps:
                            if not self._needs_wait(o, d):
                                continue
                            key = id(d.sem)
                            if key not in need or need[key][1] < d.val:
                                need[key] = (d.sem, d.val)
                        for key, (sem, val) in need.items():
                            if waited.get(key, 0) >= val:
                                continue
                            eng.wait_ge(sem, val)
                            waited[key] = val
                        ins = o.fn(eng)
                        if o.is_dma:
                            ins.then_inc(o.sem, 16)
                        elif o.need_sig:
                            ins.then_inc(o.sem, 1)

                getattr(block, self.ENGOBJ[e])(body)
        for e in self.ENGS:
            for o in self.q[e]:
                o.done = True
                o.fn = None
                o.deps = None
            self.q[e] = []


class Ctx:
    pass


def dbg_dump(C, name, ap, bufs, ncols, bf=False):
    if not getattr(C, "dbg", None) or name in C.dbg["seen"] or name not in C.dbg["want"]:
        return
    key = "b" if bf else "f"
    off = C.dbg["off"][key]
    C.dbg["off"][key] = off + ncols
    C.dbg["seen"][name] = (key, off, ncols)
    dst = C.dbg[key][:, off:off + ncols]
    bop(C.P, "sp", lambda e: e.dma_start(out=dst, in_=ap), r=bufs, slot=C.dbg["slot"])


class Buf:
    def __init__(self):
        self.w = None
        self.r = []


class PBuf(Buf):
    excl = True


def bop(P, eng, fn, r=(), w=(), slot=None, deps=()):
    xr = [b for b in r if getattr(b, "excl", False)]
    if xr:
        r = [b for b in r if not getattr(b, "excl", False)]
        w = list(w) + [b for b in xr if b not in w]
    d = list(deps)
    for b in r:
        d.append(b.w)
    for b in w:
        d.append(b.w)
        d.extend(b.r)
    o = P.dma(eng, fn, slot, d) if slot is not None else P.op(eng, fn, d)
    for b in r:
        if not o.is_dma:
            b.r = [x for x in b.r if x.is_dma or x.eng != eng]
        b.r.append(o)
    for b in w:
        b.w = o
        b.r = []
    return o


_UID = [0]


def mk_alloc(C, st):
    _UID[0] += 1
    u = _UID[0]
    nc = C.nc
    sb = lambda name, shape, dt: st.enter_context(nc.sbuf_tensor("%s_%d" % (name, u), shape, dt))
    pt = lambda name, shape, dt: st.enter_context(nc.psum_tensor("%s_%d" % (name, u), shape, dt))
    return sb, pt


def emit_norm(C, ph, gcol, out_fn=None):
    P = C.P
    sq, rs, psn = ph.sq, ph.rs, ph.psn
    last = []
    sq_rd = [None, None]
    rs_rd = [None, None]
    ps_rd = None
    for ti, (t0, n) in enumerate(TT):
        b = ti % 2
        a = P.op("act", lambda e, b=b, t0=t0, n=n: e.activation(out=sq[b][:, :, :n], in_=C.xres[:, :, t0:t0 + n], func=AF.Square),
                 [sq_rd[b]])
        mm = None
        for k in range(8):
            mm = P.op("pe", lambda e, b=b, k=k, n=n: e.matmul(psn[:, :n], lhsT=C.ones[:], rhs=sq[b][:, k, :n], start=(k == 0), stop=(k == 7)),
                      [a, ps_rd])
        sq_rd[b] = mm
        v = P.op("act", lambda e, b=b, n=n: e.activation(out=rs[b][:, :n], in_=psn[:, :n], func=AF.Ln, scale=1.0 / D, bias=C.epsc[:, 0:1]),
                 [mm, rs_rd[b]])
        ps_rd = v
        r = P.op("act", lambda e, b=b, n=n: e.activation(out=rs[b][:, :n], in_=rs[b][:, :n], func=AF.Exp, scale=-0.5), [v])
        o = None
        for k in range(8):
            if out_fn is None:
                o = P.op("dve", lambda e, b=b, k=k, t0=t0, n=n: e.scalar_tensor_tensor(
                    out=C.xn[:, k, t0:t0 + n], in0=C.xres[:, k, t0:t0 + n], scalar=C.gains[:, gcol * 8 + k:gcol * 8 + k + 1],
                    in1=rs[b][:, :n], op0=ALU.mult, op1=ALU.mult), [r])
            else:
                o = out_fn(ti, k, t0, n, rs[b], r)
        rs_rd[b] = o
        last.append(o)
    return last


def emit_normphase(C, gcol):
    with ExitStack() as st:
        sb, pt = mk_alloc(C, st)
        ph = Ctx()
        ph.sq = [sb("sq%d" % i, [128, 8, 512], BF16) for i in range(2)]
        ph.rs = [sb("rs%d" % i, [128, 512], F32) for i in range(2)]
        ph.psn = pt("psn", [128, 512], F32)
        emit_norm(C, ph, gcol)
        C.P.flush()


def emit_ffn(C, gcol, wup, wdn):
    P, nc = C.P, C.nc
    with ExitStack() as st:
        sb, pt = mk_alloc(C, st)
        hid = sb("hid", [128, 11, NTOK], BF16)
        wu = [sb("wu%d" % i, [128, 8, 2, 128], BF16) for i in range(3)]
        wd = sb("wd", [128, 11, 1024], BF16)
        sa = [sb("sa%d" % i, [128, 512], F32) for i in range(2)]
        sqs = [sb("sqs%d" % i, [128, 512], BF16) for i in range(4)]
        rs = [sb("rs%d" % i, [128, 512], F32) for i in range(2)]
        psn = pt("psn", [128, 512], F32)
        psA = [pt("psA%d" % i, [128, 512], F32) for i in range(2)]
        psB = [pt("psB%d" % i, [128, 512], F32) for i in range(2)]
        psD = [pt("psD%d" % i, [128, 512], F32) for i in range(2)]
        bwu = [Buf() for _ in range(3)]; bwd = Buf(); bsa = [Buf(), Buf()]; bsq = [Buf() for _ in range(4)]; brs = [Buf(), Buf()]
        bpsn = PBuf(); bpsA = [PBuf(), PBuf()]; bpsB = [PBuf(), PBuf()]; bpsD = [PBuf(), PBuf()]
        bxn = [Buf() for _ in TT]; bxr = [Buf() for _ in TT]; bhid = [Buf() for _ in TT]
        s_wu = [P.slot() for _ in range(3)]
        s_wd = P.slot()

        def load_wu(fi):
            s = fi % 3
            bop(P, "pool", lambda e: e.dma_start(out=wu[s][:].rearrange("p k a c -> p (k a c)"), in_=wup[fi], max_dma_last_dim=8192), w=[bwu[s]], slot=s_wu[s])

        def load_wd(half):
            bop(P, "pool", lambda e: e.dma_start(out=wd[:].rearrange("p f n -> p (f n)"), in_=wdn[half], max_dma_last_dim=8192), w=[bwd], slot=s_wd)

        sqc = [0]

        def norm(ti):
            t0, n = TT[ti]
            b = ti % 2
            for k in range(8):
                q = sqc[0] % 4
                sqc[0] += 1
                bop(P, "act", lambda e, k=k, q=q: e.activation(out=sqs[q][:, :n], in_=C.xres[:, k, t0:t0 + n], func=AF.Square), r=[bxr[ti]], w=[bsq[q]])
                bop(P, "pe", lambda e, k=k, q=q: e.matmul(psn[:, :n], lhsT=C.ones[:], rhs=sqs[q][:, :n], start=(k == 0), stop=(k == 7)), r=[bsq[q]], w=[bpsn])
            bop(P, "act", lambda e: e.activation(out=rs[b][:, :n], in_=psn[:, :n], func=AF.Ln, scale=1.0 / D, bias=C.epsc[:, 0:1]), r=[bpsn], w=[brs[b]])
            bop(P, "act", lambda e: e.activation(out=rs[b][:, :n], in_=rs[b][:, :n], func=AF.Exp, scale=-0.5), r=[brs[b]], w=[brs[b]])
            for k in range(8):
                bop(P, "dve", lambda e, k=k: e.scalar_tensor_tensor(
                    out=C.xn[:, k, t0:t0 + n], in0=C.xres[:, k, t0:t0 + n], scalar=C.gains[:, gcol * 8 + k:gcol * 8 + k + 1],
                    in1=rs[b][:, :n], op0=ALU.mult, op1=ALU.mult), r=[brs[b], bxr[ti]], w=[bxn[ti]])

        load_wd(0)
        for fi in range(3):
            load_wu(fi)
        norm(0)
        norm(1)
        cnt = 0
        dcnt = 0
        for half in range(2):
            if half == 1:
                load_wd(1)
            for f in range(11):
                fi = half * 11 + f
                s = fi % 3
                for ti, (t0, n) in enumerate(TT):
                    b = cnt % 2
                    cnt += 1
                    for k in range(8):
                        bop(P, "pe", lambda e, b=b, s=s, k=k, t0=t0, n=n: e.matmul(psA[b][:, :n], lhsT=wu[s][:, k, 0, :], rhs=C.xn[:, k, t0:t0 + n], start=(k == 0), stop=(k == 7)),
                            r=[bwu[s], bxn[ti]], w=[bpsA[b]])
                    for k in range(8):
                        bop(P, "pe", lambda e, b=b, s=s, k=k, t0=t0, n=n: e.matmul(psB[b][:, :n], lhsT=wu[s][:, k, 1, :], rhs=C.xn[:, k, t0:t0 + n], start=(k == 0), stop=(k == 7)),
                            r=[bwu[s], bxn[ti]], w=[bpsB[b]])
                    bop(P, "act", lambda e, b=b, n=n: e.activation(out=sa[b][:, :n], in_=psA[b][:, :n], func=AF.Silu), r=[bpsA[b]], w=[bsa[b]])
                    bop(P, "dve", lambda e, b=b, f=f, t0=t0, n=n: e.tensor_tensor(out=hid[:, f, t0:t0 + n], in0=sa[b][:, :n], in1=psB[b][:, :n], op=ALU.mult),
                        r=[bsa[b], bpsB[b]], w=[bhid[ti]])
                    if fi == 0 and ti + 2 < len(TT):
                        norm(ti + 2)
                if fi + 3 < NF:
                    load_wu(fi + 3)
            for ti, (t0, n) in enumerate(TT):
                for mo in range(8):
                    b = dcnt % 2
                    dcnt += 1
                    for f in range(11):
                        bop(P, "pe", lambda e, b=b, f=f, mo=mo, t0=t0, n=n: e.matmul(psD[b][:, :n], lhsT=wd[:, f, mo * 128:(mo + 1) * 128], rhs=hid[:, f, t0:t0 + n], start=(f == 0), stop=(f == 10)),
                            r=[bwd, bhid[ti]], w=[bpsD[b]])
                    bop(P, "dve", lambda e, b=b, mo=mo, t0=t0, n=n: e.scalar_tensor_tensor(
                        out=C.xres[:, mo, t0:t0 + n], in0=psD[b][:, :n], scalar=0.5, in1=C.xres[:, mo, t0:t0 + n], op0=ALU.mult, op1=ALU.add),
                        r=[bpsD[b]], w=[bxr[ti]])
        P.flush()


RET_H = 4
CR_MP = 0
CR_MS = 512
CR_KDP = 768
CR_EPP = 772
CR_KDS = 776
CR_EPS = 780
CR_RM = 784
CRW = 800


def ret_gammas():
    return [1.0 - 2.0 ** (-5.0 - h) for h in range(RET_H)]


def emit_ret(C, j, rwin, rwout, rnorm, cs, st_in, st_out_p, st_out_s):
    P, nc = C.P, C.nc
    gam = ret_gammas()
    with ExitStack() as st:
        sb, pt = mk_alloc(C, st)
        wh = [sb("wh%d" % i, [128, 8, 1536], BF16) for i in range(2)]
        wo = sb("wo", [128, 4, 1024], BF16)
        cst = sb("cs", [128, 2, 512], F32)
        qT = sb("qT", [128, 2, 512], BF16)
        kT = sb("kT", [128, 2, 512], BF16)
        vtok = [sb("vtok%d" % i, [128, 512], BF16) for i in range(2)]
        ktl = [sb("ktl%d" % i, [128, 256], BF16) for i in range(2)]
        scm = [sb("scm%d" % i, [128, 128], BF16) for i in range(2)]
        gs = sb("gs", [128, 512], F32)
        on = sb("on", [128, 512], F32)
        go = [sb("go%d" % i, [128, 512], BF16) for i in range(2)]
        goT = sb("goT", [128, 4, 512], BF16)
        S = sb("S", [128, 2, 512], F32)
        Sb = sb("Sb", [128, 2, 512], BF16)
        gn = sb("gn", [128, 512], F32)
        qz = sb("qz", [128, 2, 16, 64], BF16)
        kz = [sb("kz%d" % i, [64, 256], BF16) for i in range(2)]
        S0f = [sb("S0f%d" % i, [128, 512], F32) for i in range(2)]
        S0b = [sb("S0b%d" % i, [128, 512], BF16) for i in range(2)]
        S0f = [t[:, :] for t in S0f] + [S[:, 0, :], S[:, 1, :]]
        S0b = [t[:, :] for t in S0b] + [Sb[:, 0, :], Sb[:, 1, :]]
        st4 = sb("st4", [128, 4], F32)
        B = [pt("b%d" % i, [128, 512], F32) for i in range(8)]
        PT = [B[6][:, 0:128].bitcast(BF16), B[3][:, 0:128].bitcast(BF16)]
        SC = [B[6][:, 128:256], B[3][:, 128:256]]
        TR = B[3][:, 256:512].bitcast(BF16)
        cret = C.cret

        bwh = [Buf(), Buf()]; bwo = Buf(); bcs = Buf(); bt12 = Buf(); bqT = Buf(); bkT = Buf()
        bvtok = [Buf(), Buf()]; bktl = [Buf(), Buf()]; bscm = [Buf(), Buf()]; bgs = Buf(); bon = Buf(); bgo = [Buf(), Buf()]; bgoT = Buf()
        bSh = [Buf(), Buf()]; bSbh = [Buf(), Buf()]; bgn = Buf(); bqz = Buf(); bkz = [Buf(), Buf()]
        bS0f = [Buf(), Buf()] + bSh; bS0b = [Buf(), Buf()] + bSbh; bst4 = Buf()
        bB = [PBuf() for _ in range(8)]
        bB7t = bB[3]
        bPT = [bB[6], bB[3]]; bSC = [bB[6], bB[3]]
        bx = [[Buf() for _ in TT] for _ in range(8)]
        s_wh = [P.slot(), P.slot()]; s_wo = P.slot(); s_cs = P.slot(); s_gn = P.slot()
        s_S0f = [P.slot() for _ in range(4)]; s_S0b = [P.slot() for _ in range(4)]; s_so = [P.slot() for _ in range(4)]; s_sp = P.slot()

        def load_wh(h):
            sl = h % 2
            bop(P, "pool", lambda e, sl=sl: e.dma_start(out=wh[sl][:].rearrange("p k n -> p (k n)"), in_=rwin[h], max_dma_last_dim=8192),
                w=[bwh[sl]], slot=s_wh[sl])

        load_wh(0)
        dbg_dump(C, "xn0", C.xn[:, 0, 0:512], [], 512, bf=True)
        dbg_dump(C, "wh0", wh[0][:, 0, 0:512], [bwh[0]], 512, bf=True)
        ucnt = 0
        for h in range(RET_H):
            sl = h % 2
            g = gam[h]
            bop(P, "pool", lambda e, h=h: e.dma_start(out=wo[:].rearrange("p a n -> p (a n)"), in_=rwout[h], max_dma_last_dim=8192),
                w=[bwo], slot=s_wo)
            if h + 1 < RET_H:
                load_wh(h + 1)
            bop(P, "sp", lambda e, h=h: e.dma_start(out=gn[:], in_=rnorm[h].partition_broadcast(128)), w=[bgn], slot=s_gn)
            bop(P, "pool", lambda e: e.memset(S[:], 0.0), w=[bSh[0], bSh[1]])
            bop(P, "pool", lambda e: e.memset(Sb[:], 0.0), w=[bSbh[0], bSbh[1]])
            for ti, (t0, n) in enumerate(TT):
                sample = (ti == 4)
                bop(P, "sp", lambda e, t0=t0, n=n: e.dma_start(out=cst[:, :, :n], in_=cs[:, :, t0:t0 + n].rearrange("a p t -> p a t")),
                    w=[bcs], slot=s_cs)
                for qi in range(4):
                    for k in range(8):
                        bop(P, "pe", lambda e, sl=sl, qi=qi, k=k, t0=t0, n=n: e.matmul(B[qi][:, :n], lhsT=wh[sl][:, k, qi * 128:(qi + 1) * 128], rhs=C.xn[:, k, t0:t0 + n], start=(k == 0), stop=(k == 7)),
                            r=[bwh[sl]], w=[bB[qi]])
                for (dst, bd, b0, b1, sc) in ((qT, bqT, 0, 1, 1.0), (kT, bkT, 2, 3, 0.0625)):
                    for half in range(2):
                        ca, cb = (0, 1) if half == 0 else (1, 0)
                        bop(P, "dve", lambda e, b0=b0, ca=ca, sc=sc, n=n: e.scalar_tensor_tensor(out=gs[:, :n], in0=B[b0][:, :n], scalar=sc, in1=cst[:, ca, :n], op0=ALU.mult, op1=ALU.mult),
                            r=[bB[b0], bcs], w=[bgs])
                        bop(P, "dve", lambda e, b1=b1, cb=cb, sc=sc, n=n: e.scalar_tensor_tensor(out=on[:, :n], in0=B[b1][:, :n], scalar=sc, in1=cst[:, cb, :n], op0=ALU.mult, op1=ALU.mult),
                            r=[bB[b1], bcs], w=[bon])
                        bop(P, "dve", lambda e, dst=dst, half=half, n=n: e.tensor_tensor(out=dst[:, half, :n], in0=gs[:, :n], in1=on[:, :n], op=(ALU.subtract if half == 0 else ALU.add)),
                            r=[bgs, bon], w=[bd])
                blocks = [(0, 64)] if sample else [(c * 128, 128) for c in range(n // 128)]
                MC = (CR_MS + h * 64) if sample else (CR_MP + h * 128)
                KD = (CR_KDS if sample else CR_KDP) + h
                EP = (CR_EPS if sample else CR_EPP) + h

                def stage_A(ci, c0, nb, sl=sl, t0=t0, MC=MC, KD=KD):
                    pb = ci % 2
                    a0 = t0 + c0
                    vb = 4 + pb
                    for k in range(8):
                        bop(P, "pe", lambda e, k=k: e.matmul(B[vb][:nb, :], lhsT=C.xn[:, k, a0:a0 + nb], rhs=wh[sl][:, k, 512:1024], start=(k == 0), stop=(k == 7)),
                            r=[bwh[sl]], w=[bB[vb]])
                    bop(P, "act", lambda e: e.activation(out=vtok[pb][:nb, :], in_=B[vb][:nb, :], func=AF.Copy), r=[bB[vb]], w=[bvtok[pb]])
                    for jj in range(2):
                        bop(P, "pe", lambda e, jj=jj: e.transpose(out=PT[pb][:nb, jj * 128:(jj + 1) * 128], in_=kT[:, jj, c0:c0 + nb], identity=C.ident),
                            r=[bkT], w=[bPT[pb]])
                    for jj in range(2):
                        bop(P, "pe", lambda e, jj=jj: e.matmul(SC[pb][:nb, :nb], lhsT=kT[:, jj, c0:c0 + nb], rhs=qT[:, jj, c0:c0 + nb], start=(jj == 0), stop=(jj == 1)),
                            r=[bkT, bqT], w=[bSC[pb]])
                    bop(P, "dve", lambda e: e.tensor_scalar(out=ktl[pb][:nb, :], in0=PT[pb][:nb, :], scalar1=cret[:nb, KD:KD + 1], scalar2=None, op0=ALU.mult),
                        r=[bPT[pb]], w=[bktl[pb]])
                    bop(P, "dve", lambda e: e.tensor_tensor(out=scm[pb][:nb, :nb], in0=SC[pb][:nb, :nb], in1=cret[:nb, MC:MC + nb], op=ALU.mult),
                        r=[bSC[pb]], w=[bscm[pb]])

                def stage_B(ci, c0, nb, sl=sl, t0=t0, EP=EP, sample=sample, g=g, h=h):
                    nonlocal ucnt
                    pb = ci % 2
                    a0 = t0 + c0
                    bop(P, "pe", lambda e: e.matmul(B[7][:nb, :], lhsT=scm[pb][:nb, :nb], rhs=vtok[pb][:nb, :], start=True, stop=False),
                        r=[bscm[pb], bvtok[pb]], w=[bB[7]])
                    if not sample:
                        for jj in range(2):
                            bop(P, "pe", lambda e, jj=jj: e.matmul(B[7][:nb, :], lhsT=qT[:, jj, c0:c0 + nb], rhs=Sb[:, jj, :], start=False, stop=(jj == 1)),
                                r=[bqT, bSbh[jj]], w=[bB[7]])
                        for jj in range(2):
                            bop(P, "pe", lambda e, jj=jj: e.matmul(B[jj][:, :], lhsT=ktl[pb][:nb, jj * 128:(jj + 1) * 128], rhs=vtok[pb][:nb, :], start=True, stop=True),
                                r=[bktl[pb], bvtok[pb]], w=[bB[jj]])
                        cd = g ** 128
                        for jj in range(2):
                            bop(P, "dve", lambda e, jj=jj: e.scalar_tensor_tensor(out=S[:, jj, :], in0=S[:, jj, :], scalar=cd, in1=B[jj][:, :], op0=ALU.mult, op1=ALU.add),
                                r=[bB[jj]], w=[bSh[jj]])
                        for jj in range(2):
                            bop(P, "pool", lambda e, jj=jj: e.tensor_copy(out=Sb[:, jj, :], in_=S[:, jj, :]), r=[bSh[jj]], w=[bSbh[jj]])
                    else:
                        for jj in range(2):
                            bop(P, "dve", lambda e, jj=jj: e.tensor_tensor(out=qz[:, jj, :, :], in0=qT[:, jj, 0:64].unsqueeze(1).broadcast_to([128, 16, 64]), in1=C.bm[:, :, :], op=ALU.mult),
                                r=[bqT], w=[bqz])
                        cd = g ** 4
                        units = [(i, jj) for i in range(NSS) for jj in range(2)]

                        def issue_loads(k):
                            i, jj = units[k]
                            u = k % 4
                            bop(P, "pool", lambda e: e.dma_start(out=S0b[u], in_=st_in[i, h, jj * 128:(jj + 1) * 128, :]), w=[bS0b[u]], slot=s_S0b[u])
                            bop(P, "sp", lambda e: e.dma_start(out=S0f[u], in_=st_in[i, h, jj * 128:(jj + 1) * 128, :]), w=[bS0f[u]], slot=s_S0f[u])

                        issue_loads(0)
                        issue_loads(1)
                        for k, (i, jj) in enumerate(units):
                            if k + 2 < len(units):
                                issue_loads(k + 2)
                            kb = i % 2
                            u = k % 4
                            if jj == 0:
                                bop(P, "dve", lambda e, i=i, kb=kb: e.tensor_scalar(out=kz[kb][:, :], in0=ktl[pb][:64, :], scalar1=cret[:64, CR_RM + i:CR_RM + i + 1], scalar2=None, op0=ALU.mult),
                                    r=[bktl[pb]], w=[bkz[kb]])
                            last = (k == len(units) - 1)
                            bop(P, "pe", lambda e, u=u, i=i, jj=jj, last=last: e.matmul(B[7][:64, :], lhsT=qz[:, jj, i, :], rhs=S0b[u], start=False, stop=last),
                                r=[bqz, bS0b[u]], w=[bB[7]])
                            bop(P, "pe", lambda e, kb=kb, jj=jj: e.matmul(B[jj][:, :], lhsT=kz[kb][:, jj * 128:(jj + 1) * 128], rhs=vtok[pb][:64, :], start=True, stop=True),
                                r=[bkz[kb], bvtok[pb]], w=[bB[jj]])
                            bop(P, "dve", lambda e, u=u, jj=jj: e.scalar_tensor_tensor(out=S0f[u], in0=S0f[u], scalar=cd, in1=B[jj][:, :], op0=ALU.mult, op1=ALU.add),
                                r=[bB[jj]], w=[bS0f[u]])
                            bop(P, "sp", lambda e, u=u, i=i, jj=jj: e.dma_start(out=st_out_s[i, h, jj * 128:(jj + 1) * 128, :], in_=S0f[u]), r=[bS0f[u]], slot=s_so[u])
                    for k in range(8):
                        bop(P, "pe", lambda e, k=k: e.matmul(B[2][:nb, :], lhsT=C.xn[:, k, a0:a0 + nb], rhs=wh[sl][:, k, 1024:1536], start=(k == 0), stop=(k == 7)),
                            r=[bwh[sl]], w=[bB[2]])
                    bop(P, "act", lambda e: e.activation(out=gs[:nb, :], in_=B[2][:nb, :], func=AF.Exp, scale=-1.0), r=[bB[2]], w=[bgs])
                    bop(P, "act", lambda e: e.activation(out=gs[:nb, :], in_=gs[:nb, :], func=AF.Ln, bias=C.one[:nb, 0:1]), r=[bgs], w=[bgs])
                    bop(P, "act", lambda e: e.activation(out=gs[:nb, :], in_=gs[:nb, :], func=AF.Exp, scale=-1.0), r=[bgs], w=[bgs])
                    bop(P, "dve", lambda e: e.tensor_tensor(out=gs[:nb, :], in0=B[2][:nb, :], in1=gs[:nb, :], op=ALU.mult), r=[bgs, bB[2]], w=[bgs])
                    bop(P, "act", lambda e: e.activation(out=on[:nb, :], in_=B[7][:nb, :], func=AF.Square, accum_out=st4[:nb, 0:1]),
                        r=[bB[7]], w=[bon, bst4])
                    bop(P, "act", lambda e: e.activation(out=st4[:nb, 2:3], in_=st4[:nb, 0:1], func=AF.Ln, scale=1.0 / 512, bias=cret[:nb, EP:EP + 1]), r=[bst4], w=[bst4])
                    bop(P, "act", lambda e: e.activation(out=st4[:nb, 3:4], in_=st4[:nb, 2:3], func=AF.Exp, scale=-0.5), r=[bst4], w=[bst4])
                    bop(P, "dve", lambda e: e.scalar_tensor_tensor(out=on[:nb, :], in0=B[7][:nb, :], scalar=st4[:nb, 3:4], in1=gn[:nb, :], op0=ALU.mult, op1=ALU.mult),
                        r=[bB[7], bst4, bgn], w=[bon])
                    bop(P, "pool", lambda e: e.tensor_tensor(out=go[pb][:nb, :], in0=on[:nb, :], in1=gs[:nb, :], op=ALU.mult), r=[bon, bgs], w=[bgo[pb]])

                def stage_T(ci, c0, nb):
                    pb = ci % 2
                    for e4 in range(4):
                        bop(P, "pe", lambda e, e4=e4: e.transpose(out=TR[:, e4 * 128:e4 * 128 + nb], in_=go[pb][:nb, e4 * 128:(e4 + 1) * 128], identity=C.ident[:nb, :nb]),
                            r=[bgo[pb]], w=[bB7t])
                    bop(P, "act", lambda e: e.activation(out=goT[:, :, c0:c0 + nb], in_=TR.rearrange("p (a t) -> p a t", a=4)[:, :, :nb], func=AF.Copy),
                        r=[bB7t], w=[bgoT])

                nblk = len(blocks)
                sched = []
                if nblk == 1:
                    sched = [("A", 0), ("B", 0), ("T", 0)]
                else:
                    sched = [("A", 0), ("A", 1), ("B", 0), ("A", 2), ("B", 1), ("T", 0), ("A", 3), ("B", 2), ("T", 1), ("B", 3), ("T", 2), ("T", 3)]
                for (kind, ci) in sched:
                    c0, nb = blocks[ci]
                    if kind == "A":
                        stage_A(ci, c0, nb)
                    elif kind == "B":
                        stage_B(ci, c0, nb)
                    else:
                        stage_T(ci, c0, nb)
                for m in range(8):
                    wb = 4 + (m % 2)
                    for e4 in range(4):
                        bop(P, "pe", lambda e, m=m, e4=e4, n=n, wb=wb: e.matmul(B[wb][:, :n], lhsT=wo[:, e4, m * 128:(m + 1) * 128], rhs=goT[:, e4, :n], start=(e4 == 0), stop=(e4 == 3)),
                            r=[bwo, bgoT], w=[bB[wb]])
                    bop(P, "dve", lambda e, m=m, t0=t0, n=n, wb=wb: e.tensor_tensor(out=C.xres[:, m, t0:t0 + n], in0=C.xres[:, m, t0:t0 + n], in1=B[wb][:, :n], op=ALU.add),
                        r=[bB[wb]], w=[bx[m][ti]])
                if ti == 3:
                    bop(P, "sp", lambda e, h=h: e.dma_start(out=st_out_p[h].rearrange("(a p) n -> p a n", p=128), in_=S[:, :, :]), r=[bSh[0], bSh[1]], slot=s_sp)
        P.flush()


HG_H = 8
CH_MP = 800
CH_MS = 928
CH_RMP = 992
CH_CMP = 1000
CH_CMS = 1512
CRW2 = 1576


def emit_hg(C, j, hwin, hwout, hnorm, lbl, st_in, st_out_p, st_out_s):
    P, nc = C.P, C.nc
    with ExitStack() as st:
        sb, pt = mk_alloc(C, st)
        wh = [sb("hwh%d" % i, [128, 8, 512], BF16) for i in range(2)]
        wo = [sb("hwo%d" % i, [128, 1024], BF16) for i in range(2)]
        F = {nm: sb("h" + nm, [128, 512], F32) for nm in ("qs", "ez", "r", "f", "kk", "b", "d1", "X", "Y")}
        qc = [sb("hqc%d" % i, [128, 512], BF16) for i in range(2)]
        kc = [sb("hkc%d" % i, [128, 512], BF16) for i in range(2)]
        eB = [sb("heB%d" % i, [128, 16], F32) for i in range(2)]
        vtok = [sb("hvtok%d" % i, [128, 128], BF16) for i in range(2)]
        gsil = [sb("hgsil%d" % i, [128, 128], F32) for i in range(2)]
        ktl = [sb("hktl%d" % i, [128, 128], BF16) for i in range(2)]
        scm = [sb("hscm%d" % i, [128, 128], BF16) for i in range(2)]
        qz = [sb("hqz%d" % i, [128, 1024], BF16) for i in range(2)]
        kz = [sb("hkz%d" % i, [128, 2048], BF16) for i in range(2)]
        Sdb = [sb("hSdb%d" % i, [128, 16, 128], BF16) for i in range(2)]
        S = sb("hS", [128, 128], F32)
        S0 = sb("hS0", [128, 16, 128], F32)
        on = sb("hon", [128, 128], F32)
        go = [sb("hgo%d" % i, [128, 128], BF16) for i in range(2)]
        goT = sb("hgoT", [128, 512], BF16)
        gn = sb("hgn", [128, 128], F32)
        lb = sb("hlb", [128, 2, 8], F32)
        lbv = sb("hlbv", [128, 8], F32)
        oml = sb("homl", [128, 8], F32)
        st4 = sb("hst4", [128, 4], F32)
        B = [pt("hb%d" % i, [128, 512], F32) for i in range(8)]
        VG = [B[4][:, 0:256], B[6][:, 0:256]]
        PT = [B[4][:, 256:320].bitcast(BF16), B[6][:, 256:320].bitcast(BF16)]
        SC = [B[4][:, 320:448], B[6][:, 320:448]]
        TR = B[3][:, 256:320].bitcast(BF16)
        UR = [B[5][:, i * 128:(i + 1) * 128] for i in range(4)] + [B[7][:, i * 128:(i + 1) * 128] for i in range(4)]
        cret = C.cret
        bF = {nm: Buf() for nm in F}
        bwh = [Buf(), Buf()]; bwo = [Buf(), Buf()]; bqc = [Buf(), Buf()]; bkc = [Buf(), Buf()]; beB = [Buf(), Buf()]
        bvtok = [Buf(), Buf()]; bgsil = [Buf(), Buf()]; bktl = [Buf(), Buf()]; bscm = [Buf(), Buf()]
        bqz = [Buf(), Buf()]; bkz = [Buf(), Buf()]; bSdb = [Buf(), Buf()]; bgo = [Buf(), Buf()]
        bS = Buf(); bS0 = Buf(); bon = Buf(); bgoT = Buf(); bgn = Buf(); blb = Buf(); bst4 = Buf()
        bB = [PBuf() for _ in range(8)]
        bVG = [bB[4], bB[6]]; bPT = [bB[4], bB[6]]; bSC = [bB[4], bB[6]]; bTR = bB[3]; bUR = [bB[5]] * 4 + [bB[7]] * 4
        bx = [[Buf() for _ in TT] for _ in range(8)]
        s_wh = [P.slot(), P.slot()]; s_wo = [P.slot(), P.slot()]; s_gn = P.slot(); s_lb = P.slot()
        s_S0 = P.slot(); s_so = P.slot(); s_sp = P.slot()

        def sigmoid_act(dst, src, bdst, bsrc):
            bop(P, "act", lambda e: e.activation(out=dst, in_=src, func=AF.Exp, scale=-1.0), r=[bsrc], w=[bdst])
            bop(P, "act", lambda e: e.activation(out=dst, in_=dst, func=AF.Ln, bias=C.one[:dst.shape[0], 0:1]), r=[bdst], w=[bdst])
            bop(P, "act", lambda e: e.activation(out=dst, in_=dst, func=AF.Exp, scale=-1.0), r=[bdst], w=[bdst])

        bop(P, "sp", lambda e: e.dma_start(out=lb[:], in_=lbl), w=[blb], slot=s_lb)
        if j == 0:
            bop(P, "dve", lambda e: e.memset(lbv[:], 0.0), w=[blb])
            bop(P, "dve", lambda e: e.memset(oml[:], 1.0), w=[blb])
        else:
            bop(P, "dve", lambda e: e.tensor_tensor(out=lbv[:], in0=lb[:, 0, :], in1=lb[:, 1, :], op=ALU.subtract), r=[blb], w=[blb])
            bop(P, "act", lambda e: e.activation(out=oml[:], in_=lbv[:], func=AF.Exp), r=[blb], w=[blb])
            bop(P, "dve", lambda e: e.tensor_scalar(out=lbv[:], in0=oml[:], scalar1=1.0, scalar2=None, op0=ALU.add), r=[blb], w=[blb])
            bop(P, "dve", lambda e: e.reciprocal(out=lbv[:], in_=lbv[:]), r=[blb], w=[blb])
            bop(P, "dve", lambda e: e.tensor_tensor(out=oml[:], in0=oml[:], in1=lbv[:], op=ALU.mult), r=[blb], w=[blb])

        def load_w(h):
            sl = h % 2
            bop(P, "pool", lambda e, sl=sl, h=h: e.dma_start(out=wh[sl][:].rearrange("p k n -> p (k n)"), in_=hwin[h], max_dma_last_dim=8192),
                w=[bwh[sl]], slot=s_wh[sl])
            bop(P, "pool", lambda e, sl=sl, h=h: e.dma_start(out=wo[sl][:], in_=hwout[h], max_dma_last_dim=8192), w=[bwo[sl]], slot=s_wo[sl])

        load_w(0)
        ugl = 0
        jobs = [(h, ti) for h in range(HG_H) for ti in range(len(TT))]

        def head_setup(h):
            if h + 1 < HG_H:
                load_w(h + 1)
            bop(P, "sp", lambda e: e.dma_start(out=gn[:], in_=hnorm[h].partition_broadcast(128)), w=[bgn], slot=s_gn)
            bop(P, "pool", lambda e: e.memset(S[:], 0.0), w=[bS])
            bop(P, "sp", lambda e: e.dma_start(out=S0[:], in_=st_in[:, h, :, :].rearrange("i d e -> d i e")), w=[bS0], slot=s_S0)

        def make_s1(h, ti, kp):
            sl = h % 2
            t0, n = TT[ti]
            sample = (ti == 4)
            CL = 4 if sample else 32
            nch = n // CL
            CM = CH_CMS if sample else CH_CMP
            qcj, kcj, eBj = qc[kp], kc[kp], eB[kp]

            def p0():
                for qi in range(2):
                    for k in range(8):
                        bop(P, "pe", lambda e, qi=qi, k=k: e.matmul(B[qi][:, :n], lhsT=wh[sl][:, k, qi * 128:(qi + 1) * 128], rhs=C.xn[:, k, t0:t0 + n], start=(k == 0), stop=(k == 7)),
                            r=[bwh[sl]], w=[bB[qi]])

            def p1():
                sigmoid_act(F["qs"][:, :n], B[0][:, :n], bF["qs"], bB[0])
                bop(P, "act", lambda e: e.activation(out=F["ez"][:, :n], in_=B[1][:, :n], func=AF.Exp, scale=-1.0), r=[bB[1]], w=[bF["ez"]])
                bop(P, "act", lambda e: e.activation(out=F["r"][:, :n], in_=F["ez"][:, :n], func=AF.Ln, bias=C.one[:, 0:1]), r=[bF["ez"]], w=[bF["r"]])
                bop(P, "act", lambda e: e.activation(out=F["r"][:, :n], in_=F["r"][:, :n], func=AF.Exp, scale=-1.0), r=[bF["r"]], w=[bF["r"]])

            def p2():
                bop(P, "dve", lambda e: e.tensor_tensor(out=F["qs"][:, :n], in0=B[0][:, :n], in1=F["qs"][:, :n], op=ALU.mult), r=[bF["qs"], bB[0]], w=[bF["qs"]])
                bop(P, "dve", lambda e: e.tensor_scalar(out=F["f"][:, :n], in0=F["r"][:, :n], scalar1=oml[:, h:h + 1], scalar2=lbv[:, h:h + 1], op0=ALU.mult, op1=ALU.add),
                    r=[bF["r"], blb], w=[bF["f"]])
                bop(P, "dve", lambda e: e.scalar_tensor_tensor(out=F["kk"][:, :n], in0=F["ez"][:, :n], scalar=oml[:, h:h + 1], in1=F["r"][:, :n], op0=ALU.mult, op1=ALU.mult),
                    r=[bF["ez"], bF["r"], blb], w=[bF["kk"]])
                bop(P, "act", lambda e: e.activation(out=F["f"][:, :n], in_=F["f"][:, :n], func=AF.Ln), r=[bF["f"]], w=[bF["f"]])

            def p3():
                bop(P, "dve", lambda e: e.tensor_tensor_scan(out=F["b"][:, :n], data0=cret[:, CM:CM + n], data1=F["f"][:, :n], initial=0.0, op0=ALU.mult, op1=ALU.add),
                    r=[bF["f"]], w=[bF["b"]])
                bop(P, "pool", lambda e: e.tensor_tensor(
                    out=F["d1"][:, :n].rearrange("p (c s) -> p c s", s=CL), in0=F["b"][:, :n].rearrange("p (c s) -> p c s", s=CL),
                    in1=F["b"][:, :n].rearrange("p (c s) -> p c s", s=CL)[:, :, CL - 1:CL].broadcast_to([128, nch, CL]), op=ALU.subtract),
                    r=[bF["b"]], w=[bF["d1"]])

            def p4():
                bop(P, "act", lambda e: e.activation(out=F["X"][:, :n], in_=F["d1"][:, :n], func=AF.Exp), r=[bF["d1"]], w=[bF["X"]])
                bop(P, "act", lambda e: e.activation(out=F["Y"][:, :n], in_=F["d1"][:, :n], func=AF.Exp, scale=-1.0), r=[bF["d1"]], w=[bF["Y"]])
                bop(P, "act", lambda e: e.activation(out=eBj[:, :nch], in_=F["b"][:, :n].rearrange("p (c s) -> p c s", s=CL)[:, :, CL - 1], func=AF.Exp),
                    r=[bF["b"]], w=[beB[kp]])

            def p5():
                bop(P, "pool", lambda e: e.tensor_tensor(out=qcj[:, :n], in0=F["qs"][:, :n], in1=F["X"][:, :n], op=ALU.mult), r=[bF["qs"], bF["X"]], w=[bqc[kp]])
                bop(P, "pool", lambda e: e.tensor_tensor(out=kcj[:, :n], in0=F["kk"][:, :n], in1=F["Y"][:, :n], op=ALU.mult), r=[bF["kk"], bF["Y"]], w=[bkc[kp]])

            return [p0, p1, p2, p3, p4, p5]

        def make_blocks(h, ti, kp):
            sl = h % 2
            t0, n = TT[ti]
            sample = (ti == 4)
            CL = 4 if sample else 32
            qcj, kcj, eBj = qc[kp], kc[kp], eB[kp]
            blocks = [(0, 64)] if sample else [(c * 128, 128) for c in range(4)]
            MC = CH_MS if sample else CH_MP
            nbc = 16 if sample else 4
            if sample:
                bmask = C.bm
                rmv = cret[:64, CR_RM:CR_RM + 16]
            else:
                bmask = C.bmp
                rmv = cret[:, CH_RMP:CH_RMP + 4]
            ubase = {}

            def A_pe(ci):
                c0, nb = blocks[ci]
                pb = ci % 2
                a0 = t0 + c0
                for k in range(8):
                    bop(P, "pe", lambda e, k=k: e.matmul(VG[pb][:nb, :], lhsT=C.xn[:, k, a0:a0 + nb], rhs=wh[sl][:, k, 256:512], start=(k == 0), stop=(k == 7)),
                        r=[bwh[sl]], w=[bVG[pb]])
                bop(P, "pe", lambda e: e.transpose(out=PT[pb][:nb, :], in_=kcj[:, c0:c0 + nb], identity=C.ident), r=[bkc[kp]], w=[bPT[pb]])
                bop(P, "pe", lambda e: e.matmul(SC[pb][:nb, :nb], lhsT=kcj[:, c0:c0 + nb], rhs=qcj[:, c0:c0 + nb], start=True, stop=True), r=[bkc[kp], bqc[kp]], w=[bSC[pb]])

            def A_ev(ci):
                nonlocal ugl
                c0, nb = blocks[ci]
                pb = ci % 2
                bop(P, "dve", lambda e: e.tensor_copy(out=ktl[pb][:nb, :], in_=PT[pb][:nb, :]), r=[bPT[pb]], w=[bktl[pb]])
                bop(P, "act", lambda e: e.activation(out=vtok[pb][:nb, :], in_=VG[pb][:nb, 0:128], func=AF.Copy), r=[bVG[pb]], w=[bvtok[pb]])
                qzv = qz[pb][:, 0:nbc * nb].rearrange("p (c t) -> p c t", c=nbc)
                kzv = kz[pb][:nb, 0:nbc * 128].rearrange("p (c d) -> p c d", c=nbc)
                bop(P, "pool", lambda e: e.tensor_tensor(out=kzv, in0=ktl[pb][:nb, :].unsqueeze(1).broadcast_to([nb, nbc, 128]), in1=rmv.unsqueeze(2).broadcast_to([nb, nbc, 128]), op=ALU.mult),
                    r=[bktl[pb]], w=[bkz[pb]])
                bop(P, "pool", lambda e: e.tensor_tensor(out=qzv, in0=qcj[:, c0:c0 + nb].unsqueeze(1).broadcast_to([128, nbc, nb]), in1=bmask, op=ALU.mult),
                    r=[bqc[kp]], w=[bqz[pb]])
                bop(P, "dve", lambda e: e.tensor_tensor(out=scm[pb][:nb, :nb], in0=SC[pb][:nb, :nb], in1=cret[:nb, MC:MC + nb], op=ALU.mult), r=[bSC[pb]], w=[bscm[pb]])
                sigmoid_act(gsil[pb][:nb, :], VG[pb][:nb, 128:256], bgsil[pb], bVG[pb])
                bop(P, "dve", lambda e: e.tensor_tensor(out=gsil[pb][:nb, :], in0=VG[pb][:nb, 128:256], in1=gsil[pb][:nb, :], op=ALU.mult), r=[bgsil[pb], bVG[pb]], w=[bgsil[pb]])
                ubase[ci] = ugl
                if nbc <= 4:
                    for c in range(nbc):
                        u = 4 * pb + c
                        bop(P, "pe", lambda e, c=c, u=u: e.matmul(UR[u], lhsT=kzv[:, c, :], rhs=vtok[pb][:nb, :], start=True, stop=True),
                            r=[bkz[pb], bvtok[pb]], w=[bUR[u]])
                ugl += nbc

            def B_chain(ci):
                c0, nb = blocks[ci]
                pb = ci % 2
                kzv = kz[pb][:nb, 0:nbc * 128].rearrange("p (c d) -> p c d", c=nbc)
                for c in range(nbc):
                    ec = (c0 // CL + c)
                    u = (4 * pb + c) if nbc <= 4 else (4 * (c % 2) + (c // 2) % 4)
                    Sin = S0[:, c, :] if sample else S[:, :]
                    bSin = bS0 if sample else bS
                    if nbc > 4:
                        bop(P, "pe", lambda e, c=c, u=u: e.matmul(UR[u], lhsT=kzv[:, c, :], rhs=vtok[pb][:nb, :], start=True, stop=True),
                            r=[bkz[pb], bvtok[pb]], w=[bUR[u]])
                    bop(P, "dve", lambda e, c=c, ec=ec, Sin=Sin: e.tensor_scalar(out=Sdb[pb][:, c, :], in0=Sin, scalar1=eBj[:, ec:ec + 1], scalar2=None, op0=ALU.mult),
                        r=[bSin, beB[kp]], w=[bSdb[pb]])
                    bop(P, "dve", lambda e, ec=ec, u=u, Sin=Sin: e.scalar_tensor_tensor(out=Sin, in0=Sin, scalar=eBj[:, ec:ec + 1], in1=UR[u], op0=ALU.mult, op1=ALU.add),
                        r=[bUR[u], beB[kp]], w=[bSin])

            def B_pe(ci):
                c0, nb = blocks[ci]
                pb = ci % 2
                qzv = qz[pb][:, 0:nbc * nb].rearrange("p (c t) -> p c t", c=nbc)
                bop(P, "pe", lambda e: e.matmul(B[2][:nb, 0:128], lhsT=scm[pb][:nb, :nb], rhs=vtok[pb][:nb, :], start=True, stop=False), r=[bscm[pb], bvtok[pb]], w=[bB[2]])
                for c in range(nbc):
                    bop(P, "pe", lambda e, c=c: e.matmul(B[2][:nb, 0:128], lhsT=qzv[:, c, :], rhs=Sdb[pb][:, c, :], start=False, stop=(c == nbc - 1)),
                        r=[bqz[pb], bSdb[pb]], w=[bB[2]])

            def B_norm(ci):
                c0, nb = blocks[ci]
                pb = ci % 2
                bop(P, "act", lambda e: e.activation(out=on[:nb, :], in_=B[2][:nb, 0:128], func=AF.Square, accum_out=st4[:nb, 0:1]), r=[bB[2]], w=[bon, bst4])
                bop(P, "act", lambda e: e.activation(out=st4[:nb, 2:3], in_=st4[:nb, 0:1], func=AF.Ln, scale=1.0 / 128, bias=C.epsc[:nb, 0:1]), r=[bst4], w=[bst4])
                bop(P, "act", lambda e: e.activation(out=st4[:nb, 3:4], in_=st4[:nb, 2:3], func=AF.Exp, scale=-0.5), r=[bst4], w=[bst4])
                bop(P, "dve", lambda e: e.scalar_tensor_tensor(out=on[:nb, :], in0=B[2][:nb, 0:128], scalar=st4[:nb, 3:4], in1=gn[:nb, :], op0=ALU.mult, op1=ALU.mult),
                    r=[bB[2], bst4, bgn], w=[bon])
                bop(P, "pool", lambda e: e.tensor_tensor(out=go[pb][:nb, :], in0=on[:nb, :], in1=gsil[pb][:nb, :], op=ALU.mult), r=[bon, bgsil[pb]], w=[bgo[pb]])

            def stage_T(ci):
                c0, nb = blocks[ci]
                pb = ci % 2
                bop(P, "pe", lambda e: e.transpose(out=TR[:, :nb], in_=go[pb][:nb, :], identity=C.ident[:nb, :nb]), r=[bgo[pb]], w=[bTR])
                bop(P, "act", lambda e: e.activation(out=goT[:, c0:c0 + nb], in_=TR[:, :nb], func=AF.Copy), r=[bTR], w=[bgoT])

            def wout():
                for m in range(8):
                    bop(P, "pe", lambda e, m=m: e.matmul(B[3][:, :n], lhsT=wo[sl][:, m * 128:(m + 1) * 128], rhs=goT[:, :n], start=True, stop=True),
                        r=[bwo[sl], bgoT], w=[bB[3]])
                    bop(P, "dve", lambda e, m=m: e.tensor_tensor(out=C.xres[:, m, t0:t0 + n], in0=C.xres[:, m, t0:t0 + n], in1=B[3][:, :n], op=ALU.add),
                        r=[bB[3]], w=[bx[m][ti]])
                if ti == 3:
                    bop(P, "sp", lambda e: e.dma_start(out=st_out_p[h], in_=S[:, :]), r=[bS], slot=s_sp)
                if sample:
                    bop(P, "sp", lambda e: e.dma_start(out=st_out_s[:, h, :, :].rearrange("i d e -> d i e"), in_=S0[:, :, :]), r=[bS0], slot=s_so)

            mk = lambda fn, ci: (lambda: fn(ci))
            if len(blocks) == 1:
                sched = [mk(A_pe, 0), mk(A_ev, 0), mk(B_chain, 0), mk(B_pe, 0), mk(B_norm, 0), mk(stage_T, 0)]
                slots_after = {1: [0, 1], 3: [2, 3], 5: [4, 5]}
            else:
                sched = [mk(A_pe, 0), mk(A_ev, 0), mk(A_pe, 1),
                         mk(B_chain, 0), mk(B_pe, 0), mk(A_ev, 1), mk(B_norm, 0), mk(A_pe, 2),
                         mk(B_chain, 1), mk(B_pe, 1), mk(A_ev, 2), mk(B_norm, 1), mk(stage_T, 0), mk(A_pe, 3),
                         mk(B_chain, 2), mk(B_pe, 2), mk(A_ev, 3), mk(B_norm, 2), mk(stage_T, 1),
                         mk(B_chain, 3), mk(B_pe, 3), mk(B_norm, 3), mk(stage_T, 2), mk(stage_T, 3)]
                slots_after = {7: [0, 1], 13: [2], 18: [3, 4], 22: [5]}
            return sched, slots_after, wout

        for p in make_s1(jobs[0][0], jobs[0][1], 0):
            p()
        for kj, (h, ti) in enumerate(jobs):
            kp = kj % 2
            if ti == 0:
                head_setup(h)
            sched, slots_after, wout = make_blocks(h, ti, kp)
            nxt = make_s1(jobs[kj + 1][0], jobs[kj + 1][1], 1 - kp) if kj + 1 < len(jobs) else None
            for si, stage in enumerate(sched):
                stage()
                if nxt is not None:
                    for pi in slots_after.get(si, []):
                        nxt[pi]()
            wout()
        P.flush()


def build_program(cfg):
    nc = bass.Bass("TRN2", target_bir_lowering=False)
    dr = lambda name, shape, kind="ExternalInput", dt=F32: nc.dram_tensor(name, shape, dt, kind=kind).ap()
    xT = dr("xT", [128, 8, NTOK])
    gains_d = dr("gains", [128, 13 * 8])
    cbf_d = dr("cbf", [128, 256 + 1024 + 512], dt=BF16)
    cret_d = dr("cret", [128, CRW2])
    cs_d = dr("cs", [2, 128, NTOK])
    wup_d = dr("wup", [8, NF, 128, 2048])
    wdn_d = dr("wdn", [8, 2, 128, 11 * 1024])
    rwin_d = dr("rwin", [2, 4, 128, 12288])
    rwout_d = dr("rwout", [2, 4, 128, 4096])
    rnorm_d = dr("rnorm", [2, 4, 512])
    sret_d = dr("sret", [2, NSS, 4, 256, 512])
    hwin_d = dr("hwin", [2, 8, 128, 4096])
    hwout_d = dr("hwout", [2, 8, 128, 1024])
    hnorm_d = dr("hnorm", [2, 8, 128])
    lbl_d = dr("lbl", [128, 2, 8])
    shg_d = dr("shg", [2, NSS, 8, 128, 128])
    nhp_d = dr("nhp", [2, 8, 128, 128], kind="ExternalOutput")
    nhs_d = dr("nhs", [2, NSS, 8, 128, 128], kind="ExternalOutput")
    yT = dr("yT", [128, 8, NTOK], kind="ExternalOutput")
    nrp_d = dr("nrp", [2, 4, 256, 512], kind="ExternalOutput")
    nrs_d = dr("nrs", [2, NSS, 4, 256, 512], kind="ExternalOutput")

    with ExitStack() as st:
        P = Prog(nc, st)
        C = Ctx()
        C.P, C.nc = P, nc
        C.dbg = None
        if cfg.get("debug"):
            C.dbg = {"want": set(cfg["debug"]), "seen": {}, "off": {"f": 0, "b": 0}, "slot": P.slot(),
                     "f": dr("dbgf", [128, 8192], kind="ExternalOutput"), "b": dr("dbgb", [128, 8192], kind="ExternalOutput", dt=BF16)}
        cfg["_dbg"] = C.dbg
        sb = lambda name, shape, dt: st.enter_context(nc.sbuf_tensor(name, shape, dt))
        C.xres = sb("xres", [128, 8, NTOK], F32)
        C.xn = sb("xn", [128, 8, NTOK], BF16)
        C.gains = sb("gains_sb", [128, 13 * 8], F32)
        cbf = sb("cbf_sb", [128, 256 + 1024 + 512], BF16)
        C.ident = cbf[:, 0:128]
        C.ones = cbf[:, 128:256]
        C.bm = cbf[:, 256:1280].rearrange("p (a t) -> p a t", a=16)
        C.bmp = cbf[:, 1280:1792].rearrange("p (a t) -> p a t", a=4)
        C.epsc = sb("epsc", [128, 2], F32)
        C.one = sb("onec", [128, 2], F32)
        C.cret = sb("cret_sb", [128, CRW2], F32)

        s_in = P.slot()
        for k in range(8):
            P.dma("sp", lambda e, k=k: e.dma_start(out=C.xres[:, k, :], in_=xT[:, k, :]), s_in)
        P.dma("sp", lambda e: e.dma_start(out=C.gains[:], in_=gains_d), s_in)
        P.dma("sp", lambda e: e.dma_start(out=cbf[:], in_=cbf_d), s_in)
        P.dma("sp", lambda e: e.dma_start(out=C.cret[:], in_=cret_d), s_in)
        P.op("pool", lambda e: e.memset(C.epsc[:], EPS))
        P.op("pool", lambda e: e.memset(C.one[:], 1.0))
        P.flush()

        for blk in cfg["blocks"]:
            if blk[0] == "ffn":
                _, l, i = blk
                emit_ffn(C, l * 3 + (0 if i == 0 else 2), wup_d[l * 2 + i], wdn_d[l * 2 + i])
            elif blk[0] == "ret":
                _, l = blk
                j = l // 2
                emit_normphase(C, l * 3 + 1)
                emit_ret(C, j, rwin_d[j], rwout_d[j], rnorm_d[j], cs_d, sret_d[j], nrp_d[j], nrs_d[j])
            elif blk[0] == "hg":
                _, l = blk
                j = l // 2
                emit_normphase(C, l * 3 + 1)
                emit_hg(C, j, hwin_d[j], hwout_d[j], hnorm_d[j], lbl_d, shg_d[j], nhp_d[j], nhs_d[j])

        with ExitStack() as st2:
            sb2, pt2 = mk_alloc(C, st2)
            ph = Ctx()
            ph.sq = [sb2("sq%d" % i, [128, 8, 512], BF16) for i in range(2)]
            ph.rs = [sb2("rs%d" % i, [128, 512], F32) for i in range(2)]
            ph.psn = pt2("psn", [128, 512], F32)
            yo = [sb2("yo%d" % i, [128, 8, 512], F32) for i in range(2)]
            s_out = [P.slot(), P.slot()]
            yo_rd = [None, None]
            if cfg.get("final_norm", True):
                def out_fn2(ti, k, t0, n, rsb, r):
                    b = ti % 2
                    o = P.op("dve", lambda e: e.scalar_tensor_tensor(
                        out=yo[b][:, k, :n], in0=C.xres[:, k, t0:t0 + n], scalar=C.gains[:, 96 + k:96 + k + 1],
                        in1=rsb[:, :n], op0=ALU.mult, op1=ALU.mult), [r, yo_rd[b]])
                    if k == 7:
                        yo_rd[b] = P.dma("sp", lambda e: e.dma_start(out=yT[:, :, t0:t0 + n], in_=yo[b][:, :, :n]), s_out[b], [o])
                    return o
                emit_norm(C, ph, 12, out_fn2)
            else:
                for k in range(8):
                    P.dma("sp", lambda e, k=k: e.dma_start(out=yT[:, k, :], in_=C.xres[:, k, :]), s_out[0])
            P.flush()
    return nc


def host_consts():
    import ml_dtypes
    c = np.zeros((128, 256 + 1024 + 512), np.float32)
    c[:, 0:128] = np.eye(128, dtype=np.float32)
    c[:, 128:256] = 1.0
    bm = np.zeros((16, 64), np.float32)
    for i in range(16):
        bm[i, 4 * i:4 * i + 4] = 1.0
    c[:, 256:1280] = bm.reshape(1, 1024)
    bmp = np.zeros((4, 128), np.float32)
    for i in range(4):
        bmp[i, 32 * i:32 * i + 32] = 1.0
    c[:, 1280:1792] = bmp.reshape(1, 512)
    out = {"cbf": c.astype(ml_dtypes.bfloat16)}
    cr = np.zeros((128, CRW2), np.float64)
    t = np.arange(128)
    ts = np.arange(64)
    for h, g in enumerate(ret_gammas()):
        lg = np.log(np.float64(g))
        mp = np.where(t[:, None] <= t[None, :], np.exp(-(t[:, None] + 1.0) * lg), 0.0)
        cr[:, CR_MP + h * 128:CR_MP + (h + 1) * 128] = mp
        same = (ts[:, None] // 4) == (ts[None, :] // 4)
        ms = np.where(same & ((ts[:, None] % 4) <= (ts[None, :] % 4)), np.exp(-((ts[:, None] % 4) + 1.0) * lg), 0.0)
        cr[:64, CR_MS + h * 64:CR_MS + (h + 1) * 64] = ms
        cr[:, CR_KDP + h] = np.exp((127.0 - t) * lg)
        cr[:, CR_EPP + h] = EPS * np.exp(-2.0 * (t + 1.0) * lg)
        cr[:64, CR_KDS + h] = np.exp((3.0 - (ts % 4)) * lg)
        cr[:64, CR_EPS + h] = EPS * np.exp(-2.0 * ((ts % 4) + 1.0) * lg)
    for i in range(16):
        cr[4 * i:4 * i + 4, CR_RM + i] = 1.0
    cr[:, CH_MP:CH_MP + 128] = ((t[:, None] // 32) == (t[None, :] // 32)) & (t[:, None] <= t[None, :])
    cr[:64, CH_MS:CH_MS + 64] = ((ts[:, None] // 4) == (ts[None, :] // 4)) & (ts[:, None] <= ts[None, :])
    for i in range(4):
        cr[32 * i:32 * i + 32, CH_RMP + i] = 1.0
    cr[:, CH_CMP:CH_CMP + 512] = (np.arange(512) % 32 != 0)[None, :]
    cr[:, CH_CMS:CH_CMS + 64] = (np.arange(64) % 4 != 0)[None, :]
    out["cret"] = cr.astype(np.float32)
    half = 128
    inv_freq = (np.float32(10000.0) ** (-np.arange(half, dtype=np.float32) / np.float32(half))).astype(np.float32)
    pos = np.concatenate([np.arange(SEQ, dtype=np.float32), np.tile(np.float32(16384.0) + np.arange(DEC, dtype=np.float32), NSS)])
    ang = (pos[None, :] * inv_freq[:, None]).astype(np.float32)
    out["cs"] = np.stack([np.cos(ang), np.sin(ang)]).astype(np.float32)
    return out


def host_weights(inp):
    w = {}
    f32 = lambda a: np.asarray(a, np.float32)
    up = f32(inp["ffn_w_up"]).reshape(8, 8, 128, 2, NF, 128)
    w["wup"] = np.ascontiguousarray(up.transpose(0, 4, 2, 1, 3, 5)).reshape(8, NF, 128, 2048)
    dn = f32(inp["ffn_w_down"]).reshape(8, 2, 11, 128, 1024)
    w["wdn"] = np.ascontiguousarray(dn.transpose(0, 1, 3, 2, 4)).reshape(8, 2, 128, 11 * 1024)
    g = np.concatenate([f32(inp["norm_gain"]).reshape(12, 1024), f32(inp["final_norm"]).reshape(1, 1024)], 0)
    w["gains"] = np.ascontiguousarray(g.reshape(13, 8, 128).transpose(2, 0, 1)).reshape(128, 104)
    wi = f32(inp["ret_w_in"]).reshape(2, 8, 128, 6144)
    parts = []
    for h in range(4):
        parts.append(np.concatenate([wi[..., h * 256:(h + 1) * 256], wi[..., 1024 + h * 256:1024 + (h + 1) * 256],
                                     wi[..., 2048 + h * 512:2048 + (h + 1) * 512], wi[..., 4096 + h * 512:4096 + (h + 1) * 512]], -1))
    wih = np.stack(parts, 1)
    w["rwin"] = np.ascontiguousarray(wih.transpose(0, 1, 3, 2, 4)).reshape(2, 4, 128, 12288)
    wo = f32(inp["ret_w_out"]).reshape(2, 4, 4, 128, 1024)
    w["rwout"] = np.ascontiguousarray(wo.transpose(0, 1, 3, 2, 4)).reshape(2, 4, 128, 4096)
    w["rnorm"] = f32(inp["ret_norm"])
    hi = f32(inp["hg_w_in"]).reshape(2, 8, 128, 4, 8, 128)
    w["hwin"] = np.ascontiguousarray(hi.transpose(0, 4, 2, 1, 3, 5)).reshape(2, 8, 128, 4096)
    w["hwout"] = np.ascontiguousarray(f32(inp["hg_w_out"]).reshape(2, 8, 128, 1024))
    w["hnorm"] = f32(inp["hg_norm"])
    w["lbl"] = np.ascontiguousarray(f32(inp["hg_lb_logits"]).reshape(2, 8, 128).transpose(2, 0, 1))
    return w


def host_core_inputs(inp, c):
    xp = np.asarray(inp["x_prompt"], np.float32)[c]
    xs = np.asarray(inp["x_sample"], np.float32)[c * NSS:(c + 1) * NSS].reshape(NSS * DEC, D)
    x = np.concatenate([xp, xs], 0)
    xT = np.ascontiguousarray(x.T.reshape(8, 128, NTOK).transpose(1, 0, 2))
    m = {"xT": xT}
    m["sret"] = np.ascontiguousarray(np.asarray(inp["state_ret"], np.float32)[:, c * NSS:(c + 1) * NSS])
    m["shg"] = np.ascontiguousarray(np.asarray(inp["state_hgrn"], np.float32)[:, c * NSS:(c + 1) * NSS])
    return m


def full_cfg():
    blocks = []
    for l in range(4):
        blocks.append(("ffn", l, 0))
        blocks.append(("ret", l) if l % 2 == 0 else ("hg", l))
        blocks.append(("ffn", l, 1))
    return {"blocks": blocks, "final_norm": True}


def kernel(**inputs):
    nc = build_program(full_cfg())
    shared = {}
    shared.update(host_consts())
    shared.update(host_weights(inputs))
    in_maps = []
    for c in range(8):
        m = dict(shared)
        m.update(host_core_inputs(inputs, c))
        in_maps.append(m)
    res = run_bass_kernel_spmd(nc, in_maps, core_ids=list(range(8)))
    rs = res.results
    y_prompt = np.empty((8, SEQ, D), np.float32)
    y_sample = np.empty((8 * NSS, DEC, D), np.float32)
    nrp = np.empty((2, 8, 4, 256, 512), np.float32)
    nrs = np.empty((2, 8 * NSS, 4, 256, 512), np.float32)
    nhp = np.empty((2, 8, 8, 128, 128), np.float32)
    nhs = np.empty((2, 8 * NSS, 8, 128, 128), np.float32)
    for c in range(8):
        r = rs[c]
        y = np.asarray(r["yT"]).transpose(1, 0, 2).reshape(D, NTOK).T
        y_prompt[c] = y[:SEQ]
        y_sample[c * NSS:(c + 1) * NSS] = y[SEQ:].reshape(NSS, DEC, D)
        nrp[:, c] = r["nrp"]
        nrs[:, c * NSS:(c + 1) * NSS] = r["nrs"]
        nhp[:, c] = r["nhp"]
        nhs[:, c * NSS:(c + 1) * NSS] = r["nhs"]
    return (y_prompt, y_sample, nrp, nrs, nhp, nhs)
```

```python
import numpy as np
from contextlib import ExitStack
import concourse.bass as bass
import concourse.mybir as mybir
from concourse.bass_utils import run_bass_kernel_spmd

F32 = mybir.dt.float32
BF16 = mybir.dt.bfloat16
AF = mybir.ActivationFunctionType
ALU = mybir.AluOpType

D = 1024
SEQ = 2048
NSS = 16
DEC = 4
NTOK = SEQ + NSS * DEC
DFF = 2816
NF = DFF // 128
EPS = 1e-6
TT = [(0, 512), (512, 512), (1024, 512), (1536, 512), (2048, 64)]


class Op:
    __slots__ = ("eng", "fn", "deps", "pos", "sem", "val", "need_sig", "is_dma", "done")


class Slot:
    def __init__(self, sem):
        self.sem = sem
        self.count = 0


class Prog:
    ENGS = ["pe", "act", "dve", "pool", "sp"]
    ENGOBJ = {"pe": "tensor", "act": "scalar", "dve": "vector", "pool": "gpsimd", "sp": "sync"}

    def __init__(self, nc, stack):
        self.nc = nc
        self.stack = stack
        self.q = {e: [] for e in self.ENGS}
        self.esem = {e: stack.enter_context(nc.semaphore("s_" + e)) for e in ["pe", "act", "dve", "pool"]}
        self.ecount = {e: 0 for e in self.ENGS}
        self.slots = []
        self.nphase = 0

    def slot(self):
        s = Slot(self.stack.enter_context(self.nc.semaphore("d%d" % len(self.slots))))
        self.slots.append(s)
        return s

    def op(self, eng, fn, deps=()):
        o = Op()
        o.eng = eng
        o.fn = fn
        o.deps = [d for d in deps if d is not None]
        o.pos = len(self.q[eng])
        o.is_dma = False
        o.need_sig = False
        o.sem = None
        o.val = None
        o.done = False
        self.q[eng].append(o)
        return o

    def dma(self, eng, fn, slot, deps=()):
        o = self.op(eng, fn, deps)
        o.is_dma = True
        slot.count += 16
        o.sem = slot.sem
        o.val = slot.count
        return o

    def _needs_wait(self, o, d):
        if d.done:
            return False
        if d.is_dma:
            return True
        if d.eng == o.eng:
            if o.eng == "pe":
                return False
            return (o.pos - d.pos) <= 2
        return True

    def flush(self):
        nc = self.nc
        drain_deps = []
        for e in self.ENGS:
            last = {}
            for o in self.q[e]:
                if o.is_dma:
                    last[id(o.sem)] = o
            drain_deps += list(last.values())
        self.op("sp", lambda e: e.nop(), drain_deps)
        for e in self.ENGS:
            for o in self.q[e]:
                for d in o.deps:
                    if not d.is_dma and self._needs_wait(o, d):
                        d.need_sig = True
        for e in self.ENGS:
            c = self.ecount[e]
            for o in self.q[e]:
                if o.is_dma:
                    continue
                if o.need_sig:
                    assert e != "sp"
                    c += 1
                    o.sem = self.esem[e]
                    o.val = c
            self.ecount[e] = c
        self.nphase += 1
        with nc.Block() as block:
            for e in self.ENGS:
                ops = self.q[e]
                if not ops:
                    continue

                def body(eng, ops=ops):
                    waited = {}
                    for o in ops:
                        need = {}
                        for d in o.deps:
                            if not self._needs_wait(o, d):
                                continue
                            key = id(d.sem)
                            if key not in need or need[key][1] < d.val:
                                need[key] = (d.sem, d.val)
                        for key, (sem, val) in need.items():
                            if waited.get(key, 0) >= val:
                                continue
                            eng.wait_ge(sem, val)
                            waited[key] = val
                        ins = o.fn(eng)
                        if o.is_dma:
                            ins.then_inc(o.sem, 16)
                        elif o.need_sig:
                            ins.then_inc(o.sem, 1)

                getattr(block, self.ENGOBJ[e])(body)
        for e in self.ENGS:
            for o in self.q[e]:
                o.done = True
                o.fn = None
                o.deps = None
            self.q[e] = []


class Ctx:
    pass


def dbg_dump(C, name, ap, bufs, ncols, bf=False):
    if not getattr(C, "dbg", None) or name in C.dbg["seen"] or name not in C.dbg["want"]:
        return
    key = "b" if bf else "f"
    off = C.dbg["off"][key]
    C.dbg["off"][key] = off + ncols
    C.dbg["seen"][name] = (key, off, ncols)
    dst = C.dbg[key][:, off:off + ncols]
    bop(C.P, "sp", lambda e: e.dma_start(out=dst, in_=ap), r=bufs, slot=C.dbg["slot"])


class Buf:
    def __init__(self):
        self.w = None
        self.r = []


class PBuf(Buf):
    excl = True


def bop(P, eng, fn, r=(), w=(), slot=None, deps=()):
    xr = [b for b in r if getattr(b, "excl", False)]
    if xr:
        r = [b for b in r if not getattr(b, "excl", False)]
        w = list(w) + [b for b in xr if b not in w]
    d = list(deps)
    for b in r:
        d.append(b.w)
    for b in w:
        d.append(b.w)
        d.extend(b.r)
    o = P.dma(eng, fn, slot, d) if slot is not None else P.op(eng, fn, d)
    for b in r:
        if not o.is_dma:
            b.r = [x for x in b.r if x.is_dma or x.eng != eng]
        b.r.append(o)
    for b in w:
        b.w = o
        b.r = []
    return o


_UID = [0]


def mk_alloc(C, st):
    _UID[0] += 1
    u = _UID[0]
    nc = C.nc
    sb = lambda name, shape, dt: st.enter_context(nc.sbuf_tensor("%s_%d" % (name, u), shape, dt))
    pt = lambda name, shape, dt: st.enter_context(nc.psum_tensor("%s_%d" % (name, u), shape, dt))
    return sb, pt


def emit_norm(C, ph, gcol, out_fn=None):
    P = C.P
    sq, rs, psn = ph.sq, ph.rs, ph.psn
    last = []
    sq_rd = [None, None]
    rs_rd = [None, None]
    ps_rd = None
    for ti, (t0, n) in enumerate(TT):
        b = ti % 2
        a = P.op("act", lambda e, b=b, t0=t0, n=n: e.activation(out=sq[b][:, :, :n], in_=C.xres[:, :, t0:t0 + n], func=AF.Square),
                 [sq_rd[b]])
        mm = None
        for k in range(8):
            mm = P.op("pe", lambda e, b=b, k=k, n=n: e.matmul(psn[:, :n], lhsT=C.ones[:], rhs=sq[b][:, k, :n], start=(k == 0), stop=(k == 7)),
                      [a, ps_rd])
        sq_rd[b] = mm
        v = P.op("act", lambda e, b=b, n=n: e.activation(out=rs[b][:, :n], in_=psn[:, :n], func=AF.Ln, scale=1.0 / D, bias=C.epsc[:, 0:1]),
                 [mm, rs_rd[b]])
        ps_rd = v
        r = P.op("act", lambda e, b=b, n=n: e.activation(out=rs[b][:, :n], in_=rs[b][:, :n], func=AF.Exp, scale=-0.5), [v])
        o = None
        for k in range(8):
            if out_fn is None:
                o = P.op("dve", lambda e, b=b, k=k, t0=t0, n=n: e.scalar_tensor_tensor(
                    out=C.xn[:, k, t0:t0 + n], in0=C.xres[:, k, t0:t0 + n], scalar=C.gains[:, gcol * 8 + k:gcol * 8 + k + 1],
                    in1=rs[b][:, :n], op0=ALU.mult, op1=ALU.mult), [r])
            else:
                o = out_fn(ti, k, t0, n, rs[b], r)
        rs_rd[b] = o
        last.append(o)
    return last


def emit_normphase(C, gcol):
    with ExitStack() as st:
        sb, pt = mk_alloc(C, st)
        ph = Ctx()
        ph.sq = [sb("sq%d" % i, [128, 8, 512], BF16) for i in range(2)]
        ph.rs = [sb("rs%d" % i, [128, 512], F32) for i in range(2)]
        ph.psn = pt("psn", [128, 512], F32)
        emit_norm(C, ph, gcol)
        C.P.flush()


def emit_ffn(C, gcol, wup, wdn):
    P, nc = C.P, C.nc
    with ExitStack() as st:
        sb, pt = mk_alloc(C, st)
        hid = sb("hid", [128, 11, NTOK], BF16)
        wu = [sb("wu%d" % i, [128, 8, 2, 128], BF16) for i in range(3)]
        wd = sb("wd", [128, 11, 1024], BF16)
        sa = [sb("sa%d" % i, [128, 512], F32) for i in range(2)]
        sqs = [sb("sqs%d" % i, [128, 512], BF16) for i in range(4)]
        rs = [sb("rs%d" % i, [128, 512], F32) for i in range(2)]
        psn = pt("psn", [128, 512], F32)
        psA = [pt("psA%d" % i, [128, 512], F32) for i in range(2)]
        psB = [pt("psB%d" % i, [128, 512], F32) for i in range(2)]
        psD = [pt("psD%d" % i, [128, 512], F32) for i in range(2)]
        bwu = [Buf() for _ in range(3)]; bwd = Buf(); bsa = [Buf(), Buf()]; bsq = [Buf() for _ in range(4)]; brs = [Buf(), Buf()]
        bpsn = PBuf(); bpsA = [PBuf(), PBuf()]; bpsB = [PBuf(), PBuf()]; bpsD = [PBuf(), PBuf()]
        bxn = [Buf() for _ in TT]; bxr = [Buf() for _ in TT]; bhid = [Buf() for _ in TT]
        s_wu = [P.slot() for _ in range(3)]
        s_wd = P.slot()

        def load_wu(fi):
            s = fi % 3
            bop(P, "pool", lambda e: e.dma_start(out=wu[s][:].rearrange("p k a c -> p (k a c)"), in_=wup[fi], max_dma_last_dim=8192), w=[bwu[s]], slot=s_wu[s])

        def load_wd(half):
            bop(P, "pool", lambda e: e.dma_start(out=wd[:].rearrange("p f n -> p (f n)"), in_=wdn[half], max_dma_last_dim=8192), w=[bwd], slot=s_wd)

        sqc = [0]

        def norm(ti):
            t0, n = TT[ti]
            b = ti % 2
            for k in range(8):
                q = sqc[0] % 4
                sqc[0] += 1
                bop(P, "act", lambda e, k=k, q=q: e.activation(out=sqs[q][:, :n], in_=C.xres[:, k, t0:t0 + n], func=AF.Square), r=[bxr[ti]], w=[bsq[q]])
                bop(P, "pe", lambda e, k=k, q=q: e.matmul(psn[:, :n], lhsT=C.ones[:], rhs=sqs[q][:, :n], start=(k == 0), stop=(k == 7)), r=[bsq[q]], w=[bpsn])
            bop(P, "act", lambda e: e.activation(out=rs[b][:, :n], in_=psn[:, :n], func=AF.Ln, scale=1.0 / D, bias=C.epsc[:, 0:1]), r=[bpsn], w=[brs[b]])
            bop(P, "act", lambda e: e.activation(out=rs[b][:, :n], in_=rs[b][:, :n], func=AF.Exp, scale=-0.5), r=[brs[b]], w=[brs[b]])
            for k in range(8):
                bop(P, "dve", lambda e, k=k: e.scalar_tensor_tensor(
                    out=C.xn[:, k, t0:t0 + n], in0=C.xres[:, k, t0:t0 + n], scalar=C.gains[:, gcol * 8 + k:gcol * 8 + k + 1],
                    in1=rs[b][:, :n], op0=ALU.mult, op1=ALU.mult), r=[brs[b], bxr[ti]], w=[bxn[ti]])

        load_wd(0)
        for fi in range(3):
            load_wu(fi)
        norm(0)
        norm(1)
        cnt = 0
        dcnt = 0
        for half in range(2):
            if half == 1:
                load_wd(1)
            for f in range(11):
                fi = half * 11 + f
                s = fi % 3
                for ti, (t0, n) in enumerate(TT):
                    b = cnt % 2
                    cnt += 1
                    for k in range(8):
                        bop(P, "pe", lambda e, b=b, s=s, k=k, t0=t0, n=n: e.matmul(psA[b][:, :n], lhsT=wu[s][:, k, 0, :], rhs=C.xn[:, k, t0:t0 + n], start=(k == 0), stop=(k == 7)),
                            r=[bwu[s], bxn[ti]], w=[bpsA[b]])
                    for k in range(8):
                        bop(P, "pe", lambda e, b=b, s=s, k=k, t0=t0, n=n: e.matmul(psB[b][:, :n], lhsT=wu[s][:, k, 1, :], rhs=C.xn[:, k, t0:t0 + n], start=(k == 0), stop=(k == 7)),
                            r=[bwu[s], bxn[ti]], w=[bpsB[b]])
                    bop(P, "act", lambda e, b=b, n=n: e.activation(out=sa[b][:, :n], in_=psA[b][:, :n], func=AF.Silu), r=[bpsA[b]], w=[bsa[b]])
                    bop(P, "dve", lambda e, b=b, f=f, t0=t0, n=n: e.tensor_tensor(out=hid[:, f, t0:t0 + n], in0=sa[b][:, :n], in1=psB[b][:, :n], op=ALU.mult),
                        r=[bsa[b], bpsB[b]], w=[bhid[ti]])
                    if fi == 0 and ti + 2 < len(TT):
                        norm(ti + 2)
                if fi + 3 < NF:
                    load_wu(fi + 3)
            for ti, (t0, n) in enumerate(TT):
                for mo in range(8):
                    b = dcnt % 2
                    dcnt += 1
                    for f in range(11):
                        bop(P, "pe", lambda e, b=b, f=f, mo=mo, t0=t0, n=n: e.matmul(psD[b][:, :n], lhsT=wd[:, f, mo * 128:(mo + 1) * 128], rhs=hid[:, f, t0:t0 + n], start=(f == 0), stop=(f == 10)),
                            r=[bwd, bhid[ti]], w=[bpsD[b]])
                    bop(P, "dve", lambda e, b=b, mo=mo, t0=t0, n=n: e.scalar_tensor_tensor(
                        out=C.xres[:, mo, t0:t0 + n], in0=psD[b][:, :n], scalar=0.5, in1=C.xres[:, mo, t0:t0 + n], op0=ALU.mult, op1=ALU.add),
                        r=[bpsD[b]], w=[bxr[ti]])
        P.flush()


RET_H = 4
CR_MP = 0
CR_MS = 512
CR_KDP = 768
CR_EPP = 772
CR_KDS = 776
CR_EPS = 780
CR_RM = 784
CRW = 800


def ret_gammas():
    return [1.0 - 2.0 ** (-5.0 - h) for h in range(RET_H)]


def emit_ret(C, j, rwin, rwout, rnorm, cs, st_in, st_out_p, st_out_s):
    P, nc = C.P, C.nc
    gam = ret_gammas()
    with ExitStack() as st:
        sb, pt = mk_alloc(C, st)
        wh = [sb("wh%d" % i, [128, 8, 1536], BF16) for i in range(2)]
        wo = sb("wo", [128, 4, 1024], BF16)
        cst = sb("cs", [128, 2, 512], F32)
        qT = sb("qT", [128, 2, 512], BF16)
        kT = sb("kT", [128, 2, 512], BF16)
        vtok = [sb("vtok%d" % i, [128, 512], BF16) for i in range(2)]
        ktl = [sb("ktl%d" % i, [128, 256], BF16) for i in range(2)]
        scm = [sb("scm%d" % i, [128, 128], BF16) for i in range(2)]
        gs = sb("gs", [128, 512], F32)
        on = sb("on", [128, 512], F32)
        go = [sb("go%d" % i, [128, 512], BF16) for i in range(2)]
        goT = sb("goT", [128, 4, 512], BF16)
        S = sb("S", [128, 2, 512], F32)
        Sb = sb("Sb", [128, 2, 512], BF16)
        gn = sb("gn", [128, 512], F32)
        qz = sb("qz", [128, 2, 16, 64], BF16)
        kz = [sb("kz%d" % i, [64, 256], BF16) for i in range(2)]
        S0f = [sb("S0f%d" % i, [128, 512], F32) for i in range(2)]
        S0b = [sb("S0b%d" % i, [128, 512], BF16) for i in range(2)]
        S0f = [t[:, :] for t in S0f] + [S[:, 0, :], S[:, 1, :]]
        S0b = [t[:, :] for t in S0b] + [Sb[:, 0, :], Sb[:, 1, :]]
        st4 = sb("st4", [128, 4], F32)
        B = [pt("b%d" % i, [128, 512], F32) for i in range(8)]
        PT = [B[6][:, 0:128].bitcast(BF16), B[3][:, 0:128].bitcast(BF16)]
        SC = [B[6][:, 128:256], B[3][:, 128:256]]
        TR = B[3][:, 256:512].bitcast(BF16)
        cret = C.cret

        bwh = [Buf(), Buf()]; bwo = Buf(); bcs = Buf(); bt12 = Buf(); bqT = Buf(); bkT = Buf()
        bvtok = [Buf(), Buf()]; bktl = [Buf(), Buf()]; bscm = [Buf(), Buf()]; bgs = Buf(); bon = Buf(); bgo = [Buf(), Buf()]; bgoT = Buf()
        bSh = [Buf(), Buf()]; bSbh = [Buf(), Buf()]; bgn = Buf(); bqz = Buf(); bkz = [Buf(), Buf()]
        bS0f = [Buf(), Buf()] + bSh; bS0b = [Buf(), Buf()] + bSbh; bst4 = Buf()
        bB = [PBuf() for _ in range(8)]
        bB7t = bB[3]
        bPT = [bB[6], bB[3]]; bSC = [bB[6], bB[3]]
        bx = [[Buf() for _ in TT] for _ in range(8)]
        s_wh = [P.slot(), P.slot()]; s_wo = P.slot(); s_cs = P.slot(); s_gn = P.slot()
        s_S0f = [P.slot() for _ in range(4)]; s_S0b = [P.slot() for _ in range(4)]; s_so = [P.slot() for _ in range(4)]; s_sp = P.slot()

        def load_wh(h):
            sl = h % 2
            bop(P, "pool", lambda e, sl=sl: e.dma_start(out=wh[sl][:].rearrange("p k n -> p (k n)"), in_=rwin[h], max_dma_last_dim=8192),
                w=[bwh[sl]], slot=s_wh[sl])

        load_wh(0)
        dbg_dump(C, "xn0", C.xn[:, 0, 0:512], [], 512, bf=True)
        dbg_dump(C, "wh0", wh[0][:, 0, 0:512], [bwh[0]], 512, bf=True)
        ucnt = 0
        for h in range(RET_H):
            sl = h % 2
            g = gam[h]
            bop(P, "pool", lambda e, h=h: e.dma_start(out=wo[:].rearrange("p a n -> p (a n)"), in_=rwout[h], max_dma_last_dim=8192),
                w=[bwo], slot=s_wo)
            if h + 1 < RET_H:
                load_wh(h + 1)
            bop(P, "sp", lambda e, h=h: e.dma_start(out=gn[:], in_=rnorm[h].partition_broadcast(128)), w=[bgn], slot=s_gn)
            bop(P, "pool", lambda e: e.memset(S[:], 0.0), w=[bSh[0], bSh[1]])
            bop(P, "pool", lambda e: e.memset(Sb[:], 0.0), w=[bSbh[0], bSbh[1]])
            for ti, (t0, n) in enumerate(TT):
                sample = (ti == 4)
                bop(P, "sp", lambda e, t0=t0, n=n: e.dma_start(out=cst[:, :, :n], in_=cs[:, :, t0:t0 + n].rearrange("a p t -> p a t")),
                    w=[bcs], slot=s_cs)
                for qi in range(4):
                    for k in range(8):
                        bop(P, "pe", lambda e, sl=sl, qi=qi, k=k, t0=t0, n=n: e.matmul(B[qi][:, :n], lhsT=wh[sl][:, k, qi * 128:(qi + 1) * 128], rhs=C.xn[:, k, t0:t0 + n], start=(k == 0), stop=(k == 7)),
                            r=[bwh[sl]], w=[bB[qi]])
                for (dst, bd, b0, b1, sc) in ((qT, bqT, 0, 1, 1.0), (kT, bkT, 2, 3, 0.0625)):
                    for half in range(2):
                        ca, cb = (0, 1) if half == 0 else (1, 0)
                        bop(P, "dve", lambda e, b0=b0, ca=ca, sc=sc, n=n: e.scalar_tensor_tensor(out=gs[:, :n], in0=B[b0][:, :n], scalar=sc, in1=cst[:, ca, :n], op0=ALU.mult, op1=ALU.mult),
                            r=[bB[b0], bcs], w=[bgs])
                        bop(P, "dve", lambda e, b1=b1, cb=cb, sc=sc, n=n: e.scalar_tensor_tensor(out=on[:, :n], in0=B[b1][:, :n], scalar=sc, in1=cst[:, cb, :n], op0=ALU.mult, op1=ALU.mult),
                            r=[bB[b1], bcs], w=[bon])
                        bop(P, "dve", lambda e, dst=dst, half=half, n=n: e.tensor_tensor(out=dst[:, half, :n], in0=gs[:, :n], in1=on[:, :n], op=(ALU.subtract if half == 0 else ALU.add)),
                            r=[bgs, bon], w=[bd])
                blocks = [(0, 64)] if sample else [(c * 128, 128) for c in range(n // 128)]
                MC = (CR_MS + h * 64) if sample else (CR_MP + h * 128)
                KD = (CR_KDS if sample else CR_KDP) + h
                EP = (CR_EPS if sample else CR_EPP) + h

                def stage_A(ci, c0, nb, sl=sl, t0=t0, MC=MC, KD=KD):
                    pb = ci % 2
                    a0 = t0 + c0
                    vb = 4 + pb
                    for k in range(8):
                        bop(P, "pe", lambda e, k=k: e.matmul(B[vb][:nb, :], lhsT=C.xn[:, k, a0:a0 + nb], rhs=wh[sl][:, k, 512:1024], start=(k == 0), stop=(k == 7)),
                            r=[bwh[sl]], w=[bB[vb]])
                    bop(P, "act", lambda e: e.activation(out=vtok[pb][:nb, :], in_=B[vb][:nb, :], func=AF.Copy), r=[bB[vb]], w=[bvtok[pb]])
                    for jj in range(2):
                        bop(P, "pe", lambda e, jj=jj: e.transpose(out=PT[pb][:nb, jj * 128:(jj + 1) * 128], in_=kT[:, jj, c0:c0 + nb], identity=C.ident),
                            r=[bkT], w=[bPT[pb]])
                    for jj in range(2):
                        bop(P, "pe", lambda e, jj=jj: e.matmul(SC[pb][:nb, :nb], lhsT=kT[:, jj, c0:c0 + nb], rhs=qT[:, jj, c0:c0 + nb], start=(jj == 0), stop=(jj == 1)),
                            r=[bkT, bqT], w=[bSC[pb]])
                    bop(P, "dve", lambda e: e.tensor_scalar(out=ktl[pb][:nb, :], in0=PT[pb][:nb, :], scalar1=cret[:nb, KD:KD + 1], scalar2=None, op0=ALU.mult),
                        r=[bPT[pb]], w=[bktl[pb]])
                    bop(P, "dve", lambda e: e.tensor_tensor(out=scm[pb][:nb, :nb], in0=SC[pb][:nb, :nb], in1=cret[:nb, MC:MC + nb], op=ALU.mult),
                        r=[bSC[pb]], w=[bscm[pb]])

                def stage_B(ci, c0, nb, sl=sl, t0=t0, EP=EP, sample=sample, g=g, h=h):
                    nonlocal ucnt
                    pb = ci % 2
                    a0 = t0 + c0
                    bop(P, "pe", lambda e: e.matmul(B[7][:nb, :], lhsT=scm[pb][:nb, :nb], rhs=vtok[pb][:nb, :], start=True, stop=False),
                        r=[bscm[pb], bvtok[pb]], w=[bB[7]])
                    if not sample:
                        for jj in range(2):
                            bop(P, "pe", lambda e, jj=jj: e.matmul(B[7][:nb, :], lhsT=qT[:, jj, c0:c0 + nb], rhs=Sb[:, jj, :], start=False, stop=(jj == 1)),
                                r=[bqT, bSbh[jj]], w=[bB[7]])
                        for jj in range(2):
                            bop(P, "pe", lambda e, jj=jj: e.matmul(B[jj][:, :], lhsT=ktl[pb][:nb, jj * 128:(jj + 1) * 128], rhs=vtok[pb][:nb, :], start=True, stop=True),
                                r=[bktl[pb], bvtok[pb]], w=[bB[jj]])
                        cd = g ** 128
                        for jj in range(2):
                            bop(P, "dve", lambda e, jj=jj: e.scalar_tensor_tensor(out=S[:, jj, :], in0=S[:, jj, :], scalar=cd, in1=B[jj][:, :], op0=ALU.mult, op1=ALU.add),
                                r=[bB[jj]], w=[bSh[jj]])
                        for jj in range(2):
                            bop(P, "pool", lambda e, jj=jj: e.tensor_copy(out=Sb[:, jj, :], in_=S[:, jj, :]), r=[bSh[jj]], w=[bSbh[jj]])
                    else:
                        for jj in range(2):
                            bop(P, "dve", lambda e, jj=jj: e.tensor_tensor(out=qz[:, jj, :, :], in0=qT[:, jj, 0:64].unsqueeze(1).broadcast_to([128, 16, 64]), in1=C.bm[:, :, :], op=ALU.mult),
                                r=[bqT], w=[bqz])
                        cd = g ** 4
                        units = [(i, jj) for i in range(NSS) for jj in range(2)]

                        def issue_loads(k):
                            i, jj = units[k]
                            u = k % 4
                            bop(P, "pool", lambda e: e.dma_start(out=S0b[u], in_=st_in[i, h, jj * 128:(jj + 1) * 128, :]), w=[bS0b[u]], slot=s_S0b[u])
                            bop(P, "sp", lambda e: e.dma_start(out=S0f[u], in_=st_in[i, h, jj * 128:(jj + 1) * 128, :]), w=[bS0f[u]], slot=s_S0f[u])

                        issue_loads(0)
                        issue_loads(1)
                        for k, (i, jj) in enumerate(units):
                            if k + 2 < len(units):
                                issue_loads(k + 2)
                            kb = i % 2
                            u = k % 4
                            if jj == 0:
                                bop(P, "dve", lambda e, i=i, kb=kb: e.tensor_scalar(out=kz[kb][:, :], in0=ktl[pb][:64, :], scalar1=cret[:64, CR_RM + i:CR_RM + i + 1], scalar2=None, op0=ALU.mult),
                                    r=[bktl[pb]], w=[bkz[kb]])
                            last = (k == len(units) - 1)
                            bop(P, "pe", lambda e, u=u, i=i, jj=jj, last=last: e.matmul(B[7][:64, :], lhsT=qz[:, jj, i, :], rhs=S0b[u], start=False, stop=last),
                                r=[bqz, bS0b[u]], w=[bB[7]])
                            bop(P, "pe", lambda e, kb=kb, jj=jj: e.matmul(B[jj][:, :], lhsT=kz[kb][:, jj * 128:(jj + 1) * 128], rhs=vtok[pb][:64, :], start=True, stop=True),
                                r=[bkz[kb], bvtok[pb]], w=[bB[jj]])
                            bop(P, "dve", lambda e, u=u, jj=jj: e.scalar_tensor_tensor(out=S0f[u], in0=S0f[u], scalar=cd, in1=B[jj][:, :], op0=ALU.mult, op1=ALU.add),
                                r=[bB[jj]], w=[bS0f[u]])
                            bop(P, "sp", lambda e, u=u, i=i, jj=jj: e.dma_start(out=st_out_s[i, h, jj * 128:(jj + 1) * 128, :], in_=S0f[u]), r=[bS0f[u]], slot=s_so[u])
                    bop(P, "act", lambda e: e.activation(out=on[:nb, :], in_=B[7][:nb, :], func=AF.Square, accum_out=st4[:nb, 0:1]),
                        r=[bB[7]], w=[bon, bst4])
                    bop(P, "act", lambda e: e.activation(out=st4[:nb, 2:3], in_=st4[:nb, 0:1], func=AF.Ln, scale=1.0 / 512, bias=cret[:nb, EP:EP + 1]), r=[bst4], w=[bst4])
                    bop(P, "act", lambda e: e.activation(out=st4[:nb, 3:4], in_=st4[:nb, 2:3], func=AF.Exp, scale=-0.5), r=[bst4], w=[bst4])
                    bop(P, "dve", lambda e: e.scalar_tensor_tensor(out=on[:nb, :], in0=B[7][:nb, :], scalar=st4[:nb, 3:4], in1=gn[:nb, :], op0=ALU.mult, op1=ALU.mult),
                        r=[bB[7], bst4, bgn], w=[bon])
                    for k in range(8):
                        bop(P, "pe", lambda e, k=k: e.matmul(B[2][:nb, :], lhsT=C.xn[:, k, a0:a0 + nb], rhs=wh[sl][:, k, 1024:1536], start=(k == 0), stop=(k == 7)),
                            r=[bwh[sl]], w=[bB[2]])
                    bop(P, "act", lambda e: e.activation(out=gs[:nb, :], in_=B[2][:nb, :], func=AF.Exp, scale=-1.0), r=[bB[2]], w=[bgs])
                    bop(P, "act", lambda e: e.activation(out=gs[:nb, :], in_=gs[:nb, :], func=AF.Ln, bias=C.one[:nb, 0:1]), r=[bgs], w=[bgs])
                    bop(P, "act", lambda e: e.activation(out=gs[:nb, :], in_=gs[:nb, :], func=AF.Exp, scale=-1.0), r=[bgs], w=[bgs])
                    bop(P, "dve", lambda e: e.tensor_tensor(out=gs[:nb, :], in0=B[2][:nb, :], in1=gs[:nb, :], op=ALU.mult), r=[bgs, bB[2]], w=[bgs])
                    bop(P, "pool", lambda e: e.tensor_tensor(out=go[pb][:nb, :], in0=on[:nb, :], in1=gs[:nb, :], op=ALU.mult), r=[bon, bgs], w=[bgo[pb]])

                def stage_T(ci, c0, nb):
                    pb = ci % 2
                    for e4 in range(4):
                        bop(P, "pe", lambda e, e4=e4: e.transpose(out=TR[:, e4 * 128:e4 * 128 + nb], in_=go[pb][:nb, e4 * 128:(e4 + 1) * 128], identity=C.ident[:nb, :nb]),
                            r=[bgo[pb]], w=[bB7t])
                    bop(P, "act", lambda e: e.activation(out=goT[:, :, c0:c0 + nb], in_=TR.rearrange("p (a t) -> p a t", a=4)[:, :, :nb], func=AF.Copy),
                        r=[bB7t], w=[bgoT])

                nblk = len(blocks)
                sched = []
                if nblk == 1:
                    sched = [("A", 0), ("B", 0), ("T", 0)]
                else:
                    sched = [("A", 0), ("A", 1), ("B", 0), ("A", 2), ("B", 1), ("T", 0), ("A", 3), ("B", 2), ("T", 1), ("B", 3), ("T", 2), ("T", 3)]
                for (kind, ci) in sched:
                    c0, nb = blocks[ci]
                    if kind == "A":
                        stage_A(ci, c0, nb)
                    elif kind == "B":
                        stage_B(ci, c0, nb)
                    else:
                        stage_T(ci, c0, nb)
                for m in range(8):
                    wb = 4 + (m % 2)
                    for e4 in range(4):
                        bop(P, "pe", lambda e, m=m, e4=e4, n=n, wb=wb: e.matmul(B[wb][:, :n], lhsT=wo[:, e4, m * 128:(m + 1) * 128], rhs=goT[:, e4, :n], start=(e4 == 0), stop=(e4 == 3)),
                            r=[bwo, bgoT], w=[bB[wb]])
                    bop(P, "dve", lambda e, m=m, t0=t0, n=n, wb=wb: e.tensor_tensor(out=C.xres[:, m, t0:t0 + n], in0=C.xres[:, m, t0:t0 + n], in1=B[wb][:, :n], op=ALU.add),
                        r=[bB[wb]], w=[bx[m][ti]])
                if ti == 3:
                    bop(P, "sp", lambda e, h=h: e.dma_start(out=st_out_p[h].rearrange("(a p) n -> p a n", p=128), in_=S[:, :, :]), r=[bSh[0], bSh[1]], slot=s_sp)
        P.flush()


HG_H = 8
CH_MP = 800
CH_MS = 928
CH_RMP = 992
CH_CMP = 1000
CH_CMS = 1512
CRW2 = 1576


def emit_hg(C, j, hwin, hwout, hnorm, lbl, st_in, st_out_p, st_out_s):
    P, nc = C.P, C.nc
    with ExitStack() as st:
        sb, pt = mk_alloc(C, st)
        wh = [sb("hwh%d" % i, [128, 8, 512], BF16) for i in range(2)]
        wo = [sb("hwo%d" % i, [128, 1024], BF16) for i in range(2)]
        F = {nm: sb("h" + nm, [128, 512], F32) for nm in ("qs", "ez", "r", "f", "kk", "b", "d1", "X", "Y")}
        qc = [sb("hqc%d" % i, [128, 512], BF16) for i in range(2)]
        kc = [sb("hkc%d" % i, [128, 512], BF16) for i in range(2)]
        eB = [sb("heB%d" % i, [128, 16], F32) for i in range(2)]
        vtok = [sb("hvtok%d" % i, [128, 128], BF16) for i in range(2)]
        gsil = [sb("hgsil%d" % i, [128, 128], F32) for i in range(2)]
        ktl = [sb("hktl%d" % i, [128, 128], BF16) for i in range(2)]
        scm = [sb("hscm%d" % i, [128, 128], BF16) for i in range(2)]
        qz = [sb("hqz%d" % i, [128, 1024], BF16) for i in range(2)]
        kz = [sb("hkz%d" % i, [128, 2048], BF16) for i in range(2)]
        Sdb = [sb("hSdb%d" % i, [128, 16, 128], BF16) for i in range(2)]
        S = sb("hS", [128, 128], F32)
        S0 = sb("hS0", [128, 16, 128], F32)
        on = sb("hon", [128, 128], F32)
        go = [sb("hgo%d" % i, [128, 128], BF16) for i in range(2)]
        goT = sb("hgoT", [128, 512], BF16)
        gn = sb("hgn", [128, 128], F32)
        lb = sb("hlb", [128, 2, 8], F32)
        lbv = sb("hlbv", [128, 8], F32)
        oml = sb("homl", [128, 8], F32)
        st4 = sb("hst4", [128, 4], F32)
        B = [pt("hb%d" % i, [128, 512], F32) for i in range(8)]
        VG = [B[4][:, 0:256], B[6][:, 0:256]]
        PT = [B[4][:, 256:320].bitcast(BF16), B[6][:, 256:320].bitcast(BF16)]
        SC = [B[4][:, 320:448], B[6][:, 320:448]]
        TR = B[3][:, 256:320].bitcast(BF16)
        UR = [B[5][:, i * 128:(i + 1) * 128] for i in range(4)] + [B[7][:, i * 128:(i + 1) * 128] for i in range(4)]
        cret = C.cret
        bF = {nm: Buf() for nm in F}
        bwh = [Buf(), Buf()]; bwo = [Buf(), Buf()]; bqc = [Buf(), Buf()]; bkc = [Buf(), Buf()]; beB = [Buf(), Buf()]
        bvtok = [Buf(), Buf()]; bgsil = [Buf(), Buf()]; bktl = [Buf(), Buf()]; bscm = [Buf(), Buf()]
        bqz = [Buf(), Buf()]; bkz = [Buf(), Buf()]; bSdb = [Buf(), Buf()]; bgo = [Buf(), Buf()]
        bS = Buf(); bS0 = Buf(); bon = Buf(); bgoT = Buf(); bgn = Buf(); blb = Buf(); bst4 = Buf()
        bB = [PBuf() for _ in range(8)]
        bVG = [bB[4], bB[6]]; bPT = [bB[4], bB[6]]; bSC = [bB[4], bB[6]]; bTR = bB[3]; bUR = [bB[5]] * 4 + [bB[7]] * 4
        bx = [[Buf() for _ in TT] for _ in range(8)]
        s_wh = [P.slot(), P.slot()]; s_wo = [P.slot(), P.slot()]; s_gn = P.slot(); s_lb = P.slot()
        s_S0 = P.slot(); s_so = P.slot(); s_sp = P.slot()

        def sigmoid_act(dst, src, bdst, bsrc):
            bop(P, "act", lambda e: e.activation(out=dst, in_=src, func=AF.Exp, scale=-1.0), r=[bsrc], w=[bdst])
            bop(P, "act", lambda e: e.activation(out=dst, in_=dst, func=AF.Ln, bias=C.one[:dst.shape[0], 0:1]), r=[bdst], w=[bdst])
            bop(P, "act", lambda e: e.activation(out=dst, in_=dst, func=AF.Exp, scale=-1.0), r=[bdst], w=[bdst])

        bop(P, "sp", lambda e: e.dma_start(out=lb[:], in_=lbl), w=[blb], slot=s_lb)
        if j == 0:
            bop(P, "dve", lambda e: e.memset(lbv[:], 0.0), w=[blb])
            bop(P, "dve", lambda e: e.memset(oml[:], 1.0), w=[blb])
        else:
            bop(P, "dve", lambda e: e.tensor_tensor(out=lbv[:], in0=lb[:, 0, :], in1=lb[:, 1, :], op=ALU.subtract), r=[blb], w=[blb])
            bop(P, "act", lambda e: e.activation(out=oml[:], in_=lbv[:], func=AF.Exp), r=[blb], w=[blb])
            bop(P, "dve", lambda e: e.tensor_scalar(out=lbv[:], in0=oml[:], scalar1=1.0, scalar2=None, op0=ALU.add), r=[blb], w=[blb])
            bop(P, "dve", lambda e: e.reciprocal(out=lbv[:], in_=lbv[:]), r=[blb], w=[blb])
            bop(P, "dve", lambda e: e.tensor_tensor(out=oml[:], in0=oml[:], in1=lbv[:], op=ALU.mult), r=[blb], w=[blb])

        def load_w(h):
            sl = h % 2
            bop(P, "pool", lambda e, sl=sl, h=h: e.dma_start(out=wh[sl][:].rearrange("p k n -> p (k n)"), in_=hwin[h], max_dma_last_dim=8192),
                w=[bwh[sl]], slot=s_wh[sl])
            bop(P, "pool", lambda e, sl=sl, h=h: e.dma_start(out=wo[sl][:], in_=hwout[h], max_dma_last_dim=8192), w=[bwo[sl]], slot=s_wo[sl])

        load_w(0)
        ugl = 0
        jobs = [(h, ti) for h in range(HG_H) for ti in range(len(TT))]

        def head_setup(h):
            if h + 1 < HG_H:
                load_w(h + 1)
            bop(P, "sp", lambda e: e.dma_start(out=gn[:], in_=hnorm[h].partition_broadcast(128)), w=[bgn], slot=s_gn)
            bop(P, "pool", lambda e: e.memset(S[:], 0.0), w=[bS])
            bop(P, "sp", lambda e: e.dma_start(out=S0[:], in_=st_in[:, h, :, :].rearrange("i d e -> d i e")), w=[bS0], slot=s_S0)

        def make_s1(h, ti, kp):
            sl = h % 2
            t0, n = TT[ti]
            sample = (ti == 4)
            CL = 4 if sample else 32
            nch = n // CL
            CM = CH_CMS if sample else CH_CMP
            qcj, kcj, eBj = qc[kp], kc[kp], eB[kp]

            def p0():
                for qi in range(2):
                    for k in range(8):
                        bop(P, "pe", lambda e, qi=qi, k=k: e.matmul(B[qi][:, :n], lhsT=wh[sl][:, k, qi * 128:(qi + 1) * 128], rhs=C.xn[:, k, t0:t0 + n], start=(k == 0), stop=(k == 7)),
                            r=[bwh[sl]], w=[bB[qi]])

            def p1():
                sigmoid_act(F["qs"][:, :n], B[0][:, :n], bF["qs"], bB[0])
                bop(P, "act", lambda e: e.activation(out=F["ez"][:, :n], in_=B[1][:, :n], func=AF.Exp, scale=-1.0), r=[bB[1]], w=[bF["ez"]])
                bop(P, "act", lambda e: e.activation(out=F["r"][:, :n], in_=F["ez"][:, :n], func=AF.Ln, bias=C.one[:, 0:1]), r=[bF["ez"]], w=[bF["r"]])
                bop(P, "act", lambda e: e.activation(out=F["r"][:, :n], in_=F["r"][:, :n], func=AF.Exp, scale=-1.0), r=[bF["r"]], w=[bF["r"]])

            def p2():
                bop(P, "dve", lambda e: e.tensor_tensor(out=F["qs"][:, :n], in0=B[0][:, :n], in1=F["qs"][:, :n], op=ALU.mult), r=[bF["qs"], bB[0]], w=[bF["qs"]])
                bop(P, "dve", lambda e: e.tensor_scalar(out=F["f"][:, :n], in0=F["r"][:, :n], scalar1=oml[:, h:h + 1], scalar2=lbv[:, h:h + 1], op0=ALU.mult, op1=ALU.add),
                    r=[bF["r"], blb], w=[bF["f"]])
                bop(P, "dve", lambda e: e.scalar_tensor_tensor(out=F["kk"][:, :n], in0=F["ez"][:, :n], scalar=oml[:, h:h + 1], in1=F["r"][:, :n], op0=ALU.mult, op1=ALU.mult),
                    r=[bF["ez"], bF["r"], blb], w=[bF["kk"]])
                bop(P, "act", lambda e: e.activation(out=F["f"][:, :n], in_=F["f"][:, :n], func=AF.Ln), r=[bF["f"]], w=[bF["f"]])

            def p3():
                bop(P, "dve", lambda e: e.tensor_tensor_scan(out=F["b"][:, :n], data0=cret[:, CM:CM + n], data1=F["f"][:, :n], initial=0.0, op0=ALU.mult, op1=ALU.add),
                    r=[bF["f"]], w=[bF["b"]])
                bop(P, "pool", lambda e: e.tensor_tensor(
                    out=F["d1"][:, :n].rearrange("p (c s) -> p c s", s=CL), in0=F["b"][:, :n].rearrange("p (c s) -> p c s", s=CL),
                    in1=F["b"][:, :n].rearrange("p (c s) -> p c s", s=CL)[:, :, CL - 1:CL].broadcast_to([128, nch, CL]), op=ALU.subtract),
                    r=[bF["b"]], w=[bF["d1"]])

            def p4():
                bop(P, "act", lambda e: e.activation(out=F["X"][:, :n], in_=F["d1"][:, :n], func=AF.Exp), r=[bF["d1"]], w=[bF["X"]])
                bop(P, "act", lambda e: e.activation(out=F["Y"][:, :n], in_=F["d1"][:, :n], func=AF.Exp, scale=-1.0), r=[bF["d1"]], w=[bF["Y"]])
                bop(P, "act", lambda e: e.activation(out=eBj[:, :nch], in_=F["b"][:, :n].rearrange("p (c s) -> p c s", s=CL)[:, :, CL - 1], func=AF.Exp),
                    r=[bF["b"]], w=[beB[kp]])

            def p5():
                bop(P, "pool", lambda e: e.tensor_tensor(out=qcj[:, :n], in0=F["qs"][:, :n], in1=F["X"][:, :n], op=ALU.mult), r=[bF["qs"], bF["X"]], w=[bqc[kp]])
                bop(P, "pool", lambda e: e.tensor_tensor(out=kcj[:, :n], in0=F["kk"][:, :n], in1=F["Y"][:, :n], op=ALU.mult), r=[bF["kk"], bF["Y"]], w=[bkc[kp]])

            return [p0, p1, p2, p3, p4, p5]

        def make_blocks(h, ti, kp):
            sl = h % 2
            t0, n = TT[ti]
            sample = (ti == 4)
            CL = 4 if sample else 32
            qcj, kcj, eBj = qc[kp], kc[kp], eB[kp]
            blocks = [(0, 64)] if sample else [(c * 128, 128) for c in range(4)]
            MC = CH_MS if sample else CH_MP
            nbc = 16 if sample else 4
            if sample:
                bmask = C.bm
                rmv = cret[:64, CR_RM:CR_RM + 16]
            else:
                bmask = C.bmp
                rmv = cret[:, CH_RMP:CH_RMP + 4]
            ubase = {}

            def A_pe(ci):
                c0, nb = blocks[ci]
                pb = ci % 2
                a0 = t0 + c0
                for k in range(8):
                    bop(P, "pe", lambda e, k=k: e.matmul(VG[pb][:nb, :], lhsT=C.xn[:, k, a0:a0 + nb], rhs=wh[sl][:, k, 256:512], start=(k == 0), stop=(k == 7)),
                        r=[bwh[sl]], w=[bVG[pb]])
                bop(P, "pe", lambda e: e.transpose(out=PT[pb][:nb, :], in_=kcj[:, c0:c0 + nb], identity=C.ident), r=[bkc[kp]], w=[bPT[pb]])
                bop(P, "pe", lambda e: e.matmul(SC[pb][:nb, :nb], lhsT=kcj[:, c0:c0 + nb], rhs=qcj[:, c0:c0 + nb], start=True, stop=True), r=[bkc[kp], bqc[kp]], w=[bSC[pb]])

            def A_ev(ci):
                nonlocal ugl
                c0, nb = blocks[ci]
                pb = ci % 2
                bop(P, "dve", lambda e: e.tensor_copy(out=ktl[pb][:nb, :], in_=PT[pb][:nb, :]), r=[bPT[pb]], w=[bktl[pb]])
                bop(P, "act", lambda e: e.activation(out=vtok[pb][:nb, :], in_=VG[pb][:nb, 0:128], func=AF.Copy), r=[bVG[pb]], w=[bvtok[pb]])
                qzv = qz[pb][:, 0:nbc * nb].rearrange("p (c t) -> p c t", c=nbc)
                kzv = kz[pb][:nb, 0:nbc * 128].rearrange("p (c d) -> p c d", c=nbc)
                bop(P, "pool", lambda e: e.tensor_tensor(out=kzv, in0=ktl[pb][:nb, :].unsqueeze(1).broadcast_to([nb, nbc, 128]), in1=rmv.unsqueeze(2).broadcast_to([nb, nbc, 128]), op=ALU.mult),
                    r=[bktl[pb]], w=[bkz[pb]])
                bop(P, "pool", lambda e: e.tensor_tensor(out=qzv, in0=qcj[:, c0:c0 + nb].unsqueeze(1).broadcast_to([128, nbc, nb]), in1=bmask, op=ALU.mult),
                    r=[bqc[kp]], w=[bqz[pb]])
                bop(P, "dve", lambda e: e.tensor_tensor(out=scm[pb][:nb, :nb], in0=SC[pb][:nb, :nb], in1=cret[:nb, MC:MC + nb], op=ALU.mult), r=[bSC[pb]], w=[bscm[pb]])
                sigmoid_act(gsil[pb][:nb, :], VG[pb][:nb, 128:256], bgsil[pb], bVG[pb])
                bop(P, "dve", lambda e: e.tensor_tensor(out=gsil[pb][:nb, :], in0=VG[pb][:nb, 128:256], in1=gsil[pb][:nb, :], op=ALU.mult), r=[bgsil[pb], bVG[pb]], w=[bgsil[pb]])
                ubase[ci] = ugl
                if nbc <= 4:
                    for c in range(nbc):
                        u = 4 * pb + c
                        bop(P, "pe", lambda e, c=c, u=u: e.matmul(UR[u], lhsT=kzv[:, c, :], rhs=vtok[pb][:nb, :], start=True, stop=True),
                            r=[bkz[pb], bvtok[pb]], w=[bUR[u]])
                ugl += nbc

            def B_chain(ci):
                c0, nb = blocks[ci]
                pb = ci % 2
                kzv = kz[pb][:nb, 0:nbc * 128].rearrange("p (c d) -> p c d", c=nbc)
                for c in range(nbc):
                    ec = (c0 // CL + c)
                    u = (4 * pb + c) if nbc <= 4 else (4 * (c % 2) + (c // 2) % 4)
                    Sin = S0[:, c, :] if sample else S[:, :]
                    bSin = bS0 if sample else bS
                    if nbc > 4:
                        bop(P, "pe", lambda e, c=c, u=u: e.matmul(UR[u], lhsT=kzv[:, c, :], rhs=vtok[pb][:nb, :], start=True, stop=True),
                            r=[bkz[pb], bvtok[pb]], w=[bUR[u]])
                    bop(P, "dve", lambda e, c=c, ec=ec, Sin=Sin: e.tensor_scalar(out=Sdb[pb][:, c, :], in0=Sin, scalar1=eBj[:, ec:ec + 1], scalar2=None, op0=ALU.mult),
                        r=[bSin, beB[kp]], w=[bSdb[pb]])
                    bop(P, "dve", lambda e, ec=ec, u=u, Sin=Sin: e.scalar_tensor_tensor(out=Sin, in0=Sin, scalar=eBj[:, ec:ec + 1], in1=UR[u], op0=ALU.mult, op1=ALU.add),
                        r=[bUR[u], beB[kp]], w=[bSin])

            def B_pe(ci):
                c0, nb = blocks[ci]
                pb = ci % 2
                qzv = qz[pb][:, 0:nbc * nb].rearrange("p (c t) -> p c t", c=nbc)
                bop(P, "pe", lambda e: e.matmul(B[2][:nb, 0:128], lhsT=scm[pb][:nb, :nb], rhs=vtok[pb][:nb, :], start=True, stop=False), r=[bscm[pb], bvtok[pb]], w=[bB[2]])
                for c in range(nbc):
                    bop(P, "pe", lambda e, c=c: e.matmul(B[2][:nb, 0:128], lhsT=qzv[:, c, :], rhs=Sdb[pb][:, c, :], start=False, stop=(c == nbc - 1)),
                        r=[bqz[pb], bSdb[pb]], w=[bB[2]])

            def B_norm(ci):
                c0, nb = blocks[ci]
                pb = ci % 2
                bop(P, "act", lambda e: e.activation(out=on[:nb, :], in_=B[2][:nb, 0:128], func=AF.Square, accum_out=st4[:nb, 0:1]), r=[bB[2]], w=[bon, bst4])
                bop(P, "act", lambda e: e.activation(out=st4[:nb, 2:3], in_=st4[:nb, 0:1], func=AF.Ln, scale=1.0 / 128, bias=C.epsc[:nb, 0:1]), r=[bst4], w=[bst4])
                bop(P, "act", lambda e: e.activation(out=st4[:nb, 3:4], in_=st4[:nb, 2:3], func=AF.Exp, scale=-0.5), r=[bst4], w=[bst4])
                bop(P, "dve", lambda e: e.scalar_tensor_tensor(out=on[:nb, :], in0=B[2][:nb, 0:128], scalar=st4[:nb, 3:4], in1=gn[:nb, :], op0=ALU.mult, op1=ALU.mult),
                    r=[bB[2], bst4, bgn], w=[bon])
                bop(P, "pool", lambda e: e.tensor_tensor(out=go[pb][:nb, :], in0=on[:nb, :], in1=gsil[pb][:nb, :], op=ALU.mult), r=[bon, bgsil[pb]], w=[bgo[pb]])

            def stage_T(ci):
                c0, nb = blocks[ci]
                pb = ci % 2
                bop(P, "pe", lambda e: e.transpose(out=TR[:, :nb], in_=go[pb][:nb, :], identity=C.ident[:nb, :nb]), r=[bgo[pb]], w=[bTR])
                bop(P, "act", lambda e: e.activation(out=goT[:, c0:c0 + nb], in_=TR[:, :nb], func=AF.Copy), r=[bTR], w=[bgoT])

            def wout():
                for m in range(8):
                    wb = 3 if m % 2 == 0 else 2
                    bop(P, "pe", lambda e, m=m, wb=wb: e.matmul(B[wb][:, :n], lhsT=wo[sl][:, m * 128:(m + 1) * 128], rhs=goT[:, :n], start=True, stop=True),
                        r=[bwo[sl], bgoT], w=[bB[wb]])
                    bop(P, "dve", lambda e, m=m, wb=wb: e.tensor_tensor(out=C.xres[:, m, t0:t0 + n], in0=C.xres[:, m, t0:t0 + n], in1=B[wb][:, :n], op=ALU.add),
                        r=[bB[wb]], w=[bx[m][ti]])
                if ti == 3:
                    bop(P, "sp", lambda e: e.dma_start(out=st_out_p[h], in_=S[:, :]), r=[bS], slot=s_sp)
                if sample:
                    bop(P, "sp", lambda e: e.dma_start(out=st_out_s[:, h, :, :].rearrange("i d e -> d i e"), in_=S0[:, :, :]), r=[bS0], slot=s_so)

            mk = lambda fn, ci: (lambda: fn(ci))
            if len(blocks) == 1:
                sched = [mk(A_pe, 0), mk(A_ev, 0), mk(B_chain, 0), mk(B_pe, 0), mk(B_norm, 0), mk(stage_T, 0)]
                slots_after = {1: [0, 1], 3: [2, 3], 5: [4, 5]}
            else:
                sched = [mk(A_pe, 0), mk(A_ev, 0), mk(A_pe, 1),
                         mk(B_chain, 0), mk(B_pe, 0), mk(A_ev, 1), mk(B_norm, 0), mk(A_pe, 2),
                         mk(B_chain, 1), mk(B_pe, 1), mk(A_ev, 2), mk(B_norm, 1), mk(stage_T, 0), mk(A_pe, 3),
                         mk(B_chain, 2), mk(B_pe, 2), mk(A_ev, 3), mk(B_norm, 2), mk(stage_T, 1),
                         mk(B_chain, 3), mk(B_pe, 3), mk(B_norm, 3), mk(stage_T, 2), mk(stage_T, 3)]
                slots_after = {7: [0, 1], 13: [2], 18: [3, 4], 22: [5]}
            return sched, slots_after, wout

        for p in make_s1(jobs[0][0], jobs[0][1], 0):
            p()
        for kj, (h, ti) in enumerate(jobs):
            kp = kj % 2
            if ti == 0:
                head_setup(h)
            sched, slots_after, wout = make_blocks(h, ti, kp)
            nxt = make_s1(jobs[kj + 1][0], jobs[kj + 1][1], 1 - kp) if kj + 1 < len(jobs) else None
            for si, stage in enumerate(sched):
                stage()
                if nxt is not None:
                    for pi in slots_after.get(si, []):
                        nxt[pi]()
            wout()
        P.flush()


def build_program(cfg):
    nc = bass.Bass("TRN2", target_bir_lowering=False)
    dr = lambda name, shape, kind="ExternalInput", dt=F32: nc.dram_tensor(name, shape, dt, kind=kind).ap()
    xT = dr("xT", [128, 8, NTOK])
    gains_d = dr("gains", [128, 13 * 8])
    cbf_d = dr("cbf", [128, 256 + 1024 + 512], dt=BF16)
    cret_d = dr("cret", [128, CRW2])
    cs_d = dr("cs", [2, 128, NTOK])
    wup_d = dr("wup", [8, NF, 128, 2048])
    wdn_d = dr("wdn", [8, 2, 128, 11 * 1024])
    rwin_d = dr("rwin", [2, 4, 128, 12288])
    rwout_d = dr("rwout", [2, 4, 128, 4096])
    rnorm_d = dr("rnorm", [2, 4, 512])
    sret_d = dr("sret", [2, NSS, 4, 256, 512])
    hwin_d = dr("hwin", [2, 8, 128, 4096])
    hwout_d = dr("hwout", [2, 8, 128, 1024])
    hnorm_d = dr("hnorm", [2, 8, 128])
    lbl_d = dr("lbl", [128, 2, 8])
    shg_d = dr("shg", [2, NSS, 8, 128, 128])
    nhp_d = dr("nhp", [2, 8, 128, 128], kind="ExternalOutput")
    nhs_d = dr("nhs", [2, NSS, 8, 128, 128], kind="ExternalOutput")
    yT = dr("yT", [128, 8, NTOK], kind="ExternalOutput")
    nrp_d = dr("nrp", [2, 4, 256, 512], kind="ExternalOutput")
    nrs_d = dr("nrs", [2, NSS, 4, 256, 512], kind="ExternalOutput")

    with ExitStack() as st:
        P = Prog(nc, st)
        C = Ctx()
        C.P, C.nc = P, nc
        C.dbg = None
        if cfg.get("debug"):
            C.dbg = {"want": set(cfg["debug"]), "seen": {}, "off": {"f": 0, "b": 0}, "slot": P.slot(),
                     "f": dr("dbgf", [128, 8192], kind="ExternalOutput"), "b": dr("dbgb", [128, 8192], kind="ExternalOutput", dt=BF16)}
        cfg["_dbg"] = C.dbg
        sb = lambda name, shape, dt: st.enter_context(nc.sbuf_tensor(name, shape, dt))
        C.xres = sb("xres", [128, 8, NTOK], F32)
        C.xn = sb("xn", [128, 8, NTOK], BF16)
        C.gains = sb("gains_sb", [128, 13 * 8], F32)
        cbf = sb("cbf_sb", [128, 256 + 1024 + 512], BF16)
        C.ident = cbf[:, 0:128]
        C.ones = cbf[:, 128:256]
        C.bm = cbf[:, 256:1280].rearrange("p (a t) -> p a t", a=16)
        C.bmp = cbf[:, 1280:1792].rearrange("p (a t) -> p a t", a=4)
        C.epsc = sb("epsc", [128, 2], F32)
        C.one = sb("onec", [128, 2], F32)
        C.cret = sb("cret_sb", [128, CRW2], F32)

        s_in = P.slot()
        for k in range(8):
            P.dma("sp", lambda e, k=k: e.dma_start(out=C.xres[:, k, :], in_=xT[:, k, :]), s_in)
        P.dma("sp", lambda e: e.dma_start(out=C.gains[:], in_=gains_d), s_in)
        P.dma("sp", lambda e: e.dma_start(out=cbf[:], in_=cbf_d), s_in)
        P.dma("sp", lambda e: e.dma_start(out=C.cret[:], in_=cret_d), s_in)
        P.op("pool", lambda e: e.memset(C.epsc[:], EPS))
        P.op("pool", lambda e: e.memset(C.one[:], 1.0))
        P.flush()

        for blk in cfg["blocks"]:
            if blk[0] == "ffn":
                _, l, i = blk
                emit_ffn(C, l * 3 + (0 if i == 0 else 2), wup_d[l * 2 + i], wdn_d[l * 2 + i])
            elif blk[0] == "ret":
                _, l = blk
                j = l // 2
                emit_normphase(C, l * 3 + 1)
                emit_ret(C, j, rwin_d[j], rwout_d[j], rnorm_d[j], cs_d, sret_d[j], nrp_d[j], nrs_d[j])
            elif blk[0] == "hg":
                _, l = blk
                j = l // 2
                emit_normphase(C, l * 3 + 1)
                emit_hg(C, j, hwin_d[j], hwout_d[j], hnorm_d[j], lbl_d, shg_d[j], nhp_d[j], nhs_d[j])

        with ExitStack() as st2:
            sb2, pt2 = mk_alloc(C, st2)
            ph = Ctx()
            ph.sq = [sb2("sq%d" % i, [128, 8, 512], BF16) for i in range(2)]
            ph.rs = [sb2("rs%d" % i, [128, 512], F32) for i in range(2)]
            ph.psn = pt2("psn", [128, 512], F32)
            yo = [sb2("yo%d" % i, [128, 8, 512], F32) for i in range(2)]
            s_out = [P.slot(), P.slot()]
            yo_rd = [None, None]
            if cfg.get("final_norm", True):
                def out_fn2(ti, k, t0, n, rsb, r):
                    b = ti % 2
                    o = P.op("dve", lambda e: e.scalar_tensor_tensor(
                        out=yo[b][:, k, :n], in0=C.xres[:, k, t0:t0 + n], scalar=C.gains[:, 96 + k:96 + k + 1],
                        in1=rsb[:, :n], op0=ALU.mult, op1=ALU.mult), [r, yo_rd[b]])
                    if k == 7:
                        yo_rd[b] = P.dma("sp", lambda e: e.dma_start(out=yT[:, :, t0:t0 + n], in_=yo[b][:, :, :n]), s_out[b], [o])
                    return o
                emit_norm(C, ph, 12, out_fn2)
            else:
                for k in range(8):
                    P.dma("sp", lambda e, k=k: e.dma_start(out=yT[:, k, :], in_=C.xres[:, k, :]), s_out[0])
            P.flush()
    return nc


def host_consts():
    import ml_dtypes
    c = np.zeros((128, 256 + 1024 + 512), np.float32)
    c[:, 0:128] = np.eye(128, dtype=np.float32)
    c[:, 128:256] = 1.0
    bm = np.zeros((16, 64), np.float32)
    for i in range(16):
        bm[i, 4 * i:4 * i + 4] = 1.0
    c[:, 256:1280] = bm.reshape(1, 1024)
    bmp = np.zeros((4, 128), np.float32)
    for i in range(4):
        bmp[i, 32 * i:32 * i + 32] = 1.0
    c[:, 1280:1792] = bmp.reshape(1, 512)
    out = {"cbf": c.astype(ml_dtypes.bfloat16)}
    cr = np.zeros((128, CRW2), np.float64)
    t = np.arange(128)
    ts = np.arange(64)
    for h, g in enumerate(ret_gammas()):
        lg = np.log(np.float64(g))
        mp = np.where(t[:, None] <= t[None, :], np.exp(-(t[:, None] + 1.0) * lg), 0.0)
        cr[:, CR_MP + h * 128:CR_MP + (h + 1) * 128] = mp
        same = (ts[:, None] // 4) == (ts[None, :] // 4)
        ms = np.where(same & ((ts[:, None] % 4) <= (ts[None, :] % 4)), np.exp(-((ts[:, None] % 4) + 1.0) * lg), 0.0)
        cr[:64, CR_MS + h * 64:CR_MS + (h + 1) * 64] = ms
        cr[:, CR_KDP + h] = np.exp((127.0 - t) * lg)
        cr[:, CR_EPP + h] = EPS * np.exp(-2.0 * (t + 1.0) * lg)
        cr[:64, CR_KDS + h] = np.exp((3.0 - (ts % 4)) * lg)
        cr[:64, CR_EPS + h] = EPS * np.exp(-2.0 * ((ts % 4) + 1.0) * lg)
    for i in range(16):
        cr[4 * i:4 * i + 4, CR_RM + i] = 1.0
    cr[:, CH_MP:CH_MP + 128] = ((t[:, None] // 32) == (t[None, :] // 32)) & (t[:, None] <= t[None, :])
    cr[:64, CH_MS:CH_MS + 64] = ((ts[:, None] // 4) == (ts[None, :] // 4)) & (ts[:, None] <= ts[None, :])
    for i in range(4):
        cr[32 * i:32 * i + 32, CH_RMP + i] = 1.0
    cr[:, CH_CMP:CH_CMP + 512] = (np.arange(512) % 32 != 0)[None, :]
    cr[:, CH_CMS:CH_CMS + 64] = (np.arange(64) % 4 != 0)[None, :]
    out["cret"] = cr.astype(np.float32)
    half = 128
    inv_freq = (np.float32(10000.0) ** (-np.arange(half, dtype=np.float32) / np.float32(half))).astype(np.float32)
    pos = np.concatenate([np.arange(SEQ, dtype=np.float32), np.tile(np.float32(16384.0) + np.arange(DEC, dtype=np.float32), NSS)])
    ang = (pos[None, :] * inv_freq[:, None]).astype(np.float32)
    out["cs"] = np.stack([np.cos(ang), np.sin(ang)]).astype(np.float32)
    return out


def host_weights(inp):
    w = {}
    f32 = lambda a: np.asarray(a, np.float32)
    up = f32(inp["ffn_w_up"]).reshape(8, 8, 128, 2, NF, 128)
    w["wup"] = np.ascontiguousarray(up.transpose(0, 4, 2, 1, 3, 5)).reshape(8, NF, 128, 2048)
    dn = f32(inp["ffn_w_down"]).reshape(8, 2, 11, 128, 1024)
    w["wdn"] = np.ascontiguousarray(dn.transpose(0, 1, 3, 2, 4)).reshape(8, 2, 128, 11 * 1024)
    g = np.concatenate([f32(inp["norm_gain"]).reshape(12, 1024), f32(inp["final_norm"]).reshape(1, 1024)], 0)
    w["gains"] = np.ascontiguousarray(g.reshape(13, 8, 128).transpose(2, 0, 1)).reshape(128, 104)
    wi = f32(inp["ret_w_in"]).reshape(2, 8, 128, 6144)
    parts = []
    for h in range(4):
        parts.append(np.concatenate([wi[..., h * 256:(h + 1) * 256], wi[..., 1024 + h * 256:1024 + (h + 1) * 256],
                                     wi[..., 2048 + h * 512:2048 + (h + 1) * 512], wi[..., 4096 + h * 512:4096 + (h + 1) * 512]], -1))
    wih = np.stack(parts, 1)
    w["rwin"] = np.ascontiguousarray(wih.transpose(0, 1, 3, 2, 4)).reshape(2, 4, 128, 12288)
    wo = f32(inp["ret_w_out"]).reshape(2, 4, 4, 128, 1024)
    w["rwout"] = np.ascontiguousarray(wo.transpose(0, 1, 3, 2, 4)).reshape(2, 4, 128, 4096)
    w["rnorm"] = f32(inp["ret_norm"])
    hi = f32(inp["hg_w_in"]).reshape(2, 8, 128, 4, 8, 128)
    w["hwin"] = np.ascontiguousarray(hi.transpose(0, 4, 2, 1, 3, 5)).reshape(2, 8, 128, 4096)
    w["hwout"] = np.ascontiguousarray(f32(inp["hg_w_out"]).reshape(2, 8, 128, 1024))
    w["hnorm"] = f32(inp["hg_norm"])
    w["lbl"] = np.ascontiguousarray(f32(inp["hg_lb_logits"]).reshape(2, 8, 128).transpose(2, 0, 1))
    return w


def host_core_inputs(inp, c):
    xp = np.asarray(inp["x_prompt"], np.float32)[c]
    xs = np.asarray(inp["x_sample"], np.float32)[c * NSS:(c + 1) * NSS].reshape(NSS * DEC, D)
    x = np.concatenate([xp, xs], 0)
    xT = np.ascontiguousarray(x.T.reshape(8, 128, NTOK).transpose(1, 0, 2))
    m = {"xT": xT}
    m["sret"] = np.ascontiguousarray(np.asarray(inp["state_ret"], np.float32)[:, c * NSS:(c + 1) * NSS])
    m["shg"] = np.ascontiguousarray(np.asarray(inp["state_hgrn"], np.float32)[:, c * NSS:(c + 1) * NSS])
    return m


def full_cfg():
    blocks = []
    for l in range(4):
        blocks.append(("ffn", l, 0))
        blocks.append(("ret", l) if l % 2 == 0 else ("hg", l))
        blocks.append(("ffn", l, 1))
    return {"blocks": blocks, "final_norm": True}


def kernel(**inputs):
    nc = build_program(full_cfg())
    shared = {}
    shared.update(host_consts())
    shared.update(host_weights(inputs))
    in_maps = []
    for c in range(8):
        m = dict(shared)
        m.update(host_core_inputs(inputs, c))
        in_maps.append(m)
    res = run_bass_kernel_spmd(nc, in_maps, core_ids=list(range(8)))
    rs = res.results
    y_prompt = np.empty((8, SEQ, D), np.float32)
    y_sample = np.empty((8 * NSS, DEC, D), np.float32)
    nrp = np.empty((2, 8, 4, 256, 512), np.float32)
    nrs = np.empty((2, 8 * NSS, 4, 256, 512), np.float32)
    nhp = np.empty((2, 8, 8, 128, 128), np.float32)
    nhs = np.empty((2, 8 * NSS, 8, 128, 128), np.float32)
    for c in range(8):
        r = rs[c]
        y = np.asarray(r["yT"]).transpose(1, 0, 2).reshape(D, NTOK).T
        y_prompt[c] = y[:SEQ]
        y_sample[c * NSS:(c + 1) * NSS] = y[SEQ:].reshape(NSS, DEC, D)
        nrp[:, c] = r["nrp"]
        nrs[:, c * NSS:(c + 1) * NSS] = r["nrs"]
        nhp[:, c] = r["nhp"]
        nhs[:, c * NSS:(c + 1) * NSS] = r["nhs"]
    return (y_prompt, y_sample, nrp, nrs, nhp, nhs)
```

```python
import numpy as np
from contextlib import ExitStack
import concourse.bass as bass
import concourse.mybir as mybir
from concourse.bass_utils import run_bass_kernel_spmd

F32 = mybir.dt.float32
BF16 = mybir.dt.bfloat16
AF = mybir.ActivationFunctionType
ALU = mybir.AluOpType

D = 1024
SEQ = 2048
NSS = 16
DEC = 4
NTOK = SEQ + NSS * DEC
DFF = 2816
NF = DFF // 128
EPS = 1e-6
TT = [(0, 512), (512, 512), (1024, 512), (1536, 512), (2048, 64)]


class Op:
    __slots__ = ("eng", "fn", "deps", "pos", "sem", "val", "need_sig", "is_dma", "done")


class Slot:
    def __init__(self, sem):
        self.sem = sem
        self.count = 0


class Prog:
    ENGS = ["pe", "act", "dve", "pool", "sp"]
    ENGOBJ = {"pe": "tensor", "act": "scalar", "dve": "vector", "pool": "gpsimd", "sp": "sync"}

    def __init__(self, nc, stack):
        self.nc = nc
        self.stack = stack
        self.q = {e: [] for e in self.ENGS}
        self.esem = {e: stack.enter_context(nc.semaphore("s_" + e)) for e in ["pe", "act", "dve", "pool"]}
        self.ecount = {e: 0 for e in self.ENGS}
        self.slots = []
        self.nphase = 0

    def slot(self):
        s = Slot(self.stack.enter_context(self.nc.semaphore("d%d" % len(self.slots))))
        self.slots.append(s)
        return s

    def op(self, eng, fn, deps=()):
        o = Op()
        o.eng = eng
        o.fn = fn
        o.deps = [d for d in deps if d is not None]
        o.pos = len(self.q[eng])
        o.is_dma = False
        o.need_sig = False
        o.sem = None
        o.val = None
        o.done = False
        self.q[eng].append(o)
        return o

    def dma(self, eng, fn, slot, deps=()):
        o = self.op(eng, fn, deps)
        o.is_dma = True
        slot.count += 16
        o.sem = slot.sem
        o.val = slot.count
        return o

    def _needs_wait(self, o, d):
        if d.done:
            return False
        if d.is_dma:
            return True
        if d.eng == o.eng:
            if o.eng == "pe":
                return False
            return (o.pos - d.pos) <= 2
        return True

    def flush(self):
        nc = self.nc
        drain_deps = []
        for e in self.ENGS:
            last = {}
            for o in self.q[e]:
                if o.is_dma:
                    last[id(o.sem)] = o
            drain_deps += list(last.values())
        self.op("sp", lambda e: e.nop(), drain_deps)
        for e in self.ENGS:
            for o in self.q[e]:
                for d in o.deps:
                    if not d.is_dma and self._needs_wait(o, d):
                        d.need_sig = True
        for e in self.ENGS:
            c = self.ecount[e]
            for o in self.q[e]:
                if o.is_dma:
                    continue
                if o.need_sig:
                    assert e != "sp"
                    c += 1
                    o.sem = self.esem[e]
                    o.val = c
            self.ecount[e] = c
        self.nphase += 1
        with nc.Block() as block:
            for e in self.ENGS:
                ops = self.q[e]
                if not ops:
                    continue

                def body(eng, ops=ops):
                    waited = {}
                    for o in ops:
                        need = {}
                        for d in o.deps:
                            if not self._needs_wait(o, d):
                                continue
                            key = id(d.sem)
                            if key not in need or need[key][1] < d.val:
                                need[key] = (d.sem, d.val)
                        for key, (sem, val) in need.items():
                            if waited.get(key, 0) >= val:
                                continue
                            eng.wait_ge(sem, val)
                            waited[key] = val
                        ins = o.fn(eng)
                        if o.is_dma:
                            ins.then_inc(o.sem, 16)
                        elif o.need_sig:
                            ins.then_inc(o.sem, 1)

                getattr(block, self.ENGOBJ[e])(body)
        for e in self.ENGS:
            for o in self.q[e]:
                o.done = True
                o.fn = None
                o.deps = None
            self.q[e] = []


class Ctx:
    pass


def dbg_dump(C, name, ap, bufs, ncols, bf=False):
    if not getattr(C, "dbg", None) or name in C.dbg["seen"] or name not in C.dbg["want"]:
        return
    key = "b" if bf else "f"
    off = C.dbg["off"][key]
    C.dbg["off"][key] = off + ncols
    C.dbg["seen"][name] = (key, off, ncols)
    dst = C.dbg[key][:, off:off + ncols]
    bop(C.P, "sp", lambda e: e.dma_start(out=dst, in_=ap), r=bufs, slot=C.dbg["slot"])


class Buf:
    def __init__(self):
        self.w = None
        self.r = []


class PBuf(Buf):
    excl = True


def bop(P, eng, fn, r=(), w=(), slot=None, deps=()):
    xr = [b for b in r if getattr(b, "excl", False)]
    if xr:
        r = [b for b in r if not getattr(b, "excl", False)]
        w = list(w) + [b for b in xr if b not in w]
    d = list(deps)
    for b in r:
        d.append(b.w)
    for b in w:
        d.append(b.w)
        d.extend(b.r)
    o = P.dma(eng, fn, slot, d) if slot is not None else P.op(eng, fn, d)
    for b in r:
        if not o.is_dma:
            b.r = [x for x in b.r if x.is_dma or x.eng != eng]
        b.r.append(o)
    for b in w:
        b.w = o
        b.r = []
    return o


_UID = [0]


def mk_alloc(C, st):
    _UID[0] += 1
    u = _UID[0]
    nc = C.nc
    sb = lambda name, shape, dt: st.enter_context(nc.sbuf_tensor("%s_%d" % (name, u), shape, dt))
    pt = lambda name, shape, dt: st.enter_context(nc.psum_tensor("%s_%d" % (name, u), shape, dt))
    return sb, pt


def emit_norm(C, ph, gcol, out_fn=None):
    P = C.P
    sq, rs, psn = ph.sq, ph.rs, ph.psn
    last = []
    sq_rd = [None, None]
    rs_rd = [None, None]
    ps_rd = None
    for ti, (t0, n) in enumerate(TT):
        b = ti % 2
        a = P.op("act", lambda e, b=b, t0=t0, n=n: e.activation(out=sq[b][:, :, :n], in_=C.xres[:, :, t0:t0 + n], func=AF.Square),
                 [sq_rd[b]])
        mm = None
        for k in range(8):
            mm = P.op("pe", lambda e, b=b, k=k, n=n: e.matmul(psn[:, :n], lhsT=C.ones[:], rhs=sq[b][:, k, :n], start=(k == 0), stop=(k == 7)),
                      [a, ps_rd])
        sq_rd[b] = mm
        v = P.op("act", lambda e, b=b, n=n: e.activation(out=rs[b][:, :n], in_=psn[:, :n], func=AF.Ln, scale=1.0 / D, bias=C.epsc[:, 0:1]),
                 [mm, rs_rd[b]])
        ps_rd = v
        r = P.op("act", lambda e, b=b, n=n: e.activation(out=rs[b][:, :n], in_=rs[b][:, :n], func=AF.Exp, scale=-0.5), [v])
        o = None
        for k in range(8):
            if out_fn is None:
                o = P.op("dve", lambda e, b=b, k=k, t0=t0, n=n: e.scalar_tensor_tensor(
                    out=C.xn[:, k, t0:t0 + n], in0=C.xres[:, k, t0:t0 + n], scalar=C.gains[:, gcol * 8 + k:gcol * 8 + k + 1],
                    in1=rs[b][:, :n], op0=ALU.mult, op1=ALU.mult), [r])
            else:
                o = out_fn(ti, k, t0, n, rs[b], r)
        rs_rd[b] = o
        last.append(o)
    return last


def emit_normphase(C, gcol):
    with ExitStack() as st:
        sb, pt = mk_alloc(C, st)
        ph = Ctx()
        ph.sq = [sb("sq%d" % i, [128, 8, 512], BF16) for i in range(2)]
        ph.rs = [sb("rs%d" % i, [128, 512], F32) for i in range(2)]
        ph.psn = pt("psn", [128, 512], F32)
        emit_norm(C, ph, gcol)
        C.P.flush()


def emit_ffn(C, gcol, wup, wdn):
    P, nc = C.P, C.nc
    with ExitStack() as st:
        sb, pt = mk_alloc(C, st)
        hid = sb("hid", [128, 11, NTOK], BF16)
        wu = [sb("wu%d" % i, [128, 8, 2, 128], BF16) for i in range(3)]
        wd = sb("wd", [128, 11, 1024], BF16)
        sa = [sb("sa%d" % i, [128, 512], F32) for i in range(2)]
        sqs = [sb("sqs%d" % i, [128, 512], BF16) for i in range(4)]
        rs = [sb("rs%d" % i, [128, 512], F32) for i in range(2)]
        psn = pt("psn", [128, 512], F32)
        psA = [pt("psA%d" % i, [128, 512], F32) for i in range(2)]
        psB = [pt("psB%d" % i, [128, 512], F32) for i in range(2)]
        psD = [pt("psD%d" % i, [128, 512], F32) for i in range(2)]
        bwu = [Buf() for _ in range(3)]; bwd = Buf(); bsa = [Buf(), Buf()]; bsq = [Buf() for _ in range(4)]; brs = [Buf(), Buf()]
        bpsn = PBuf(); bpsA = [PBuf(), PBuf()]; bpsB = [PBuf(), PBuf()]; bpsD = [PBuf(), PBuf()]
        bxn = [Buf() for _ in TT]; bxr = [Buf() for _ in TT]; bhid = [Buf() for _ in TT]
        s_wu = [P.slot() for _ in range(3)]
        s_wd = P.slot()

        def load_wu(fi):
            s = fi % 3
            bop(P, "pool", lambda e: e.dma_start(out=wu[s][:].rearrange("p k a c -> p (k a c)"), in_=wup[fi], max_dma_last_dim=8192), w=[bwu[s]], slot=s_wu[s])

        def load_wd(half):
            bop(P, "pool", lambda e: e.dma_start(out=wd[:].rearrange("p f n -> p (f n)"), in_=wdn[half], max_dma_last_dim=8192), w=[bwd], slot=s_wd)

        sqc = [0]

        def norm(ti):
            t0, n = TT[ti]
            b = ti % 2
            for k in range(8):
                q = sqc[0] % 4
                sqc[0] += 1
                bop(P, "act", lambda e, k=k, q=q: e.activation(out=sqs[q][:, :n], in_=C.xres[:, k, t0:t0 + n], func=AF.Square), r=[bxr[ti]], w=[bsq[q]])
                bop(P, "pe", lambda e, k=k, q=q: e.matmul(psn[:, :n], lhsT=C.ones[:], rhs=sqs[q][:, :n], start=(k == 0), stop=(k == 7)), r=[bsq[q]], w=[bpsn])
            bop(P, "act", lambda e: e.activation(out=rs[b][:, :n], in_=psn[:, :n], func=AF.Ln, scale=1.0 / D, bias=C.epsc[:, 0:1]), r=[bpsn], w=[brs[b]])
            bop(P, "act", lambda e: e.activation(out=rs[b][:, :n], in_=rs[b][:, :n], func=AF.Exp, scale=-0.5), r=[brs[b]], w=[brs[b]])
            for k in range(8):
                bop(P, "dve", lambda e, k=k: e.scalar_tensor_tensor(
                    out=C.xn[:, k, t0:t0 + n], in0=C.xres[:, k, t0:t0 + n], scalar=C.gains[:, gcol * 8 + k:gcol * 8 + k + 1],
                    in1=rs[b][:, :n], op0=ALU.mult, op1=ALU.mult), r=[brs[b], bxr[ti]], w=[bxn[ti]])

        load_wd(0)
        for fi in range(3):
            load_wu(fi)
        norm(0)
        norm(1)
        cnt = 0
        dcnt = 0
        for half in range(2):
            if half == 1:
                load_wd(1)
            for f in range(11):
                fi = half * 11 + f
                s = fi % 3
                for ti, (t0, n) in enumerate(TT):
                    b = cnt % 2
                    cnt += 1
                    for k in range(8):
                        bop(P, "pe", lambda e, b=b, s=s, k=k, t0=t0, n=n: e.matmul(psA[b][:, :n], lhsT=wu[s][:, k, 0, :], rhs=C.xn[:, k, t0:t0 + n], start=(k == 0), stop=(k == 7)),
                            r=[bwu[s], bxn[ti]], w=[bpsA[b]])
                    for k in range(8):
                        bop(P, "pe", lambda e, b=b, s=s, k=k, t0=t0, n=n: e.matmul(psB[b][:, :n], lhsT=wu[s][:, k, 1, :], rhs=C.xn[:, k, t0:t0 + n], start=(k == 0), stop=(k == 7)),
                            r=[bwu[s], bxn[ti]], w=[bpsB[b]])
                    bop(P, "act", lambda e, b=b, n=n: e.activation(out=sa[b][:, :n], in_=psA[b][:, :n], func=AF.Silu), r=[bpsA[b]], w=[bsa[b]])
                    bop(P, "dve", lambda e, b=b, f=f, t0=t0, n=n: e.tensor_tensor(out=hid[:, f, t0:t0 + n], in0=sa[b][:, :n], in1=psB[b][:, :n], op=ALU.mult),
                        r=[bsa[b], bpsB[b]], w=[bhid[ti]])
                    if fi == 0 and ti + 2 < len(TT):
                        norm(ti + 2)
                if fi + 3 < NF:
                    load_wu(fi + 3)
            for ti, (t0, n) in enumerate(TT):
                for mo in range(8):
                    b = dcnt % 2
                    dcnt += 1
                    for f in range(11):
                        bop(P, "pe", lambda e, b=b, f=f, mo=mo, t0=t0, n=n: e.matmul(psD[b][:, :n], lhsT=wd[:, f, mo * 128:(mo + 1) * 128], rhs=hid[:, f, t0:t0 + n], start=(f == 0), stop=(f == 10)),
                            r=[bwd, bhid[ti]], w=[bpsD[b]])
                    bop(P, "dve", lambda e, b=b, mo=mo, t0=t0, n=n: e.scalar_tensor_tensor(
                        out=C.xres[:, mo, t0:t0 + n], in0=psD[b][:, :n], scalar=0.5, in1=C.xres[:, mo, t0:t0 + n], op0=ALU.mult, op1=ALU.add),
                        r=[bpsD[b]], w=[bxr[ti]])
        P.flush()


RET_H = 4
CR_MP = 0
CR_MS = 512
CR_KDP = 768
CR_EPP = 772
CR_KDS = 776
CR_EPS = 780
CR_RM = 784
CRW = 800


def ret_gammas():
    return [1.0 - 2.0 ** (-5.0 - h) for h in range(RET_H)]


def emit_ret(C, j, rwin, rwout, rnorm, cs, st_in, st_out_p, st_out_s):
    P, nc = C.P, C.nc
    gam = ret_gammas()
    with ExitStack() as st:
        sb, pt = mk_alloc(C, st)
        wh = [sb("wh%d" % i, [128, 8, 1536], BF16) for i in range(2)]
        wo = sb("wo", [128, 4, 1024], BF16)
        cst = sb("cs", [128, 2, 512], F32)
        qT = sb("qT", [128, 2, 512], BF16)
        kT = sb("kT", [128, 2, 512], BF16)
        vtok = [sb("vtok%d" % i, [128, 512], BF16) for i in range(2)]
        ktl = [sb("ktl%d" % i, [128, 256], BF16) for i in range(2)]
        scm = [sb("scm%d" % i, [128, 128], BF16) for i in range(2)]
        gs = sb("gs", [128, 512], F32)
        on = sb("on", [128, 512], F32)
        go = [sb("go%d" % i, [128, 512], BF16) for i in range(2)]
        goT = sb("goT", [128, 4, 512], BF16)
        S = sb("S", [128, 2, 512], F32)
        Sb = sb("Sb", [128, 2, 512], BF16)
        gn = sb("gn", [128, 512], F32)
        qz = sb("qz", [128, 2, 16, 64], BF16)
        kz = [sb("kz%d" % i, [64, 256], BF16) for i in range(2)]
        S0f = [sb("S0f%d" % i, [128, 512], F32) for i in range(2)]
        S0b = [sb("S0b%d" % i, [128, 512], BF16) for i in range(2)]
        S0f = [t[:, :] for t in S0f] + [S[:, 0, :], S[:, 1, :]]
        S0b = [t[:, :] for t in S0b] + [Sb[:, 0, :], Sb[:, 1, :]]
        st4 = sb("st4", [128, 4], F32)
        B = [pt("b%d" % i, [128, 512], F32) for i in range(8)]
        PT = [B[6][:, 0:128].bitcast(BF16), B[3][:, 0:128].bitcast(BF16)]
        SC = [B[6][:, 128:256], B[3][:, 128:256]]
        TR = B[3][:, 256:512].bitcast(BF16)
        cret = C.cret

        bwh = [Buf(), Buf()]; bwo = Buf(); bcs = Buf(); bt12 = Buf(); bqT = Buf(); bkT = Buf()
        bvtok = [Buf(), Buf()]; bktl = [Buf(), Buf()]; bscm = [Buf(), Buf()]; bgs = Buf(); bon = Buf(); bgo = [Buf(), Buf()]; bgoT = Buf()
        bSh = [Buf(), Buf()]; bSbh = [Buf(), Buf()]; bgn = Buf(); bqz = Buf(); bkz = [Buf(), Buf()]
        bS0f = [Buf(), Buf()] + bSh; bS0b = [Buf(), Buf()] + bSbh; bst4 = Buf()
        bB = [PBuf() for _ in range(8)]
        bB7t = bB[3]
        bPT = [bB[6], bB[3]]; bSC = [bB[6], bB[3]]
        bx = [[Buf() for _ in TT] for _ in range(8)]
        s_wh = [P.slot(), P.slot()]; s_wo = P.slot(); s_cs = P.slot(); s_gn = P.slot()
        s_S0f = [P.slot() for _ in range(4)]; s_S0b = [P.slot() for _ in range(4)]; s_so = [P.slot() for _ in range(4)]; s_sp = P.slot()

        def load_wh(h):
            sl = h % 2
            bop(P, "pool", lambda e, sl=sl: e.dma_start(out=wh[sl][:].rearrange("p k n -> p (k n)"), in_=rwin[h], max_dma_last_dim=8192),
                w=[bwh[sl]], slot=s_wh[sl])

        load_wh(0)
        dbg_dump(C, "xn0", C.xn[:, 0, 0:512], [], 512, bf=True)
        dbg_dump(C, "wh0", wh[0][:, 0, 0:512], [bwh[0]], 512, bf=True)
        ucnt = 0
        for h in range(RET_H):
            sl = h % 2
            g = gam[h]
            bop(P, "pool", lambda e, h=h: e.dma_start(out=wo[:].rearrange("p a n -> p (a n)"), in_=rwout[h], max_dma_last_dim=8192),
                w=[bwo], slot=s_wo)
            if h + 1 < RET_H:
                load_wh(h + 1)
            bop(P, "sp", lambda e, h=h: e.dma_start(out=gn[:], in_=rnorm[h].partition_broadcast(128)), w=[bgn], slot=s_gn)
            bop(P, "pool", lambda e: e.memset(S[:], 0.0), w=[bSh[0], bSh[1]])
            bop(P, "pool", lambda e: e.memset(Sb[:], 0.0), w=[bSbh[0], bSbh[1]])
            for ti, (t0, n) in enumerate(TT):
                sample = (ti == 4)
                bop(P, "sp", lambda e, t0=t0, n=n: e.dma_start(out=cst[:, :, :n], in_=cs[:, :, t0:t0 + n].rearrange("a p t -> p a t")),
                    w=[bcs], slot=s_cs)
                for qi in range(4):
                    for k in range(8):
                        bop(P, "pe", lambda e, sl=sl, qi=qi, k=k, t0=t0, n=n: e.matmul(B[qi][:, :n], lhsT=wh[sl][:, k, qi * 128:(qi + 1) * 128], rhs=C.xn[:, k, t0:t0 + n], start=(k == 0), stop=(k == 7)),
                            r=[bwh[sl]], w=[bB[qi]])
                for (dst, bd, b0, b1, sc) in ((qT, bqT, 0, 1, 1.0), (kT, bkT, 2, 3, 0.0625)):
                    for half in range(2):
                        ca, cb = (0, 1) if half == 0 else (1, 0)
                        bop(P, "dve", lambda e, b0=b0, ca=ca, sc=sc, n=n: e.scalar_tensor_tensor(out=gs[:, :n], in0=B[b0][:, :n], scalar=sc, in1=cst[:, ca, :n], op0=ALU.mult, op1=ALU.mult),
                            r=[bB[b0], bcs], w=[bgs])
                        bop(P, "dve", lambda e, b1=b1, cb=cb, sc=sc, n=n: e.scalar_tensor_tensor(out=on[:, :n], in0=B[b1][:, :n], scalar=sc, in1=cst[:, cb, :n], op0=ALU.mult, op1=ALU.mult),
                            r=[bB[b1], bcs], w=[bon])
                        bop(P, "dve", lambda e, dst=dst, half=half, n=n: e.tensor_tensor(out=dst[:, half, :n], in0=gs[:, :n], in1=on[:, :n], op=(ALU.subtract if half == 0 else ALU.add)),
                            r=[bgs, bon], w=[bd])
                blocks = [(0, 64)] if sample else [(c * 128, 128) for c in range(n // 128)]
                MC = (CR_MS + h * 64) if sample else (CR_MP + h * 128)
                KD = (CR_KDS if sample else CR_KDP) + h
                EP = (CR_EPS if sample else CR_EPP) + h

                def stage_A(ci, c0, nb, sl=sl, t0=t0, MC=MC, KD=KD):
                    pb = ci % 2
                    a0 = t0 + c0
                    vb = 4 + pb
                    for k in range(8):
                        bop(P, "pe", lambda e, k=k: e.matmul(B[vb][:nb, :], lhsT=C.xn[:, k, a0:a0 + nb], rhs=wh[sl][:, k, 512:1024], start=(k == 0), stop=(k == 7)),
                            r=[bwh[sl]], w=[bB[vb]])
                    bop(P, "act", lambda e: e.activation(out=vtok[pb][:nb, :], in_=B[vb][:nb, :], func=AF.Copy), r=[bB[vb]], w=[bvtok[pb]])
                    for jj in range(2):
                        bop(P, "pe", lambda e, jj=jj: e.transpose(out=PT[pb][:nb, jj * 128:(jj + 1) * 128], in_=kT[:, jj, c0:c0 + nb], identity=C.ident),
                            r=[bkT], w=[bPT[pb]])
                    for jj in range(2):
                        bop(P, "pe", lambda e, jj=jj: e.matmul(SC[pb][:nb, :nb], lhsT=kT[:, jj, c0:c0 + nb], rhs=qT[:, jj, c0:c0 + nb], start=(jj == 0), stop=(jj == 1)),
                            r=[bkT, bqT], w=[bSC[pb]])
                    bop(P, "dve", lambda e: e.tensor_scalar(out=ktl[pb][:nb, :], in0=PT[pb][:nb, :], scalar1=cret[:nb, KD:KD + 1], scalar2=None, op0=ALU.mult),
                        r=[bPT[pb]], w=[bktl[pb]])
                    bop(P, "dve", lambda e: e.tensor_tensor(out=scm[pb][:nb, :nb], in0=SC[pb][:nb, :nb], in1=cret[:nb, MC:MC + nb], op=ALU.mult),
                        r=[bSC[pb]], w=[bscm[pb]])

                def stage_B(ci, c0, nb, sl=sl, t0=t0, EP=EP, sample=sample, g=g, h=h):
                    nonlocal ucnt
                    pb = ci % 2
                    a0 = t0 + c0
                    bop(P, "pe", lambda e: e.matmul(B[7][:nb, :], lhsT=scm[pb][:nb, :nb], rhs=vtok[pb][:nb, :], start=True, stop=False),
                        r=[bscm[pb], bvtok[pb]], w=[bB[7]])
                    if not sample:
                        for jj in range(2):
                            bop(P, "pe", lambda e, jj=jj: e.matmul(B[7][:nb, :], lhsT=qT[:, jj, c0:c0 + nb], rhs=Sb[:, jj, :], start=False, stop=(jj == 1)),
                                r=[bqT, bSbh[jj]], w=[bB[7]])
                        for jj in range(2):
                            bop(P, "pe", lambda e, jj=jj: e.matmul(B[jj][:, :], lhsT=ktl[pb][:nb, jj * 128:(jj + 1) * 128], rhs=vtok[pb][:nb, :], start=True, stop=True),
                                r=[bktl[pb], bvtok[pb]], w=[bB[jj]])
                        cd = g ** 128
                        for jj in range(2):
                            bop(P, "dve", lambda e, jj=jj: e.scalar_tensor_tensor(out=S[:, jj, :], in0=S[:, jj, :], scalar=cd, in1=B[jj][:, :], op0=ALU.mult, op1=ALU.add),
                                r=[bB[jj]], w=[bSh[jj]])
                        for jj in range(2):
                            bop(P, "pool", lambda e, jj=jj: e.tensor_copy(out=Sb[:, jj, :], in_=S[:, jj, :]), r=[bSh[jj]], w=[bSbh[jj]])
                    else:
                        for jj in range(2):
                            bop(P, "dve", lambda e, jj=jj: e.tensor_tensor(out=qz[:, jj, :, :], in0=qT[:, jj, 0:64].unsqueeze(1).broadcast_to([128, 16, 64]), in1=C.bm[:, :, :], op=ALU.mult),
                                r=[bqT], w=[bqz])
                        cd = g ** 4
                        units = [(i, jj) for i in range(NSS) for jj in range(2)]

                        def issue_loads(k):
                            i, jj = units[k]
                            u = k % 4
                            bop(P, "pool", lambda e: e.dma_start(out=S0b[u], in_=st_in[i, h, jj * 128:(jj + 1) * 128, :]), w=[bS0b[u]], slot=s_S0b[u])
                            bop(P, "sp", lambda e: e.dma_start(out=S0f[u], in_=st_in[i, h, jj * 128:(jj + 1) * 128, :]), w=[bS0f[u]], slot=s_S0f[u])

                        issue_loads(0)
                        issue_loads(1)
                        for k, (i, jj) in enumerate(units):
                            if k + 2 < len(units):
                                issue_loads(k + 2)
                            kb = i % 2
                            u = k % 4
                            if jj == 0:
                                bop(P, "dve", lambda e, i=i, kb=kb: e.tensor_scalar(out=kz[kb][:, :], in0=ktl[pb][:64, :], scalar1=cret[:64, CR_RM + i:CR_RM + i + 1], scalar2=None, op0=ALU.mult),
                                    r=[bktl[pb]], w=[bkz[kb]])
                            last = (k == len(units) - 1)
                            bop(P, "pe", lambda e, u=u, i=i, jj=jj, last=last: e.matmul(B[7][:64, :], lhsT=qz[:, jj, i, :], rhs=S0b[u], start=False, stop=last),
                                r=[bqz, bS0b[u]], w=[bB[7]])
                            bop(P, "pe", lambda e, kb=kb, jj=jj: e.matmul(B[jj][:, :], lhsT=kz[kb][:, jj * 128:(jj + 1) * 128], rhs=vtok[pb][:64, :], start=True, stop=True),
                                r=[bkz[kb], bvtok[pb]], w=[bB[jj]])
                            bop(P, "dve", lambda e, u=u, jj=jj: e.scalar_tensor_tensor(out=S0f[u], in0=S0f[u], scalar=cd, in1=B[jj][:, :], op0=ALU.mult, op1=ALU.add),
                                r=[bB[jj]], w=[bS0f[u]])
                            bop(P, "sp", lambda e, u=u, i=i, jj=jj: e.dma_start(out=st_out_s[i, h, jj * 128:(jj + 1) * 128, :], in_=S0f[u]), r=[bS0f[u]], slot=s_so[u])
                    bop(P, "act", lambda e: e.activation(out=on[:nb, :], in_=B[7][:nb, :], func=AF.Square, accum_out=st4[:nb, 0:1]),
                        r=[bB[7]], w=[bon, bst4])
                    bop(P, "act", lambda e: e.activation(out=st4[:nb, 2:3], in_=st4[:nb, 0:1], func=AF.Ln, scale=1.0 / 512, bias=cret[:nb, EP:EP + 1]), r=[bst4], w=[bst4])
                    bop(P, "act", lambda e: e.activation(out=st4[:nb, 3:4], in_=st4[:nb, 2:3], func=AF.Exp, scale=-0.5), r=[bst4], w=[bst4])
                    bop(P, "dve", lambda e: e.scalar_tensor_tensor(out=on[:nb, :], in0=B[7][:nb, :], scalar=st4[:nb, 3:4], in1=gn[:nb, :], op0=ALU.mult, op1=ALU.mult),
                        r=[bB[7], bst4, bgn], w=[bon])
                    for k in range(8):
                        bop(P, "pe", lambda e, k=k: e.matmul(B[2][:nb, :], lhsT=C.xn[:, k, a0:a0 + nb], rhs=wh[sl][:, k, 1024:1536], start=(k == 0), stop=(k == 7)),
                            r=[bwh[sl]], w=[bB[2]])
                    bop(P, "act", lambda e: e.activation(out=gs[:nb, :], in_=B[2][:nb, :], func=AF.Exp, scale=-1.0), r=[bB[2]], w=[bgs])
                    bop(P, "act", lambda e: e.activation(out=gs[:nb, :], in_=gs[:nb, :], func=AF.Ln, bias=C.one[:nb, 0:1]), r=[bgs], w=[bgs])
                    bop(P, "act", lambda e: e.activation(out=gs[:nb, :], in_=gs[:nb, :], func=AF.Exp, scale=-1.0), r=[bgs], w=[bgs])
                    bop(P, "dve", lambda e: e.tensor_tensor(out=gs[:nb, :], in0=B[2][:nb, :], in1=gs[:nb, :], op=ALU.mult), r=[bgs, bB[2]], w=[bgs])
                    bop(P, "pool", lambda e: e.tensor_tensor(out=go[pb][:nb, :], in0=on[:nb, :], in1=gs[:nb, :], op=ALU.mult), r=[bon, bgs], w=[bgo[pb]])

                def stage_T(ci, c0, nb):
                    pb = ci % 2
                    for e4 in range(4):
                        bop(P, "pe", lambda e, e4=e4: e.transpose(out=TR[:, e4 * 128:e4 * 128 + nb], in_=go[pb][:nb, e4 * 128:(e4 + 1) * 128], identity=C.ident[:nb, :nb]),
                            r=[bgo[pb]], w=[bB7t])
                    bop(P, "act", lambda e: e.activation(out=goT[:, :, c0:c0 + nb], in_=TR.rearrange("p (a t) -> p a t", a=4)[:, :, :nb], func=AF.Copy),
                        r=[bB7t], w=[bgoT])

                nblk = len(blocks)
                sched = []
                if nblk == 1:
                    sched = [("A", 0), ("B", 0), ("T", 0)]
                else:
                    sched = [("A", 0), ("A", 1), ("B", 0), ("A", 2), ("B", 1), ("T", 0), ("A", 3), ("B", 2), ("T", 1), ("B", 3), ("T", 2), ("T", 3)]
                for (kind, ci) in sched:
                    c0, nb = blocks[ci]
                    if kind == "A":
                        stage_A(ci, c0, nb)
                    elif kind == "B":
                        stage_B(ci, c0, nb)
                    else:
                        stage_T(ci, c0, nb)
                for m in range(8):
                    wb = 4 + (m % 2)
                    for e4 in range(4):
                        bop(P, "pe", lambda e, m=m, e4=e4, n=n, wb=wb: e.matmul(B[wb][:, :n], lhsT=wo[:, e4, m * 128:(m + 1) * 128], rhs=goT[:, e4, :n], start=(e4 == 0), stop=(e4 == 3)),
                            r=[bwo, bgoT], w=[bB[wb]])
                    bop(P, "dve", lambda e, m=m, t0=t0, n=n, wb=wb: e.tensor_tensor(out=C.xres[:, m, t0:t0 + n], in0=C.xres[:, m, t0:t0 + n], in1=B[wb][:, :n], op=ALU.add),
                        r=[bB[wb]], w=[bx[m][ti]])
                if ti == 3:
                    bop(P, "sp", lambda e, h=h: e.dma_start(out=st_out_p[h].rearrange("(a p) n -> p a n", p=128), in_=S[:, :, :]), r=[bSh[0], bSh[1]], slot=s_sp)
        P.flush()


HG_H = 8
CH_MP = 800
CH_MS = 928
CH_RMP = 992
CH_CMP = 1000
CH_CMS = 1512
CRW2 = 1576


def emit_hg(C, j, hwin, hwout, hnorm, lbl, st_in, st_out_p, st_out_s):
    P, nc = C.P, C.nc
    with ExitStack() as st:
        sb, pt = mk_alloc(C, st)
        wh = [sb("hwh%d" % i, [128, 8, 512], BF16) for i in range(2)]
        wo = [sb("hwo%d" % i, [128, 1024], BF16) for i in range(2)]
        F = {nm: sb("h" + nm, [128, 512], F32) for nm in ("qs", "ez", "r", "f", "kk", "b", "d1", "X", "Y")}
        qc = [sb("hqc%d" % i, [128, 512], BF16) for i in range(2)]
        kc = [sb("hkc%d" % i, [128, 512], BF16) for i in range(2)]
        eB = [sb("heB%d" % i, [128, 16], F32) for i in range(2)]
        vtok = [sb("hvtok%d" % i, [128, 128], BF16) for i in range(2)]
        gsil = [sb("hgsil%d" % i, [128, 128], F32) for i in range(2)]
        ktl = [sb("hktl%d" % i, [128, 128], BF16) for i in range(2)]
        scm = [sb("hscm%d" % i, [128, 128], BF16) for i in range(2)]
        qz = [sb("hqz%d" % i, [128, 1024], BF16) for i in range(2)]
        kz = [sb("hkz%d" % i, [128, 2048], BF16) for i in range(2)]
        Sdb = [sb("hSdb%d" % i, [128, 16, 128], BF16) for i in range(2)]
        S = sb("hS", [128, 128], F32)
        S0 = sb("hS0", [128, 16, 128], F32)
        on = sb("hon", [128, 128], F32)
        go = [sb("hgo%d" % i, [128, 128], BF16) for i in range(2)]
        goT = sb("hgoT", [128, 512], BF16)
        gn = sb("hgn", [128, 128], F32)
        lb = sb("hlb", [128, 2, 8], F32)
        lbv = sb("hlbv", [128, 8], F32)
        oml = sb("homl", [128, 8], F32)
        st4 = sb("hst4", [128, 4], F32)
        B = [pt("hb%d" % i, [128, 512], F32) for i in range(8)]
        VG = [B[4][:, 0:256], B[6][:, 0:256]]
        PT = [B[4][:, 256:320].bitcast(BF16), B[6][:, 256:320].bitcast(BF16)]
        SC = [B[4][:, 320:448], B[6][:, 320:448]]
        TR = B[3][:, 256:320].bitcast(BF16)
        UR = [B[5][:, i * 128:(i + 1) * 128] for i in range(4)] + [B[7][:, i * 128:(i + 1) * 128] for i in range(4)]
        cret = C.cret
        bF = {nm: Buf() for nm in F}
        bwh = [Buf(), Buf()]; bwo = [Buf(), Buf()]; bqc = [Buf(), Buf()]; bkc = [Buf(), Buf()]; beB = [Buf(), Buf()]
        bvtok = [Buf(), Buf()]; bgsil = [Buf(), Buf()]; bktl = [Buf(), Buf()]; bscm = [Buf(), Buf()]
        bqz = [Buf(), Buf()]; bkz = [Buf(), Buf()]; bSdb = [Buf(), Buf()]; bgo = [Buf(), Buf()]
        bS = Buf(); bS0 = Buf(); bon = Buf(); bgoT = Buf(); bgn = Buf(); blb = Buf(); bst4 = Buf()
        bB = [PBuf() for _ in range(8)]
        bVG = [bB[4], bB[6]]; bPT = [bB[4], bB[6]]; bSC = [bB[4], bB[6]]; bTR = bB[3]; bUR = [bB[5]] * 4 + [bB[7]] * 4
        bx = [[Buf() for _ in TT] for _ in range(8)]
        s_wh = [P.slot(), P.slot()]; s_wo = [P.slot(), P.slot()]; s_gn = P.slot(); s_lb = P.slot()
        s_S0 = P.slot(); s_so = P.slot(); s_sp = P.slot()

        def sigmoid_act(dst, src, bdst, bsrc):
            bop(P, "act", lambda e: e.activation(out=dst, in_=src, func=AF.Exp, scale=-1.0), r=[bsrc], w=[bdst])
            bop(P, "act", lambda e: e.activation(out=dst, in_=dst, func=AF.Ln, bias=C.one[:dst.shape[0], 0:1]), r=[bdst], w=[bdst])
            bop(P, "act", lambda e: e.activation(out=dst, in_=dst, func=AF.Exp, scale=-1.0), r=[bdst], w=[bdst])

        bop(P, "sp", lambda e: e.dma_start(out=lb[:], in_=lbl), w=[blb], slot=s_lb)
        if j == 0:
            bop(P, "dve", lambda e: e.memset(lbv[:], 0.0), w=[blb])
            bop(P, "dve", lambda e: e.memset(oml[:], 1.0), w=[blb])
        else:
            bop(P, "dve", lambda e: e.tensor_tensor(out=lbv[:], in0=lb[:, 0, :], in1=lb[:, 1, :], op=ALU.subtract), r=[blb], w=[blb])
            bop(P, "act", lambda e: e.activation(out=oml[:], in_=lbv[:], func=AF.Exp), r=[blb], w=[blb])
            bop(P, "dve", lambda e: e.tensor_scalar(out=lbv[:], in0=oml[:], scalar1=1.0, scalar2=None, op0=ALU.add), r=[blb], w=[blb])
            bop(P, "dve", lambda e: e.reciprocal(out=lbv[:], in_=lbv[:]), r=[blb], w=[blb])
            bop(P, "dve", lambda e: e.tensor_tensor(out=oml[:], in0=oml[:], in1=lbv[:], op=ALU.mult), r=[blb], w=[blb])

        def load_w(h):
            sl = h % 2
            bop(P, "pool", lambda e, sl=sl, h=h: e.dma_start(out=wh[sl][:].rearrange("p k n -> p (k n)"), in_=hwin[h], max_dma_last_dim=8192),
                w=[bwh[sl]], slot=s_wh[sl])
            bop(P, "pool", lambda e, sl=sl, h=h: e.dma_start(out=wo[sl][:], in_=hwout[h], max_dma_last_dim=8192), w=[bwo[sl]], slot=s_wo[sl])

        load_w(0)
        ugl = 0
        jobs = [(h, ti) for h in range(HG_H) for ti in range(len(TT))]

        def head_setup(h):
            if h + 1 < HG_H:
                load_w(h + 1)
            bop(P, "sp", lambda e: e.dma_start(out=gn[:], in_=hnorm[h].partition_broadcast(128)), w=[bgn], slot=s_gn)
            bop(P, "pool", lambda e: e.memset(S[:], 0.0), w=[bS])
            bop(P, "sp", lambda e: e.dma_start(out=S0[:], in_=st_in[:, h, :, :].rearrange("i d e -> d i e")), w=[bS0], slot=s_S0)

        def make_s1(h, ti, kp):
            sl = h % 2
            t0, n = TT[ti]
            sample = (ti == 4)
            CL = 4 if sample else 32
            nch = n // CL
            CM = CH_CMS if sample else CH_CMP
            qcj, kcj, eBj = qc[kp], kc[kp], eB[kp]

            def p0():
                for qi in range(2):
                    for k in range(8):
                        bop(P, "pe", lambda e, qi=qi, k=k: e.matmul(B[qi][:, :n], lhsT=wh[sl][:, k, qi * 128:(qi + 1) * 128], rhs=C.xn[:, k, t0:t0 + n], start=(k == 0), stop=(k == 7)),
                            r=[bwh[sl]], w=[bB[qi]])

            def p1():
                sigmoid_act(F["qs"][:, :n], B[0][:, :n], bF["qs"], bB[0])
                bop(P, "act", lambda e: e.activation(out=F["ez"][:, :n], in_=B[1][:, :n], func=AF.Exp, scale=-1.0), r=[bB[1]], w=[bF["ez"]])
                bop(P, "act", lambda e: e.activation(out=F["r"][:, :n], in_=F["ez"][:, :n], func=AF.Ln, bias=C.one[:, 0:1]), r=[bF["ez"]], w=[bF["r"]])
                bop(P, "act", lambda e: e.activation(out=F["r"][:, :n], in_=F["r"][:, :n], func=AF.Exp, scale=-1.0), r=[bF["r"]], w=[bF["r"]])

            def p2():
                bop(P, "dve", lambda e: e.tensor_tensor(out=F["qs"][:, :n], in0=B[0][:, :n], in1=F["qs"][:, :n], op=ALU.mult), r=[bF["qs"], bB[0]], w=[bF["qs"]])
                bop(P, "dve", lambda e: e.tensor_scalar(out=F["f"][:, :n], in0=F["r"][:, :n], scalar1=oml[:, h:h + 1], scalar2=lbv[:, h:h + 1], op0=ALU.mult, op1=ALU.add),
                    r=[bF["r"], blb], w=[bF["f"]])
                bop(P, "dve", lambda e: e.scalar_tensor_tensor(out=F["kk"][:, :n], in0=F["ez"][:, :n], scalar=oml[:, h:h + 1], in1=F["r"][:, :n], op0=ALU.mult, op1=ALU.mult),
                    r=[bF["ez"], bF["r"], blb], w=[bF["kk"]])
                bop(P, "act", lambda e: e.activation(out=F["f"][:, :n], in_=F["f"][:, :n], func=AF.Ln), r=[bF["f"]], w=[bF["f"]])

            def p3():
                bop(P, "dve", lambda e: e.tensor_tensor_scan(out=F["b"][:, :n], data0=cret[:, CM:CM + n], data1=F["f"][:, :n], initial=0.0, op0=ALU.mult, op1=ALU.add),
                    r=[bF["f"]], w=[bF["b"]])
                bop(P, "pool", lambda e: e.tensor_tensor(
                    out=F["d1"][:, :n].rearrange("p (c s) -> p c s", s=CL), in0=F["b"][:, :n].rearrange("p (c s) -> p c s", s=CL),
                    in1=F["b"][:, :n].rearrange("p (c s) -> p c s", s=CL)[:, :, CL - 1:CL].broadcast_to([128, nch, CL]), op=ALU.subtract),
                    r=[bF["b"]], w=[bF["d1"]])

            def p4():
                bop(P, "act", lambda e: e.activation(out=F["X"][:, :n], in_=F["d1"][:, :n], func=AF.Exp), r=[bF["d1"]], w=[bF["X"]])
                bop(P, "act", lambda e: e.activation(out=F["Y"][:, :n], in_=F["d1"][:, :n], func=AF.Exp, scale=-1.0), r=[bF["d1"]], w=[bF["Y"]])
                bop(P, "act", lambda e: e.activation(out=eBj[:, :nch], in_=F["b"][:, :n].rearrange("p (c s) -> p c s", s=CL)[:, :, CL - 1], func=AF.Exp),
                    r=[bF["b"]], w=[beB[kp]])

            def p5():
                bop(P, "pool", lambda e: e.tensor_tensor(out=qcj[:, :n], in0=F["qs"][:, :n], in1=F["X"][:, :n], op=ALU.mult), r=[bF["qs"], bF["X"]], w=[bqc[kp]])
                bop(P, "pool", lambda e: e.tensor_tensor(out=kcj[:, :n], in0=F["kk"][:, :n], in1=F["Y"][:, :n], op=ALU.mult), r=[bF["kk"], bF["Y"]], w=[bkc[kp]])

            return [p0, p1, p2, p3, p4, p5]

        def make_blocks(h, ti, kp):
            sl = h % 2
            t0, n = TT[ti]
            sample = (ti == 4)
            CL = 4 if sample else 32
            qcj, kcj, eBj = qc[kp], kc[kp], eB[kp]
            blocks = [(0, 64)] if sample else [(c * 128, 128) for c in range(4)]
            MC = CH_MS if sample else CH_MP
            nbc = 16 if sample else 4
            if sample:
                bmask = C.bm
                rmv = cret[:64, CR_RM:CR_RM + 16]
            else:
                bmask = C.bmp
                rmv = cret[:, CH_RMP:CH_RMP + 4]
            ubase = {}

            def A_pe(ci):
                c0, nb = blocks[ci]
                pb = ci % 2
                a0 = t0 + c0
                for k in range(8):
                    bop(P, "pe", lambda e, k=k: e.matmul(VG[pb][:nb, :], lhsT=C.xn[:, k, a0:a0 + nb], rhs=wh[sl][:, k, 256:512], start=(k == 0), stop=(k == 7)),
                        r=[bwh[sl]], w=[bVG[pb]])
                bop(P, "pe", lambda e: e.transpose(out=PT[pb][:nb, :], in_=kcj[:, c0:c0 + nb], identity=C.ident), r=[bkc[kp]], w=[bPT[pb]])
                bop(P, "pe", lambda e: e.matmul(SC[pb][:nb, :nb], lhsT=kcj[:, c0:c0 + nb], rhs=qcj[:, c0:c0 + nb], start=True, stop=True), r=[bkc[kp], bqc[kp]], w=[bSC[pb]])

            def A_ev1(ci):
                nonlocal ugl
                c0, nb = blocks[ci]
                pb = ci % 2
                bop(P, "dve", lambda e: e.tensor_copy(out=ktl[pb][:nb, :], in_=PT[pb][:nb, :]), r=[bPT[pb]], w=[bktl[pb]])
                bop(P, "act", lambda e: e.activation(out=vtok[pb][:nb, :], in_=VG[pb][:nb, 0:128], func=AF.Copy), r=[bVG[pb]], w=[bvtok[pb]])
                kzv = kz[pb][:nb, 0:nbc * 128].rearrange("p (c d) -> p c d", c=nbc)
                bop(P, "pool", lambda e: e.tensor_tensor(out=kzv, in0=ktl[pb][:nb, :].unsqueeze(1).broadcast_to([nb, nbc, 128]), in1=rmv.unsqueeze(2).broadcast_to([nb, nbc, 128]), op=ALU.mult),
                    r=[bktl[pb]], w=[bkz[pb]])
                ubase[ci] = ugl
                if nbc <= 4:
                    for c in range(nbc):
                        u = 4 * pb + c
                        bop(P, "pe", lambda e, c=c, u=u: e.matmul(UR[u], lhsT=kzv[:, c, :], rhs=vtok[pb][:nb, :], start=True, stop=True),
                            r=[bkz[pb], bvtok[pb]], w=[bUR[u]])
                ugl += nbc

            def A_ev2(ci):
                c0, nb = blocks[ci]
                pb = ci % 2
                qzv = qz[pb][:, 0:nbc * nb].rearrange("p (c t) -> p c t", c=nbc)
                bop(P, "pool", lambda e: e.tensor_tensor(out=qzv, in0=qcj[:, c0:c0 + nb].unsqueeze(1).broadcast_to([128, nbc, nb]), in1=bmask, op=ALU.mult),
                    r=[bqc[kp]], w=[bqz[pb]])
                bop(P, "dve", lambda e: e.tensor_tensor(out=scm[pb][:nb, :nb], in0=SC[pb][:nb, :nb], in1=cret[:nb, MC:MC + nb], op=ALU.mult), r=[bSC[pb]], w=[bscm[pb]])
                sigmoid_act(gsil[pb][:nb, :], VG[pb][:nb, 128:256], bgsil[pb], bVG[pb])
                bop(P, "dve", lambda e: e.tensor_tensor(out=gsil[pb][:nb, :], in0=VG[pb][:nb, 128:256], in1=gsil[pb][:nb, :], op=ALU.mult), r=[bgsil[pb], bVG[pb]], w=[bgsil[pb]])

            def B_chain(ci):
                c0, nb = blocks[ci]
                pb = ci % 2
                kzv = kz[pb][:nb, 0:nbc * 128].rearrange("p (c d) -> p c d", c=nbc)
                for c in range(nbc):
                    ec = (c0 // CL + c)
                    u = (4 * pb + c) if nbc <= 4 else (4 * (c % 2) + (c // 2) % 4)
                    Sin = S0[:, c, :] if sample else S[:, :]
                    bSin = bS0 if sample else bS
                    if nbc > 4:
                        bop(P, "pe", lambda e, c=c, u=u: e.matmul(UR[u], lhsT=kzv[:, c, :], rhs=vtok[pb][:nb, :], start=True, stop=True),
                            r=[bkz[pb], bvtok[pb]], w=[bUR[u]])
                    bop(P, "dve", lambda e, c=c, ec=ec, Sin=Sin: e.tensor_scalar(out=Sdb[pb][:, c, :], in0=Sin, scalar1=eBj[:, ec:ec + 1], scalar2=None, op0=ALU.mult),
                        r=[bSin, beB[kp]], w=[bSdb[pb]])
                    bop(P, "dve", lambda e, ec=ec, u=u, Sin=Sin: e.scalar_tensor_tensor(out=Sin, in0=Sin, scalar=eBj[:, ec:ec + 1], in1=UR[u], op0=ALU.mult, op1=ALU.add),
                        r=[bUR[u], beB[kp]], w=[bSin])

            def B_pe(ci):
                c0, nb = blocks[ci]
                pb = ci % 2
                qzv = qz[pb][:, 0:nbc * nb].rearrange("p (c t) -> p c t", c=nbc)
                bop(P, "pe", lambda e: e.matmul(B[2][:nb, 0:128], lhsT=scm[pb][:nb, :nb], rhs=vtok[pb][:nb, :], start=True, stop=False), r=[bscm[pb], bvtok[pb]], w=[bB[2]])
                for c in range(nbc):
                    bop(P, "pe", lambda e, c=c: e.matmul(B[2][:nb, 0:128], lhsT=qzv[:, c, :], rhs=Sdb[pb][:, c, :], start=False, stop=(c == nbc - 1)),
                        r=[bqz[pb], bSdb[pb]], w=[bB[2]])

            def B_norm(ci):
                c0, nb = blocks[ci]
                pb = ci % 2
                bop(P, "act", lambda e: e.activation(out=on[:nb, :], in_=B[2][:nb, 0:128], func=AF.Square, accum_out=st4[:nb, 0:1]), r=[bB[2]], w=[bon, bst4])
                bop(P, "act", lambda e: e.activation(out=st4[:nb, 2:3], in_=st4[:nb, 0:1], func=AF.Ln, scale=1.0 / 128, bias=C.epsc[:nb, 0:1]), r=[bst4], w=[bst4])
                bop(P, "act", lambda e: e.activation(out=st4[:nb, 3:4], in_=st4[:nb, 2:3], func=AF.Exp, scale=-0.5), r=[bst4], w=[bst4])
                bop(P, "dve", lambda e: e.scalar_tensor_tensor(out=on[:nb, :], in0=B[2][:nb, 0:128], scalar=st4[:nb, 3:4], in1=gn[:nb, :], op0=ALU.mult, op1=ALU.mult),
                    r=[bB[2], bst4, bgn], w=[bon])
                bop(P, "pool", lambda e: e.tensor_tensor(out=go[pb][:nb, :], in0=on[:nb, :], in1=gsil[pb][:nb, :], op=ALU.mult), r=[bon, bgsil[pb]], w=[bgo[pb]])

            def stage_T(ci):
                c0, nb = blocks[ci]
                pb = ci % 2
                bop(P, "pe", lambda e: e.transpose(out=TR[:, :nb], in_=go[pb][:nb, :], identity=C.ident[:nb, :nb]), r=[bgo[pb]], w=[bTR])
                bop(P, "act", lambda e: e.activation(out=goT[:, c0:c0 + nb], in_=TR[:, :nb], func=AF.Copy), r=[bTR], w=[bgoT])

            def wout():
                for m in range(8):
                    wb = 3 if m % 2 == 0 else 2
                    bop(P, "pe", lambda e, m=m, wb=wb: e.matmul(B[wb][:, :n], lhsT=wo[sl][:, m * 128:(m + 1) * 128], rhs=goT[:, :n], start=True, stop=True),
                        r=[bwo[sl], bgoT], w=[bB[wb]])
                    bop(P, "dve", lambda e, m=m, wb=wb: e.tensor_tensor(out=C.xres[:, m, t0:t0 + n], in0=C.xres[:, m, t0:t0 + n], in1=B[wb][:, :n], op=ALU.add),
                        r=[bB[wb]], w=[bx[m][ti]])
                if ti == 3:
                    bop(P, "sp", lambda e: e.dma_start(out=st_out_p[h], in_=S[:, :]), r=[bS], slot=s_sp)
                if sample:
                    bop(P, "sp", lambda e: e.dma_start(out=st_out_s[:, h, :, :].rearrange("i d e -> d i e"), in_=S0[:, :, :]), r=[bS0], slot=s_so)

            mk = lambda fn, ci: (lambda: fn(ci))
            if len(blocks) == 1:
                sched = [mk(A_pe, 0), mk(A_ev1, 0), mk(A_ev2, 0), mk(B_chain, 0), mk(B_pe, 0), mk(B_norm, 0), mk(stage_T, 0)]
                slots_after = {2: [0, 1], 4: [2, 3], 6: [4, 5]}
            else:
                sched = [mk(A_pe, 0), mk(A_ev1, 0), mk(A_ev2, 0), mk(A_pe, 1),
                         mk(A_ev1, 1), mk(B_chain, 0), mk(B_pe, 0), mk(A_ev2, 1), mk(B_norm, 0), mk(A_pe, 2),
                         mk(A_ev1, 2), mk(B_chain, 1), mk(B_pe, 1), mk(A_ev2, 2), mk(B_norm, 1), mk(stage_T, 0), mk(A_pe, 3),
                         mk(A_ev1, 3), mk(B_chain, 2), mk(B_pe, 2), mk(A_ev2, 3), mk(B_norm, 2), mk(stage_T, 1),
                         mk(B_chain, 3), mk(B_pe, 3), mk(B_norm, 3), mk(stage_T, 2), mk(stage_T, 3)]
                slots_after = {9: [0, 1], 16: [2], 22: [3, 4], 26: [5]}
            return sched, slots_after, wout

        for p in make_s1(jobs[0][0], jobs[0][1], 0):
            p()
        for kj, (h, ti) in enumerate(jobs):
            kp = kj % 2
            if ti == 0:
                head_setup(h)
            sched, slots_after, wout = make_blocks(h, ti, kp)
            nxt = make_s1(jobs[kj + 1][0], jobs[kj + 1][1], 1 - kp) if kj + 1 < len(jobs) else None
            for si, stage in enumerate(sched):
                stage()
                if nxt is not None:
                    for pi in slots_after.get(si, []):
                        nxt[pi]()
            wout()
        P.flush()


def build_program(cfg):
    nc = bass.Bass("TRN2", target_bir_lowering=False)
    dr = lambda name, shape, kind="ExternalInput", dt=F32: nc.dram_tensor(name, shape, dt, kind=kind).ap()
    xT = dr("xT", [128, 8, NTOK])
    gains_d = dr("gains", [128, 13 * 8])
    cbf_d = dr("cbf", [128, 256 + 1024 + 512], dt=BF16)
    cret_d = dr("cret", [128, CRW2])
    cs_d = dr("cs", [2, 128, NTOK])
    wup_d = dr("wup", [8, NF, 128, 2048])
    wdn_d = dr("wdn", [8, 2, 128, 11 * 1024])
    rwin_d = dr("rwin", [2, 4, 128, 12288])
    rwout_d = dr("rwout", [2, 4, 128, 4096])
    rnorm_d = dr("rnorm", [2, 4, 512])
    sret_d = dr("sret", [2, NSS, 4, 256, 512])
    hwin_d = dr("hwin", [2, 8, 128, 4096])
    hwout_d = dr("hwout", [2, 8, 128, 1024])
    hnorm_d = dr("hnorm", [2, 8, 128])
    lbl_d = dr("lbl", [128, 2, 8])
    shg_d = dr("shg", [2, NSS, 8, 128, 128])
    nhp_d = dr("nhp", [2, 8, 128, 128], kind="ExternalOutput")
    nhs_d = dr("nhs", [2, NSS, 8, 128, 128], kind="ExternalOutput")
    yT = dr("yT", [128, 8, NTOK], kind="ExternalOutput")
    nrp_d = dr("nrp", [2, 4, 256, 512], kind="ExternalOutput")
    nrs_d = dr("nrs", [2, NSS, 4, 256, 512], kind="ExternalOutput")

    with ExitStack() as st:
        P = Prog(nc, st)
        C = Ctx()
        C.P, C.nc = P, nc
        C.dbg = None
        if cfg.get("debug"):
            C.dbg = {"want": set(cfg["debug"]), "seen": {}, "off": {"f": 0, "b": 0}, "slot": P.slot(),
                     "f": dr("dbgf", [128, 8192], kind="ExternalOutput"), "b": dr("dbgb", [128, 8192], kind="ExternalOutput", dt=BF16)}
        cfg["_dbg"] = C.dbg
        sb = lambda name, shape, dt: st.enter_context(nc.sbuf_tensor(name, shape, dt))
        C.xres = sb("xres", [128, 8, NTOK], F32)
        C.xn = sb("xn", [128, 8, NTOK], BF16)
        C.gains = sb("gains_sb", [128, 13 * 8], F32)
        cbf = sb("cbf_sb", [128, 256 + 1024 + 512], BF16)
        C.ident = cbf[:, 0:128]
        C.ones = cbf[:, 128:256]
        C.bm = cbf[:, 256:1280].rearrange("p (a t) -> p a t", a=16)
        C.bmp = cbf[:, 1280:1792].rearrange("p (a t) -> p a t", a=4)
        C.epsc = sb("epsc", [128, 2], F32)
        C.one = sb("onec", [128, 2], F32)
        C.cret = sb("cret_sb", [128, CRW2], F32)

        s_in = P.slot()
        for k in range(8):
            P.dma("sp", lambda e, k=k: e.dma_start(out=C.xres[:, k, :], in_=xT[:, k, :]), s_in)
        P.dma("sp", lambda e: e.dma_start(out=C.gains[:], in_=gains_d), s_in)
        P.dma("sp", lambda e: e.dma_start(out=cbf[:], in_=cbf_d), s_in)
        P.dma("sp", lambda e: e.dma_start(out=C.cret[:], in_=cret_d), s_in)
        P.op("pool", lambda e: e.memset(C.epsc[:], EPS))
        P.op("pool", lambda e: e.memset(C.one[:], 1.0))
        P.flush()

        for blk in cfg["blocks"]:
            if blk[0] == "ffn":
                _, l, i = blk
                emit_ffn(C, l * 3 + (0 if i == 0 else 2), wup_d[l * 2 + i], wdn_d[l * 2 + i])
            elif blk[0] == "ret":
                _, l = blk
                j = l // 2
                emit_normphase(C, l * 3 + 1)
                emit_ret(C, j, rwin_d[j], rwout_d[j], rnorm_d[j], cs_d, sret_d[j], nrp_d[j], nrs_d[j])
            elif blk[0] == "hg":
                _, l = blk
                j = l // 2
                emit_normphase(C, l * 3 + 1)
                emit_hg(C, j, hwin_d[j], hwout_d[j], hnorm_d[j], lbl_d, shg_d[j], nhp_d[j], nhs_d[j])

        with ExitStack() as st2:
            sb2, pt2 = mk_alloc(C, st2)
            ph = Ctx()
            ph.sq = [sb2("sq%d" % i, [128, 8, 512], BF16) for i in range(2)]
            ph.rs = [sb2("rs%d" % i, [128, 512], F32) for i in range(2)]
            ph.psn = pt2("psn", [128, 512], F32)
            yo = [sb2("yo%d" % i, [128, 8, 512], F32) for i in range(2)]
            s_out = [P.slot(), P.slot()]
            yo_rd = [None, None]
            if cfg.get("final_norm", True):
                def out_fn2(ti, k, t0, n, rsb, r):
                    b = ti % 2
                    o = P.op("dve", lambda e: e.scalar_tensor_tensor(
                        out=yo[b][:, k, :n], in0=C.xres[:, k, t0:t0 + n], scalar=C.gains[:, 96 + k:96 + k + 1],
                        in1=rsb[:, :n], op0=ALU.mult, op1=ALU.mult), [r, yo_rd[b]])
                    if k == 7:
                        yo_rd[b] = P.dma("sp", lambda e: e.dma_start(out=yT[:, :, t0:t0 + n], in_=yo[b][:, :, :n]), s_out[b], [o])
                    return o
                emit_norm(C, ph, 12, out_fn2)
            else:
                for k in range(8):
                    P.dma("sp", lambda e, k=k: e.dma_start(out=yT[:, k, :], in_=C.xres[:, k, :]), s_out[0])
            P.flush()
    return nc


def host_consts():
    import ml_dtypes
    c = np.zeros((128, 256 + 1024 + 512), np.float32)
    c[:, 0:128] = np.eye(128, dtype=np.float32)
    c[:, 128:256] = 1.0
    bm = np.zeros((16, 64), np.float32)
    for i in range(16):
        bm[i, 4 * i:4 * i + 4] = 1.0
    c[:, 256:1280] = bm.reshape(1, 1024)
    bmp = np.zeros((4, 128), np.float32)
    for i in range(4):
        bmp[i, 32 * i:32 * i + 32] = 1.0
    c[:, 1280:1792] = bmp.reshape(1, 512)
    out = {"cbf": c.astype(ml_dtypes.bfloat16)}
    cr = np.zeros((128, CRW2), np.float64)
    t = np.arange(128)
    ts = np.arange(64)
    for h, g in enumerate(ret_gammas()):
        lg = np.log(np.float64(g))
        mp = np.where(t[:, None] <= t[None, :], np.exp(-(t[:, None] + 1.0) * lg), 0.0)
        cr[:, CR_MP + h * 128:CR_MP + (h + 1) * 128] = mp
        same = (ts[:, None] // 4) == (ts[None, :] // 4)
        ms = np.where(same & ((ts[:, None] % 4) <= (ts[None, :] % 4)), np.exp(-((ts[:, None] % 4) + 1.0) * lg), 0.0)
        cr[:64, CR_MS + h * 64:CR_MS + (h + 1) * 64] = ms
        cr[:, CR_KDP + h] = np.exp((127.0 - t) * lg)
        cr[:, CR_EPP + h] = EPS * np.exp(-2.0 * (t + 1.0) * lg)
        cr[:64, CR_KDS + h] = np.exp((3.0 - (ts % 4)) * lg)
        cr[:64, CR_EPS + h] = EPS * np.exp(-2.0 * ((ts % 4) + 1.0) * lg)
    for i in range(16):
        cr[4 * i:4 * i + 4, CR_RM + i] = 1.0
    cr[:, CH_MP:CH_MP + 128] = ((t[:, None] // 32) == (t[None, :] // 32)) & (t[:, None] <= t[None, :])
    cr[:64, CH_MS:CH_MS + 64] = ((ts[:, None] // 4) == (ts[None, :] // 4)) & (ts[:, None] <= ts[None, :])
    for i in range(4):
        cr[32 * i:32 * i + 32, CH_RMP + i] = 1.0
    cr[:, CH_CMP:CH_CMP + 512] = (np.arange(512) % 32 != 0)[None, :]
    cr[:, CH_CMS:CH_CMS + 64] = (np.arange(64) % 4 != 0)[None, :]
    out["cret"] = cr.astype(np.float32)
    half = 128
    inv_freq = (np.float32(10000.0) ** (-np.arange(half, dtype=np.float32) / np.float32(half))).astype(np.float32)
    pos = np.concatenate([np.arange(SEQ, dtype=np.float32), np.tile(np.float32(16384.0) + np.arange(DEC, dtype=np.float32), NSS)])
    ang = (pos[None, :] * inv_freq[:, None]).astype(np.float32)
    out["cs"] = np.stack([np.cos(ang), np.sin(ang)]).astype(np.float32)
    return out


def host_weights(inp):
    w = {}
    f32 = lambda a: np.asarray(a, np.float32)
    up = f32(inp["ffn_w_up"]).reshape(8, 8, 128, 2, NF, 128)
    w["wup"] = np.ascontiguousarray(up.transpose(0, 4, 2, 1, 3, 5)).reshape(8, NF, 128, 2048)
    dn = f32(inp["ffn_w_down"]).reshape(8, 2, 11, 128, 1024)
    w["wdn"] = np.ascontiguousarray(dn.transpose(0, 1, 3, 2, 4)).reshape(8, 2, 128, 11 * 1024)
    g = np.concatenate([f32(inp["norm_gain"]).reshape(12, 1024), f32(inp["final_norm"]).reshape(1, 1024)], 0)
    w["gains"] = np.ascontiguousarray(g.reshape(13, 8, 128).transpose(2, 0, 1)).reshape(128, 104)
    wi = f32(inp["ret_w_in"]).reshape(2, 8, 128, 6144)
    parts = []
    for h in range(4):
        parts.append(np.concatenate([wi[..., h * 256:(h + 1) * 256], wi[..., 1024 + h * 256:1024 + (h + 1) * 256],
                                     wi[..., 2048 + h * 512:2048 + (h + 1) * 512], wi[..., 4096 + h * 512:4096 + (h + 1) * 512]], -1))
    wih = np.stack(parts, 1)
    w["rwin"] = np.ascontiguousarray(wih.transpose(0, 1, 3, 2, 4)).reshape(2, 4, 128, 12288)
    wo = f32(inp["ret_w_out"]).reshape(2, 4, 4, 128, 1024)
    w["rwout"] = np.ascontiguousarray(wo.transpose(0, 1, 3, 2, 4)).reshape(2, 4, 128, 4096)
    w["rnorm"] = f32(inp["ret_norm"])
    hi = f32(inp["hg_w_in"]).reshape(2, 8, 128, 4, 8, 128)
    w["hwin"] = np.ascontiguousarray(hi.transpose(0, 4, 2, 1, 3, 5)).reshape(2, 8, 128, 4096)
    w["hwout"] = np.ascontiguousarray(f32(inp["hg_w_out"]).reshape(2, 8, 128, 1024))
    w["hnorm"] = f32(inp["hg_norm"])
    w["lbl"] = np.ascontiguousarray(f32(inp["hg_lb_logits"]).reshape(2, 8, 128).transpose(2, 0, 1))
    return w


def host_core_inputs(inp, c):
    xp = np.asarray(inp["x_prompt"], np.float32)[c]
    xs = np.asarray(inp["x_sample"], np.float32)[c * NSS:(c + 1) * NSS].reshape(NSS * DEC, D)
    x = np.concatenate([xp, xs], 0)
    xT = np.ascontiguousarray(x.T.reshape(8, 128, NTOK).transpose(1, 0, 2))
    m = {"xT": xT}
    m["sret"] = np.ascontiguousarray(np.asarray(inp["state_ret"], np.float32)[:, c * NSS:(c + 1) * NSS])
    m["shg"] = np.ascontiguousarray(np.asarray(inp["state_hgrn"], np.float32)[:, c * NSS:(c + 1) * NSS])
    return m


def full_cfg():
    blocks = []
    for l in range(4):
        blocks.append(("ffn", l, 0))
        blocks.append(("ret", l) if l % 2 == 0 else ("hg", l))
        blocks.append(("ffn", l, 1))
    return {"blocks": blocks, "final_norm": True}


def kernel(**inputs):
    nc = build_program(full_cfg())
    shared = {}
    shared.update(host_consts())
    shared.update(host_weights(inputs))
    in_maps = []
    for c in range(8):
        m = dict(shared)
        m.update(host_core_inputs(inputs, c))
        in_maps.append(m)
    res = run_bass_kernel_spmd(nc, in_maps, core_ids=list(range(8)))
    rs = res.results
    y_prompt = np.empty((8, SEQ, D), np.float32)
    y_sample = np.empty((8 * NSS, DEC, D), np.float32)
    nrp = np.empty((2, 8, 4, 256, 512), np.float32)
    nrs = np.empty((2, 8 * NSS, 4, 256, 512), np.float32)
    nhp = np.empty((2, 8, 8, 128, 128), np.float32)
    nhs = np.empty((2, 8 * NSS, 8, 128, 128), np.float32)
    for c in range(8):
        r = rs[c]
        y = np.asarray(r["yT"]).transpose(1, 0, 2).reshape(D, NTOK).T
        y_prompt[c] = y[:SEQ]
        y_sample[c * NSS:(c + 1) * NSS] = y[SEQ:].reshape(NSS, DEC, D)
        nrp[:, c] = r["nrp"]
        nrs[:, c * NSS:(c + 1) * NSS] = r["nrs"]
        nhp[:, c] = r["nhp"]
        nhs[:, c * NSS:(c + 1) * NSS] = r["nhs"]
    return (y_prompt, y_sample, nrp, nrs, nhp, nhs)
```

```python
import numpy as np
from contextlib import ExitStack
import concourse.bass as bass
import concourse.mybir as mybir
from concourse.bass_utils import run_bass_kernel_spmd

F32 = mybir.dt.float32
BF16 = mybir.dt.bfloat16
AF = mybir.ActivationFunctionType
ALU = mybir.AluOpType

D = 1024
SEQ = 2048
NSS = 16
DEC = 4
NTOK = SEQ + NSS * DEC
DFF = 2816
NF = DFF // 128
EPS = 1e-6
TT = [(0, 512), (512, 512), (1024, 512), (1536, 512), (2048, 64)]


class Op:
    __slots__ = ("eng", "fn", "deps", "pos", "sem", "val", "need_sig", "is_dma", "done")


class Slot:
    def __init__(self, sem):
        self.sem = sem
        self.count = 0


class Prog:
    ENGS = ["pe", "act", "dve", "pool", "sp"]
    ENGOBJ = {"pe": "tensor", "act": "scalar", "dve": "vector", "pool": "gpsimd", "sp": "sync"}

    def __init__(self, nc, stack):
        self.nc = nc
        self.stack = stack
        self.q = {e: [] for e in self.ENGS}
        self.esem = {e: stack.enter_context(nc.semaphore("s_" + e)) for e in ["pe", "act", "dve", "pool"]}
        self.ecount = {e: 0 for e in self.ENGS}
        self.slots = []
        self.nphase = 0

    def slot(self):
        s = Slot(self.stack.enter_context(self.nc.semaphore("d%d" % len(self.slots))))
        self.slots.append(s)
        return s

    def op(self, eng, fn, deps=()):
        o = Op()
        o.eng = eng
        o.fn = fn
        o.deps = [d for d in deps if d is not None]
        o.pos = len(self.q[eng])
        o.is_dma = False
        o.need_sig = False
        o.sem = None
        o.val = None
        o.done = False
        self.q[eng].append(o)
        return o

    def dma(self, eng, fn, slot, deps=()):
        o = self.op(eng, fn, deps)
        o.is_dma = True
        slot.count += 16
        o.sem = slot.sem
        o.val = slot.count
        return o

    def _needs_wait(self, o, d):
        if d.done:
            return False
        if d.is_dma:
            return True
        if d.eng == o.eng:
            if o.eng == "pe":
                return False
            return (o.pos - d.pos) <= 2
        return True

    def flush(self):
        nc = self.nc
        drain_deps = []
        for e in self.ENGS:
            last = {}
            for o in self.q[e]:
                if o.is_dma:
                    last[id(o.sem)] = o
            drain_deps += list(last.values())
        self.op("sp", lambda e: e.nop(), drain_deps)
        for e in self.ENGS:
            for o in self.q[e]:
                for d in o.deps:
                    if not d.is_dma and self._needs_wait(o, d):
                        d.need_sig = True
        for e in self.ENGS:
            c = self.ecount[e]
            for o in self.q[e]:
                if o.is_dma:
                    continue
                if o.need_sig:
                    assert e != "sp"
                    c += 1
                    o.sem = self.esem[e]
                    o.val = c
            self.ecount[e] = c
        self.nphase += 1
        with nc.Block() as block:
            for e in self.ENGS:
                ops = self.q[e]
                if not ops:
                    continue

                def body(eng, ops=ops):
                    waited = {}
                    for o in ops:
                        need = {}
                        for d in o.deps:
                            if not self._needs_wait(o, d):
                                continue
                            key = id(d.sem)
                            if key not in need or need[key][1] < d.val:
                                need[key] = (d.sem, d.val)
                        for key, (sem, val) in need.items():
                            if waited.get(key, 0) >= val:
                                continue
                            eng.wait_ge(sem, val)
                            waited[key] = val
                        ins = o.fn(eng)
                        if o.is_dma:
                            ins.then_inc(o.sem, 16)
                        elif o.need_sig:
                            ins.then_inc(o.sem, 1)

                getattr(block, self.ENGOBJ[e])(body)
        for e in self.ENGS:
            for o in self.q[e]:
                o.done = True
                o.fn = None
                o.deps = None
            self.q[e] = []


class Ctx:
    pass


def dbg_dump(C, name, ap, bufs, ncols, bf=False):
    if not getattr(C, "dbg", None) or name in C.dbg["seen"] or name not in C.dbg["want"]:
        return
    key = "b" if bf else "f"
    off = C.dbg["off"][key]
    C.dbg["off"][key] = off + ncols
    C.dbg["seen"][name] = (key, off, ncols)
    dst = C.dbg[key][:, off:off + ncols]
    bop(C.P, "sp", lambda e: e.dma_start(out=dst, in_=ap), r=bufs, slot=C.dbg["slot"])


class Buf:
    def __init__(self):
        self.w = None
        self.r = []


class PBuf(Buf):
    excl = True


def bop(P, eng, fn, r=(), w=(), slot=None, deps=()):
    xr = [b for b in r if getattr(b, "excl", False)]
    if xr:
        r = [b for b in r if not getattr(b, "excl", False)]
        w = list(w) + [b for b in xr if b not in w]
    d = list(deps)
    for b in r:
        d.append(b.w)
    for b in w:
        d.append(b.w)
        d.extend(b.r)
    o = P.dma(eng, fn, slot, d) if slot is not None else P.op(eng, fn, d)
    for b in r:
        if not o.is_dma:
            b.r = [x for x in b.r if x.is_dma or x.eng != eng]
        b.r.append(o)
    for b in w:
        b.w = o
        b.r = []
    return o


_UID = [0]


def mk_alloc(C, st):
    _UID[0] += 1
    u = _UID[0]
    nc = C.nc
    sb = lambda name, shape, dt: st.enter_context(nc.sbuf_tensor("%s_%d" % (name, u), shape, dt))
    pt = lambda name, shape, dt: st.enter_context(nc.psum_tensor("%s_%d" % (name, u), shape, dt))
    return sb, pt


def emit_norm(C, ph, gcol, out_fn=None):
    P = C.P
    sq, rs, psn = ph.sq, ph.rs, ph.psn
    last = []
    sq_rd = [None, None]
    rs_rd = [None, None]
    ps_rd = None
    for ti, (t0, n) in enumerate(TT):
        b = ti % 2
        a = P.op("act", lambda e, b=b, t0=t0, n=n: e.activation(out=sq[b][:, :, :n], in_=C.xres[:, :, t0:t0 + n], func=AF.Square),
                 [sq_rd[b]])
        mm = None
        for k in range(8):
            mm = P.op("pe", lambda e, b=b, k=k, n=n: e.matmul(psn[:, :n], lhsT=C.ones[:], rhs=sq[b][:, k, :n], start=(k == 0), stop=(k == 7)),
                      [a, ps_rd])
        sq_rd[b] = mm
        v = P.op("act", lambda e, b=b, n=n: e.activation(out=rs[b][:, :n], in_=psn[:, :n], func=AF.Ln, scale=1.0 / D, bias=C.epsc[:, 0:1]),
                 [mm, rs_rd[b]])
        ps_rd = v
        r = P.op("act", lambda e, b=b, n=n: e.activation(out=rs[b][:, :n], in_=rs[b][:, :n], func=AF.Exp, scale=-0.5), [v])
        o = None
        for k in range(8):
            if out_fn is None:
                o = P.op("dve", lambda e, b=b, k=k, t0=t0, n=n: e.scalar_tensor_tensor(
                    out=C.xn[:, k, t0:t0 + n], in0=C.xres[:, k, t0:t0 + n], scalar=C.gains[:, gcol * 8 + k:gcol * 8 + k + 1],
                    in1=rs[b][:, :n], op0=ALU.mult, op1=ALU.mult), [r])
            else:
                o = out_fn(ti, k, t0, n, rs[b], r)
        rs_rd[b] = o
        last.append(o)
    return last


def emit_normphase(C, gcol):
    with ExitStack() as st:
        sb, pt = mk_alloc(C, st)
        ph = Ctx()
        ph.sq = [sb("sq%d" % i, [128, 8, 512], BF16) for i in range(2)]
        ph.rs = [sb("rs%d" % i, [128, 512], F32) for i in range(2)]
        ph.psn = pt("psn", [128, 512], F32)
        emit_norm(C, ph, gcol)
        C.P.flush()


def emit_ffn(C, gcol, wup, wdn):
    P, nc = C.P, C.nc
    with ExitStack() as st:
        sb, pt = mk_alloc(C, st)
        hid = sb("hid", [128, 11, NTOK], BF16)
        wu = [sb("wu%d" % i, [128, 8, 2, 128], BF16) for i in range(3)]
        wd = sb("wd", [128, 11, 1024], BF16)
        sa = [sb("sa%d" % i, [128, 512], F32) for i in range(2)]
        sqs = [sb("sqs%d" % i, [128, 512], BF16) for i in range(4)]
        rs = [sb("rs%d" % i, [128, 512], F32) for i in range(2)]
        psn = pt("psn", [128, 512], F32)
        psA = [pt("psA%d" % i, [128, 512], F32) for i in range(2)]
        psB = [pt("psB%d" % i, [128, 512], F32) for i in range(2)]
        psD = [pt("psD%d" % i, [128, 512], F32) for i in range(2)]
        bwu = [Buf() for _ in range(3)]; bwd = Buf(); bsa = [Buf(), Buf()]; bsq = [Buf() for _ in range(4)]; brs = [Buf(), Buf()]
        bpsn = PBuf(); bpsA = [PBuf(), PBuf()]; bpsB = [PBuf(), PBuf()]; bpsD = [PBuf(), PBuf()]
        bxn = [Buf() for _ in TT]; bxr = [Buf() for _ in TT]; bhid = [Buf() for _ in TT]
        s_wu = [P.slot() for _ in range(3)]
        s_wd = P.slot()

        def load_wu(fi):
            s = fi % 3
            bop(P, "pool", lambda e: e.dma_start(out=wu[s][:].rearrange("p k a c -> p (k a c)"), in_=wup[fi], max_dma_last_dim=8192), w=[bwu[s]], slot=s_wu[s])

        def load_wd(half):
            bop(P, "pool", lambda e: e.dma_start(out=wd[:].rearrange("p f n -> p (f n)"), in_=wdn[half], max_dma_last_dim=8192), w=[bwd], slot=s_wd)

        sqc = [0]

        def norm(ti):
            t0, n = TT[ti]
            b = ti % 2
            for k in range(8):
                q = sqc[0] % 4
                sqc[0] += 1
                bop(P, "act", lambda e, k=k, q=q: e.activation(out=sqs[q][:, :n], in_=C.xres[:, k, t0:t0 + n], func=AF.Square), r=[bxr[ti]], w=[bsq[q]])
                bop(P, "pe", lambda e, k=k, q=q: e.matmul(psn[:, :n], lhsT=C.ones[:], rhs=sqs[q][:, :n], start=(k == 0), stop=(k == 7)), r=[bsq[q]], w=[bpsn])
            bop(P, "act", lambda e: e.activation(out=rs[b][:, :n], in_=psn[:, :n], func=AF.Ln, scale=1.0 / D, bias=C.epsc[:, 0:1]), r=[bpsn], w=[brs[b]])
            bop(P, "act", lambda e: e.activation(out=rs[b][:, :n], in_=rs[b][:, :n], func=AF.Exp, scale=-0.5), r=[brs[b]], w=[brs[b]])
            for k in range(8):
                bop(P, "dve", lambda e, k=k: e.scalar_tensor_tensor(
                    out=C.xn[:, k, t0:t0 + n], in0=C.xres[:, k, t0:t0 + n], scalar=C.gains[:, gcol * 8 + k:gcol * 8 + k + 1],
                    in1=rs[b][:, :n], op0=ALU.mult, op1=ALU.mult), r=[brs[b], bxr[ti]], w=[bxn[ti]])

        load_wd(0)
        for fi in range(3):
            load_wu(fi)
        norm(0)
        norm(1)
        cnt = 0
        dcnt = 0
        for half in range(2):
            if half == 1:
                load_wd(1)
            for f in range(11):
                fi = half * 11 + f
                s = fi % 3
                for ti, (t0, n) in enumerate(TT):
                    b = cnt % 2
                    cnt += 1
                    for k in range(8):
                        bop(P, "pe", lambda e, b=b, s=s, k=k, t0=t0, n=n: e.matmul(psA[b][:, :n], lhsT=wu[s][:, k, 0, :], rhs=C.xn[:, k, t0:t0 + n], start=(k == 0), stop=(k == 7)),
                            r=[bwu[s], bxn[ti]], w=[bpsA[b]])
                    for k in range(8):
                        bop(P, "pe", lambda e, b=b, s=s, k=k, t0=t0, n=n: e.matmul(psB[b][:, :n], lhsT=wu[s][:, k, 1, :], rhs=C.xn[:, k, t0:t0 + n], start=(k == 0), stop=(k == 7)),
                            r=[bwu[s], bxn[ti]], w=[bpsB[b]])
                    bop(P, "act", lambda e, b=b, n=n: e.activation(out=sa[b][:, :n], in_=psA[b][:, :n], func=AF.Silu), r=[bpsA[b]], w=[bsa[b]])
                    bop(P, "dve", lambda e, b=b, f=f, t0=t0, n=n: e.tensor_tensor(out=hid[:, f, t0:t0 + n], in0=sa[b][:, :n], in1=psB[b][:, :n], op=ALU.mult),
                        r=[bsa[b], bpsB[b]], w=[bhid[ti]])
                    if fi == 0 and ti + 2 < len(TT):
                        norm(ti + 2)
                if fi + 3 < NF:
                    load_wu(fi + 3)
            for ti, (t0, n) in enumerate(TT):
                for mo in range(8):
                    b = dcnt % 2
                    dcnt += 1
                    for f in range(11):
                        bop(P, "pe", lambda e, b=b, f=f, mo=mo, t0=t0, n=n: e.matmul(psD[b][:, :n], lhsT=wd[:, f, mo * 128:(mo + 1) * 128], rhs=hid[:, f, t0:t0 + n], start=(f == 0), stop=(f == 10)),
                            r=[bwd, bhid[ti]], w=[bpsD[b]])
                    bop(P, "dve", lambda e, b=b, mo=mo, t0=t0, n=n: e.scalar_tensor_tensor(
                        out=C.xres[:, mo, t0:t0 + n], in0=psD[b][:, :n], scalar=0.5, in1=C.xres[:, mo, t0:t0 + n], op0=ALU.mult, op1=ALU.add),
                        r=[bpsD[b]], w=[bxr[ti]])
        P.flush()


RET_H = 4
CR_MP = 0
CR_MS = 512
CR_KDP = 768
CR_EPP = 772
CR_KDS = 776
CR_EPS = 780
CR_RM = 784
CRW = 800


def ret_gammas():
    return [1.0 - 2.0 ** (-5.0 - h) for h in range(RET_H)]


def emit_ret(C, j, rwin, rwout, rnorm, cs, st_in, st_out_p, st_out_s):
    P, nc = C.P, C.nc
    gam = ret_gammas()
    with ExitStack() as st:
        sb, pt = mk_alloc(C, st)
        wh = [sb("wh%d" % i, [128, 8, 1536], BF16) for i in range(2)]
        wo = sb("wo", [128, 4, 1024], BF16)
        cst = sb("cs", [128, 2, 512], F32)
        qT = sb("qT", [128, 2, 512], BF16)
        kT = sb("kT", [128, 2, 512], BF16)
        vtok = [sb("vtok%d" % i, [128, 512], BF16) for i in range(2)]
        ktl = [sb("ktl%d" % i, [128, 256], BF16) for i in range(2)]
        scm = [sb("scm%d" % i, [128, 128], BF16) for i in range(2)]
        gs = sb("gs", [128, 512], F32)
        on = sb("on", [128, 512], F32)
        go = [sb("go%d" % i, [128, 512], BF16) for i in range(2)]
        goT = sb("goT", [128, 4, 512], BF16)
        S = sb("S", [128, 2, 512], F32)
        Sb = sb("Sb", [128, 2, 512], BF16)
        gn = sb("gn", [128, 512], F32)
        qz = sb("qz", [128, 2, 16, 64], BF16)
        kz = [sb("kz%d" % i, [64, 256], BF16) for i in range(2)]
        S0f = [sb("S0f%d" % i, [128, 512], F32) for i in range(2)]
        S0b = [sb("S0b%d" % i, [128, 512], BF16) for i in range(2)]
        S0f = [t[:, :] for t in S0f] + [S[:, 0, :], S[:, 1, :]]
        S0b = [t[:, :] for t in S0b] + [Sb[:, 0, :], Sb[:, 1, :]]
        st4 = sb("st4", [128, 4], F32)
        B = [pt("b%d" % i, [128, 512], F32) for i in range(8)]
        PT = [B[6][:, 0:128].bitcast(BF16), B[3][:, 0:128].bitcast(BF16)]
        SC = [B[6][:, 128:256], B[3][:, 128:256]]
        TR = B[3][:, 256:512].bitcast(BF16)
        cret = C.cret

        bwh = [Buf(), Buf()]; bwo = Buf(); bcs = Buf(); bt12 = Buf(); bqT = Buf(); bkT = Buf()
        bvtok = [Buf(), Buf()]; bktl = [Buf(), Buf()]; bscm = [Buf(), Buf()]; bgs = Buf(); bon = Buf(); bgo = [Buf(), Buf()]; bgoT = Buf()
        bSh = [Buf(), Buf()]; bSbh = [Buf(), Buf()]; bgn = Buf(); bqz = Buf(); bkz = [Buf(), Buf()]
        bS0f = [Buf(), Buf()] + bSh; bS0b = [Buf(), Buf()] + bSbh; bst4 = Buf()
        bB = [PBuf() for _ in range(8)]
        bB7t = bB[3]
        bPT = [bB[6], bB[3]]; bSC = [bB[6], bB[3]]
        bx = [[Buf() for _ in TT] for _ in range(8)]
        s_wh = [P.slot(), P.slot()]; s_wo = P.slot(); s_cs = P.slot(); s_gn = P.slot()
        s_S0f = [P.slot() for _ in range(4)]; s_S0b = [P.slot() for _ in range(4)]; s_so = [P.slot() for _ in range(4)]; s_sp = P.slot()

        def load_wh(h):
            sl = h % 2
            bop(P, "pool", lambda e, sl=sl: e.dma_start(out=wh[sl][:].rearrange("p k n -> p (k n)"), in_=rwin[h], max_dma_last_dim=8192),
                w=[bwh[sl]], slot=s_wh[sl])

        load_wh(0)
        dbg_dump(C, "xn0", C.xn[:, 0, 0:512], [], 512, bf=True)
        dbg_dump(C, "wh0", wh[0][:, 0, 0:512], [bwh[0]], 512, bf=True)
        ucnt = 0
        pending = [None]

        def flush_pending(rot_k=None):
            items = pending[0]
            pending[0] = None
            rk = list(rot_k) if rot_k else []
            if items is None:
                for f in rk:
                    f()
                return
            for m, (mm_f, add_f) in enumerate(items):
                mm_f()
                add_f()
                if rk and m < 6:
                    rk.pop(0)()
            for f in rk:
                f()

        for h in range(RET_H):
            sl = h % 2
            g = gam[h]
            flush_pending()
            bop(P, "pool", lambda e, h=h: e.dma_start(out=wo[:].rearrange("p a n -> p (a n)"), in_=rwout[h], max_dma_last_dim=8192),
                w=[bwo], slot=s_wo)
            if h + 1 < RET_H:
                load_wh(h + 1)
            bop(P, "sp", lambda e, h=h: e.dma_start(out=gn[:], in_=rnorm[h].partition_broadcast(128)), w=[bgn], slot=s_gn)
            bop(P, "pool", lambda e: e.memset(S[:], 0.0), w=[bSh[0], bSh[1]])
            bop(P, "pool", lambda e: e.memset(Sb[:], 0.0), w=[bSbh[0], bSbh[1]])
            for ti, (t0, n) in enumerate(TT):
                sample = (ti == 4)
                bop(P, "sp", lambda e, t0=t0, n=n: e.dma_start(out=cst[:, :, :n], in_=cs[:, :, t0:t0 + n].rearrange("a p t -> p a t")),
                    w=[bcs], slot=s_cs)
                for qi in range(4):
                    for k in range(8):
                        bop(P, "pe", lambda e, sl=sl, qi=qi, k=k, t0=t0, n=n: e.matmul(B[qi][:, :n], lhsT=wh[sl][:, k, qi * 128:(qi + 1) * 128], rhs=C.xn[:, k, t0:t0 + n], start=(k == 0), stop=(k == 7)),
                            r=[bwh[sl]], w=[bB[qi]])
                rot = {0: [], 2: []}
                for (dst, bd, b0, b1, sc) in ((qT, bqT, 0, 1, 1.0), (kT, bkT, 2, 3, 0.0625)):
                    for half in range(2):
                        ca, cb = (0, 1) if half == 0 else (1, 0)
                        rot[b0].append(lambda b0=b0, ca=ca, sc=sc, n=n: bop(P, "dve", lambda e: e.scalar_tensor_tensor(out=gs[:, :n], in0=B[b0][:, :n], scalar=sc, in1=cst[:, ca, :n], op0=ALU.mult, op1=ALU.mult),
                                                                             r=[bB[b0], bcs], w=[bgs]))
                        rot[b0].append(lambda b1=b1, cb=cb, sc=sc, n=n: bop(P, "dve", lambda e: e.scalar_tensor_tensor(out=on[:, :n], in0=B[b1][:, :n], scalar=sc, in1=cst[:, cb, :n], op0=ALU.mult, op1=ALU.mult),
                                                                             r=[bB[b1], bcs], w=[bon]))
                        rot[b0].append(lambda dst=dst, bd=bd, half=half, n=n: bop(P, "dve", lambda e: e.tensor_tensor(out=dst[:, half, :n], in0=gs[:, :n], in1=on[:, :n], op=(ALU.subtract if half == 0 else ALU.add)),
                                                                                   r=[bgs, bon], w=[bd]))
                for f in rot[0]:
                    f()
                flush_pending(rot[2])
                blocks = [(0, 64)] if sample else [(c * 128, 128) for c in range(n // 128)]
                MC = (CR_MS + h * 64) if sample else (CR_MP + h * 128)
                KD = (CR_KDS if sample else CR_KDP) + h
                EP = (CR_EPS if sample else CR_EPP) + h

                def stage_A(ci, c0, nb, sl=sl, t0=t0, MC=MC, KD=KD):
                    pb = ci % 2
                    a0 = t0 + c0
                    vb = 4 + pb
                    for k in range(8):
                        bop(P, "pe", lambda e, k=k: e.matmul(B[vb][:nb, :], lhsT=C.xn[:, k, a0:a0 + nb], rhs=wh[sl][:, k, 512:1024], start=(k == 0), stop=(k == 7)),
                            r=[bwh[sl]], w=[bB[vb]])
                    bop(P, "act", lambda e: e.activation(out=vtok[pb][:nb, :], in_=B[vb][:nb, :], func=AF.Copy), r=[bB[vb]], w=[bvtok[pb]])
                    for jj in range(2):
                        bop(P, "pe", lambda e, jj=jj: e.transpose(out=PT[pb][:nb, jj * 128:(jj + 1) * 128], in_=kT[:, jj, c0:c0 + nb], identity=C.ident),
                            r=[bkT], w=[bPT[pb]])
                    for jj in range(2):
                        bop(P, "pe", lambda e, jj=jj: e.matmul(SC[pb][:nb, :nb], lhsT=kT[:, jj, c0:c0 + nb], rhs=qT[:, jj, c0:c0 + nb], start=(jj == 0), stop=(jj == 1)),
                            r=[bkT, bqT], w=[bSC[pb]])
                    bop(P, "dve", lambda e: e.tensor_scalar(out=ktl[pb][:nb, :], in0=PT[pb][:nb, :], scalar1=cret[:nb, KD:KD + 1], scalar2=None, op0=ALU.mult),
                        r=[bPT[pb]], w=[bktl[pb]])
                    bop(P, "dve", lambda e: e.tensor_tensor(out=scm[pb][:nb, :nb], in0=SC[pb][:nb, :nb], in1=cret[:nb, MC:MC + nb], op=ALU.mult),
                        r=[bSC[pb]], w=[bscm[pb]])

                def stage_B(ci, c0, nb, sl=sl, t0=t0, EP=EP, sample=sample, g=g, h=h):
                    nonlocal ucnt
                    pb = ci % 2
                    a0 = t0 + c0
                    bop(P, "pe", lambda e: e.matmul(B[7][:nb, :], lhsT=scm[pb][:nb, :nb], rhs=vtok[pb][:nb, :], start=True, stop=False),
                        r=[bscm[pb], bvtok[pb]], w=[bB[7]])
                    if not sample:
                        for jj in range(2):
                            bop(P, "pe", lambda e, jj=jj: e.matmul(B[7][:nb, :], lhsT=qT[:, jj, c0:c0 + nb], rhs=Sb[:, jj, :], start=False, stop=(jj == 1)),
                                r=[bqT, bSbh[jj]], w=[bB[7]])
                        for jj in range(2):
                            bop(P, "pe", lambda e, jj=jj: e.matmul(B[jj][:, :], lhsT=ktl[pb][:nb, jj * 128:(jj + 1) * 128], rhs=vtok[pb][:nb, :], start=True, stop=True),
                                r=[bktl[pb], bvtok[pb]], w=[bB[jj]])
                        cd = g ** 128
                        for jj in range(2):
                            bop(P, "dve", lambda e, jj=jj: e.scalar_tensor_tensor(out=S[:, jj, :], in0=S[:, jj, :], scalar=cd, in1=B[jj][:, :], op0=ALU.mult, op1=ALU.add),
                                r=[bB[jj]], w=[bSh[jj]])
                        for jj in range(2):
                            bop(P, "pool", lambda e, jj=jj: e.tensor_copy(out=Sb[:, jj, :], in_=S[:, jj, :]), r=[bSh[jj]], w=[bSbh[jj]])
                    else:
                        for jj in range(2):
                            bop(P, "dve", lambda e, jj=jj: e.tensor_tensor(out=qz[:, jj, :, :], in0=qT[:, jj, 0:64].unsqueeze(1).broadcast_to([128, 16, 64]), in1=C.bm[:, :, :], op=ALU.mult),
                                r=[bqT], w=[bqz])
                        cd = g ** 4
                        units = [(i, jj) for i in range(NSS) for jj in range(2)]

                        def issue_loads(k):
                            i, jj = units[k]
                            u = k % 4
                            bop(P, "pool", lambda e: e.dma_start(out=S0b[u], in_=st_in[i, h, jj * 128:(jj + 1) * 128, :]), w=[bS0b[u]], slot=s_S0b[u])
                            bop(P, "sp", lambda e: e.dma_start(out=S0f[u], in_=st_in[i, h, jj * 128:(jj + 1) * 128, :]), w=[bS0f[u]], slot=s_S0f[u])

                        issue_loads(0)
                        issue_loads(1)
                        for k, (i, jj) in enumerate(units):
                            if k + 2 < len(units):
                                issue_loads(k + 2)
                            kb = i % 2
                            u = k % 4
                            if jj == 0:
                                bop(P, "dve", lambda e, i=i, kb=kb: e.tensor_scalar(out=kz[kb][:, :], in0=ktl[pb][:64, :], scalar1=cret[:64, CR_RM + i:CR_RM + i + 1], scalar2=None, op0=ALU.mult),
                                    r=[bktl[pb]], w=[bkz[kb]])
                            last = (k == len(units) - 1)
                            bop(P, "pe", lambda e, u=u, i=i, jj=jj, last=last: e.matmul(B[7][:64, :], lhsT=qz[:, jj, i, :], rhs=S0b[u], start=False, stop=last),
                                r=[bqz, bS0b[u]], w=[bB[7]])
                            bop(P, "pe", lambda e, kb=kb, jj=jj: e.matmul(B[jj][:, :], lhsT=kz[kb][:, jj * 128:(jj + 1) * 128], rhs=vtok[pb][:64, :], start=True, stop=True),
                                r=[bkz[kb], bvtok[pb]], w=[bB[jj]])
                            bop(P, "dve", lambda e, u=u, jj=jj: e.scalar_tensor_tensor(out=S0f[u], in0=S0f[u], scalar=cd, in1=B[jj][:, :], op0=ALU.mult, op1=ALU.add),
                                r=[bB[jj]], w=[bS0f[u]])
                            bop(P, "sp", lambda e, u=u, i=i, jj=jj: e.dma_start(out=st_out_s[i, h, jj * 128:(jj + 1) * 128, :], in_=S0f[u]), r=[bS0f[u]], slot=s_so[u])
                    bop(P, "act", lambda e: e.activation(out=on[:nb, :], in_=B[7][:nb, :], func=AF.Square, accum_out=st4[:nb, 0:1]),
                        r=[bB[7]], w=[bon, bst4])
                    bop(P, "act", lambda e: e.activation(out=st4[:nb, 2:3], in_=st4[:nb, 0:1], func=AF.Ln, scale=1.0 / 512, bias=cret[:nb, EP:EP + 1]), r=[bst4], w=[bst4])
                    bop(P, "act", lambda e: e.activation(out=st4[:nb, 3:4], in_=st4[:nb, 2:3], func=AF.Exp, scale=-0.5), r=[bst4], w=[bst4])
                    bop(P, "dve", lambda e: e.scalar_tensor_tensor(out=on[:nb, :], in0=B[7][:nb, :], scalar=st4[:nb, 3:4], in1=gn[:nb, :], op0=ALU.mult, op1=ALU.mult),
                        r=[bB[7], bst4, bgn], w=[bon])
                    for k in range(8):
                        bop(P, "pe", lambda e, k=k: e.matmul(B[2][:nb, :], lhsT=C.xn[:, k, a0:a0 + nb], rhs=wh[sl][:, k, 1024:1536], start=(k == 0), stop=(k == 7)),
                            r=[bwh[sl]], w=[bB[2]])
                    bop(P, "act", lambda e: e.activation(out=gs[:nb, :], in_=B[2][:nb, :], func=AF.Exp, scale=-1.0), r=[bB[2]], w=[bgs])
                    bop(P, "act", lambda e: e.activation(out=gs[:nb, :], in_=gs[:nb, :], func=AF.Ln, bias=C.one[:nb, 0:1]), r=[bgs], w=[bgs])
                    bop(P, "act", lambda e: e.activation(out=gs[:nb, :], in_=gs[:nb, :], func=AF.Exp, scale=-1.0), r=[bgs], w=[bgs])
                    bop(P, "dve", lambda e: e.tensor_tensor(out=gs[:nb, :], in0=B[2][:nb, :], in1=gs[:nb, :], op=ALU.mult), r=[bgs, bB[2]], w=[bgs])
                    bop(P, "pool", lambda e: e.tensor_tensor(out=go[pb][:nb, :], in0=on[:nb, :], in1=gs[:nb, :], op=ALU.mult), r=[bon, bgs], w=[bgo[pb]])

                def stage_T(ci, c0, nb):
                    pb = ci % 2
                    for e4 in range(4):
                        bop(P, "pe", lambda e, e4=e4: e.transpose(out=TR[:, e4 * 128:e4 * 128 + nb], in_=go[pb][:nb, e4 * 128:(e4 + 1) * 128], identity=C.ident[:nb, :nb]),
                            r=[bgo[pb]], w=[bB7t])
                    bop(P, "act", lambda e: e.activation(out=goT[:, :, c0:c0 + nb], in_=TR.rearrange("p (a t) -> p a t", a=4)[:, :, :nb], func=AF.Copy),
                        r=[bB7t], w=[bgoT])

                nblk = len(blocks)
                sched = []
                if nblk == 1:
                    sched = [("A", 0), ("B", 0), ("T", 0)]
                else:
                    sched = [("A", 0), ("A", 1), ("B", 0), ("A", 2), ("B", 1), ("T", 0), ("A", 3), ("B", 2), ("T", 1), ("B", 3), ("T", 2), ("T", 3)]
                for (kind, ci) in sched:
                    c0, nb = blocks[ci]
                    if kind == "A":
                        stage_A(ci, c0, nb)
                    elif kind == "B":
                        stage_B(ci, c0, nb)
                    else:
                        stage_T(ci, c0, nb)
                items = []
                for m in range(8):
                    wb = 4 + (m % 2)

                    def mm_f(m=m, wb=wb, n=n):
                        for e4 in range(4):
                            bop(P, "pe", lambda e, e4=e4: e.matmul(B[wb][:, :n], lhsT=wo[:, e4, m * 128:(m + 1) * 128], rhs=goT[:, e4, :n], start=(e4 == 0), stop=(e4 == 3)),
                                r=[bwo, bgoT], w=[bB[wb]])

                    def add_f(m=m, wb=wb, t0=t0, n=n, ti=ti):
                        bop(P, "dve", lambda e: e.tensor_tensor(out=C.xres[:, m, t0:t0 + n], in0=C.xres[:, m, t0:t0 + n], in1=B[wb][:, :n], op=ALU.add),
                            r=[bB[wb]], w=[bx[m][ti]])
                    items.append((mm_f, add_f))
                pending[0] = items
                if ti == 3:
                    bop(P, "sp", lambda e, h=h: e.dma_start(out=st_out_p[h].rearrange("(a p) n -> p a n", p=128), in_=S[:, :, :]), r=[bSh[0], bSh[1]], slot=s_sp)
        flush_pending()
        P.flush()


HG_H = 8
CH_MP = 800
CH_MS = 928
CH_RMP = 992
CH_CMP = 1000
CH_CMS = 1512
CRW2 = 1576


def emit_hg(C, j, hwin, hwout, hnorm, lbl, st_in, st_out_p, st_out_s):
    P, nc = C.P, C.nc
    with ExitStack() as st:
        sb, pt = mk_alloc(C, st)
        wh = [sb("hwh%d" % i, [128, 8, 512], BF16) for i in range(2)]
        wo = [sb("hwo%d" % i, [128, 1024], BF16) for i in range(2)]
        F = {nm: sb("h" + nm, [128, 512], F32) for nm in ("qs", "ez", "r", "f", "kk", "b", "d1", "X", "Y")}
        qc = [sb("hqc%d" % i, [128, 512], BF16) for i in range(2)]
        kc = [sb("hkc%d" % i, [128, 512], BF16) for i in range(2)]
        eB = [sb("heB%d" % i, [128, 16], F32) for i in range(2)]
        vtok = [sb("hvtok%d" % i, [128, 128], BF16) for i in range(2)]
        gsil = [sb("hgsil%d" % i, [128, 128], F32) for i in range(2)]
        ktl = [sb("hktl%d" % i, [128, 128], BF16) for i in range(2)]
        scm = [sb("hscm%d" % i, [128, 128], BF16) for i in range(2)]
        qz = [sb("hqz%d" % i, [128, 1024], BF16) for i in range(2)]
        kz = [sb("hkz%d" % i, [128, 2048], BF16) for i in range(2)]
        Sdb = [sb("hSdb%d" % i, [128, 16, 128], BF16) for i in range(2)]
        S = sb("hS", [128, 128], F32)
        S0 = sb("hS0", [128, 16, 128], F32)
        on = sb("hon", [128, 128], F32)
        go = [sb("hgo%d" % i, [128, 128], BF16) for i in range(2)]
        goT = sb("hgoT", [128, 512], BF16)
        gn = sb("hgn", [128, 128], F32)
        lb = sb("hlb", [128, 2, 8], F32)
        lbv = sb("hlbv", [128, 8], F32)
        oml = sb("homl", [128, 8], F32)
        st4 = sb("hst4", [128, 4], F32)
        B = [pt("hb%d" % i, [128, 512], F32) for i in range(8)]
        VG = [B[4][:, 0:256], B[6][:, 0:256]]
        PT = [B[4][:, 256:320].bitcast(BF16), B[6][:, 256:320].bitcast(BF16)]
        SC = [B[4][:, 320:448], B[6][:, 320:448]]
        TR = B[3][:, 256:320].bitcast(BF16)
        UR = [B[5][:, i * 128:(i + 1) * 128] for i in range(4)] + [B[7][:, i * 128:(i + 1) * 128] for i in range(4)]
        cret = C.cret
        bF = {nm: Buf() for nm in F}
        bwh = [Buf(), Buf()]; bwo = [Buf(), Buf()]; bqc = [Buf(), Buf()]; bkc = [Buf(), Buf()]; beB = [Buf(), Buf()]
        bvtok = [Buf(), Buf()]; bgsil = [Buf(), Buf()]; bktl = [Buf(), Buf()]; bscm = [Buf(), Buf()]
        bqz = [Buf(), Buf()]; bkz = [Buf(), Buf()]; bSdb = [Buf(), Buf()]; bgo = [Buf(), Buf()]
        bS = Buf(); bS0 = Buf(); bon = Buf(); bgoT = Buf(); bgn = Buf(); blb = Buf(); bst4 = Buf()
        bB = [PBuf() for _ in range(8)]
        bVG = [bB[4], bB[6]]; bPT = [bB[4], bB[6]]; bSC = [bB[4], bB[6]]; bTR = bB[3]; bUR = [bB[5]] * 4 + [bB[7]] * 4
        bx = [[Buf() for _ in TT] for _ in range(8)]
        s_wh = [P.slot(), P.slot()]; s_wo = [P.slot(), P.slot()]; s_gn = P.slot(); s_lb = P.slot()
        s_S0 = P.slot(); s_so = P.slot(); s_sp = P.slot()

        def sigmoid_act(dst, src, bdst, bsrc):
            bop(P, "act", lambda e: e.activation(out=dst, in_=src, func=AF.Exp, scale=-1.0), r=[bsrc], w=[bdst])
            bop(P, "act", lambda e: e.activation(out=dst, in_=dst, func=AF.Ln, bias=C.one[:dst.shape[0], 0:1]), r=[bdst], w=[bdst])
            bop(P, "act", lambda e: e.activation(out=dst, in_=dst, func=AF.Exp, scale=-1.0), r=[bdst], w=[bdst])

        bop(P, "sp", lambda e: e.dma_start(out=lb[:], in_=lbl), w=[blb], slot=s_lb)
        if j == 0:
            bop(P, "dve", lambda e: e.memset(lbv[:], 0.0), w=[blb])
            bop(P, "dve", lambda e: e.memset(oml[:], 1.0), w=[blb])
        else:
            bop(P, "dve", lambda e: e.tensor_tensor(out=lbv[:], in0=lb[:, 0, :], in1=lb[:, 1, :], op=ALU.subtract), r=[blb], w=[blb])
            bop(P, "act", lambda e: e.activation(out=oml[:], in_=lbv[:], func=AF.Exp), r=[blb], w=[blb])
            bop(P, "dve", lambda e: e.tensor_scalar(out=lbv[:], in0=oml[:], scalar1=1.0, scalar2=None, op0=ALU.add), r=[blb], w=[blb])
            bop(P, "dve", lambda e: e.reciprocal(out=lbv[:], in_=lbv[:]), r=[blb], w=[blb])
            bop(P, "dve", lambda e: e.tensor_tensor(out=oml[:], in0=oml[:], in1=lbv[:], op=ALU.mult), r=[blb], w=[blb])

        def load_w(h):
            sl = h % 2
            bop(P, "pool", lambda e, sl=sl, h=h: e.dma_start(out=wh[sl][:].rearrange("p k n -> p (k n)"), in_=hwin[h], max_dma_last_dim=8192),
                w=[bwh[sl]], slot=s_wh[sl])
            bop(P, "pool", lambda e, sl=sl, h=h: e.dma_start(out=wo[sl][:], in_=hwout[h], max_dma_last_dim=8192), w=[bwo[sl]], slot=s_wo[sl])

        load_w(0)
        ugl = 0
        jobs = [(h, ti) for h in range(HG_H) for ti in range(len(TT))]

        def head_setup(h):
            if h + 1 < HG_H:
                load_w(h + 1)
            bop(P, "sp", lambda e: e.dma_start(out=gn[:], in_=hnorm[h].partition_broadcast(128)), w=[bgn], slot=s_gn)
            bop(P, "pool", lambda e: e.memset(S[:], 0.0), w=[bS])
            bop(P, "sp", lambda e: e.dma_start(out=S0[:], in_=st_in[:, h, :, :].rearrange("i d e -> d i e")), w=[bS0], slot=s_S0)

        def make_s1(h, ti, kp):
            sl = h % 2
            t0, n = TT[ti]
            sample = (ti == 4)
            CL = 4 if sample else 32
            nch = n // CL
            CM = CH_CMS if sample else CH_CMP
            qcj, kcj, eBj = qc[kp], kc[kp], eB[kp]

            def p0():
                for qi in range(2):
                    for k in range(8):
                        bop(P, "pe", lambda e, qi=qi, k=k: e.matmul(B[qi][:, :n], lhsT=wh[sl][:, k, qi * 128:(qi + 1) * 128], rhs=C.xn[:, k, t0:t0 + n], start=(k == 0), stop=(k == 7)),
                            r=[bwh[sl]], w=[bB[qi]])

            def p1():
                sigmoid_act(F["qs"][:, :n], B[0][:, :n], bF["qs"], bB[0])
                bop(P, "act", lambda e: e.activation(out=F["ez"][:, :n], in_=B[1][:, :n], func=AF.Exp, scale=-1.0), r=[bB[1]], w=[bF["ez"]])
                bop(P, "act", lambda e: e.activation(out=F["r"][:, :n], in_=F["ez"][:, :n], func=AF.Ln, bias=C.one[:, 0:1]), r=[bF["ez"]], w=[bF["r"]])
                bop(P, "act", lambda e: e.activation(out=F["r"][:, :n], in_=F["r"][:, :n], func=AF.Exp, scale=-1.0), r=[bF["r"]], w=[bF["r"]])

            def p2():
                bop(P, "dve", lambda e: e.tensor_tensor(out=F["qs"][:, :n], in0=B[0][:, :n], in1=F["qs"][:, :n], op=ALU.mult), r=[bF["qs"], bB[0]], w=[bF["qs"]])
                bop(P, "dve", lambda e: e.tensor_scalar(out=F["f"][:, :n], in0=F["r"][:, :n], scalar1=oml[:, h:h + 1], scalar2=lbv[:, h:h + 1], op0=ALU.mult, op1=ALU.add),
                    r=[bF["r"], blb], w=[bF["f"]])
                bop(P, "dve", lambda e: e.scalar_tensor_tensor(out=F["kk"][:, :n], in0=F["ez"][:, :n], scalar=oml[:, h:h + 1], in1=F["r"][:, :n], op0=ALU.mult, op1=ALU.mult),
                    r=[bF["ez"], bF["r"], blb], w=[bF["kk"]])
                bop(P, "act", lambda e: e.activation(out=F["f"][:, :n], in_=F["f"][:, :n], func=AF.Ln), r=[bF["f"]], w=[bF["f"]])

            def p3():
                bop(P, "dve", lambda e: e.tensor_tensor_scan(out=F["b"][:, :n], data0=cret[:, CM:CM + n], data1=F["f"][:, :n], initial=0.0, op0=ALU.mult, op1=ALU.add),
                    r=[bF["f"]], w=[bF["b"]])
                bop(P, "pool", lambda e: e.tensor_tensor(
                    out=F["d1"][:, :n].rearrange("p (c s) -> p c s", s=CL), in0=F["b"][:, :n].rearrange("p (c s) -> p c s", s=CL),
                    in1=F["b"][:, :n].rearrange("p (c s) -> p c s", s=CL)[:, :, CL - 1:CL].broadcast_to([128, nch, CL]), op=ALU.subtract),
                    r=[bF["b"]], w=[bF["d1"]])

            def p4():
                bop(P, "act", lambda e: e.activation(out=F["X"][:, :n], in_=F["d1"][:, :n], func=AF.Exp), r=[bF["d1"]], w=[bF["X"]])
                bop(P, "act", lambda e: e.activation(out=F["Y"][:, :n], in_=F["d1"][:, :n], func=AF.Exp, scale=-1.0), r=[bF["d1"]], w=[bF["Y"]])
                bop(P, "act", lambda e: e.activation(out=eBj[:, :nch], in_=F["b"][:, :n].rearrange("p (c s) -> p c s", s=CL)[:, :, CL - 1], func=AF.Exp),
                    r=[bF["b"]], w=[beB[kp]])

            def p5():
                bop(P, "pool", lambda e: e.tensor_tensor(out=qcj[:, :n], in0=F["qs"][:, :n], in1=F["X"][:, :n], op=ALU.mult), r=[bF["qs"], bF["X"]], w=[bqc[kp]])
                bop(P, "pool", lambda e: e.tensor_tensor(out=kcj[:, :n], in0=F["kk"][:, :n], in1=F["Y"][:, :n], op=ALU.mult), r=[bF["kk"], bF["Y"]], w=[bkc[kp]])

            return [p0, p1, p2, p3, p4, p5]

        def make_blocks(h, ti, kp):
            sl = h % 2
            t0, n = TT[ti]
            sample = (ti == 4)
            CL = 4 if sample else 32
            qcj, kcj, eBj = qc[kp], kc[kp], eB[kp]
            blocks = [(0, 64)] if sample else [(c * 128, 128) for c in range(4)]
            MC = CH_MS if sample else CH_MP
            nbc = 16 if sample else 4
            if sample:
                bmask = C.bm
                rmv = cret[:64, CR_RM:CR_RM + 16]
            else:
                bmask = C.bmp
                rmv = cret[:, CH_RMP:CH_RMP + 4]
            ubase = {}

            def A_pe(ci):
                c0, nb = blocks[ci]
                pb = ci % 2
                a0 = t0 + c0
                for k in range(8):
                    bop(P, "pe", lambda e, k=k: e.matmul(VG[pb][:nb, :], lhsT=C.xn[:, k, a0:a0 + nb], rhs=wh[sl][:, k, 256:512], start=(k == 0), stop=(k == 7)),
                        r=[bwh[sl]], w=[bVG[pb]])
                bop(P, "pe", lambda e: e.transpose(out=PT[pb][:nb, :], in_=kcj[:, c0:c0 + nb], identity=C.ident), r=[bkc[kp]], w=[bPT[pb]])
                bop(P, "pe", lambda e: e.matmul(SC[pb][:nb, :nb], lhsT=kcj[:, c0:c0 + nb], rhs=qcj[:, c0:c0 + nb], start=True, stop=True), r=[bkc[kp], bqc[kp]], w=[bSC[pb]])

            def A_ev1(ci):
                nonlocal ugl
                c0, nb = blocks[ci]
                pb = ci % 2
                bop(P, "dve", lambda e: e.tensor_copy(out=ktl[pb][:nb, :], in_=PT[pb][:nb, :]), r=[bPT[pb]], w=[bktl[pb]])
                bop(P, "act", lambda e: e.activation(out=vtok[pb][:nb, :], in_=VG[pb][:nb, 0:128], func=AF.Copy), r=[bVG[pb]], w=[bvtok[pb]])
                kzv = kz[pb][:nb, 0:nbc * 128].rearrange("p (c d) -> p c d", c=nbc)
                bop(P, "pool", lambda e: e.tensor_tensor(out=kzv, in0=ktl[pb][:nb, :].unsqueeze(1).broadcast_to([nb, nbc, 128]), in1=rmv.unsqueeze(2).broadcast_to([nb, nbc, 128]), op=ALU.mult),
                    r=[bktl[pb]], w=[bkz[pb]])
                ubase[ci] = ugl
                if nbc <= 4:
                    for c in range(nbc):
                        u = 4 * pb + c
                        bop(P, "pe", lambda e, c=c, u=u: e.matmul(UR[u], lhsT=kzv[:, c, :], rhs=vtok[pb][:nb, :], start=True, stop=True),
                            r=[bkz[pb], bvtok[pb]], w=[bUR[u]])
                ugl += nbc

            def A_ev2(ci):
                c0, nb = blocks[ci]
                pb = ci % 2
                qzv = qz[pb][:, 0:nbc * nb].rearrange("p (c t) -> p c t", c=nbc)
                bop(P, "pool", lambda e: e.tensor_tensor(out=qzv, in0=qcj[:, c0:c0 + nb].unsqueeze(1).broadcast_to([128, nbc, nb]), in1=bmask, op=ALU.mult),
                    r=[bqc[kp]], w=[bqz[pb]])
                bop(P, "dve", lambda e: e.tensor_tensor(out=scm[pb][:nb, :nb], in0=SC[pb][:nb, :nb], in1=cret[:nb, MC:MC + nb], op=ALU.mult), r=[bSC[pb]], w=[bscm[pb]])
                sigmoid_act(gsil[pb][:nb, :], VG[pb][:nb, 128:256], bgsil[pb], bVG[pb])
                bop(P, "dve", lambda e: e.tensor_tensor(out=gsil[pb][:nb, :], in0=VG[pb][:nb, 128:256], in1=gsil[pb][:nb, :], op=ALU.mult), r=[bgsil[pb], bVG[pb]], w=[bgsil[pb]])

            def B_chain(ci):
                c0, nb = blocks[ci]
                pb = ci % 2
                kzv = kz[pb][:nb, 0:nbc * 128].rearrange("p (c d) -> p c d", c=nbc)
                for c in range(nbc):
                    ec = (c0 // CL + c)
                    u = (4 * pb + c) if nbc <= 4 else (4 * (c % 2) + (c // 2) % 4)
                    Sin = S0[:, c, :] if sample else S[:, :]
                    bSin = bS0 if sample else bS
                    if nbc > 4:
                        bop(P, "pe", lambda e, c=c, u=u: e.matmul(UR[u], lhsT=kzv[:, c, :], rhs=vtok[pb][:nb, :], start=True, stop=True),
                            r=[bkz[pb], bvtok[pb]], w=[bUR[u]])
                    bop(P, "dve", lambda e, c=c, ec=ec, Sin=Sin: e.tensor_scalar(out=Sdb[pb][:, c, :], in0=Sin, scalar1=eBj[:, ec:ec + 1], scalar2=None, op0=ALU.mult),
                        r=[bSin, beB[kp]], w=[bSdb[pb]])
                    bop(P, "dve", lambda e, ec=ec, u=u, Sin=Sin: e.scalar_tensor_tensor(out=Sin, in0=Sin, scalar=eBj[:, ec:ec + 1], in1=UR[u], op0=ALU.mult, op1=ALU.add),
                        r=[bUR[u], beB[kp]], w=[bSin])

            def B_pe(ci):
                c0, nb = blocks[ci]
                pb = ci % 2
                qzv = qz[pb][:, 0:nbc * nb].rearrange("p (c t) -> p c t", c=nbc)
                bop(P, "pe", lambda e: e.matmul(B[2][:nb, 0:128], lhsT=scm[pb][:nb, :nb], rhs=vtok[pb][:nb, :], start=True, stop=False), r=[bscm[pb], bvtok[pb]], w=[bB[2]])
                for c in range(nbc):
                    bop(P, "pe", lambda e, c=c: e.matmul(B[2][:nb, 0:128], lhsT=qzv[:, c, :], rhs=Sdb[pb][:, c, :], start=False, stop=(c == nbc - 1)),
                        r=[bqz[pb], bSdb[pb]], w=[bB[2]])

            def B_norm(ci):
                c0, nb = blocks[ci]
                pb = ci % 2
                bop(P, "act", lambda e: e.activation(out=on[:nb, :], in_=B[2][:nb, 0:128], func=AF.Square, accum_out=st4[:nb, 0:1]), r=[bB[2]], w=[bon, bst4])
                bop(P, "act", lambda e: e.activation(out=st4[:nb, 2:3], in_=st4[:nb, 0:1], func=AF.Ln, scale=1.0 / 128, bias=C.epsc[:nb, 0:1]), r=[bst4], w=[bst4])
                bop(P, "act", lambda e: e.activation(out=st4[:nb, 3:4], in_=st4[:nb, 2:3], func=AF.Exp, scale=-0.5), r=[bst4], w=[bst4])
                bop(P, "dve", lambda e: e.scalar_tensor_tensor(out=on[:nb, :], in0=B[2][:nb, 0:128], scalar=st4[:nb, 3:4], in1=gn[:nb, :], op0=ALU.mult, op1=ALU.mult),
                    r=[bB[2], bst4, bgn], w=[bon])
                bop(P, "pool", lambda e: e.tensor_tensor(out=go[pb][:nb, :], in0=on[:nb, :], in1=gsil[pb][:nb, :], op=ALU.mult), r=[bon, bgsil[pb]], w=[bgo[pb]])

            def stage_T(ci):
                c0, nb = blocks[ci]
                pb = ci % 2
                bop(P, "pe", lambda e: e.transpose(out=TR[:, :nb], in_=go[pb][:nb, :], identity=C.ident[:nb, :nb]), r=[bgo[pb]], w=[bTR])
                bop(P, "act", lambda e: e.activation(out=goT[:, c0:c0 + nb], in_=TR[:, :nb], func=AF.Copy), r=[bTR], w=[bgoT])

            def wout():
                for m in range(8):
                    wb = 3 if m % 2 == 0 else 2
                    bop(P, "pe", lambda e, m=m, wb=wb: e.matmul(B[wb][:, :n], lhsT=wo[sl][:, m * 128:(m + 1) * 128], rhs=goT[:, :n], start=True, stop=True),
                        r=[bwo[sl], bgoT], w=[bB[wb]])
                    bop(P, "dve", lambda e, m=m, wb=wb: e.tensor_tensor(out=C.xres[:, m, t0:t0 + n], in0=C.xres[:, m, t0:t0 + n], in1=B[wb][:, :n], op=ALU.add),
                        r=[bB[wb]], w=[bx[m][ti]])
                if ti == 3:
                    bop(P, "sp", lambda e: e.dma_start(out=st_out_p[h], in_=S[:, :]), r=[bS], slot=s_sp)
                if sample:
                    bop(P, "sp", lambda e: e.dma_start(out=st_out_s[:, h, :, :].rearrange("i d e -> d i e"), in_=S0[:, :, :]), r=[bS0], slot=s_so)

            mk = lambda fn, ci: (lambda: fn(ci))
            if len(blocks) == 1:
                sched = [mk(A_pe, 0), mk(A_ev1, 0), mk(A_ev2, 0), mk(B_chain, 0), mk(B_pe, 0), mk(B_norm, 0), mk(stage_T, 0)]
                slots_after = {2: [0, 1], 4: [2, 3], 6: [4, 5]}
            else:
                sched = [mk(A_pe, 0), mk(A_ev1, 0), mk(A_ev2, 0), mk(A_pe, 1),
                         mk(A_ev1, 1), mk(B_chain, 0), mk(B_pe, 0), mk(A_ev2, 1), mk(B_norm, 0), mk(A_pe, 2),
                         mk(A_ev1, 2), mk(B_chain, 1), mk(B_pe, 1), mk(A_ev2, 2), mk(B_norm, 1), mk(stage_T, 0), mk(A_pe, 3),
                         mk(A_ev1, 3), mk(B_chain, 2), mk(B_pe, 2), mk(A_ev2, 3), mk(B_norm, 2), mk(stage_T, 1),
                         mk(B_chain, 3), mk(B_pe, 3), mk(B_norm, 3), mk(stage_T, 2), mk(stage_T, 3)]
                slots_after = {9: [0, 1], 16: [2], 22: [3, 4], 26: [5]}
            return sched, slots_after, wout

        for p in make_s1(jobs[0][0], jobs[0][1], 0):
            p()
        for kj, (h, ti) in enumerate(jobs):
            kp = kj % 2
            if ti == 0:
                head_setup(h)
            sched, slots_after, wout = make_blocks(h, ti, kp)
            nxt = make_s1(jobs[kj + 1][0], jobs[kj + 1][1], 1 - kp) if kj + 1 < len(jobs) else None
            for si, stage in enumerate(sched):
                stage()
                if nxt is not None:
                    for pi in slots_after.get(si, []):
                        nxt[pi]()
            wout()
        P.flush()


def build_program(cfg):
    nc = bass.Bass("TRN2", target_bir_lowering=False)
    dr = lambda name, shape, kind="ExternalInput", dt=F32: nc.dram_tensor(name, shape, dt, kind=kind).ap()
    xT = dr("xT", [128, 8, NTOK])
    gains_d = dr("gains", [128, 13 * 8])
    cbf_d = dr("cbf", [128, 256 + 1024 + 512], dt=BF16)
    cret_d = dr("cret", [128, CRW2])
    cs_d = dr("cs", [2, 128, NTOK])
    wup_d = dr("wup", [8, NF, 128, 2048])
    wdn_d = dr("wdn", [8, 2, 128, 11 * 1024])
    rwin_d = dr("rwin", [2, 4, 128, 12288])
    rwout_d = dr("rwout", [2, 4, 128, 4096])
    rnorm_d = dr("rnorm", [2, 4, 512])
    sret_d = dr("sret", [2, NSS, 4, 256, 512])
    hwin_d = dr("hwin", [2, 8, 128, 4096])
    hwout_d = dr("hwout", [2, 8, 128, 1024])
    hnorm_d = dr("hnorm", [2, 8, 128])
    lbl_d = dr("lbl", [128, 2, 8])
    shg_d = dr("shg", [2, NSS, 8, 128, 128])
    nhp_d = dr("nhp", [2, 8, 128, 128], kind="ExternalOutput")
    nhs_d = dr("nhs", [2, NSS, 8, 128, 128], kind="ExternalOutput")
    yT = dr("yT", [128, 8, NTOK], kind="ExternalOutput")
    nrp_d = dr("nrp", [2, 4, 256, 512], kind="ExternalOutput")
    nrs_d = dr("nrs", [2, NSS, 4, 256, 512], kind="ExternalOutput")

    with ExitStack() as st:
        P = Prog(nc, st)
        C = Ctx()
        C.P, C.nc = P, nc
        C.dbg = None
        if cfg.get("debug"):
            C.dbg = {"want": set(cfg["debug"]), "seen": {}, "off": {"f": 0, "b": 0}, "slot": P.slot(),
                     "f": dr("dbgf", [128, 8192], kind="ExternalOutput"), "b": dr("dbgb", [128, 8192], kind="ExternalOutput", dt=BF16)}
        cfg["_dbg"] = C.dbg
        sb = lambda name, shape, dt: st.enter_context(nc.sbuf_tensor(name, shape, dt))
        C.xres = sb("xres", [128, 8, NTOK], F32)
        C.xn = sb("xn", [128, 8, NTOK], BF16)
        C.gains = sb("gains_sb", [128, 13 * 8], F32)
        cbf = sb("cbf_sb", [128, 256 + 1024 + 512], BF16)
        C.ident = cbf[:, 0:128]
        C.ones = cbf[:, 128:256]
        C.bm = cbf[:, 256:1280].rearrange("p (a t) -> p a t", a=16)
        C.bmp = cbf[:, 1280:1792].rearrange("p (a t) -> p a t", a=4)
        C.epsc = sb("epsc", [128, 2], F32)
        C.one = sb("onec", [128, 2], F32)
        C.cret = sb("cret_sb", [128, CRW2], F32)

        s_in = P.slot()
        for k in range(8):
            P.dma("sp", lambda e, k=k: e.dma_start(out=C.xres[:, k, :], in_=xT[:, k, :]), s_in)
        P.dma("sp", lambda e: e.dma_start(out=C.gains[:], in_=gains_d), s_in)
        P.dma("sp", lambda e: e.dma_start(out=cbf[:], in_=cbf_d), s_in)
        P.dma("sp", lambda e: e.dma_start(out=C.cret[:], in_=cret_d), s_in)
        P.op("pool", lambda e: e.memset(C.epsc[:], EPS))
        P.op("pool", lambda e: e.memset(C.one[:], 1.0))
        P.flush()

        for blk in cfg["blocks"]:
            if blk[0] == "ffn":
                _, l, i = blk
                emit_ffn(C, l * 3 + (0 if i == 0 else 2), wup_d[l * 2 + i], wdn_d[l * 2 + i])
            elif blk[0] == "ret":
                _, l = blk
                j = l // 2
                emit_normphase(C, l * 3 + 1)
                emit_ret(C, j, rwin_d[j], rwout_d[j], rnorm_d[j], cs_d, sret_d[j], nrp_d[j], nrs_d[j])
            elif blk[0] == "hg":
                _, l = blk
                j = l // 2
                emit_normphase(C, l * 3 + 1)
                emit_hg(C, j, hwin_d[j], hwout_d[j], hnorm_d[j], lbl_d, shg_d[j], nhp_d[j], nhs_d[j])

        with ExitStack() as st2:
            sb2, pt2 = mk_alloc(C, st2)
            ph = Ctx()
            ph.sq = [sb2("sq%d" % i, [128, 8, 512], BF16) for i in range(2)]
            ph.rs = [sb2("rs%d" % i, [128, 512], F32) for i in range(2)]
            ph.psn = pt2("psn", [128, 512], F32)
            yo = [sb2("yo%d" % i, [128, 8, 512], F32) for i in range(2)]
            s_out = [P.slot(), P.slot()]
            yo_rd = [None, None]
            if cfg.get("final_norm", True):
                def out_fn2(ti, k, t0, n, rsb, r):
                    b = ti % 2
                    o = P.op("dve", lambda e: e.scalar_tensor_tensor(
                        out=yo[b][:, k, :n], in0=C.xres[:, k, t0:t0 + n], scalar=C.gains[:, 96 + k:96 + k + 1],
                        in1=rsb[:, :n], op0=ALU.mult, op1=ALU.mult), [r, yo_rd[b]])
                    if k == 7:
                        yo_rd[b] = P.dma("sp", lambda e: e.dma_start(out=yT[:, :, t0:t0 + n], in_=yo[b][:, :, :n]), s_out[b], [o])
                    return o
                emit_norm(C, ph, 12, out_fn2)
            else:
                for k in range(8):
                    P.dma("sp", lambda e, k=k: e.dma_start(out=yT[:, k, :], in_=C.xres[:, k, :]), s_out[0])
            P.flush()
    return nc


def host_consts():
    import ml_dtypes
    c = np.zeros((128, 256 + 1024 + 512), np.float32)
    c[:, 0:128] = np.eye(128, dtype=np.float32)
    c[:, 128:256] = 1.0
    bm = np.zeros((16, 64), np.float32)
    for i in range(16):
        bm[i, 4 * i:4 * i + 4] = 1.0
    c[:, 256:1280] = bm.reshape(1, 1024)
    bmp = np.zeros((4, 128), np.float32)
    for i in range(4):
        bmp[i, 32 * i:32 * i + 32] = 1.0
    c[:, 1280:1792] = bmp.reshape(1, 512)
    out = {"cbf": c.astype(ml_dtypes.bfloat16)}
    cr = np.zeros((128, CRW2), np.float64)
    t = np.arange(128)
    ts = np.arange(64)
    for h, g in enumerate(ret_gammas()):
        lg = np.log(np.float64(g))
        mp = np.where(t[:, None] <= t[None, :], np.exp(-(t[:, None] + 1.0) * lg), 0.0)
        cr[:, CR_MP + h * 128:CR_MP + (h + 1) * 128] = mp
        same = (ts[:, None] // 4) == (ts[None, :] // 4)
        ms = np.where(same & ((ts[:, None] % 4) <= (ts[None, :] % 4)), np.exp(-((ts[:, None] % 4) + 1.0) * lg), 0.0)
        cr[:64, CR_MS + h * 64:CR_MS + (h + 1) * 64] = ms
        cr[:, CR_KDP + h] = np.exp((127.0 - t) * lg)
        cr[:, CR_EPP + h] = EPS * np.exp(-2.0 * (t + 1.0) * lg)
        cr[:64, CR_KDS + h] = np.exp((3.0 - (ts % 4)) * lg)
        cr[:64, CR_EPS + h] = EPS * np.exp(-2.0 * ((ts % 4) + 1.0) * lg)
    for i in range(16):
        cr[4 * i:4 * i + 4, CR_RM + i] = 1.0
    cr[:, CH_MP:CH_MP + 128] = ((t[:, None] // 32) == (t[None, :] // 32)) & (t[:, None] <= t[None, :])
    cr[:64, CH_MS:CH_MS + 64] = ((ts[:, None] // 4) == (ts[None, :] // 4)) & (ts[:, None] <= ts[None, :])
    for i in range(4):
        cr[32 * i:32 * i + 32, CH_RMP + i] = 1.0
    cr[:, CH_CMP:CH_CMP + 512] = (np.arange(512) % 32 != 0)[None, :]
    cr[:, CH_CMS:CH_CMS + 64] = (np.arange(64) % 4 != 0)[None, :]
    out["cret"] = cr.astype(np.float32)
    half = 128
    inv_freq = (np.float32(10000.0) ** (-np.arange(half, dtype=np.float32) / np.float32(half))).astype(np.float32)
    pos = np.concatenate([np.arange(SEQ, dtype=np.float32), np.tile(np.float32(16384.0) + np.arange(DEC, dtype=np.float32), NSS)])
    ang = (pos[None, :] * inv_freq[:, None]).astype(np.float32)
    out["cs"] = np.stack([np.cos(ang), np.sin(ang)]).astype(np.float32)
    return out


def host_weights(inp):
    w = {}
    f32 = lambda a: np.asarray(a, np.float32)
    up = f32(inp["ffn_w_up"]).reshape(8, 8, 128, 2, NF, 128)
    w["wup"] = np.ascontiguousarray(up.transpose(0, 4, 2, 1, 3, 5)).reshape(8, NF, 128, 2048)
    dn = f32(inp["ffn_w_down"]).reshape(8, 2, 11, 128, 1024)
    w["wdn"] = np.ascontiguousarray(dn.transpose(0, 1, 3, 2, 4)).reshape(8, 2, 128, 11 * 1024)
    g = np.concatenate([f32(inp["norm_gain"]).reshape(12, 1024), f32(inp["final_norm"]).reshape(1, 1024)], 0)
    w["gains"] = np.ascontiguousarray(g.reshape(13, 8, 128).transpose(2, 0, 1)).reshape(128, 104)
    wi = f32(inp["ret_w_in"]).reshape(2, 8, 128, 6144)
    parts = []
    for h in range(4):
        parts.append(np.concatenate([wi[..., h * 256:(h + 1) * 256], wi[..., 1024 + h * 256:1024 + (h + 1) * 256],
                                     wi[..., 2048 + h * 512:2048 + (h + 1) * 512], wi[..., 4096 + h * 512:4096 + (h + 1) * 512]], -1))
    wih = np.stack(parts, 1)
    w["rwin"] = np.ascontiguousarray(wih.transpose(0, 1, 3, 2, 4)).reshape(2, 4, 128, 12288)
    wo = f32(inp["ret_w_out"]).reshape(2, 4, 4, 128, 1024)
    w["rwout"] = np.ascontiguousarray(wo.transpose(0, 1, 3, 2, 4)).reshape(2, 4, 128, 4096)
    w["rnorm"] = f32(inp["ret_norm"])
    hi = f32(inp["hg_w_in"]).reshape(2, 8, 128, 4, 8, 128)
    w["hwin"] = np.ascontiguousarray(hi.transpose(0, 4, 2, 1, 3, 5)).reshape(2, 8, 128, 4096)
    w["hwout"] = np.ascontiguousarray(f32(inp["hg_w_out"]).reshape(2, 8, 128, 1024))
    w["hnorm"] = f32(inp["hg_norm"])
    w["lbl"] = np.ascontiguousarray(f32(inp["hg_lb_logits"]).reshape(2, 8, 128).transpose(2, 0, 1))
    return w


def host_core_inputs(inp, c):
    xp = np.asarray(inp["x_prompt"], np.float32)[c]
    xs = np.asarray(inp["x_sample"], np.float32)[c * NSS:(c + 1) * NSS].reshape(NSS * DEC, D)
    x = np.concatenate([xp, xs], 0)
    xT = np.ascontiguousarray(x.T.reshape(8, 128, NTOK).transpose(1, 0, 2))
    m = {"xT": xT}
    m["sret"] = np.ascontiguousarray(np.asarray(inp["state_ret"], np.float32)[:, c * NSS:(c + 1) * NSS])
    m["shg"] = np.ascontiguousarray(np.asarray(inp["state_hgrn"], np.float32)[:, c * NSS:(c + 1) * NSS])
    return m


def full_cfg():
    blocks = []
    for l in range(4):
        blocks.append(("ffn", l, 0))
        blocks.append(("ret", l) if l % 2 == 0 else ("hg", l))
        blocks.append(("ffn", l, 1))
    return {"blocks": blocks, "final_norm": True}


def kernel(**inputs):
    nc = build_program(full_cfg())
    shared = {}
    shared.update(host_consts())
    shared.update(host_weights(inputs))
    in_maps = []
    for c in range(8):
        m = dict(shared)
        m.update(host_core_inputs(inputs, c))
        in_maps.append(m)
    res = run_bass_kernel_spmd(nc, in_maps, core_ids=list(range(8)))
    rs = res.results
    y_prompt = np.empty((8, SEQ, D), np.float32)
    y_sample = np.empty((8 * NSS, DEC, D), np.float32)
    nrp = np.empty((2, 8, 4, 256, 512), np.float32)
    nrs = np.empty((2, 8 * NSS, 4, 256, 512), np.float32)
    nhp = np.empty((2, 8, 8, 128, 128), np.float32)
    nhs = np.empty((2, 8 * NSS, 8, 128, 128), np.float32)
    for c in range(8):
        r = rs[c]
        y = np.asarray(r["yT"]).transpose(1, 0, 2).reshape(D, NTOK).T
        y_prompt[c] = y[:SEQ]
        y_sample[c * NSS:(c + 1) * NSS] = y[SEQ:].reshape(NSS, DEC, D)
        nrp[:, c] = r["nrp"]
        nrs[:, c * NSS:(c + 1) * NSS] = r["nrs"]
        nhp[:, c] = r["nhp"]
        nhs[:, c * NSS:(c + 1) * NSS] = r["nhs"]
    return (y_prompt, y_sample, nrp, nrs, nhp, nhs)
```

```python
import numpy as np
from contextlib import ExitStack
import concourse.bass as bass
import concourse.mybir as mybir
from concourse.bass_utils import run_bass_kernel_spmd

F32 = mybir.dt.float32
BF16 = mybir.dt.bfloat16
AF = mybir.ActivationFunctionType
ALU = mybir.AluOpType

D = 1024
SEQ = 2048
NSS = 16
DEC = 4
NTOK = SEQ + NSS * DEC
DFF = 2816
NF = DFF // 128
EPS = 1e-6
TT = [(0, 512), (512, 512), (1024, 512), (1536, 512), (2048, 64)]


class Op:
    __slots__ = ("eng", "fn", "deps", "pos", "sem", "val", "need_sig", "is_dma", "done")


class Slot:
    def __init__(self, sem):
        self.sem = sem
        self.count = 0


class Prog:
    ENGS = ["pe", "act", "dve", "pool", "sp"]
    ENGOBJ = {"pe": "tensor", "act": "scalar", "dve": "vector", "pool": "gpsimd", "sp": "sync"}

    def __init__(self, nc, stack):
        self.nc = nc
        self.stack = stack
        self.q = {e: [] for e in self.ENGS}
        self.esem = {e: stack.enter_context(nc.semaphore("s_" + e)) for e in ["pe", "act", "dve", "pool"]}
        self.ecount = {e: 0 for e in self.ENGS}
        self.slots = []
        self.nphase = 0

    def slot(self):
        s = Slot(self.stack.enter_context(self.nc.semaphore("d%d" % len(self.slots))))
        self.slots.append(s)
        return s

    def op(self, eng, fn, deps=()):
        o = Op()
        o.eng = eng
        o.fn = fn
        o.deps = [d for d in deps if d is not None]
        o.pos = len(self.q[eng])
        o.is_dma = False
        o.need_sig = False
        o.sem = None
        o.val = None
        o.done = False
        self.q[eng].append(o)
        return o

    def dma(self, eng, fn, slot, deps=()):
        o = self.op(eng, fn, deps)
        o.is_dma = True
        slot.count += 16
        o.sem = slot.sem
        o.val = slot.count
        return o

    def _needs_wait(self, o, d):
        if d.done:
            return False
        if d.is_dma:
            return True
        if d.eng == o.eng:
            if o.eng == "pe":
                return False
            return (o.pos - d.pos) <= 2
        return True

    def flush(self):
        nc = self.nc
        drain_deps = []
        for e in self.ENGS:
            last = {}
            for o in self.q[e]:
                if o.is_dma:
                    last[id(o.sem)] = o
            drain_deps += list(last.values())
        self.op("sp", lambda e: e.nop(), drain_deps)
        for e in self.ENGS:
            for o in self.q[e]:
                for d in o.deps:
                    if not d.is_dma and self._needs_wait(o, d):
                        d.need_sig = True
        for e in self.ENGS:
            c = self.ecount[e]
            for o in self.q[e]:
                if o.is_dma:
                    continue
                if o.need_sig:
                    assert e != "sp"
                    c += 1
                    o.sem = self.esem[e]
                    o.val = c
            self.ecount[e] = c
        self.nphase += 1
        with nc.Block() as block:
            for e in self.ENGS:
                ops = self.q[e]
                if not ops:
                    continue

                def body(eng, ops=ops):
                    waited = {}
                    for o in ops:
                        need = {}
                        for d in o.deps:
                            if not self._needs_wait(o, d):
                                continue
                            key = id(d.sem)
                            if key not in need or need[key][1] < d.val:
                                need[key] = (d.sem, d.val)
                        for key, (sem, val) in need.items():
                            if waited.get(key, 0) >= val:
                                continue
                            eng.wait_ge(sem, val)
                            waited[key] = val
                        ins = o.fn(eng)
                        if o.is_dma:
                            ins.then_inc(o.sem, 16)
                        elif o.need_sig:
                            ins.then_inc(o.sem, 1)

                getattr(block, self.ENGOBJ[e])(body)
        for e in self.ENGS:
            for o in self.q[e]:
                o.done = True
                o.fn = None
                o.deps = None
            self.q[e] = []


class Ctx:
    pass


def dbg_dump(C, name, ap, bufs, ncols, bf=False):
    if not getattr(C, "dbg", None) or name in C.dbg["seen"] or name not in C.dbg["want"]:
        return
    key = "b" if bf else "f"
    off = C.dbg["off"][key]
    C.dbg["off"][key] = off + ncols
    C.dbg["seen"][name] = (key, off, ncols)
    dst = C.dbg[key][:, off:off + ncols]
    bop(C.P, "sp", lambda e: e.dma_start(out=dst, in_=ap), r=bufs, slot=C.dbg["slot"])


class Buf:
    def __init__(self):
        self.w = None
        self.r = []


class PBuf(Buf):
    excl = True


def bop(P, eng, fn, r=(), w=(), slot=None, deps=()):
    xr = [b for b in r if getattr(b, "excl", False)]
    if xr:
        r = [b for b in r if not getattr(b, "excl", False)]
        w = list(w) + [b for b in xr if b not in w]
    d = list(deps)
    for b in r:
        d.append(b.w)
    for b in w:
        d.append(b.w)
        d.extend(b.r)
    o = P.dma(eng, fn, slot, d) if slot is not None else P.op(eng, fn, d)
    for b in r:
        if not o.is_dma:
            b.r = [x for x in b.r if x.is_dma or x.eng != eng]
        b.r.append(o)
    for b in w:
        b.w = o
        b.r = []
    return o


_UID = [0]


def mk_alloc(C, st):
    _UID[0] += 1
    u = _UID[0]
    nc = C.nc
    sb = lambda name, shape, dt: st.enter_context(nc.sbuf_tensor("%s_%d" % (name, u), shape, dt))
    pt = lambda name, shape, dt: st.enter_context(nc.psum_tensor("%s_%d" % (name, u), shape, dt))
    return sb, pt


def emit_norm(C, ph, gcol, out_fn=None):
    P = C.P
    sq, rs, psn = ph.sq, ph.rs, ph.psn
    last = []
    sq_rd = [None, None]
    rs_rd = [None, None]
    ps_rd = None
    for ti, (t0, n) in enumerate(TT):
        b = ti % 2
        a = P.op("act", lambda e, b=b, t0=t0, n=n: e.activation(out=sq[b][:, :, :n], in_=C.xres[:, :, t0:t0 + n], func=AF.Square),
                 [sq_rd[b]])
        mm = None
        for k in range(8):
            mm = P.op("pe", lambda e, b=b, k=k, n=n: e.matmul(psn[:, :n], lhsT=C.ones[:], rhs=sq[b][:, k, :n], start=(k == 0), stop=(k == 7)),
                      [a, ps_rd])
        sq_rd[b] = mm
        v = P.op("act", lambda e, b=b, n=n: e.activation(out=rs[b][:, :n], in_=psn[:, :n], func=AF.Ln, scale=1.0 / D, bias=C.epsc[:, 0:1]),
                 [mm, rs_rd[b]])
        ps_rd = v
        r = P.op("act", lambda e, b=b, n=n: e.activation(out=rs[b][:, :n], in_=rs[b][:, :n], func=AF.Exp, scale=-0.5), [v])
        o = None
        for k in range(8):
            if out_fn is None:
                o = P.op("dve", lambda e, b=b, k=k, t0=t0, n=n: e.scalar_tensor_tensor(
                    out=C.xn[:, k, t0:t0 + n], in0=C.xres[:, k, t0:t0 + n], scalar=C.gains[:, gcol * 8 + k:gcol * 8 + k + 1],
                    in1=rs[b][:, :n], op0=ALU.mult, op1=ALU.mult), [r])
            else:
                o = out_fn(ti, k, t0, n, rs[b], r)
        rs_rd[b] = o
        last.append(o)
    return last


def emit_normphase(C, gcol):
    with ExitStack() as st:
        sb, pt = mk_alloc(C, st)
        ph = Ctx()
        ph.sq = [sb("sq%d" % i, [128, 8, 512], BF16) for i in range(2)]
        ph.rs = [sb("rs%d" % i, [128, 512], F32) for i in range(2)]
        ph.psn = pt("psn", [128, 512], F32)
        emit_norm(C, ph, gcol)
        C.P.flush()


def emit_ffn(C, gcol, wup, wdn):
    P, nc = C.P, C.nc
    with ExitStack() as st:
        sb, pt = mk_alloc(C, st)
        hid = sb("hid", [128, 11, NTOK], BF16)
        wu = [sb("wu%d" % i, [128, 8, 2, 128], BF16) for i in range(3)]
        wd = sb("wd", [128, 11, 1024], BF16)
        sa = [sb("sa%d" % i, [128, 512], F32) for i in range(2)]
        sqs = [sb("sqs%d" % i, [128, 512], BF16) for i in range(4)]
        rs = [sb("rs%d" % i, [128, 512], F32) for i in range(2)]
        psn = pt("psn", [128, 512], F32)
        psA = [pt("psA%d" % i, [128, 512], F32) for i in range(2)]
        psB = [pt("psB%d" % i, [128, 512], F32) for i in range(2)]
        psD = [pt("psD%d" % i, [128, 512], F32) for i in range(2)]
        bwu = [Buf() for _ in range(3)]; bwd = Buf(); bsa = [Buf(), Buf()]; bsq = [Buf() for _ in range(4)]; brs = [Buf(), Buf()]
        bpsn = PBuf(); bpsA = [PBuf(), PBuf()]; bpsB = [PBuf(), PBuf()]; bpsD = [PBuf(), PBuf()]
        bxn = [Buf() for _ in TT]; bxr = [Buf() for _ in TT]; bhid = [Buf() for _ in TT]
        s_wu = [P.slot() for _ in range(3)]
        s_wd = P.slot()

        def load_wu(fi):
            s = fi % 3
            bop(P, "pool", lambda e: e.dma_start(out=wu[s][:].rearrange("p k a c -> p (k a c)"), in_=wup[fi], max_dma_last_dim=8192), w=[bwu[s]], slot=s_wu[s])

        def load_wd(half):
            bop(P, "pool", lambda e: e.dma_start(out=wd[:].rearrange("p f n -> p (f n)"), in_=wdn[half], max_dma_last_dim=8192), w=[bwd], slot=s_wd)

        sqc = [0]

        def norm(ti):
            t0, n = TT[ti]
            b = ti % 2
            for k in range(8):
                q = sqc[0] % 4
                sqc[0] += 1
                bop(P, "act", lambda e, k=k, q=q: e.activation(out=sqs[q][:, :n], in_=C.xres[:, k, t0:t0 + n], func=AF.Square), r=[bxr[ti]], w=[bsq[q]])
                bop(P, "pe", lambda e, k=k, q=q: e.matmul(psn[:, :n], lhsT=C.ones[:], rhs=sqs[q][:, :n], start=(k == 0), stop=(k == 7)), r=[bsq[q]], w=[bpsn])
            bop(P, "act", lambda e: e.activation(out=rs[b][:, :n], in_=psn[:, :n], func=AF.Ln, scale=1.0 / D, bias=C.epsc[:, 0:1]), r=[bpsn], w=[brs[b]])
            bop(P, "act", lambda e: e.activation(out=rs[b][:, :n], in_=rs[b][:, :n], func=AF.Exp, scale=-0.5), r=[brs[b]], w=[brs[b]])
            for k in range(8):
                bop(P, "dve", lambda e, k=k: e.scalar_tensor_tensor(
                    out=C.xn[:, k, t0:t0 + n], in0=C.xres[:, k, t0:t0 + n], scalar=C.gains[:, gcol * 8 + k:gcol * 8 + k + 1],
                    in1=rs[b][:, :n], op0=ALU.mult, op1=ALU.mult), r=[brs[b], bxr[ti]], w=[bxn[ti]])

        load_wd(0)
        for fi in range(3):
            load_wu(fi)
        norm(0)
        norm(1)
        cnt = 0
        dcnt = 0
        for half in range(2):
            if half == 1:
                load_wd(1)
            for f in range(11):
                fi = half * 11 + f
                s = fi % 3
                for ti, (t0, n) in enumerate(TT):
                    b = cnt % 2
                    cnt += 1
                    for k in range(8):
                        bop(P, "pe", lambda e, b=b, s=s, k=k, t0=t0, n=n: e.matmul(psA[b][:, :n], lhsT=wu[s][:, k, 0, :], rhs=C.xn[:, k, t0:t0 + n], start=(k == 0), stop=(k == 7)),
                            r=[bwu[s], bxn[ti]], w=[bpsA[b]])
                    for k in range(8):
                        bop(P, "pe", lambda e, b=b, s=s, k=k, t0=t0, n=n: e.matmul(psB[b][:, :n], lhsT=wu[s][:, k, 1, :], rhs=C.xn[:, k, t0:t0 + n], start=(k == 0), stop=(k == 7)),
                            r=[bwu[s], bxn[ti]], w=[bpsB[b]])
                    bop(P, "act", lambda e, b=b, n=n: e.activation(out=sa[b][:, :n], in_=psA[b][:, :n], func=AF.Silu), r=[bpsA[b]], w=[bsa[b]])
                    bop(P, "dve", lambda e, b=b, f=f, t0=t0, n=n: e.tensor_tensor(out=hid[:, f, t0:t0 + n], in0=sa[b][:, :n], in1=psB[b][:, :n], op=ALU.mult),
                        r=[bsa[b], bpsB[b]], w=[bhid[ti]])
                    if fi == 0 and ti + 2 < len(TT):
                        norm(ti + 2)
                if fi + 3 < NF:
                    load_wu(fi + 3)
            for ti, (t0, n) in enumerate(TT):
                for mo in range(8):
                    b = dcnt % 2
                    dcnt += 1
                    for f in range(11):
                        bop(P, "pe", lambda e, b=b, f=f, mo=mo, t0=t0, n=n: e.matmul(psD[b][:, :n], lhsT=wd[:, f, mo * 128:(mo + 1) * 128], rhs=hid[:, f, t0:t0 + n], start=(f == 0), stop=(f == 10)),
                            r=[bwd, bhid[ti]], w=[bpsD[b]])
                    bop(P, "dve", lambda e, b=b, mo=mo, t0=t0, n=n: e.scalar_tensor_tensor(
                        out=C.xres[:, mo, t0:t0 + n], in0=psD[b][:, :n], scalar=0.5, in1=C.xres[:, mo, t0:t0 + n], op0=ALU.mult, op1=ALU.add),
                        r=[bpsD[b]], w=[bxr[ti]])
        P.flush()


RET_H = 4
CR_MP = 0
CR_MS = 512
CR_KDP = 768
CR_EPP = 772
CR_KDS = 776
CR_EPS = 780
CR_RM = 784
CRW = 800


def ret_gammas():
    return [1.0 - 2.0 ** (-5.0 - h) for h in range(RET_H)]


def emit_ret(C, j, rwin, rwout, rnorm, cs, st_in, st_out_p, st_out_s):
    P, nc = C.P, C.nc
    gam = ret_gammas()
    with ExitStack() as st:
        sb, pt = mk_alloc(C, st)
        wh = [sb("wh%d" % i, [128, 8, 1536], BF16) for i in range(2)]
        wo = sb("wo", [128, 4, 1024], BF16)
        cst = sb("cs", [128, 2, 512], F32)
        qT = sb("qT", [128, 2, 512], BF16)
        kT = sb("kT", [128, 2, 512], BF16)
        vtok = [sb("vtok%d" % i, [128, 512], BF16) for i in range(2)]
        ktl = [sb("ktl%d" % i, [128, 256], BF16) for i in range(2)]
        scm = [sb("scm%d" % i, [128, 128], BF16) for i in range(2)]
        gs = sb("gs", [128, 512], F32)
        on = sb("on", [128, 512], F32)
        go = [sb("go%d" % i, [128, 512], BF16) for i in range(2)]
        goT = sb("goT", [128, 4, 512], BF16)
        S = sb("S", [128, 2, 512], F32)
        Sb = sb("Sb", [128, 2, 512], BF16)
        gn = sb("gn", [128, 512], F32)
        qz = sb("qz", [128, 2, 16, 64], BF16)
        kz = [sb("kz%d" % i, [64, 256], BF16) for i in range(2)]
        S0f = [sb("S0f%d" % i, [128, 512], F32) for i in range(2)]
        S0b = [sb("S0b%d" % i, [128, 512], BF16) for i in range(2)]
        S0f = [t[:, :] for t in S0f] + [S[:, 0, :], S[:, 1, :]]
        S0b = [t[:, :] for t in S0b] + [Sb[:, 0, :], Sb[:, 1, :]]
        st4 = sb("st4", [128, 4], F32)
        B = [pt("b%d" % i, [128, 512], F32) for i in range(8)]
        PT = [B[6][:, 0:128].bitcast(BF16), B[3][:, 0:128].bitcast(BF16)]
        SC = [B[6][:, 128:256], B[3][:, 128:256]]
        TR = B[3][:, 256:512].bitcast(BF16)
        cret = C.cret

        bwh = [Buf(), Buf()]; bwo = Buf(); bcs = Buf(); bt12 = Buf(); bqT = Buf(); bkT = Buf()
        bvtok = [Buf(), Buf()]; bktl = [Buf(), Buf()]; bscm = [Buf(), Buf()]; bgs = Buf(); bon = Buf(); bgo = [Buf(), Buf()]; bgoT = Buf()
        bSh = [Buf(), Buf()]; bSbh = [Buf(), Buf()]; bgn = Buf(); bqz = Buf(); bkz = [Buf(), Buf()]
        bS0f = [Buf(), Buf()] + bSh; bS0b = [Buf(), Buf()] + bSbh; bst4 = Buf()
        bB = [PBuf() for _ in range(8)]
        bB7t = bB[3]
        bPT = [bB[6], bB[3]]; bSC = [bB[6], bB[3]]
        bx = [[Buf() for _ in TT] for _ in range(8)]
        s_wh = [P.slot(), P.slot()]; s_wo = P.slot(); s_cs = P.slot(); s_gn = P.slot()
        s_S0f = [P.slot() for _ in range(4)]; s_S0b = [P.slot() for _ in range(4)]; s_so = [P.slot() for _ in range(4)]; s_sp = P.slot()

        def load_wh(h):
            sl = h % 2
            bop(P, "pool", lambda e, sl=sl: e.dma_start(out=wh[sl][:].rearrange("p k n -> p (k n)"), in_=rwin[h], max_dma_last_dim=8192),
                w=[bwh[sl]], slot=s_wh[sl])

        load_wh(0)
        dbg_dump(C, "xn0", C.xn[:, 0, 0:512], [], 512, bf=True)
        dbg_dump(C, "wh0", wh[0][:, 0, 0:512], [bwh[0]], 512, bf=True)
        ucnt = 0
        for h in range(RET_H):
            sl = h % 2
            g = gam[h]
            bop(P, "pool", lambda e, h=h: e.dma_start(out=wo[:].rearrange("p a n -> p (a n)"), in_=rwout[h], max_dma_last_dim=8192),
                w=[bwo], slot=s_wo)
            if h + 1 < RET_H:
                load_wh(h + 1)
            bop(P, "sp", lambda e, h=h: e.dma_start(out=gn[:], in_=rnorm[h].partition_broadcast(128)), w=[bgn], slot=s_gn)
            bop(P, "pool", lambda e: e.memset(S[:], 0.0), w=[bSh[0], bSh[1]])
            bop(P, "pool", lambda e: e.memset(Sb[:], 0.0), w=[bSbh[0], bSbh[1]])
            for ti, (t0, n) in enumerate(TT):
                sample = (ti == 4)
                bop(P, "sp", lambda e, t0=t0, n=n: e.dma_start(out=cst[:, :, :n], in_=cs[:, :, t0:t0 + n].rearrange("a p t -> p a t")),
                    w=[bcs], slot=s_cs)
                for qi in range(4):
                    for k in range(8):
                        bop(P, "pe", lambda e, sl=sl, qi=qi, k=k, t0=t0, n=n: e.matmul(B[qi][:, :n], lhsT=wh[sl][:, k, qi * 128:(qi + 1) * 128], rhs=C.xn[:, k, t0:t0 + n], start=(k == 0), stop=(k == 7)),
                            r=[bwh[sl]], w=[bB[qi]])
                for (dst, bd, b0, b1, sc) in ((qT, bqT, 0, 1, 1.0), (kT, bkT, 2, 3, 0.0625)):
                    for half in range(2):
                        ca, cb = (0, 1) if half == 0 else (1, 0)
                        bop(P, "dve", lambda e, b0=b0, ca=ca, sc=sc, n=n: e.scalar_tensor_tensor(out=gs[:, :n], in0=B[b0][:, :n], scalar=sc, in1=cst[:, ca, :n], op0=ALU.mult, op1=ALU.mult),
                            r=[bB[b0], bcs], w=[bgs])
                        bop(P, "dve", lambda e, b1=b1, cb=cb, sc=sc, n=n: e.scalar_tensor_tensor(out=on[:, :n], in0=B[b1][:, :n], scalar=sc, in1=cst[:, cb, :n], op0=ALU.mult, op1=ALU.mult),
                            r=[bB[b1], bcs], w=[bon])
                        bop(P, "dve", lambda e, dst=dst, half=half, n=n: e.tensor_tensor(out=dst[:, half, :n], in0=gs[:, :n], in1=on[:, :n], op=(ALU.subtract if half == 0 else ALU.add)),
                            r=[bgs, bon], w=[bd])
                blocks = [(0, 64)] if sample else [(c * 128, 128) for c in range(n // 128)]
                MC = (CR_MS + h * 64) if sample else (CR_MP + h * 128)
                KD = (CR_KDS if sample else CR_KDP) + h
                EP = (CR_EPS if sample else CR_EPP) + h

                def stage_A(ci, c0, nb, sl=sl, t0=t0, MC=MC, KD=KD):
                    pb = ci % 2
                    a0 = t0 + c0
                    vb = 4 + pb
                    for k in range(8):
                        bop(P, "pe", lambda e, k=k: e.matmul(B[vb][:nb, :], lhsT=C.xn[:, k, a0:a0 + nb], rhs=wh[sl][:, k, 512:1024], start=(k == 0), stop=(k == 7)),
                            r=[bwh[sl]], w=[bB[vb]])
                    bop(P, "act", lambda e: e.activation(out=vtok[pb][:nb, :], in_=B[vb][:nb, :], func=AF.Copy), r=[bB[vb]], w=[bvtok[pb]])
                    for jj in range(2):
                        bop(P, "pe", lambda e, jj=jj: e.transpose(out=PT[pb][:nb, jj * 128:(jj + 1) * 128], in_=kT[:, jj, c0:c0 + nb], identity=C.ident),
                            r=[bkT], w=[bPT[pb]])
                    for jj in range(2):
                        bop(P, "pe", lambda e, jj=jj: e.matmul(SC[pb][:nb, :nb], lhsT=kT[:, jj, c0:c0 + nb], rhs=qT[:, jj, c0:c0 + nb], start=(jj == 0), stop=(jj == 1)),
                            r=[bkT, bqT], w=[bSC[pb]])
                    bop(P, "dve", lambda e: e.tensor_scalar(out=ktl[pb][:nb, :], in0=PT[pb][:nb, :], scalar1=cret[:nb, KD:KD + 1], scalar2=None, op0=ALU.mult),
                        r=[bPT[pb]], w=[bktl[pb]])
                    bop(P, "dve", lambda e: e.tensor_tensor(out=scm[pb][:nb, :nb], in0=SC[pb][:nb, :nb], in1=cret[:nb, MC:MC + nb], op=ALU.mult),
                        r=[bSC[pb]], w=[bscm[pb]])

                def stage_B(ci, c0, nb, sl=sl, t0=t0, EP=EP, sample=sample, g=g, h=h):
                    nonlocal ucnt
                    pb = ci % 2
                    a0 = t0 + c0
                    bop(P, "pe", lambda e: e.matmul(B[7][:nb, :], lhsT=scm[pb][:nb, :nb], rhs=vtok[pb][:nb, :], start=True, stop=False),
                        r=[bscm[pb], bvtok[pb]], w=[bB[7]])
                    if not sample:
                        for jj in range(2):
                            bop(P, "pe", lambda e, jj=jj: e.matmul(B[7][:nb, :], lhsT=qT[:, jj, c0:c0 + nb], rhs=Sb[:, jj, :], start=False, stop=(jj == 1)),
                                r=[bqT, bSbh[jj]], w=[bB[7]])
                        for jj in range(2):
                            bop(P, "pe", lambda e, jj=jj: e.matmul(B[jj][:, :], lhsT=ktl[pb][:nb, jj * 128:(jj + 1) * 128], rhs=vtok[pb][:nb, :], start=True, stop=True),
                                r=[bktl[pb], bvtok[pb]], w=[bB[jj]])
                        cd = g ** 128
                        for jj in range(2):
                            bop(P, "dve", lambda e, jj=jj: e.scalar_tensor_tensor(out=S[:, jj, :], in0=S[:, jj, :], scalar=cd, in1=B[jj][:, :], op0=ALU.mult, op1=ALU.add),
                                r=[bB[jj]], w=[bSh[jj]])
                        for jj in range(2):
                            bop(P, "pool", lambda e, jj=jj: e.tensor_copy(out=Sb[:, jj, :], in_=S[:, jj, :]), r=[bSh[jj]], w=[bSbh[jj]])
                    else:
                        for jj in range(2):
                            bop(P, "dve", lambda e, jj=jj: e.tensor_tensor(out=qz[:, jj, :, :], in0=qT[:, jj, 0:64].unsqueeze(1).broadcast_to([128, 16, 64]), in1=C.bm[:, :, :], op=ALU.mult),
                                r=[bqT], w=[bqz])
                        cd = g ** 4
                        units = [(i, jj) for i in range(NSS) for jj in range(2)]

                        def issue_loads(k):
                            i, jj = units[k]
                            u = k % 4
                            bop(P, "pool", lambda e: e.dma_start(out=S0b[u], in_=st_in[i, h, jj * 128:(jj + 1) * 128, :]), w=[bS0b[u]], slot=s_S0b[u])
                            bop(P, "sp", lambda e: e.dma_start(out=S0f[u], in_=st_in[i, h, jj * 128:(jj + 1) * 128, :]), w=[bS0f[u]], slot=s_S0f[u])

                        issue_loads(0)
                        issue_loads(1)
                        for k, (i, jj) in enumerate(units):
                            if k + 2 < len(units):
                                issue_loads(k + 2)
                            kb = i % 2
                            u = k % 4
                            if jj == 0:
                                bop(P, "dve", lambda e, i=i, kb=kb: e.tensor_scalar(out=kz[kb][:, :], in0=ktl[pb][:64, :], scalar1=cret[:64, CR_RM + i:CR_RM + i + 1], scalar2=None, op0=ALU.mult),
                                    r=[bktl[pb]], w=[bkz[kb]])
                            last = (k == len(units) - 1)
                            bop(P, "pe", lambda e, u=u, i=i, jj=jj, last=last: e.matmul(B[7][:64, :], lhsT=qz[:, jj, i, :], rhs=S0b[u], start=False, stop=last),
                                r=[bqz, bS0b[u]], w=[bB[7]])
                            bop(P, "pe", lambda e, kb=kb, jj=jj: e.matmul(B[jj][:, :], lhsT=kz[kb][:, jj * 128:(jj + 1) * 128], rhs=vtok[pb][:64, :], start=True, stop=True),
                                r=[bkz[kb], bvtok[pb]], w=[bB[jj]])
                            bop(P, "dve", lambda e, u=u, jj=jj: e.scalar_tensor_tensor(out=S0f[u], in0=S0f[u], scalar=cd, in1=B[jj][:, :], op0=ALU.mult, op1=ALU.add),
                                r=[bB[jj]], w=[bS0f[u]])
                            bop(P, "sp", lambda e, u=u, i=i, jj=jj: e.dma_start(out=st_out_s[i, h, jj * 128:(jj + 1) * 128, :], in_=S0f[u]), r=[bS0f[u]], slot=s_so[u])
                    bop(P, "act", lambda e: e.activation(out=on[:nb, :], in_=B[7][:nb, :], func=AF.Square, accum_out=st4[:nb, 0:1]),
                        r=[bB[7]], w=[bon, bst4])
                    bop(P, "act", lambda e: e.activation(out=st4[:nb, 2:3], in_=st4[:nb, 0:1], func=AF.Ln, scale=1.0 / 512, bias=cret[:nb, EP:EP + 1]), r=[bst4], w=[bst4])
                    bop(P, "act", lambda e: e.activation(out=st4[:nb, 3:4], in_=st4[:nb, 2:3], func=AF.Exp, scale=-0.5), r=[bst4], w=[bst4])
                    bop(P, "dve", lambda e: e.scalar_tensor_tensor(out=on[:nb, :], in0=B[7][:nb, :], scalar=st4[:nb, 3:4], in1=gn[:nb, :], op0=ALU.mult, op1=ALU.mult),
                        r=[bB[7], bst4, bgn], w=[bon])
                    for k in range(8):
                        bop(P, "pe", lambda e, k=k: e.matmul(B[2][:nb, :], lhsT=C.xn[:, k, a0:a0 + nb], rhs=wh[sl][:, k, 1024:1536], start=(k == 0), stop=(k == 7)),
                            r=[bwh[sl]], w=[bB[2]])
                    bop(P, "act", lambda e: e.activation(out=gs[:nb, :], in_=B[2][:nb, :], func=AF.Exp, scale=-1.0), r=[bB[2]], w=[bgs])
                    bop(P, "act", lambda e: e.activation(out=gs[:nb, :], in_=gs[:nb, :], func=AF.Ln, bias=C.one[:nb, 0:1]), r=[bgs], w=[bgs])
                    bop(P, "act", lambda e: e.activation(out=gs[:nb, :], in_=gs[:nb, :], func=AF.Exp, scale=-1.0), r=[bgs], w=[bgs])
                    bop(P, "dve", lambda e: e.tensor_tensor(out=gs[:nb, :], in0=B[2][:nb, :], in1=gs[:nb, :], op=ALU.mult), r=[bgs, bB[2]], w=[bgs])
                    bop(P, "pool", lambda e: e.tensor_tensor(out=go[pb][:nb, :], in0=on[:nb, :], in1=gs[:nb, :], op=ALU.mult), r=[bon, bgs], w=[bgo[pb]])

                def stage_T(ci, c0, nb):
                    pb = ci % 2
                    for e4 in range(4):
                        bop(P, "pe", lambda e, e4=e4: e.transpose(out=TR[:, e4 * 128:e4 * 128 + nb], in_=go[pb][:nb, e4 * 128:(e4 + 1) * 128], identity=C.ident[:nb, :nb]),
                            r=[bgo[pb]], w=[bB7t])
                    bop(P, "act", lambda e: e.activation(out=goT[:, :, c0:c0 + nb], in_=TR.rearrange("p (a t) -> p a t", a=4)[:, :, :nb], func=AF.Copy),
                        r=[bB7t], w=[bgoT])

                nblk = len(blocks)
                sched = []
                if nblk == 1:
                    sched = [("A", 0), ("B", 0), ("T", 0)]
                else:
                    sched = [("A", 0), ("A", 1), ("B", 0), ("A", 2), ("B", 1), ("T", 0), ("A", 3), ("B", 2), ("T", 1), ("B", 3), ("T", 2), ("T", 3)]
                for (kind, ci) in sched:
                    c0, nb = blocks[ci]
                    if kind == "A":
                        stage_A(ci, c0, nb)
                    elif kind == "B":
                        stage_B(ci, c0, nb)
                    else:
                        stage_T(ci, c0, nb)
                for m in range(8):
                    wb = 4 + (m % 2)
                    for e4 in range(4):
                        bop(P, "pe", lambda e, m=m, e4=e4, n=n, wb=wb: e.matmul(B[wb][:, :n], lhsT=wo[:, e4, m * 128:(m + 1) * 128], rhs=goT[:, e4, :n], start=(e4 == 0), stop=(e4 == 3)),
                            r=[bwo, bgoT], w=[bB[wb]])
                    bop(P, "dve", lambda e, m=m, t0=t0, n=n, wb=wb: e.tensor_tensor(out=C.xres[:, m, t0:t0 + n], in0=C.xres[:, m, t0:t0 + n], in1=B[wb][:, :n], op=ALU.add),
                        r=[bB[wb]], w=[bx[m][ti]])
                if ti == 3:
                    bop(P, "sp", lambda e, h=h: e.dma_start(out=st_out_p[h].rearrange("(a p) n -> p a n", p=128), in_=S[:, :, :]), r=[bSh[0], bSh[1]], slot=s_sp)
        P.flush()


HG_H = 8
CH_MP = 800
CH_MS = 928
CH_RMP = 992
CH_CMP = 1000
CH_CMS = 1512
CRW2 = 1576


def emit_hg(C, j, hwin, hwout, hnorm, lbl, st_in, st_out_p, st_out_s):
    P, nc = C.P, C.nc
    with ExitStack() as st:
        sb, pt = mk_alloc(C, st)
        wh = [sb("hwh%d" % i, [128, 8, 512], BF16) for i in range(2)]
        wo = [sb("hwo%d" % i, [128, 1024], BF16) for i in range(2)]
        F = {nm: sb("h" + nm, [128, 512], F32) for nm in ("qs", "ez", "r", "f", "kk", "b", "d1", "X", "Y")}
        qc = [sb("hqc%d" % i, [128, 512], BF16) for i in range(2)]
        kc = [sb("hkc%d" % i, [128, 512], BF16) for i in range(2)]
        eB = [sb("heB%d" % i, [128, 16], F32) for i in range(2)]
        vtok = [sb("hvtok%d" % i, [128, 128], BF16) for i in range(2)]
        gsil = [sb("hgsil%d" % i, [128, 128], F32) for i in range(2)]
        ktl = [sb("hktl%d" % i, [128, 128], BF16) for i in range(2)]
        scm = [sb("hscm%d" % i, [128, 128], BF16) for i in range(2)]
        qz = [sb("hqz%d" % i, [128, 1024], BF16) for i in range(2)]
        kz = [sb("hkz%d" % i, [128, 2048], BF16) for i in range(2)]
        Sdb = [sb("hSdb%d" % i, [128, 16, 128], BF16) for i in range(2)]
        S = sb("hS", [128, 128], F32)
        S0 = sb("hS0", [128, 16, 128], F32)
        on = sb("hon", [128, 128], F32)
        go = [sb("hgo%d" % i, [128, 128], BF16) for i in range(2)]
        goT = sb("hgoT", [128, 512], BF16)
        gn = sb("hgn", [128, 128], F32)
        lb = sb("hlb", [128, 2, 8], F32)
        lbv = sb("hlbv", [128, 8], F32)
        oml = sb("homl", [128, 8], F32)
        st4 = sb("hst4", [128, 4], F32)
        B = [pt("hb%d" % i, [128, 512], F32) for i in range(8)]
        VG = [B[4][:, 0:256], B[6][:, 0:256]]
        PT = [B[4][:, 256:320].bitcast(BF16), B[6][:, 256:320].bitcast(BF16)]
        SC = [B[4][:, 320:448], B[6][:, 320:448]]
        TR = B[3][:, 256:320].bitcast(BF16)
        UR = [B[5][:, i * 128:(i + 1) * 128] for i in range(4)] + [B[7][:, i * 128:(i + 1) * 128] for i in range(4)]
        cret = C.cret
        bF = {nm: Buf() for nm in F}
        bwh = [Buf(), Buf()]; bwo = [Buf(), Buf()]; bqc = [Buf(), Buf()]; bkc = [Buf(), Buf()]; beB = [Buf(), Buf()]
        bvtok = [Buf(), Buf()]; bgsil = [Buf(), Buf()]; bktl = [Buf(), Buf()]; bscm = [Buf(), Buf()]
        bqz = [Buf(), Buf()]; bkz = [Buf(), Buf()]; bSdb = [Buf(), Buf()]; bgo = [Buf(), Buf()]
        bS = Buf(); bS0 = Buf(); bon = Buf(); bgoT = Buf(); bgn = Buf(); blb = Buf(); bst4 = Buf()
        bB = [PBuf() for _ in range(8)]
        bVG = [bB[4], bB[6]]; bPT = [bB[4], bB[6]]; bSC = [bB[4], bB[6]]; bTR = bB[3]; bUR = [bB[5]] * 4 + [bB[7]] * 4
        bx = [[Buf() for _ in TT] for _ in range(8)]
        s_wh = [P.slot(), P.slot()]; s_wo = [P.slot(), P.slot()]; s_gn = P.slot(); s_lb = P.slot()
        s_S0 = P.slot(); s_so = P.slot(); s_sp = P.slot()

        def sigmoid_act(dst, src, bdst, bsrc):
            bop(P, "act", lambda e: e.activation(out=dst, in_=src, func=AF.Exp, scale=-1.0), r=[bsrc], w=[bdst])
            bop(P, "act", lambda e: e.activation(out=dst, in_=dst, func=AF.Ln, bias=C.one[:dst.shape[0], 0:1]), r=[bdst], w=[bdst])
            bop(P, "act", lambda e: e.activation(out=dst, in_=dst, func=AF.Exp, scale=-1.0), r=[bdst], w=[bdst])

        bop(P, "sp", lambda e: e.dma_start(out=lb[:], in_=lbl), w=[blb], slot=s_lb)
        if j == 0:
            bop(P, "dve", lambda e: e.memset(lbv[:], 0.0), w=[blb])
            bop(P, "dve", lambda e: e.memset(oml[:], 1.0), w=[blb])
        else:
            bop(P, "dve", lambda e: e.tensor_tensor(out=lbv[:], in0=lb[:, 0, :], in1=lb[:, 1, :], op=ALU.subtract), r=[blb], w=[blb])
            bop(P, "act", lambda e: e.activation(out=oml[:], in_=lbv[:], func=AF.Exp), r=[blb], w=[blb])
            bop(P, "dve", lambda e: e.tensor_scalar(out=lbv[:], in0=oml[:], scalar1=1.0, scalar2=None, op0=ALU.add), r=[blb], w=[blb])
            bop(P, "dve", lambda e: e.reciprocal(out=lbv[:], in_=lbv[:]), r=[blb], w=[blb])
            bop(P, "dve", lambda e: e.tensor_tensor(out=oml[:], in0=oml[:], in1=lbv[:], op=ALU.mult), r=[blb], w=[blb])

        def load_w(h):
            sl = h % 2
            bop(P, "pool", lambda e, sl=sl, h=h: e.dma_start(out=wh[sl][:].rearrange("p k n -> p (k n)"), in_=hwin[h], max_dma_last_dim=8192),
                w=[bwh[sl]], slot=s_wh[sl])
            bop(P, "pool", lambda e, sl=sl, h=h: e.dma_start(out=wo[sl][:], in_=hwout[h], max_dma_last_dim=8192), w=[bwo[sl]], slot=s_wo[sl])

        load_w(0)
        ugl = 0
        jobs = [(h, ti) for h in range(HG_H) for ti in range(len(TT))]

        def head_setup(h):
            if h + 1 < HG_H:
                load_w(h + 1)
            bop(P, "sp", lambda e: e.dma_start(out=gn[:], in_=hnorm[h].partition_broadcast(128)), w=[bgn], slot=s_gn)
            bop(P, "pool", lambda e: e.memset(S[:], 0.0), w=[bS])
            bop(P, "sp", lambda e: e.dma_start(out=S0[:], in_=st_in[:, h, :, :].rearrange("i d e -> d i e")), w=[bS0], slot=s_S0)

        def make_s1(h, ti, kp):
            sl = h % 2
            t0, n = TT[ti]
            sample = (ti == 4)
            CL = 4 if sample else 32
            nch = n // CL
            CM = CH_CMS if sample else CH_CMP
            qcj, kcj, eBj = qc[kp], kc[kp], eB[kp]

            def p0():
                for qi in range(2):
                    for k in range(8):
                        bop(P, "pe", lambda e, qi=qi, k=k: e.matmul(B[qi][:, :n], lhsT=wh[sl][:, k, qi * 128:(qi + 1) * 128], rhs=C.xn[:, k, t0:t0 + n], start=(k == 0), stop=(k == 7)),
                            r=[bwh[sl]], w=[bB[qi]])

            def p1():
                sigmoid_act(F["qs"][:, :n], B[0][:, :n], bF["qs"], bB[0])

            def p1b():
                bop(P, "act", lambda e: e.activation(out=F["ez"][:, :n], in_=B[1][:, :n], func=AF.Exp, scale=-1.0), r=[bB[1]], w=[bF["ez"]])
                bop(P, "act", lambda e: e.activation(out=F["r"][:, :n], in_=F["ez"][:, :n], func=AF.Ln, bias=C.one[:, 0:1]), r=[bF["ez"]], w=[bF["r"]])
                bop(P, "act", lambda e: e.activation(out=F["r"][:, :n], in_=F["r"][:, :n], func=AF.Exp, scale=-1.0), r=[bF["r"]], w=[bF["r"]])

            def p2():
                bop(P, "dve", lambda e: e.tensor_tensor(out=F["qs"][:, :n], in0=B[0][:, :n], in1=F["qs"][:, :n], op=ALU.mult), r=[bF["qs"], bB[0]], w=[bF["qs"]])
                bop(P, "dve", lambda e: e.tensor_scalar(out=F["f"][:, :n], in0=F["r"][:, :n], scalar1=oml[:, h:h + 1], scalar2=lbv[:, h:h + 1], op0=ALU.mult, op1=ALU.add),
                    r=[bF["r"], blb], w=[bF["f"]])
                bop(P, "dve", lambda e: e.scalar_tensor_tensor(out=F["kk"][:, :n], in0=F["ez"][:, :n], scalar=oml[:, h:h + 1], in1=F["r"][:, :n], op0=ALU.mult, op1=ALU.mult),
                    r=[bF["ez"], bF["r"], blb], w=[bF["kk"]])
                bop(P, "act", lambda e: e.activation(out=F["f"][:, :n], in_=F["f"][:, :n], func=AF.Ln), r=[bF["f"]], w=[bF["f"]])

            def p3():
                bop(P, "dve", lambda e: e.tensor_tensor_scan(out=F["b"][:, :n], data0=cret[:, CM:CM + n], data1=F["f"][:, :n], initial=0.0, op0=ALU.mult, op1=ALU.add),
                    r=[bF["f"]], w=[bF["b"]])
                bop(P, "pool", lambda e: e.tensor_tensor(
                    out=F["d1"][:, :n].rearrange("p (c s) -> p c s", s=CL), in0=F["b"][:, :n].rearrange("p (c s) -> p c s", s=CL),
                    in1=F["b"][:, :n].rearrange("p (c s) -> p c s", s=CL)[:, :, CL - 1:CL].broadcast_to([128, nch, CL]), op=ALU.subtract),
                    r=[bF["b"]], w=[bF["d1"]])

            def p4():
                bop(P, "act", lambda e: e.activation(out=F["X"][:, :n], in_=F["d1"][:, :n], func=AF.Exp), r=[bF["d1"]], w=[bF["X"]])
                bop(P, "act", lambda e: e.activation(out=F["Y"][:, :n], in_=F["d1"][:, :n], func=AF.Exp, scale=-1.0), r=[bF["d1"]], w=[bF["Y"]])
                bop(P, "act", lambda e: e.activation(out=eBj[:, :nch], in_=F["b"][:, :n].rearrange("p (c s) -> p c s", s=CL)[:, :, CL - 1], func=AF.Exp),
                    r=[bF["b"]], w=[beB[kp]])

            def p5():
                bop(P, "pool", lambda e: e.tensor_tensor(out=qcj[:, :n], in0=F["qs"][:, :n], in1=F["X"][:, :n], op=ALU.mult), r=[bF["qs"], bF["X"]], w=[bqc[kp]])
                bop(P, "pool", lambda e: e.tensor_tensor(out=kcj[:, :n], in0=F["kk"][:, :n], in1=F["Y"][:, :n], op=ALU.mult), r=[bF["kk"], bF["Y"]], w=[bkc[kp]])

            return [p0, p1, p1b, p2, p3, p4, p5]

        def make_blocks(h, ti, kp):
            sl = h % 2
            t0, n = TT[ti]
            sample = (ti == 4)
            CL = 4 if sample else 32
            qcj, kcj, eBj = qc[kp], kc[kp], eB[kp]
            blocks = [(0, 64)] if sample else [(c * 128, 128) for c in range(4)]
            MC = CH_MS if sample else CH_MP
            nbc = 16 if sample else 4
            if sample:
                bmask = C.bm
                rmv = cret[:64, CR_RM:CR_RM + 16]
            else:
                bmask = C.bmp
                rmv = cret[:, CH_RMP:CH_RMP + 4]
            ubase = {}

            def A_pe(ci):
                c0, nb = blocks[ci]
                pb = ci % 2
                a0 = t0 + c0
                for k in range(8):
                    bop(P, "pe", lambda e, k=k: e.matmul(VG[pb][:nb, :], lhsT=C.xn[:, k, a0:a0 + nb], rhs=wh[sl][:, k, 256:512], start=(k == 0), stop=(k == 7)),
                        r=[bwh[sl]], w=[bVG[pb]])
                bop(P, "pe", lambda e: e.transpose(out=PT[pb][:nb, :], in_=kcj[:, c0:c0 + nb], identity=C.ident), r=[bkc[kp]], w=[bPT[pb]])
                bop(P, "pe", lambda e: e.matmul(SC[pb][:nb, :nb], lhsT=kcj[:, c0:c0 + nb], rhs=qcj[:, c0:c0 + nb], start=True, stop=True), r=[bkc[kp], bqc[kp]], w=[bSC[pb]])

            def A_ev1(ci):
                nonlocal ugl
                c0, nb = blocks[ci]
                pb = ci % 2
                bop(P, "dve", lambda e: e.tensor_copy(out=ktl[pb][:nb, :], in_=PT[pb][:nb, :]), r=[bPT[pb]], w=[bktl[pb]])
                bop(P, "act", lambda e: e.activation(out=vtok[pb][:nb, :], in_=VG[pb][:nb, 0:128], func=AF.Copy), r=[bVG[pb]], w=[bvtok[pb]])
                kzv = kz[pb][:nb, 0:nbc * 128].rearrange("p (c d) -> p c d", c=nbc)
                bop(P, "pool", lambda e: e.tensor_tensor(out=kzv, in0=ktl[pb][:nb, :].unsqueeze(1).broadcast_to([nb, nbc, 128]), in1=rmv.unsqueeze(2).broadcast_to([nb, nbc, 128]), op=ALU.mult),
                    r=[bktl[pb]], w=[bkz[pb]])
                ubase[ci] = ugl
                if nbc <= 4:
                    for c in range(nbc):
                        u = 4 * pb + c
                        bop(P, "pe", lambda e, c=c, u=u: e.matmul(UR[u], lhsT=kzv[:, c, :], rhs=vtok[pb][:nb, :], start=True, stop=True),
                            r=[bkz[pb], bvtok[pb]], w=[bUR[u]])
                ugl += nbc

            def A_ev2(ci):
                c0, nb = blocks[ci]
                pb = ci % 2
                qzv = qz[pb][:, 0:nbc * nb].rearrange("p (c t) -> p c t", c=nbc)
                bop(P, "pool", lambda e: e.tensor_tensor(out=qzv, in0=qcj[:, c0:c0 + nb].unsqueeze(1).broadcast_to([128, nbc, nb]), in1=bmask, op=ALU.mult),
                    r=[bqc[kp]], w=[bqz[pb]])
                bop(P, "dve", lambda e: e.tensor_tensor(out=scm[pb][:nb, :nb], in0=SC[pb][:nb, :nb], in1=cret[:nb, MC:MC + nb], op=ALU.mult), r=[bSC[pb]], w=[bscm[pb]])
                sigmoid_act(gsil[pb][:nb, :], VG[pb][:nb, 128:256], bgsil[pb], bVG[pb])
                bop(P, "dve", lambda e: e.tensor_tensor(out=gsil[pb][:nb, :], in0=VG[pb][:nb, 128:256], in1=gsil[pb][:nb, :], op=ALU.mult), r=[bgsil[pb], bVG[pb]], w=[bgsil[pb]])

            def B_chain(ci):
                c0, nb = blocks[ci]
                pb = ci % 2
                kzv = kz[pb][:nb, 0:nbc * 128].rearrange("p (c d) -> p c d", c=nbc)
                for c in range(nbc):
                    ec = (c0 // CL + c)
                    u = (4 * pb + c) if nbc <= 4 else (4 * (c % 2) + (c // 2) % 4)
                    Sin = S0[:, c, :] if sample else S[:, :]
                    bSin = bS0 if sample else bS
                    if nbc > 4:
                        bop(P, "pe", lambda e, c=c, u=u: e.matmul(UR[u], lhsT=kzv[:, c, :], rhs=vtok[pb][:nb, :], start=True, stop=True),
                            r=[bkz[pb], bvtok[pb]], w=[bUR[u]])
                    bop(P, "dve", lambda e, c=c, ec=ec, Sin=Sin: e.tensor_scalar(out=Sdb[pb][:, c, :], in0=Sin, scalar1=eBj[:, ec:ec + 1], scalar2=None, op0=ALU.mult),
                        r=[bSin, beB[kp]], w=[bSdb[pb]])
                    bop(P, "dve", lambda e, ec=ec, u=u, Sin=Sin: e.scalar_tensor_tensor(out=Sin, in0=Sin, scalar=eBj[:, ec:ec + 1], in1=UR[u], op0=ALU.mult, op1=ALU.add),
                        r=[bUR[u], beB[kp]], w=[bSin])

            def B_pe(ci):
                c0, nb = blocks[ci]
                pb = ci % 2
                qzv = qz[pb][:, 0:nbc * nb].rearrange("p (c t) -> p c t", c=nbc)
                bop(P, "pe", lambda e: e.matmul(B[2][:nb, 0:128], lhsT=scm[pb][:nb, :nb], rhs=vtok[pb][:nb, :], start=True, stop=False), r=[bscm[pb], bvtok[pb]], w=[bB[2]])
                for c in range(nbc):
                    bop(P, "pe", lambda e, c=c: e.matmul(B[2][:nb, 0:128], lhsT=qzv[:, c, :], rhs=Sdb[pb][:, c, :], start=False, stop=(c == nbc - 1)),
                        r=[bqz[pb], bSdb[pb]], w=[bB[2]])

            def B_norm(ci):
                c0, nb = blocks[ci]
                pb = ci % 2
                bop(P, "act", lambda e: e.activation(out=on[:nb, :], in_=B[2][:nb, 0:128], func=AF.Square, accum_out=st4[:nb, 0:1]), r=[bB[2]], w=[bon, bst4])
                bop(P, "act", lambda e: e.activation(out=st4[:nb, 2:3], in_=st4[:nb, 0:1], func=AF.Ln, scale=1.0 / 128, bias=C.epsc[:nb, 0:1]), r=[bst4], w=[bst4])
                bop(P, "act", lambda e: e.activation(out=st4[:nb, 3:4], in_=st4[:nb, 2:3], func=AF.Exp, scale=-0.5), r=[bst4], w=[bst4])
                bop(P, "dve", lambda e: e.scalar_tensor_tensor(out=on[:nb, :], in0=B[2][:nb, 0:128], scalar=st4[:nb, 3:4], in1=gn[:nb, :], op0=ALU.mult, op1=ALU.mult),
                    r=[bB[2], bst4, bgn], w=[bon])
                bop(P, "pool", lambda e: e.tensor_tensor(out=go[pb][:nb, :], in0=on[:nb, :], in1=gsil[pb][:nb, :], op=ALU.mult), r=[bon, bgsil[pb]], w=[bgo[pb]])

            def stage_T(ci):
                c0, nb = blocks[ci]
                pb = ci % 2
                bop(P, "pe", lambda e: e.transpose(out=TR[:, :nb], in_=go[pb][:nb, :], identity=C.ident[:nb, :nb]), r=[bgo[pb]], w=[bTR])
                bop(P, "act", lambda e: e.activation(out=goT[:, c0:c0 + nb], in_=TR[:, :nb], func=AF.Copy), r=[bTR], w=[bgoT])

            def wout():
                for m in range(8):
                    wb = 3 if m % 2 == 0 else 2
                    bop(P, "pe", lambda e, m=m, wb=wb: e.matmul(B[wb][:, :n], lhsT=wo[sl][:, m * 128:(m + 1) * 128], rhs=goT[:, :n], start=True, stop=True),
                        r=[bwo[sl], bgoT], w=[bB[wb]])
                    bop(P, "dve", lambda e, m=m, wb=wb: e.tensor_tensor(out=C.xres[:, m, t0:t0 + n], in0=C.xres[:, m, t0:t0 + n], in1=B[wb][:, :n], op=ALU.add),
                        r=[bB[wb]], w=[bx[m][ti]])
                if ti == 3:
                    bop(P, "sp", lambda e: e.dma_start(out=st_out_p[h], in_=S[:, :]), r=[bS], slot=s_sp)
                if sample:
                    bop(P, "sp", lambda e: e.dma_start(out=st_out_s[:, h, :, :].rearrange("i d e -> d i e"), in_=S0[:, :, :]), r=[bS0], slot=s_so)

            mk = lambda fn, ci: (lambda: fn(ci))
            if len(blocks) == 1:
                sched = [mk(A_pe, 0), mk(A_ev1, 0), mk(A_ev2, 0), mk(B_chain, 0), mk(B_pe, 0), mk(B_norm, 0), mk(stage_T, 0)]
                slots_after = {2: [0, 1, 2], 4: [3, 4], 6: [5, 6]}
            else:
                sched = [mk(A_pe, 0), mk(A_ev1, 0), mk(A_ev2, 0), mk(A_pe, 1),
                         mk(A_ev1, 1), mk(B_chain, 0), mk(B_pe, 0), mk(A_ev2, 1), mk(B_norm, 0), mk(A_pe, 2),
                         mk(A_ev1, 2), mk(B_chain, 1), mk(B_pe, 1), mk(A_ev2, 2), mk(B_norm, 1), mk(stage_T, 0), mk(A_pe, 3),
                         mk(A_ev1, 3), mk(B_chain, 2), mk(B_pe, 2), mk(A_ev2, 3), mk(B_norm, 2), mk(stage_T, 1),
                         mk(B_chain, 3), mk(B_pe, 3), mk(B_norm, 3), mk(stage_T, 2), mk(stage_T, 3)]
                slots_after = {3: [0], 6: [1], 9: [2], 16: [3], 22: [4, 5], 26: [6]}
            return sched, slots_after, wout

        for p in make_s1(jobs[0][0], jobs[0][1], 0):
            p()
        for kj, (h, ti) in enumerate(jobs):
            kp = kj % 2
            if ti == 0:
                head_setup(h)
            sched, slots_after, wout = make_blocks(h, ti, kp)
            nxt = make_s1(jobs[kj + 1][0], jobs[kj + 1][1], 1 - kp) if kj + 1 < len(jobs) else None
            for si, stage in enumerate(sched):
                stage()
                if nxt is not None:
                    for pi in slots_after.get(si, []):
                        nxt[pi]()
            wout()
        P.flush()


def build_program(cfg):
    nc = bass.Bass("TRN2", target_bir_lowering=False)
    dr = lambda name, shape, kind="ExternalInput", dt=F32: nc.dram_tensor(name, shape, dt, kind=kind).ap()
    xT = dr("xT", [128, 8, NTOK])
    gains_d = dr("gains", [128, 13 * 8])
    cbf_d = dr("cbf", [128, 256 + 1024 + 512], dt=BF16)
    cret_d = dr("cret", [128, CRW2])
    cs_d = dr("cs", [2, 128, NTOK])
    wup_d = dr("wup", [8, NF, 128, 2048])
    wdn_d = dr("wdn", [8, 2, 128, 11 * 1024])
    rwin_d = dr("rwin", [2, 4, 128, 12288])
    rwout_d = dr("rwout", [2, 4, 128, 4096])
    rnorm_d = dr("rnorm", [2, 4, 512])
    sret_d = dr("sret", [2, NSS, 4, 256, 512])
    hwin_d = dr("hwin", [2, 8, 128, 4096])
    hwout_d = dr("hwout", [2, 8, 128, 1024])
    hnorm_d = dr("hnorm", [2, 8, 128])
    lbl_d = dr("lbl", [128, 2, 8])
    shg_d = dr("shg", [2, NSS, 8, 128, 128])
    nhp_d = dr("nhp", [2, 8, 128, 128], kind="ExternalOutput")
    nhs_d = dr("nhs", [2, NSS, 8, 128, 128], kind="ExternalOutput")
    yT = dr("yT", [128, 8, NTOK], kind="ExternalOutput")
    nrp_d = dr("nrp", [2, 4, 256, 512], kind="ExternalOutput")
    nrs_d = dr("nrs", [2, NSS, 4, 256, 512], kind="ExternalOutput")

    with ExitStack() as st:
        P = Prog(nc, st)
        C = Ctx()
        C.P, C.nc = P, nc
        C.dbg = None
        if cfg.get("debug"):
            C.dbg = {"want": set(cfg["debug"]), "seen": {}, "off": {"f": 0, "b": 0}, "slot": P.slot(),
                     "f": dr("dbgf", [128, 8192], kind="ExternalOutput"), "b": dr("dbgb", [128, 8192], kind="ExternalOutput", dt=BF16)}
        cfg["_dbg"] = C.dbg
        sb = lambda name, shape, dt: st.enter_context(nc.sbuf_tensor(name, shape, dt))
        C.xres = sb("xres", [128, 8, NTOK], F32)
        C.xn = sb("xn", [128, 8, NTOK], BF16)
        C.gains = sb("gains_sb", [128, 13 * 8], F32)
        cbf = sb("cbf_sb", [128, 256 + 1024 + 512], BF16)
        C.ident = cbf[:, 0:128]
        C.ones = cbf[:, 128:256]
        C.bm = cbf[:, 256:1280].rearrange("p (a t) -> p a t", a=16)
        C.bmp = cbf[:, 1280:1792].rearrange("p (a t) -> p a t", a=4)
        C.epsc = sb("epsc", [128, 2], F32)
        C.one = sb("onec", [128, 2], F32)
        C.cret = sb("cret_sb", [128, CRW2], F32)

        s_in = P.slot()
        for k in range(8):
            P.dma("sp", lambda e, k=k: e.dma_start(out=C.xres[:, k, :], in_=xT[:, k, :]), s_in)
        P.dma("sp", lambda e: e.dma_start(out=C.gains[:], in_=gains_d), s_in)
        P.dma("sp", lambda e: e.dma_start(out=cbf[:], in_=cbf_d), s_in)
        P.dma("sp", lambda e: e.dma_start(out=C.cret[:], in_=cret_d), s_in)
        P.op("pool", lambda e: e.memset(C.epsc[:], EPS))
        P.op("pool", lambda e: e.memset(C.one[:], 1.0))
        P.flush()

        for blk in cfg["blocks"]:
            if blk[0] == "ffn":
                _, l, i = blk
                emit_ffn(C, l * 3 + (0 if i == 0 else 2), wup_d[l * 2 + i], wdn_d[l * 2 + i])
            elif blk[0] == "ret":
                _, l = blk
                j = l // 2
                emit_normphase(C, l * 3 + 1)
                emit_ret(C, j, rwin_d[j], rwout_d[j], rnorm_d[j], cs_d, sret_d[j], nrp_d[j], nrs_d[j])
            elif blk[0] == "hg":
                _, l = blk
                j = l // 2
                emit_normphase(C, l * 3 + 1)
                emit_hg(C, j, hwin_d[j], hwout_d[j], hnorm_d[j], lbl_d, shg_d[j], nhp_d[j], nhs_d[j])

        with ExitStack() as st2:
            sb2, pt2 = mk_alloc(C, st2)
            ph = Ctx()
            ph.sq = [sb2("sq%d" % i, [128, 8, 512], BF16) for i in range(2)]
            ph.rs = [sb2("rs%d" % i, [128, 512], F32) for i in range(2)]
            ph.psn = pt2("psn", [128, 512], F32)
            yo = [sb2("yo%d" % i, [128, 8, 512], F32) for i in range(2)]
            s_out = [P.slot(), P.slot()]
            yo_rd = [None, None]
            if cfg.get("final_norm", True):
                def out_fn2(ti, k, t0, n, rsb, r):
                    b = ti % 2
                    o = P.op("dve", lambda e: e.scalar_tensor_tensor(
                        out=yo[b][:, k, :n], in0=C.xres[:, k, t0:t0 + n], scalar=C.gains[:, 96 + k:96 + k + 1],
                        in1=rsb[:, :n], op0=ALU.mult, op1=ALU.mult), [r, yo_rd[b]])
                    if k == 7:
                        yo_rd[b] = P.dma("sp", lambda e: e.dma_start(out=yT[:, :, t0:t0 + n], in_=yo[b][:, :, :n]), s_out[b], [o])
                    return o
                emit_norm(C, ph, 12, out_fn2)
            else:
                for k in range(8):
                    P.dma("sp", lambda e, k=k: e.dma_start(out=yT[:, k, :], in_=C.xres[:, k, :]), s_out[0])
            P.flush()
    return nc


def host_consts():
    import ml_dtypes
    c = np.zeros((128, 256 + 1024 + 512), np.float32)
    c[:, 0:128] = np.eye(128, dtype=np.float32)
    c[:, 128:256] = 1.0
    bm = np.zeros((16, 64), np.float32)
    for i in range(16):
        bm[i, 4 * i:4 * i + 4] = 1.0
    c[:, 256:1280] = bm.reshape(1, 1024)
    bmp = np.zeros((4, 128), np.float32)
    for i in range(4):
        bmp[i, 32 * i:32 * i + 32] = 1.0
    c[:, 1280:1792] = bmp.reshape(1, 512)
    out = {"cbf": c.astype(ml_dtypes.bfloat16)}
    cr = np.zeros((128, CRW2), np.float64)
    t = np.arange(128)
    ts = np.arange(64)
    for h, g in enumerate(ret_gammas()):
        lg = np.log(np.float64(g))
        mp = np.where(t[:, None] <= t[None, :], np.exp(-(t[:, None] + 1.0) * lg), 0.0)
        cr[:, CR_MP + h * 128:CR_MP + (h + 1) * 128] = mp
        same = (ts[:, None] // 4) == (ts[None, :] // 4)
        ms = np.where(same & ((ts[:, None] % 4) <= (ts[None, :] % 4)), np.exp(-((ts[:, None] % 4) + 1.0) * lg), 0.0)
        cr[:64, CR_MS + h * 64:CR_MS + (h + 1) * 64] = ms
        cr[:, CR_KDP + h] = np.exp((127.0 - t) * lg)
        cr[:, CR_EPP + h] = EPS * np.exp(-2.0 * (t + 1.0) * lg)
        cr[:64, CR_KDS + h] = np.exp((3.0 - (ts % 4)) * lg)
        cr[:64, CR_EPS + h] = EPS * np.exp(-2.0 * ((ts % 4) + 1.0) * lg)
    for i in range(16):
        cr[4 * i:4 * i + 4, CR_RM + i] = 1.0
    cr[:, CH_MP:CH_MP + 128] = ((t[:, None] // 32) == (t[None, :] // 32)) & (t[:, None] <= t[None, :])
    cr[:64, CH_MS:CH_MS + 64] = ((ts[:, None] // 4) == (ts[None, :] // 4)) & (ts[:, None] <= ts[None, :])
    for i in range(4):
        cr[32 * i:32 * i + 32, CH_RMP + i] = 1.0
    cr[:, CH_CMP:CH_CMP + 512] = (np.arange(512) % 32 != 0)[None, :]
    cr[:, CH_CMS:CH_CMS + 64] = (np.arange(64) % 4 != 0)[None, :]
    out["cret"] = cr.astype(np.float32)
    half = 128
    inv_freq = (np.float32(10000.0) ** (-np.arange(half, dtype=np.float32) / np.float32(half))).astype(np.float32)
    pos = np.concatenate([np.arange(SEQ, dtype=np.float32), np.tile(np.float32(16384.0) + np.arange(DEC, dtype=np.float32), NSS)])
    ang = (pos[None, :] * inv_freq[:, None]).astype(np.float32)
    out["cs"] = np.stack([np.cos(ang), np.sin(ang)]).astype(np.float32)
    return out


def host_weights(inp):
    w = {}
    f32 = lambda a: np.asarray(a, np.float32)
    up = f32(inp["ffn_w_up"]).reshape(8, 8, 128, 2, NF, 128)
    w["wup"] = np.ascontiguousarray(up.transpose(0, 4, 2, 1, 3, 5)).reshape(8, NF, 128, 2048)
    dn = f32(inp["ffn_w_down"]).reshape(8, 2, 11, 128, 1024)
    w["wdn"] = np.ascontiguousarray(dn.transpose(0, 1, 3, 2, 4)).reshape(8, 2, 128, 11 * 1024)
    g = np.concatenate([f32(inp["norm_gain"]).reshape(12, 1024), f32(inp["final_norm"]).reshape(1, 1024)], 0)
    w["gains"] = np.ascontiguousarray(g.reshape(13, 8, 128).transpose(2, 0, 1)).reshape(128, 104)
    wi = f32(inp["ret_w_in"]).reshape(2, 8, 128, 6144)
    parts = []
    for h in range(4):
        parts.append(np.concatenate([wi[..., h * 256:(h + 1) * 256], wi[..., 1024 + h * 256:1024 + (h + 1) * 256],
                                     wi[..., 2048 + h * 512:2048 + (h + 1) * 512], wi[..., 4096 + h * 512:4096 + (h + 1) * 512]], -1))
    wih = np.stack(parts, 1)
    w["rwin"] = np.ascontiguousarray(wih.transpose(0, 1, 3, 2, 4)).reshape(2, 4, 128, 12288)
    wo = f32(inp["ret_w_out"]).reshape(2, 4, 4, 128, 1024)
    w["rwout"] = np.ascontiguousarray(wo.transpose(0, 1, 3, 2, 4)).reshape(2, 4, 128, 4096)
    w["rnorm"] = f32(inp["ret_norm"])
    hi = f32(inp["hg_w_in"]).reshape(2, 8, 128, 4, 8, 128)
    w["hwin"] = np.ascontiguousarray(hi.transpose(0, 4, 2, 1, 3, 5)).reshape(2, 8, 128, 4096)
    w["hwout"] = np.ascontiguousarray(f32(inp["hg_w_out"]).reshape(2, 8, 128, 1024))
    w["hnorm"] = f32(inp["hg_norm"])
    w["lbl"] = np.ascontiguousarray(f32(inp["hg_lb_logits"]).reshape(2, 8, 128).transpose(2, 0, 1))
    return w


def host_core_inputs(inp, c):
    xp = np.asarray(inp["x_prompt"], np.float32)[c]
    xs = np.asarray(inp["x_sample"], np.float32)[c * NSS:(c + 1) * NSS].reshape(NSS * DEC, D)
    x = np.concatenate([xp, xs], 0)
    xT = np.ascontiguousarray(x.T.reshape(8, 128, NTOK).transpose(1, 0, 2))
    m = {"xT": xT}
    m["sret"] = np.ascontiguousarray(np.asarray(inp["state_ret"], np.float32)[:, c * NSS:(c + 1) * NSS])
    m["shg"] = np.ascontiguousarray(np.asarray(inp["state_hgrn"], np.float32)[:, c * NSS:(c + 1) * NSS])
    return m


def full_cfg():
    blocks = []
    for l in range(4):
        blocks.append(("ffn", l, 0))
        blocks.append(("ret", l) if l % 2 == 0 else ("hg", l))
        blocks.append(("ffn", l, 1))
    return {"blocks": blocks, "final_norm": True}


def kernel(**inputs):
    nc = build_program(full_cfg())
    shared = {}
    shared.update(host_consts())
    shared.update(host_weights(inputs))
    in_maps = []
    for c in range(8):
        m = dict(shared)
        m.update(host_core_inputs(inputs, c))
        in_maps.append(m)
    res = run_bass_kernel_spmd(nc, in_maps, core_ids=list(range(8)))
    rs = res.results
    y_prompt = np.empty((8, SEQ, D), np.float32)
    y_sample = np.empty((8 * NSS, DEC, D), np.float32)
    nrp = np.empty((2, 8, 4, 256, 512), np.float32)
    nrs = np.empty((2, 8 * NSS, 4, 256, 512), np.float32)
    nhp = np.empty((2, 8, 8, 128, 128), np.float32)
    nhs = np.empty((2, 8 * NSS, 8, 128, 128), np.float32)
    for c in range(8):
        r = rs[c]
        y = np.asarray(r["yT"]).transpose(1, 0, 2).reshape(D, NTOK).T
        y_prompt[c] = y[:SEQ]
        y_sample[c * NSS:(c + 1) * NSS] = y[SEQ:].reshape(NSS, DEC, D)
        nrp[:, c] = r["nrp"]
        nrs[:, c * NSS:(c + 1) * NSS] = r["nrs"]
        nhp[:, c] = r["nhp"]
        nhs[:, c * NSS:(c + 1) * NSS] = r["nhs"]
    return (y_prompt, y_sample, nrp, nrs, nhp, nhs)
```

```python
import numpy as np
from contextlib import ExitStack
import concourse.bass as bass
import concourse.mybir as mybir
from concourse.bass_utils import run_bass_kernel_spmd

F32 = mybir.dt.float32
BF16 = mybir.dt.bfloat16
AF = mybir.ActivationFunctionType
ALU = mybir.AluOpType

D = 1024
SEQ = 2048
NSS = 16
DEC = 4
NTOK = SEQ + NSS * DEC
DFF = 2816
NF = DFF // 128
EPS = 1e-6
TT = [(0, 512), (512, 512), (1024, 512), (1536, 512), (2048, 64)]


class Op:
    __slots__ = ("eng", "fn", "deps", "pos", "sem", "val", "need_sig", "is_dma", "done")


class Slot:
    def __init__(self, sem):
        self.sem = sem
        self.count = 0


class Prog:
    ENGS = ["pe", "act", "dve", "pool", "sp"]
    ENGOBJ = {"pe": "tensor", "act": "scalar", "dve": "vector", "pool": "gpsimd", "sp": "sync"}

    def __init__(self, nc, stack):
        self.nc = nc
        self.stack = stack
        self.q = {e: [] for e in self.ENGS}
        self.esem = {e: stack.enter_context(nc.semaphore("s_" + e)) for e in ["pe", "act", "dve", "pool"]}
        self.ecount = {e: 0 for e in self.ENGS}
        self.slots = []
        self.nphase = 0

    def slot(self):
        s = Slot(self.stack.enter_context(self.nc.semaphore("d%d" % len(self.slots))))
        self.slots.append(s)
        return s

    def op(self, eng, fn, deps=()):
        o = Op()
        o.eng = eng
        o.fn = fn
        o.deps = [d for d in deps if d is not None]
        o.pos = len(self.q[eng])
        o.is_dma = False
        o.need_sig = False
        o.sem = None
        o.val = None
        o.done = False
        self.q[eng].append(o)
        return o

    def dma(self, eng, fn, slot, deps=()):
        o = self.op(eng, fn, deps)
        o.is_dma = True
        slot.count += 16
        o.sem = slot.sem
        o.val = slot.count
        return o

    def _needs_wait(self, o, d):
        if d.done:
            return False
        if d.is_dma:
            return True
        if d.eng == o.eng:
            if o.eng == "pe":
                return False
            return (o.pos - d.pos) <= 2
        return True

    def flush(self):
        nc = self.nc
        drain_deps = []
        for e in self.ENGS:
            last = {}
            for o in self.q[e]:
                if o.is_dma:
                    last[id(o.sem)] = o
            drain_deps += list(last.values())
        self.op("sp", lambda e: e.nop(), drain_deps)
        for e in self.ENGS:
            for o in self.q[e]:
                for d in o.deps:
                    if not d.is_dma and self._needs_wait(o, d):
                        d.need_sig = True
        for e in self.ENGS:
            c = self.ecount[e]
            for o in self.q[e]:
                if o.is_dma:
                    continue
                if o.need_sig:
                    assert e != "sp"
                    c += 1
                    o.sem = self.esem[e]
                    o.val = c
            self.ecount[e] = c
        self.nphase += 1
        with nc.Block() as block:
            for e in self.ENGS:
                ops = self.q[e]
                if not ops:
                    continue

                def body(eng, ops=ops):
                    waited = {}
                    for o in ops:
                        need = {}
                        for d in o.deps:
                            if not self._needs_wait(o, d):
                                continue
                            key = id(d.sem)
                            if key not in need or need[key][1] < d.val:
                                need[key] = (d.sem, d.val)
                        for key, (sem, val) in need.items():
                            if waited.get(key, 0) >= val:
                                continue
                            eng.wait_ge(sem, val)
                            waited[key] = val
                        ins = o.fn(eng)
                        if o.is_dma:
                            ins.then_inc(o.sem, 16)
                        elif o.need_sig:
                            ins.then_inc(o.sem, 1)

                getattr(block, self.ENGOBJ[e])(body)
        for e in self.ENGS:
            for o in self.q[e]:
                o.done = True
                o.fn = None
                o.deps = None
            self.q[e] = []


class Ctx:
    pass


def dbg_dump(C, name, ap, bufs, ncols, bf=False):
    if not getattr(C, "dbg", None) or name in C.dbg["seen"] or name not in C.dbg["want"]:
        return
    key = "b" if bf else "f"
    off = C.dbg["off"][key]
    C.dbg["off"][key] = off + ncols
    C.dbg["seen"][name] = (key, off, ncols)
    dst = C.dbg[key][:, off:off + ncols]
    bop(C.P, "sp", lambda e: e.dma_start(out=dst, in_=ap), r=bufs, slot=C.dbg["slot"])


class Buf:
    def __init__(self):
        self.w = None
        self.r = []


class PBuf(Buf):
    excl = True


def bop(P, eng, fn, r=(), w=(), slot=None, deps=()):
    xr = [b for b in r if getattr(b, "excl", False)]
    if xr:
        r = [b for b in r if not getattr(b, "excl", False)]
        w = list(w) + [b for b in xr if b not in w]
    d = list(deps)
    for b in r:
        d.append(b.w)
    for b in w:
        d.append(b.w)
        d.extend(b.r)
    o = P.dma(eng, fn, slot, d) if slot is not None else P.op(eng, fn, d)
    for b in r:
        if not o.is_dma:
            b.r = [x for x in b.r if x.is_dma or x.eng != eng]
        b.r.append(o)
    for b in w:
        b.w = o
        b.r = []
    return o


_UID = [0]


def mk_alloc(C, st):
    _UID[0] += 1
    u = _UID[0]
    nc = C.nc
    sb = lambda name, shape, dt: st.enter_context(nc.sbuf_tensor("%s_%d" % (name, u), shape, dt))
    pt = lambda name, shape, dt: st.enter_context(nc.psum_tensor("%s_%d" % (name, u), shape, dt))
    return sb, pt


def emit_norm(C, ph, gcol, out_fn=None):
    P = C.P
    sq, rs, psn = ph.sq, ph.rs, ph.psn
    last = []
    sq_rd = [None, None]
    rs_rd = [None, None]
    ps_rd = None
    for ti, (t0, n) in enumerate(TT):
        b = ti % 2
        a = P.op("act", lambda e, b=b, t0=t0, n=n: e.activation(out=sq[b][:, :, :n], in_=C.xres[:, :, t0:t0 + n], func=AF.Square),
                 [sq_rd[b]])
        mm = None
        for k in range(8):
            mm = P.op("pe", lambda e, b=b, k=k, n=n: e.matmul(psn[:, :n], lhsT=C.ones[:], rhs=sq[b][:, k, :n], start=(k == 0), stop=(k == 7)),
                      [a, ps_rd])
        sq_rd[b] = mm
        v = P.op("act", lambda e, b=b, n=n: e.activation(out=rs[b][:, :n], in_=psn[:, :n], func=AF.Ln, scale=1.0 / D, bias=C.epsc[:, 0:1]),
                 [mm, rs_rd[b]])
        ps_rd = v
        r = P.op("act", lambda e, b=b, n=n: e.activation(out=rs[b][:, :n], in_=rs[b][:, :n], func=AF.Exp, scale=-0.5), [v])
        o = None
        for k in range(8):
            if out_fn is None:
                o = P.op("dve", lambda e, b=b, k=k, t0=t0, n=n: e.scalar_tensor_tensor(
                    out=C.xn[:, k, t0:t0 + n], in0=C.xres[:, k, t0:t0 + n], scalar=C.gains[:, gcol * 8 + k:gcol * 8 + k + 1],
                    in1=rs[b][:, :n], op0=ALU.mult, op1=ALU.mult), [r])
            else:
                o = out_fn(ti, k, t0, n, rs[b], r)
        rs_rd[b] = o
        last.append(o)
    return last


def emit_normphase(C, gcol):
    with ExitStack() as st:
        sb, pt = mk_alloc(C, st)
        ph = Ctx()
        ph.sq = [sb("sq%d" % i, [128, 8, 512], BF16) for i in range(2)]
        ph.rs = [sb("rs%d" % i, [128, 512], F32) for i in range(2)]
        ph.psn = pt("psn", [128, 512], F32)
        emit_norm(C, ph, gcol)
        C.P.flush()


def emit_ffn(C, gcol, wup, wdn):
    P, nc = C.P, C.nc
    with ExitStack() as st:
        sb, pt = mk_alloc(C, st)
        hid = sb("hid", [128, 11, NTOK], BF16)
        wu = [sb("wu%d" % i, [128, 8, 2, 128], BF16) for i in range(3)]
        wd = sb("wd", [128, 11, 1024], BF16)
        sa = [sb("sa%d" % i, [128, 512], F32) for i in range(2)]
        sqs = [sb("sqs%d" % i, [128, 512], BF16) for i in range(4)]
        rs = [sb("rs%d" % i, [128, 512], F32) for i in range(2)]
        psn = pt("psn", [128, 512], F32)
        psA = [pt("psA%d" % i, [128, 512], F32) for i in range(2)]
        psB = [pt("psB%d" % i, [128, 512], F32) for i in range(2)]
        psD = [pt("psD%d" % i, [128, 512], F32) for i in range(2)]
        bwu = [Buf() for _ in range(3)]; bwd = Buf(); bsa = [Buf(), Buf()]; bsq = [Buf() for _ in range(4)]; brs = [Buf(), Buf()]
        bpsn = PBuf(); bpsA = [PBuf(), PBuf()]; bpsB = [PBuf(), PBuf()]; bpsD = [PBuf(), PBuf()]
        bxn = [Buf() for _ in TT]; bxr = [Buf() for _ in TT]; bhid = [Buf() for _ in TT]
        s_wu = [P.slot() for _ in range(3)]
        s_wd = P.slot()

        def load_wu(fi):
            s = fi % 3
            bop(P, "pool", lambda e: e.dma_start(out=wu[s][:].rearrange("p k a c -> p (k a c)"), in_=wup[fi], max_dma_last_dim=8192), w=[bwu[s]], slot=s_wu[s])

        def load_wd(half):
            bop(P, "pool", lambda e: e.dma_start(out=wd[:].rearrange("p f n -> p (f n)"), in_=wdn[half], max_dma_last_dim=8192), w=[bwd], slot=s_wd)

        sqc = [0]

        def norm(ti):
            t0, n = TT[ti]
            b = ti % 2
            for k in range(8):
                q = sqc[0] % 4
                sqc[0] += 1
                bop(P, "act", lambda e, k=k, q=q: e.activation(out=sqs[q][:, :n], in_=C.xres[:, k, t0:t0 + n], func=AF.Square), r=[bxr[ti]], w=[bsq[q]])
                bop(P, "pe", lambda e, k=k, q=q: e.matmul(psn[:, :n], lhsT=C.ones[:], rhs=sqs[q][:, :n], start=(k == 0), stop=(k == 7)), r=[bsq[q]], w=[bpsn])
            bop(P, "act", lambda e: e.activation(out=rs[b][:, :n], in_=psn[:, :n], func=AF.Ln, scale=1.0 / D, bias=C.epsc[:, 0:1]), r=[bpsn], w=[brs[b]])
            bop(P, "act", lambda e: e.activation(out=rs[b][:, :n], in_=rs[b][:, :n], func=AF.Exp, scale=-0.5), r=[brs[b]], w=[brs[b]])
            for k in range(8):
                bop(P, "dve", lambda e, k=k: e.scalar_tensor_tensor(
                    out=C.xn[:, k, t0:t0 + n], in0=C.xres[:, k, t0:t0 + n], scalar=C.gains[:, gcol * 8 + k:gcol * 8 + k + 1],
                    in1=rs[b][:, :n], op0=ALU.mult, op1=ALU.mult), r=[brs[b], bxr[ti]], w=[bxn[ti]])

        load_wd(0)
        for fi in range(3):
            load_wu(fi)
        norm(0)
        norm(1)
        cnt = 0
        dcnt = 0
        for half in range(2):
            if half == 1:
                load_wd(1)
            for f in range(11):
                fi = half * 11 + f
                s = fi % 3
                for ti, (t0, n) in enumerate(TT):
                    b = cnt % 2
                    cnt += 1
                    for k in range(8):
                        bop(P, "pe", lambda e, b=b, s=s, k=k, t0=t0, n=n: e.matmul(psA[b][:, :n], lhsT=wu[s][:, k, 0, :], rhs=C.xn[:, k, t0:t0 + n], start=(k == 0), stop=(k == 7)),
                            r=[bwu[s], bxn[ti]], w=[bpsA[b]])
                    for k in range(8):
                        bop(P, "pe", lambda e, b=b, s=s, k=k, t0=t0, n=n: e.matmul(psB[b][:, :n], lhsT=wu[s][:, k, 1, :], rhs=C.xn[:, k, t0:t0 + n], start=(k == 0), stop=(k == 7)),
                            r=[bwu[s], bxn[ti]], w=[bpsB[b]])
                    bop(P, "act", lambda e, b=b, n=n: e.activation(out=sa[b][:, :n], in_=psA[b][:, :n], func=AF.Silu), r=[bpsA[b]], w=[bsa[b]])
                    bop(P, "dve", lambda e, b=b, f=f, t0=t0, n=n: e.tensor_tensor(out=hid[:, f, t0:t0 + n], in0=sa[b][:, :n], in1=psB[b][:, :n], op=ALU.mult),
                        r=[bsa[b], bpsB[b]], w=[bhid[ti]])
                    if fi == 0 and ti + 2 < len(TT):
                        norm(ti + 2)
                if fi + 3 < NF:
                    load_wu(fi + 3)
            for ti, (t0, n) in enumerate(TT):
                for mo in range(8):
                    b = dcnt % 2
                    dcnt += 1
                    for f in range(11):
                        bop(P, "pe", lambda e, b=b, f=f, mo=mo, t0=t0, n=n: e.matmul(psD[b][:, :n], lhsT=wd[:, f, mo * 128:(mo + 1) * 128], rhs=hid[:, f, t0:t0 + n], start=(f == 0), stop=(f == 10)),
                            r=[bwd, bhid[ti]], w=[bpsD[b]])
                    bop(P, "dve", lambda e, b=b, mo=mo, t0=t0, n=n: e.scalar_tensor_tensor(
                        out=C.xres[:, mo, t0:t0 + n], in0=psD[b][:, :n], scalar=0.5, in1=C.xres[:, mo, t0:t0 + n], op0=ALU.mult, op1=ALU.add),
                        r=[bpsD[b]], w=[bxr[ti]])
        P.flush()


RET_H = 4
CR_MP = 0
CR_MS = 512
CR_KDP = 768
CR_EPP = 772
CR_KDS = 776
CR_EPS = 780
CR_RM = 784
CRW = 800


def ret_gammas():
    return [1.0 - 2.0 ** (-5.0 - h) for h in range(RET_H)]


def emit_ret(C, j, rwin, rwout, rnorm, cs, st_in, st_out_p, st_out_s):
    P, nc = C.P, C.nc
    gam = ret_gammas()
    with ExitStack() as st:
        sb, pt = mk_alloc(C, st)
        wh = [sb("wh%d" % i, [128, 8, 1536], BF16) for i in range(2)]
        wo = sb("wo", [128, 4, 1024], BF16)
        cst = sb("cs", [128, 2, 512], F32)
        qT = sb("qT", [128, 2, 512], BF16)
        kT = sb("kT", [128, 2, 512], BF16)
        vtok = [sb("vtok%d" % i, [128, 512], BF16) for i in range(2)]
        ktl = [sb("ktl%d" % i, [128, 256], BF16) for i in range(2)]
        scm = [sb("scm%d" % i, [128, 128], BF16) for i in range(2)]
        gs = sb("gs", [128, 512], F32)
        on = sb("on", [128, 512], F32)
        go = [sb("go%d" % i, [128, 512], BF16) for i in range(2)]
        goT = sb("goT", [128, 4, 512], BF16)
        S = sb("S", [128, 2, 512], F32)
        Sb = sb("Sb", [128, 2, 512], BF16)
        gn = sb("gn", [128, 512], F32)
        qz = sb("qz", [128, 2, 16, 64], BF16)
        kz = [sb("kz%d" % i, [64, 256], BF16) for i in range(2)]
        S0f = [sb("S0f%d" % i, [128, 512], F32) for i in range(2)]
        S0b = [sb("S0b%d" % i, [128, 512], BF16) for i in range(2)]
        S0f = [t[:, :] for t in S0f] + [S[:, 0, :], S[:, 1, :]]
        S0b = [t[:, :] for t in S0b] + [Sb[:, 0, :], Sb[:, 1, :]]
        st4 = sb("st4", [128, 4], F32)
        B = [pt("b%d" % i, [128, 512], F32) for i in range(8)]
        PT = [B[6][:, 0:128].bitcast(BF16), B[3][:, 0:128].bitcast(BF16)]
        SC = [B[6][:, 128:256], B[3][:, 128:256]]
        TR = B[3][:, 256:512].bitcast(BF16)
        cret = C.cret

        bwh = [Buf(), Buf()]; bwo = Buf(); bcs = Buf(); bt12 = Buf(); bqT = Buf(); bkT = Buf()
        bvtok = [Buf(), Buf()]; bktl = [Buf(), Buf()]; bscm = [Buf(), Buf()]; bgs = Buf(); bon = Buf(); bgo = [Buf(), Buf()]; bgoT = Buf()
        bSh = [Buf(), Buf()]; bSbh = [Buf(), Buf()]; bgn = Buf(); bqz = Buf(); bkz = [Buf(), Buf()]
        bS0f = [Buf(), Buf()] + bSh; bS0b = [Buf(), Buf()] + bSbh; bst4 = Buf()
        bB = [PBuf() for _ in range(8)]
        bB7t = bB[3]
        bPT = [bB[6], bB[3]]; bSC = [bB[6], bB[3]]
        bx = [[Buf() for _ in TT] for _ in range(8)]
        s_wh = [P.slot(), P.slot()]; s_wo = P.slot(); s_cs = P.slot(); s_gn = P.slot()
        s_S0f = [P.slot() for _ in range(4)]; s_S0b = [P.slot() for _ in range(4)]; s_so = [P.slot() for _ in range(4)]; s_sp = P.slot()

        def load_wh(h):
            sl = h % 2
            bop(P, "pool", lambda e, sl=sl: e.dma_start(out=wh[sl][:].rearrange("p k n -> p (k n)"), in_=rwin[h], max_dma_last_dim=8192),
                w=[bwh[sl]], slot=s_wh[sl])

        load_wh(0)
        dbg_dump(C, "xn0", C.xn[:, 0, 0:512], [], 512, bf=True)
        dbg_dump(C, "wh0", wh[0][:, 0, 0:512], [bwh[0]], 512, bf=True)
        ucnt = 0
        pending = [None]

        def flush_pending(rot_k=None):
            items = pending[0]
            pending[0] = None
            rk = list(rot_k) if rot_k else []
            if items is None:
                for f in rk:
                    f()
                return
            for m, (mm_f, add_f) in enumerate(items):
                mm_f()
                add_f()
                if rk and m < 6:
                    rk.pop(0)()
            for f in rk:
                f()

        for h in range(RET_H):
            sl = h % 2
            g = gam[h]
            flush_pending()
            bop(P, "pool", lambda e, h=h: e.dma_start(out=wo[:].rearrange("p a n -> p (a n)"), in_=rwout[h], max_dma_last_dim=8192),
                w=[bwo], slot=s_wo)
            if h + 1 < RET_H:
                load_wh(h + 1)
            bop(P, "sp", lambda e, h=h: e.dma_start(out=gn[:], in_=rnorm[h].partition_broadcast(128)), w=[bgn], slot=s_gn)
            bop(P, "pool", lambda e: e.memset(S[:], 0.0), w=[bSh[0], bSh[1]])
            bop(P, "pool", lambda e: e.memset(Sb[:], 0.0), w=[bSbh[0], bSbh[1]])
            for ti, (t0, n) in enumerate(TT):
                sample = (ti == 4)
                bop(P, "sp", lambda e, t0=t0, n=n: e.dma_start(out=cst[:, :, :n], in_=cs[:, :, t0:t0 + n].rearrange("a p t -> p a t")),
                    w=[bcs], slot=s_cs)
                for qi in range(4):
                    for k in range(8):
                        bop(P, "pe", lambda e, sl=sl, qi=qi, k=k, t0=t0, n=n: e.matmul(B[qi][:, :n], lhsT=wh[sl][:, k, qi * 128:(qi + 1) * 128], rhs=C.xn[:, k, t0:t0 + n], start=(k == 0), stop=(k == 7)),
                            r=[bwh[sl]], w=[bB[qi]])
                rot = {0: [], 2: []}
                for (dst, bd, b0, b1, sc) in ((qT, bqT, 0, 1, 1.0), (kT, bkT, 2, 3, 0.0625)):
                    for half in range(2):
                        ca, cb = (0, 1) if half == 0 else (1, 0)
                        rot[b0].append(lambda b0=b0, ca=ca, sc=sc, n=n: bop(P, "dve", lambda e: e.scalar_tensor_tensor(out=gs[:, :n], in0=B[b0][:, :n], scalar=sc, in1=cst[:, ca, :n], op0=ALU.mult, op1=ALU.mult),
                                                                             r=[bB[b0], bcs], w=[bgs]))
                        rot[b0].append(lambda b1=b1, cb=cb, sc=sc, n=n: bop(P, "dve", lambda e: e.scalar_tensor_tensor(out=on[:, :n], in0=B[b1][:, :n], scalar=sc, in1=cst[:, cb, :n], op0=ALU.mult, op1=ALU.mult),
                                                                             r=[bB[b1], bcs], w=[bon]))
                        rot[b0].append(lambda dst=dst, bd=bd, half=half, n=n: bop(P, "dve", lambda e: e.tensor_tensor(out=dst[:, half, :n], in0=gs[:, :n], in1=on[:, :n], op=(ALU.subtract if half == 0 else ALU.add)),
                                                                                   r=[bgs, bon], w=[bd]))
                for f in rot[0]:
                    f()
                flush_pending(rot[2])
                blocks = [(0, 64)] if sample else [(c * 128, 128) for c in range(n // 128)]
                MC = (CR_MS + h * 64) if sample else (CR_MP + h * 128)
                KD = (CR_KDS if sample else CR_KDP) + h
                EP = (CR_EPS if sample else CR_EPP) + h

                def stage_A(ci, c0, nb, sl=sl, t0=t0, MC=MC, KD=KD):
                    pb = ci % 2
                    a0 = t0 + c0
                    vb = 4 + pb
                    for k in range(8):
                        bop(P, "pe", lambda e, k=k: e.matmul(B[vb][:nb, :], lhsT=C.xn[:, k, a0:a0 + nb], rhs=wh[sl][:, k, 512:1024], start=(k == 0), stop=(k == 7)),
                            r=[bwh[sl]], w=[bB[vb]])
                    bop(P, "act", lambda e: e.activation(out=vtok[pb][:nb, :], in_=B[vb][:nb, :], func=AF.Copy), r=[bB[vb]], w=[bvtok[pb]])
                    for jj in range(2):
                        bop(P, "pe", lambda e, jj=jj: e.transpose(out=PT[pb][:nb, jj * 128:(jj + 1) * 128], in_=kT[:, jj, c0:c0 + nb], identity=C.ident),
                            r=[bkT], w=[bPT[pb]])
                    for jj in range(2):
                        bop(P, "pe", lambda e, jj=jj: e.matmul(SC[pb][:nb, :nb], lhsT=kT[:, jj, c0:c0 + nb], rhs=qT[:, jj, c0:c0 + nb], start=(jj == 0), stop=(jj == 1)),
                            r=[bkT, bqT], w=[bSC[pb]])
                    bop(P, "dve", lambda e: e.tensor_scalar(out=ktl[pb][:nb, :], in0=PT[pb][:nb, :], scalar1=cret[:nb, KD:KD + 1], scalar2=None, op0=ALU.mult),
                        r=[bPT[pb]], w=[bktl[pb]])
                    bop(P, "dve", lambda e: e.tensor_tensor(out=scm[pb][:nb, :nb], in0=SC[pb][:nb, :nb], in1=cret[:nb, MC:MC + nb], op=ALU.mult),
                        r=[bSC[pb]], w=[bscm[pb]])

                def stage_B(ci, c0, nb, sl=sl, t0=t0, EP=EP, sample=sample, g=g, h=h):
                    nonlocal ucnt
                    pb = ci % 2
                    a0 = t0 + c0
                    bop(P, "pe", lambda e: e.matmul(B[7][:nb, :], lhsT=scm[pb][:nb, :nb], rhs=vtok[pb][:nb, :], start=True, stop=False),
                        r=[bscm[pb], bvtok[pb]], w=[bB[7]])
                    if not sample:
                        for jj in range(2):
                            bop(P, "pe", lambda e, jj=jj: e.matmul(B[7][:nb, :], lhsT=qT[:, jj, c0:c0 + nb], rhs=Sb[:, jj, :], start=False, stop=(jj == 1)),
                                r=[bqT, bSbh[jj]], w=[bB[7]])
                        for jj in range(2):
                            bop(P, "pe", lambda e, jj=jj: e.matmul(B[jj][:, :], lhsT=ktl[pb][:nb, jj * 128:(jj + 1) * 128], rhs=vtok[pb][:nb, :], start=True, stop=True),
                                r=[bktl[pb], bvtok[pb]], w=[bB[jj]])
                        cd = g ** 128
                        for jj in range(2):
                            bop(P, "dve", lambda e, jj=jj: e.scalar_tensor_tensor(out=S[:, jj, :], in0=S[:, jj, :], scalar=cd, in1=B[jj][:, :], op0=ALU.mult, op1=ALU.add),
                                r=[bB[jj]], w=[bSh[jj]])
                        for jj in range(2):
                            bop(P, "pool", lambda e, jj=jj: e.tensor_copy(out=Sb[:, jj, :], in_=S[:, jj, :]), r=[bSh[jj]], w=[bSbh[jj]])
                    else:
                        for jj in range(2):
                            bop(P, "dve", lambda e, jj=jj: e.tensor_tensor(out=qz[:, jj, :, :], in0=qT[:, jj, 0:64].unsqueeze(1).broadcast_to([128, 16, 64]), in1=C.bm[:, :, :], op=ALU.mult),
                                r=[bqT], w=[bqz])
                        cd = g ** 4
                        units = [(i, jj) for i in range(NSS) for jj in range(2)]

                        def issue_loads(k):
                            i, jj = units[k]
                            u = k % 4
                            bop(P, "pool", lambda e: e.dma_start(out=S0b[u], in_=st_in[i, h, jj * 128:(jj + 1) * 128, :]), w=[bS0b[u]], slot=s_S0b[u])
                            bop(P, "sp", lambda e: e.dma_start(out=S0f[u], in_=st_in[i, h, jj * 128:(jj + 1) * 128, :]), w=[bS0f[u]], slot=s_S0f[u])

                        issue_loads(0)
                        issue_loads(1)
                        for k, (i, jj) in enumerate(units):
                            if k + 2 < len(units):
                                issue_loads(k + 2)
                            kb = i % 2
                            u = k % 4
                            if jj == 0:
                                bop(P, "dve", lambda e, i=i, kb=kb: e.tensor_scalar(out=kz[kb][:, :], in0=ktl[pb][:64, :], scalar1=cret[:64, CR_RM + i:CR_RM + i + 1], scalar2=None, op0=ALU.mult),
                                    r=[bktl[pb]], w=[bkz[kb]])
                            last = (k == len(units) - 1)
                            bop(P, "pe", lambda e, u=u, i=i, jj=jj, last=last: e.matmul(B[7][:64, :], lhsT=qz[:, jj, i, :], rhs=S0b[u], start=False, stop=last),
                                r=[bqz, bS0b[u]], w=[bB[7]])
                            bop(P, "pe", lambda e, kb=kb, jj=jj: e.matmul(B[jj][:, :], lhsT=kz[kb][:, jj * 128:(jj + 1) * 128], rhs=vtok[pb][:64, :], start=True, stop=True),
                                r=[bkz[kb], bvtok[pb]], w=[bB[jj]])
                            bop(P, "dve", lambda e, u=u, jj=jj: e.scalar_tensor_tensor(out=S0f[u], in0=S0f[u], scalar=cd, in1=B[jj][:, :], op0=ALU.mult, op1=ALU.add),
                                r=[bB[jj]], w=[bS0f[u]])
                            bop(P, "sp", lambda e, u=u, i=i, jj=jj: e.dma_start(out=st_out_s[i, h, jj * 128:(jj + 1) * 128, :], in_=S0f[u]), r=[bS0f[u]], slot=s_so[u])
                    bop(P, "act", lambda e: e.activation(out=on[:nb, :], in_=B[7][:nb, :], func=AF.Square, accum_out=st4[:nb, 0:1]),
                        r=[bB[7]], w=[bon, bst4])
                    bop(P, "act", lambda e: e.activation(out=st4[:nb, 2:3], in_=st4[:nb, 0:1], func=AF.Ln, scale=1.0 / 512, bias=cret[:nb, EP:EP + 1]), r=[bst4], w=[bst4])
                    bop(P, "act", lambda e: e.activation(out=st4[:nb, 3:4], in_=st4[:nb, 2:3], func=AF.Exp, scale=-0.5), r=[bst4], w=[bst4])
                    bop(P, "dve", lambda e: e.scalar_tensor_tensor(out=on[:nb, :], in0=B[7][:nb, :], scalar=st4[:nb, 3:4], in1=gn[:nb, :], op0=ALU.mult, op1=ALU.mult),
                        r=[bB[7], bst4, bgn], w=[bon])
                    for k in range(8):
                        bop(P, "pe", lambda e, k=k: e.matmul(B[2][:nb, :], lhsT=C.xn[:, k, a0:a0 + nb], rhs=wh[sl][:, k, 1024:1536], start=(k == 0), stop=(k == 7)),
                            r=[bwh[sl]], w=[bB[2]])
                    bop(P, "act", lambda e: e.activation(out=gs[:nb, :], in_=B[2][:nb, :], func=AF.Exp, scale=-1.0), r=[bB[2]], w=[bgs])
                    bop(P, "act", lambda e: e.activation(out=gs[:nb, :], in_=gs[:nb, :], func=AF.Ln, bias=C.one[:nb, 0:1]), r=[bgs], w=[bgs])
                    bop(P, "act", lambda e: e.activation(out=gs[:nb, :], in_=gs[:nb, :], func=AF.Exp, scale=-1.0), r=[bgs], w=[bgs])
                    bop(P, "dve", lambda e: e.tensor_tensor(out=gs[:nb, :], in0=B[2][:nb, :], in1=gs[:nb, :], op=ALU.mult), r=[bgs, bB[2]], w=[bgs])
                    bop(P, "pool", lambda e: e.tensor_tensor(out=go[pb][:nb, :], in0=on[:nb, :], in1=gs[:nb, :], op=ALU.mult), r=[bon, bgs], w=[bgo[pb]])

                def stage_T(ci, c0, nb):
                    pb = ci % 2
                    for e4 in range(4):
                        bop(P, "pe", lambda e, e4=e4: e.transpose(out=TR[:, e4 * 128:e4 * 128 + nb], in_=go[pb][:nb, e4 * 128:(e4 + 1) * 128], identity=C.ident[:nb, :nb]),
                            r=[bgo[pb]], w=[bB7t])
                    bop(P, "act", lambda e: e.activation(out=goT[:, :, c0:c0 + nb], in_=TR.rearrange("p (a t) -> p a t", a=4)[:, :, :nb], func=AF.Copy),
                        r=[bB7t], w=[bgoT])

                nblk = len(blocks)
                sched = []
                if nblk == 1:
                    sched = [("A", 0), ("B", 0), ("T", 0)]
                else:
                    sched = [("A", 0), ("A", 1), ("B", 0), ("A", 2), ("B", 1), ("T", 0), ("A", 3), ("B", 2), ("T", 1), ("B", 3), ("T", 2), ("T", 3)]
                for (kind, ci) in sched:
                    c0, nb = blocks[ci]
                    if kind == "A":
                        stage_A(ci, c0, nb)
                    elif kind == "B":
                        stage_B(ci, c0, nb)
                    else:
                        stage_T(ci, c0, nb)
                items = []
                for m in range(8):
                    wb = 4 + (m % 2)

                    def mm_f(m=m, wb=wb, n=n):
                        for e4 in range(4):
                            bop(P, "pe", lambda e, e4=e4: e.matmul(B[wb][:, :n], lhsT=wo[:, e4, m * 128:(m + 1) * 128], rhs=goT[:, e4, :n], start=(e4 == 0), stop=(e4 == 3)),
                                r=[bwo, bgoT], w=[bB[wb]])

                    def add_f(m=m, wb=wb, t0=t0, n=n, ti=ti):
                        bop(P, "dve", lambda e: e.tensor_tensor(out=C.xres[:, m, t0:t0 + n], in0=C.xres[:, m, t0:t0 + n], in1=B[wb][:, :n], op=ALU.add),
                            r=[bB[wb]], w=[bx[m][ti]])
                    items.append((mm_f, add_f))
                pending[0] = items
                if ti == 3:
                    bop(P, "sp", lambda e, h=h: e.dma_start(out=st_out_p[h].rearrange("(a p) n -> p a n", p=128), in_=S[:, :, :]), r=[bSh[0], bSh[1]], slot=s_sp)
        flush_pending()
        P.flush()


HG_H = 8
CH_MP = 800
CH_MS = 928
CH_RMP = 992
CH_CMP = 1000
CH_CMS = 1512
CRW2 = 1576


def emit_hg(C, j, hwin, hwout, hnorm, lbl, st_in, st_out_p, st_out_s):
    P, nc = C.P, C.nc
    with ExitStack() as st:
        sb, pt = mk_alloc(C, st)
        wh = [sb("hwh%d" % i, [128, 8, 512], BF16) for i in range(2)]
        wo = [sb("hwo%d" % i, [128, 1024], BF16) for i in range(2)]
        F = {nm: sb("h" + nm, [128, 512], F32) for nm in ("qs", "ez", "r", "f", "kk", "b", "d1", "X", "Y")}
        qc = [sb("hqc%d" % i, [128, 512], BF16) for i in range(2)]
        kc = [sb("hkc%d" % i, [128, 512], BF16) for i in range(2)]
        eB = [sb("heB%d" % i, [128, 16], F32) for i in range(2)]
        vtok = [sb("hvtok%d" % i, [128, 128], BF16) for i in range(2)]
        gsil = [sb("hgsil%d" % i, [128, 128], F32) for i in range(2)]
        ktl = [sb("hktl%d" % i, [128, 128], BF16) for i in range(2)]
        scm = [sb("hscm%d" % i, [128, 128], BF16) for i in range(2)]
        qz = [sb("hqz%d" % i, [128, 1024], BF16) for i in range(2)]
        kz = [sb("hkz%d" % i, [128, 2048], BF16) for i in range(2)]
        Sdb = [sb("hSdb%d" % i, [128, 16, 128], BF16) for i in range(2)]
        S = sb("hS", [128, 128], F32)
        S0 = sb("hS0", [128, 16, 128], F32)
        on = sb("hon", [128, 128], F32)
        go = [sb("hgo%d" % i, [128, 128], BF16) for i in range(2)]
        goT = sb("hgoT", [128, 512], BF16)
        gn = sb("hgn", [128, 128], F32)
        lb = sb("hlb", [128, 2, 8], F32)
        lbv = sb("hlbv", [128, 8], F32)
        oml = sb("homl", [128, 8], F32)
        st4 = sb("hst4", [128, 4], F32)
        B = [pt("hb%d" % i, [128, 512], F32) for i in range(8)]
        VG = [B[4][:, 0:256], B[6][:, 0:256]]
        PT = [B[4][:, 256:320].bitcast(BF16), B[6][:, 256:320].bitcast(BF16)]
        SC = [B[4][:, 320:448], B[6][:, 320:448]]
        TR = B[3][:, 256:320].bitcast(BF16)
        UR = [B[5][:, i * 128:(i + 1) * 128] for i in range(4)] + [B[7][:, i * 128:(i + 1) * 128] for i in range(4)]
        cret = C.cret
        bF = {nm: Buf() for nm in F}
        bwh = [Buf(), Buf()]; bwo = [Buf(), Buf()]; bqc = [Buf(), Buf()]; bkc = [Buf(), Buf()]; beB = [Buf(), Buf()]
        bvtok = [Buf(), Buf()]; bgsil = [Buf(), Buf()]; bktl = [Buf(), Buf()]; bscm = [Buf(), Buf()]
        bqz = [Buf(), Buf()]; bkz = [Buf(), Buf()]; bSdb = [Buf(), Buf()]; bgo = [Buf(), Buf()]
        bS = Buf(); bS0 = Buf(); bon = Buf(); bgoT = Buf(); bgn = Buf(); blb = Buf(); bst4 = Buf()
        bB = [PBuf() for _ in range(8)]
        bVG = [bB[4], bB[6]]; bPT = [bB[4], bB[6]]; bSC = [bB[4], bB[6]]; bTR = bB[3]; bUR = [bB[5]] * 4 + [bB[7]] * 4
        bx = [[Buf() for _ in TT] for _ in range(8)]
        s_wh = [P.slot(), P.slot()]; s_wo = [P.slot(), P.slot()]; s_gn = P.slot(); s_lb = P.slot()
        s_S0 = P.slot(); s_so = P.slot(); s_sp = P.slot()

        def sigmoid_act(dst, src, bdst, bsrc):
            bop(P, "act", lambda e: e.activation(out=dst, in_=src, func=AF.Exp, scale=-1.0), r=[bsrc], w=[bdst])
            bop(P, "act", lambda e: e.activation(out=dst, in_=dst, func=AF.Ln, bias=C.one[:dst.shape[0], 0:1]), r=[bdst], w=[bdst])
            bop(P, "act", lambda e: e.activation(out=dst, in_=dst, func=AF.Exp, scale=-1.0), r=[bdst], w=[bdst])

        bop(P, "sp", lambda e: e.dma_start(out=lb[:], in_=lbl), w=[blb], slot=s_lb)
        if j == 0:
            bop(P, "dve", lambda e: e.memset(lbv[:], 0.0), w=[blb])
            bop(P, "dve", lambda e: e.memset(oml[:], 1.0), w=[blb])
        else:
            bop(P, "dve", lambda e: e.tensor_tensor(out=lbv[:], in0=lb[:, 0, :], in1=lb[:, 1, :], op=ALU.subtract), r=[blb], w=[blb])
            bop(P, "act", lambda e: e.activation(out=oml[:], in_=lbv[:], func=AF.Exp), r=[blb], w=[blb])
            bop(P, "dve", lambda e: e.tensor_scalar(out=lbv[:], in0=oml[:], scalar1=1.0, scalar2=None, op0=ALU.add), r=[blb], w=[blb])
            bop(P, "dve", lambda e: e.reciprocal(out=lbv[:], in_=lbv[:]), r=[blb], w=[blb])
            bop(P, "dve", lambda e: e.tensor_tensor(out=oml[:], in0=oml[:], in1=lbv[:], op=ALU.mult), r=[blb], w=[blb])

        def load_w(h):
            sl = h % 2
            bop(P, "pool", lambda e, sl=sl, h=h: e.dma_start(out=wh[sl][:].rearrange("p k n -> p (k n)"), in_=hwin[h], max_dma_last_dim=8192),
                w=[bwh[sl]], slot=s_wh[sl])
            bop(P, "pool", lambda e, sl=sl, h=h: e.dma_start(out=wo[sl][:], in_=hwout[h], max_dma_last_dim=8192), w=[bwo[sl]], slot=s_wo[sl])

        load_w(0)
        ugl = 0
        jobs = [(h, ti) for h in range(HG_H) for ti in range(len(TT))]

        def head_setup(h):
            if h + 1 < HG_H:
                load_w(h + 1)
            bop(P, "sp", lambda e: e.dma_start(out=gn[:], in_=hnorm[h].partition_broadcast(128)), w=[bgn], slot=s_gn)
            bop(P, "pool", lambda e: e.memset(S[:], 0.0), w=[bS])
            bop(P, "sp", lambda e: e.dma_start(out=S0[:], in_=st_in[:, h, :, :].rearrange("i d e -> d i e")), w=[bS0], slot=s_S0)

        def make_s1(h, ti, kp):
            sl = h % 2
            t0, n = TT[ti]
            sample = (ti == 4)
            CL = 4 if sample else 32
            nch = n // CL
            CM = CH_CMS if sample else CH_CMP
            qcj, kcj, eBj = qc[kp], kc[kp], eB[kp]

            def p0():
                for qi in range(2):
                    for k in range(8):
                        bop(P, "pe", lambda e, qi=qi, k=k: e.matmul(B[qi][:, :n], lhsT=wh[sl][:, k, qi * 128:(qi + 1) * 128], rhs=C.xn[:, k, t0:t0 + n], start=(k == 0), stop=(k == 7)),
                            r=[bwh[sl]], w=[bB[qi]])

            def p1():
                sigmoid_act(F["qs"][:, :n], B[0][:, :n], bF["qs"], bB[0])

            def p1b():
                bop(P, "act", lambda e: e.activation(out=F["ez"][:, :n], in_=B[1][:, :n], func=AF.Exp, scale=-1.0), r=[bB[1]], w=[bF["ez"]])
                bop(P, "act", lambda e: e.activation(out=F["r"][:, :n], in_=F["ez"][:, :n], func=AF.Ln, bias=C.one[:, 0:1]), r=[bF["ez"]], w=[bF["r"]])
                bop(P, "act", lambda e: e.activation(out=F["r"][:, :n], in_=F["r"][:, :n], func=AF.Exp, scale=-1.0), r=[bF["r"]], w=[bF["r"]])

            def p2():
                bop(P, "dve", lambda e: e.tensor_tensor(out=F["qs"][:, :n], in0=B[0][:, :n], in1=F["qs"][:, :n], op=ALU.mult), r=[bF["qs"], bB[0]], w=[bF["qs"]])
                bop(P, "dve", lambda e: e.tensor_scalar(out=F["f"][:, :n], in0=F["r"][:, :n], scalar1=oml[:, h:h + 1], scalar2=lbv[:, h:h + 1], op0=ALU.mult, op1=ALU.add),
                    r=[bF["r"], blb], w=[bF["f"]])
                bop(P, "dve", lambda e: e.scalar_tensor_tensor(out=F["kk"][:, :n], in0=F["ez"][:, :n], scalar=oml[:, h:h + 1], in1=F["r"][:, :n], op0=ALU.mult, op1=ALU.mult),
                    r=[bF["ez"], bF["r"], blb], w=[bF["kk"]])
                bop(P, "act", lambda e: e.activation(out=F["f"][:, :n], in_=F["f"][:, :n], func=AF.Ln), r=[bF["f"]], w=[bF["f"]])

            def p3():
                bop(P, "dve", lambda e: e.tensor_tensor_scan(out=F["b"][:, :n], data0=cret[:, CM:CM + n], data1=F["f"][:, :n], initial=0.0, op0=ALU.mult, op1=ALU.add),
                    r=[bF["f"]], w=[bF["b"]])
                bop(P, "pool", lambda e: e.tensor_tensor(
                    out=F["d1"][:, :n].rearrange("p (c s) -> p c s", s=CL), in0=F["b"][:, :n].rearrange("p (c s) -> p c s", s=CL),
                    in1=F["b"][:, :n].rearrange("p (c s) -> p c s", s=CL)[:, :, CL - 1:CL].broadcast_to([128, nch, CL]), op=ALU.subtract),
                    r=[bF["b"]], w=[bF["d1"]])

            def p4():
                bop(P, "act", lambda e: e.activation(out=F["X"][:, :n], in_=F["d1"][:, :n], func=AF.Exp), r=[bF["d1"]], w=[bF["X"]])
                bop(P, "act", lambda e: e.activation(out=F["Y"][:, :n], in_=F["d1"][:, :n], func=AF.Exp, scale=-1.0), r=[bF["d1"]], w=[bF["Y"]])
                bop(P, "act", lambda e: e.activation(out=eBj[:, :nch], in_=F["b"][:, :n].rearrange("p (c s) -> p c s", s=CL)[:, :, CL - 1], func=AF.Exp),
                    r=[bF["b"]], w=[beB[kp]])

            def p5():
                bop(P, "pool", lambda e: e.tensor_tensor(out=qcj[:, :n], in0=F["qs"][:, :n], in1=F["X"][:, :n], op=ALU.mult), r=[bF["qs"], bF["X"]], w=[bqc[kp]])
                bop(P, "pool", lambda e: e.tensor_tensor(out=kcj[:, :n], in0=F["kk"][:, :n], in1=F["Y"][:, :n], op=ALU.mult), r=[bF["kk"], bF["Y"]], w=[bkc[kp]])

            return [p0, p1, p1b, p2, p3, p4, p5]

        def make_blocks(h, ti, kp):
            sl = h % 2
            t0, n = TT[ti]
            sample = (ti == 4)
            CL = 4 if sample else 32
            qcj, kcj, eBj = qc[kp], kc[kp], eB[kp]
            blocks = [(0, 64)] if sample else [(c * 128, 128) for c in range(4)]
            MC = CH_MS if sample else CH_MP
            nbc = 16 if sample else 4
            if sample:
                bmask = C.bm
                rmv = cret[:64, CR_RM:CR_RM + 16]
            else:
                bmask = C.bmp
                rmv = cret[:, CH_RMP:CH_RMP + 4]
            ubase = {}

            def A_pe(ci):
                c0, nb = blocks[ci]
                pb = ci % 2
                a0 = t0 + c0
                for k in range(8):
                    bop(P, "pe", lambda e, k=k: e.matmul(VG[pb][:nb, :], lhsT=C.xn[:, k, a0:a0 + nb], rhs=wh[sl][:, k, 256:512], start=(k == 0), stop=(k == 7)),
                        r=[bwh[sl]], w=[bVG[pb]])
                bop(P, "pe", lambda e: e.transpose(out=PT[pb][:nb, :], in_=kcj[:, c0:c0 + nb], identity=C.ident), r=[bkc[kp]], w=[bPT[pb]])
                bop(P, "pe", lambda e: e.matmul(SC[pb][:nb, :nb], lhsT=kcj[:, c0:c0 + nb], rhs=qcj[:, c0:c0 + nb], start=True, stop=True), r=[bkc[kp], bqc[kp]], w=[bSC[pb]])

            def A_ev1(ci):
                nonlocal ugl
                c0, nb = blocks[ci]
                pb = ci % 2
                bop(P, "dve", lambda e: e.tensor_copy(out=ktl[pb][:nb, :], in_=PT[pb][:nb, :]), r=[bPT[pb]], w=[bktl[pb]])
                bop(P, "act", lambda e: e.activation(out=vtok[pb][:nb, :], in_=VG[pb][:nb, 0:128], func=AF.Copy), r=[bVG[pb]], w=[bvtok[pb]])
                kzv = kz[pb][:nb, 0:nbc * 128].rearrange("p (c d) -> p c d", c=nbc)
                bop(P, "pool", lambda e: e.tensor_tensor(out=kzv, in0=ktl[pb][:nb, :].unsqueeze(1).broadcast_to([nb, nbc, 128]), in1=rmv.unsqueeze(2).broadcast_to([nb, nbc, 128]), op=ALU.mult),
                    r=[bktl[pb]], w=[bkz[pb]])
                ubase[ci] = ugl
                if nbc <= 4:
                    for c in range(nbc):
                        u = 4 * pb + c
                        bop(P, "pe", lambda e, c=c, u=u: e.matmul(UR[u], lhsT=kzv[:, c, :], rhs=vtok[pb][:nb, :], start=True, stop=True),
                            r=[bkz[pb], bvtok[pb]], w=[bUR[u]])
                ugl += nbc

            def A_ev2(ci):
                c0, nb = blocks[ci]
                pb = ci % 2
                qzv = qz[pb][:, 0:nbc * nb].rearrange("p (c t) -> p c t", c=nbc)
                bop(P, "pool", lambda e: e.tensor_tensor(out=qzv, in0=qcj[:, c0:c0 + nb].unsqueeze(1).broadcast_to([128, nbc, nb]), in1=bmask, op=ALU.mult),
                    r=[bqc[kp]], w=[bqz[pb]])
                bop(P, "dve", lambda e: e.tensor_tensor(out=scm[pb][:nb, :nb], in0=SC[pb][:nb, :nb], in1=cret[:nb, MC:MC + nb], op=ALU.mult), r=[bSC[pb]], w=[bscm[pb]])
                sigmoid_act(gsil[pb][:nb, :], VG[pb][:nb, 128:256], bgsil[pb], bVG[pb])
                bop(P, "dve", lambda e: e.tensor_tensor(out=gsil[pb][:nb, :], in0=VG[pb][:nb, 128:256], in1=gsil[pb][:nb, :], op=ALU.mult), r=[bgsil[pb], bVG[pb]], w=[bgsil[pb]])

            def B_chain(ci):
                c0, nb = blocks[ci]
                pb = ci % 2
                kzv = kz[pb][:nb, 0:nbc * 128].rearrange("p (c d) -> p c d", c=nbc)
                for c in range(nbc):
                    ec = (c0 // CL + c)
                    u = (4 * pb + c) if nbc <= 4 else (4 * (c % 2) + (c // 2) % 4)
                    Sin = S0[:, c, :] if sample else S[:, :]
                    bSin = bS0 if sample else bS
                    if nbc > 4:
                        bop(P, "pe", lambda e, c=c, u=u: e.matmul(UR[u], lhsT=kzv[:, c, :], rhs=vtok[pb][:nb, :], start=True, stop=True),
                            r=[bkz[pb], bvtok[pb]], w=[bUR[u]])
                    bop(P, "dve", lambda e, c=c, ec=ec, Sin=Sin: e.tensor_scalar(out=Sdb[pb][:, c, :], in0=Sin, scalar1=eBj[:, ec:ec + 1], scalar2=None, op0=ALU.mult),
                        r=[bSin, beB[kp]], w=[bSdb[pb]])
                    bop(P, "dve", lambda e, ec=ec, u=u, Sin=Sin: e.scalar_tensor_tensor(out=Sin, in0=Sin, scalar=eBj[:, ec:ec + 1], in1=UR[u], op0=ALU.mult, op1=ALU.add),
                        r=[bUR[u], beB[kp]], w=[bSin])

            def B_pe(ci):
                c0, nb = blocks[ci]
                pb = ci % 2
                qzv = qz[pb][:, 0:nbc * nb].rearrange("p (c t) -> p c t", c=nbc)
                bop(P, "pe", lambda e: e.matmul(B[2][:nb, 0:128], lhsT=scm[pb][:nb, :nb], rhs=vtok[pb][:nb, :], start=True, stop=False), r=[bscm[pb], bvtok[pb]], w=[bB[2]])
                for c in range(nbc):
                    bop(P, "pe", lambda e, c=c: e.matmul(B[2][:nb, 0:128], lhsT=qzv[:, c, :], rhs=Sdb[pb][:, c, :], start=False, stop=(c == nbc - 1)),
                        r=[bqz[pb], bSdb[pb]], w=[bB[2]])

            def B_norm(ci):
                c0, nb = blocks[ci]
                pb = ci % 2
                bop(P, "act", lambda e: e.activation(out=on[:nb, :], in_=B[2][:nb, 0:128], func=AF.Square, accum_out=st4[:nb, 0:1]), r=[bB[2]], w=[bon, bst4])
                bop(P, "act", lambda e: e.activation(out=st4[:nb, 2:3], in_=st4[:nb, 0:1], func=AF.Ln, scale=1.0 / 128, bias=C.epsc[:nb, 0:1]), r=[bst4], w=[bst4])
                bop(P, "act", lambda e: e.activation(out=st4[:nb, 3:4], in_=st4[:nb, 2:3], func=AF.Exp, scale=-0.5), r=[bst4], w=[bst4])
                bop(P, "dve", lambda e: e.scalar_tensor_tensor(out=on[:nb, :], in0=B[2][:nb, 0:128], scalar=st4[:nb, 3:4], in1=gn[:nb, :], op0=ALU.mult, op1=ALU.mult),
                    r=[bB[2], bst4, bgn], w=[bon])
                bop(P, "pool", lambda e: e.tensor_tensor(out=go[pb][:nb, :], in0=on[:nb, :], in1=gsil[pb][:nb, :], op=ALU.mult), r=[bon, bgsil[pb]], w=[bgo[pb]])

            def stage_T(ci):
                c0, nb = blocks[ci]
                pb = ci % 2
                bop(P, "pe", lambda e: e.transpose(out=TR[:, :nb], in_=go[pb][:nb, :], identity=C.ident[:nb, :nb]), r=[bgo[pb]], w=[bTR])
                bop(P, "act", lambda e: e.activation(out=goT[:, c0:c0 + nb], in_=TR[:, :nb], func=AF.Copy), r=[bTR], w=[bgoT])

            def wout():
                for m in range(8):
                    wb = 3 if m % 2 == 0 else 2
                    bop(P, "pe", lambda e, m=m, wb=wb: e.matmul(B[wb][:, :n], lhsT=wo[sl][:, m * 128:(m + 1) * 128], rhs=goT[:, :n], start=True, stop=True),
                        r=[bwo[sl], bgoT], w=[bB[wb]])
                    bop(P, "dve", lambda e, m=m, wb=wb: e.tensor_tensor(out=C.xres[:, m, t0:t0 + n], in0=C.xres[:, m, t0:t0 + n], in1=B[wb][:, :n], op=ALU.add),
                        r=[bB[wb]], w=[bx[m][ti]])
                if ti == 3:
                    bop(P, "sp", lambda e: e.dma_start(out=st_out_p[h], in_=S[:, :]), r=[bS], slot=s_sp)
                if sample:
                    bop(P, "sp", lambda e: e.dma_start(out=st_out_s[:, h, :, :].rearrange("i d e -> d i e"), in_=S0[:, :, :]), r=[bS0], slot=s_so)

            mk = lambda fn, ci: (lambda: fn(ci))
            if len(blocks) == 1:
                sched = [mk(A_pe, 0), mk(A_ev1, 0), mk(A_ev2, 0), mk(B_chain, 0), mk(B_pe, 0), mk(B_norm, 0), mk(stage_T, 0)]
                slots_after = {2: [0, 1, 2], 4: [3, 4], 6: [5, 6]}
            else:
                sched = [mk(A_pe, 0), mk(A_ev1, 0), mk(A_ev2, 0), mk(A_pe, 1),
                         mk(A_ev1, 1), mk(B_chain, 0), mk(B_pe, 0), mk(A_ev2, 1), mk(B_norm, 0), mk(A_pe, 2),
                         mk(A_ev1, 2), mk(B_chain, 1), mk(B_pe, 1), mk(A_ev2, 2), mk(B_norm, 1), mk(stage_T, 0), mk(A_pe, 3),
                         mk(A_ev1, 3), mk(B_chain, 2), mk(B_pe, 2), mk(A_ev2, 3), mk(B_norm, 2), mk(stage_T, 1),
                         mk(B_chain, 3), mk(B_pe, 3), mk(B_norm, 3), mk(stage_T, 2), mk(stage_T, 3)]
                slots_after = {3: [0], 6: [1], 9: [2], 16: [3], 22: [4, 5], 26: [6]}
            return sched, slots_after, wout

        for p in make_s1(jobs[0][0], jobs[0][1], 0):
            p()
        for kj, (h, ti) in enumerate(jobs):
            kp = kj % 2
            if ti == 0:
                head_setup(h)
            sched, slots_after, wout = make_blocks(h, ti, kp)
            nxt = make_s1(jobs[kj + 1][0], jobs[kj + 1][1], 1 - kp) if kj + 1 < len(jobs) else None
            for si, stage in enumerate(sched):
                stage()
                if nxt is not None:
                    for pi in slots_after.get(si, []):
                        nxt[pi]()
            wout()
        P.flush()


def build_program(cfg):
    nc = bass.Bass("TRN2", target_bir_lowering=False)
    dr = lambda name, shape, kind="ExternalInput", dt=F32: nc.dram_tensor(name, shape, dt, kind=kind).ap()
    xT = dr("xT", [128, 8, NTOK])
    gains_d = dr("gains", [128, 13 * 8])
    cbf_d = dr("cbf", [128, 256 + 1024 + 512], dt=BF16)
    cret_d = dr("cret", [128, CRW2])
    cs_d = dr("cs", [2, 128, NTOK])
    wup_d = dr("wup", [8, NF, 128, 2048])
    wdn_d = dr("wdn", [8, 2, 128, 11 * 1024])
    rwin_d = dr("rwin", [2, 4, 128, 12288])
    rwout_d = dr("rwout", [2, 4, 128, 4096])
    rnorm_d = dr("rnorm", [2, 4, 512])
    sret_d = dr("sret", [2, NSS, 4, 256, 512])
    hwin_d = dr("hwin", [2, 8, 128, 4096])
    hwout_d = dr("hwout", [2, 8, 128, 1024])
    hnorm_d = dr("hnorm", [2, 8, 128])
    lbl_d = dr("lbl", [128, 2, 8])
    shg_d = dr("shg", [2, NSS, 8, 128, 128])
    nhp_d = dr("nhp", [2, 8, 128, 128], kind="ExternalOutput")
    nhs_d = dr("nhs", [2, NSS, 8, 128, 128], kind="ExternalOutput")
    yT = dr("yT", [128, 8, NTOK], kind="ExternalOutput")
    nrp_d = dr("nrp", [2, 4, 256, 512], kind="ExternalOutput")
    nrs_d = dr("nrs", [2, NSS, 4, 256, 512], kind="ExternalOutput")

    with ExitStack() as st:
        P = Prog(nc, st)
        C = Ctx()
        C.P, C.nc = P, nc
        C.dbg = None
        if cfg.get("debug"):
            C.dbg = {"want": set(cfg["debug"]), "seen": {}, "off": {"f": 0, "b": 0}, "slot": P.slot(),
                     "f": dr("dbgf", [128, 8192], kind="ExternalOutput"), "b": dr("dbgb", [128, 8192], kind="ExternalOutput", dt=BF16)}
        cfg["_dbg"] = C.dbg
        sb = lambda name, shape, dt: st.enter_context(nc.sbuf_tensor(name, shape, dt))
        C.xres = sb("xres", [128, 8, NTOK], F32)
        C.xn = sb("xn", [128, 8, NTOK], BF16)
        C.gains = sb("gains_sb", [128, 13 * 8], F32)
        cbf = sb("cbf_sb", [128, 256 + 1024 + 512], BF16)
        C.ident = cbf[:, 0:128]
        C.ones = cbf[:, 128:256]
        C.bm = cbf[:, 256:1280].rearrange("p (a t) -> p a t", a=16)
        C.bmp = cbf[:, 1280:1792].rearrange("p (a t) -> p a t", a=4)
        C.epsc = sb("epsc", [128, 2], F32)
        C.one = sb("onec", [128, 2], F32)
        C.cret = sb("cret_sb", [128, CRW2], F32)

        s_in = P.slot()
        for k in range(8):
            P.dma("sp", lambda e, k=k: e.dma_start(out=C.xres[:, k, :], in_=xT[:, k, :]), s_in)
        P.dma("sp", lambda e: e.dma_start(out=C.gains[:], in_=gains_d), s_in)
        P.dma("sp", lambda e: e.dma_start(out=cbf[:], in_=cbf_d), s_in)
        P.dma("sp", lambda e: e.dma_start(out=C.cret[:], in_=cret_d), s_in)
        P.op("pool", lambda e: e.memset(C.epsc[:], EPS))
        P.op("pool", lambda e: e.memset(C.one[:], 1.0))
        P.flush()

        for blk in cfg["blocks"]:
            if blk[0] == "ffn":
                _, l, i = blk
                emit_ffn(C, l * 3 + (0 if i == 0 else 2), wup_d[l * 2 + i], wdn_d[l * 2 + i])
            elif blk[0] == "ret":
                _, l = blk
                j = l // 2
                emit_normphase(C, l * 3 + 1)
                emit_ret(C, j, rwin_d[j], rwout_d[j], rnorm_d[j], cs_d, sret_d[j], nrp_d[j], nrs_d[j])
            elif blk[0] == "hg":
                _, l = blk
                j = l // 2
                emit_normphase(C, l * 3 + 1)
                emit_hg(C, j, hwin_d[j], hwout_d[j], hnorm_d[j], lbl_d, shg_d[j], nhp_d[j], nhs_d[j])

        with ExitStack() as st2:
            sb2, pt2 = mk_alloc(C, st2)
            ph = Ctx()
            ph.sq = [sb2("sq%d" % i, [128, 8, 512], BF16) for i in range(2)]
            ph.rs = [sb2("rs%d" % i, [128, 512], F32) for i in range(2)]
            ph.psn = pt2("psn", [128, 512], F32)
            yo = [sb2("yo%d" % i, [128, 8, 512], F32) for i in range(2)]
            s_out = [P.slot(), P.slot()]
            yo_rd = [None, None]
            if cfg.get("final_norm", True):
                def out_fn2(ti, k, t0, n, rsb, r):
                    b = ti % 2
                    o = P.op("dve", lambda e: e.scalar_tensor_tensor(
                        out=yo[b][:, k, :n], in0=C.xres[:, k, t0:t0 + n], scalar=C.gains[:, 96 + k:96 + k + 1],
                        in1=rsb[:, :n], op0=ALU.mult, op1=ALU.mult), [r, yo_rd[b]])
                    if k == 7:
                        yo_rd[b] = P.dma("sp", lambda e: e.dma_start(out=yT[:, :, t0:t0 + n], in_=yo[b][:, :, :n]), s_out[b], [o])
                    return o
                emit_norm(C, ph, 12, out_fn2)
            else:
                for k in range(8):
                    P.dma("sp", lambda e, k=k: e.dma_start(out=yT[:, k, :], in_=C.xres[:, k, :]), s_out[0])
            P.flush()
    return nc


def host_consts():
    import ml_dtypes
    c = np.zeros((128, 256 + 1024 + 512), np.float32)
    c[:, 0:128] = np.eye(128, dtype=np.float32)
    c[:, 128:256] = 1.0
    bm = np.zeros((16, 64), np.float32)
    for i in range(16):
        bm[i, 4 * i:4 * i + 4] = 1.0
    c[:, 256:1280] = bm.reshape(1, 1024)
    bmp = np.zeros((4, 128), np.float32)
    for i in range(4):
        bmp[i, 32 * i:32 * i + 32] = 1.0
    c[:, 1280:1792] = bmp.reshape(1, 512)
    out = {"cbf": c.astype(ml_dtypes.bfloat16)}
    cr = np.zeros((128, CRW2), np.float64)
    t = np.arange(128)
    ts = np.arange(64)
    for h, g in enumerate(ret_gammas()):
        lg = np.log(np.float64(g))
        mp = np.where(t[:, None] <= t[None, :], np.exp(-(t[:, None] + 1.0) * lg), 0.0)
        cr[:, CR_MP + h * 128:CR_MP + (h + 1) * 128] = mp
        same = (ts[:, None] // 4) == (ts[None, :] // 4)
        ms = np.where(same & ((ts[:, None] % 4) <= (ts[None, :] % 4)), np.exp(-((ts[:, None] % 4) + 1.0) * lg), 0.0)
        cr[:64, CR_MS + h * 64:CR_MS + (h + 1) * 64] = ms
        cr[:, CR_KDP + h] = np.exp((127.0 - t) * lg)
        cr[:, CR_EPP + h] = EPS * np.exp(-2.0 * (t + 1.0) * lg)
        cr[:64, CR_KDS + h] = np.exp((3.0 - (ts % 4)) * lg)
        cr[:64, CR_EPS + h] = EPS * np.exp(-2.0 * ((ts % 4) + 1.0) * lg)
    for i in range(16):
        cr[4 * i:4 * i + 4, CR_RM + i] = 1.0
    cr[:, CH_MP:CH_MP + 128] = ((t[:, None] // 32) == (t[None, :] // 32)) & (t[:, None] <= t[None, :])
    cr[:64, CH_MS:CH_MS + 64] = ((ts[:, None] // 4) == (ts[None, :] // 4)) & (ts[:, None] <= ts[None, :])
    for i in range(4):
        cr[32 * i:32 * i + 32, CH_RMP + i] = 1.0
    cr[:, CH_CMP:CH_CMP + 512] = (np.arange(512) % 32 != 0)[None, :]
    cr[:, CH_CMS:CH_CMS + 64] = (np.arange(64) % 4 != 0)[None, :]
    out["cret"] = cr.astype(np.float32)
    half = 128
    inv_freq = (np.float32(10000.0) ** (-np.arange(half, dtype=np.float32) / np.float32(half))).astype(np.float32)
    pos = np.concatenate([np.arange(SEQ, dtype=np.float32), np.tile(np.float32(16384.0) + np.arange(DEC, dtype=np.float32), NSS)])
    ang = (pos[None, :] * inv_freq[:, None]).astype(np.float32)
    out["cs"] = np.stack([np.cos(ang), np.sin(ang)]).astype(np.float32)
    return out


def host_weights(inp):
    w = {}
    f32 = lambda a: np.asarray(a, np.float32)
    up = f32(inp["ffn_w_up"]).reshape(8, 8, 128, 2, NF, 128)
    w["wup"] = np.ascontiguousarray(up.transpose(0, 4, 2, 1, 3, 5)).reshape(8, NF, 128, 2048)
    dn = f32(inp["ffn_w_down"]).reshape(8, 2, 11, 128, 1024)
    w["wdn"] = np.ascontiguousarray(dn.transpose(0, 1, 3, 2, 4)).reshape(8, 2, 128, 11 * 1024)
    g = np.concatenate([f32(inp["norm_gain"]).reshape(12, 1024), f32(inp["final_norm"]).reshape(1, 1024)], 0)
    w["gains"] = np.ascontiguousarray(g.reshape(13, 8, 128).transpose(2, 0, 1)).reshape(128, 104)
    wi = f32(inp["ret_w_in"]).reshape(2, 8, 128, 6144)
    parts = []
    for h in range(4):
        parts.append(np.concatenate([wi[..., h * 256:(h + 1) * 256], wi[..., 1024 + h * 256:1024 + (h + 1) * 256],
                                     wi[..., 2048 + h * 512:2048 + (h + 1) * 512], wi[..., 4096 + h * 512:4096 + (h + 1) * 512]], -1))
    wih = np.stack(parts, 1)
    w["rwin"] = np.ascontiguousarray(wih.transpose(0, 1, 3, 2, 4)).reshape(2, 4, 128, 12288)
    wo = f32(inp["ret_w_out"]).reshape(2, 4, 4, 128, 1024)
    w["rwout"] = np.ascontiguousarray(wo.transpose(0, 1, 3, 2, 4)).reshape(2, 4, 128, 4096)
    w["rnorm"] = f32(inp["ret_norm"])
    hi = f32(inp["hg_w_in"]).reshape(2, 8, 128, 4, 8, 128)
    w["hwin"] = np.ascontiguousarray(hi.transpose(0, 4, 2, 1, 3, 5)).reshape(2, 8, 128, 4096)
    w["hwout"] = np.ascontiguousarray(f32(inp["hg_w_out"]).reshape(2, 8, 128, 1024))
    w["hnorm"] = f32(inp["hg_norm"])
    w["lbl"] = np.ascontiguousarray(f32(inp["hg_lb_logits"]).reshape(2, 8, 128).transpose(2, 0, 1))
    return w


def host_core_inputs(inp, c):
    xp = np.asarray(inp["x_prompt"], np.float32)[c]
    xs = np.asarray(inp["x_sample"], np.float32)[c * NSS:(c + 1) * NSS].reshape(NSS * DEC, D)
    x = np.concatenate([xp, xs], 0)
    xT = np.ascontiguousarray(x.T.reshape(8, 128, NTOK).transpose(1, 0, 2))
    m = {"xT": xT}
    m["sret"] = np.ascontiguousarray(np.asarray(inp["state_ret"], np.float32)[:, c * NSS:(c + 1) * NSS])
    m["shg"] = np.ascontiguousarray(np.asarray(inp["state_hgrn"], np.float32)[:, c * NSS:(c + 1) * NSS])
    return m


def full_cfg():
    blocks = []
    for l in range(4):
        blocks.append(("ffn", l, 0))
        blocks.append(("ret", l) if l % 2 == 0 else ("hg", l))
        blocks.append(("ffn", l, 1))
    return {"blocks": blocks, "final_norm": True}


def kernel(**inputs):
    nc = build_program(full_cfg())
    shared = {}
    shared.update(host_consts())
    shared.update(host_weights(inputs))
    in_maps = []
    for c in range(8):
        m = dict(shared)
        m.update(host_core_inputs(inputs, c))
        in_maps.append(m)
    res = run_bass_kernel_spmd(nc, in_maps, core_ids=list(range(8)))
    rs = res.results
    y_prompt = np.empty((8, SEQ, D), np.float32)
    y_sample = np.empty((8 * NSS, DEC, D), np.float32)
    nrp = np.empty((2, 8, 4, 256, 512), np.float32)
    nrs = np.empty((2, 8 * NSS, 4, 256, 512), np.float32)
    nhp = np.empty((2, 8, 8, 128, 128), np.float32)
    nhs = np.empty((2, 8 * NSS, 8, 128, 128), np.float32)
    for c in range(8):
        r = rs[c]
        y = np.asarray(r["yT"]).transpose(1, 0, 2).reshape(D, NTOK).T
        y_prompt[c] = y[:SEQ]
        y_sample[c * NSS:(c + 1) * NSS] = y[SEQ:].reshape(NSS, DEC, D)
        nrp[:, c] = r["nrp"]
        nrs[:, c * NSS:(c + 1) * NSS] = r["nrs"]
        nhp[:, c] = r["nhp"]
        nhs[:, c * NSS:(c + 1) * NSS] = r["nhs"]
    return (y_prompt, y_sample, nrp, nrs, nhp, nhs)
```
